# Optimizing a Trainium2 kernel written in Bass

```python
import math
import jax, jax.numpy as jnp
from jax import lax
import numpy as np

D_MODEL = 1024
BATCH = 4
SEQ = 4096
DEPTH = 2

D_MIX = D_MODEL
RET_WIDTH = D_MIX // 4
SSM_WIDTH = D_MIX // 4
ATT_WIDTH = D_MIX // 2
RET_DK = 64
RET_DV = 64
RET_HEADS = RET_WIDTH // RET_DV
RET_CHUNK = 128
SSM_CPG = 16
SSM_GROUPS = SSM_WIDTH // SSM_CPG
SSM_STATE = 64
ATT_HEAD_DIM = 64
ATT_HEADS = ATT_WIDTH // ATT_HEAD_DIM
ATT_KV_HEADS = 2
ATT_GQ = ATT_HEADS // ATT_KV_HEADS
KV_WIDTH = ATT_KV_HEADS * ATT_HEAD_DIM
ATT_WINDOW = 128
ATT_BLOCK = 128
D_IN = 4 * RET_WIDTH + SSM_WIDTH + ATT_WIDTH + 2 * KV_WIDTH
N_EXPERTS = 16
EXPERT_FF = 2 * D_MODEL
EC_FACTOR = 2
DEEPNORM_ALPHA = (2.0 * DEPTH) ** 0.25
DEEPNORM_BETA = (8.0 * DEPTH) ** -0.25
LN_EPS = 1e-5
NEG_INF = -1e30

kernel_name = "hybrid_retention_s5_swa_ec_encoder"


def _split_points():
    sizes = (RET_WIDTH, RET_WIDTH, RET_WIDTH, RET_WIDTH, SSM_WIDTH, ATT_WIDTH, KV_WIDTH, KV_WIDTH)
    points, acc = [], 0
    for s in sizes[:-1]:
        acc += s
        points.append(acc)
    return points


def layer_norm(x, g, b):
    xf = x.astype(jnp.float32)
    mu = jnp.mean(xf, axis=-1, keepdims=True)
    var = jnp.mean(jnp.square(xf - mu), axis=-1, keepdims=True)
    y = (xf - mu) * lax.rsqrt(var + LN_EPS) * g.astype(jnp.float32) + b.astype(jnp.float32)
    return y.astype(x.dtype)


def retention(q, k, v, g, theta):
    f32 = jnp.float32
    bsz, seq, _ = q.shape
    nc = seq // RET_CHUNK
    log_gamma = jax.nn.log_sigmoid(theta.astype(f32))
    lg_f, lg_b = log_gamma[0], log_gamma[1]
    shp = (bsz, nc, RET_CHUNK, RET_HEADS, RET_DK)
    qc = q.astype(f32).reshape(shp)
    kc = k.astype(f32).reshape(shp) * RET_DK ** -0.5
    vc = v.astype(f32).reshape(bsz, nc, RET_CHUNK, RET_HEADS, RET_DV)
    pos = jnp.arange(RET_CHUNK, dtype=f32)
    dist = pos[:, None] - pos[None, :]
    adist = jnp.abs(dist)
    dmat = jnp.where(dist >= 0, jnp.exp(lg_f[:, None, None] * adist),
                     jnp.exp(lg_b[:, None, None] * adist))
    scores = jnp.einsum('bcihd,bcjhd->bchij', qc, kc) * dmat
    inner = jnp.einsum('bchij,bcjhe->bcihe', scores, vc)
    w_f = jnp.exp(lg_f[None, :] * (RET_CHUNK - 1 - pos)[:, None])
    w_b = jnp.exp(lg_b[None, :] * pos[:, None])
    kv_f = jnp.einsum('bcjhd,bcjhe->cbhde', kc * w_f[:, :, None], vc)
    kv_b = jnp.einsum('bcjhd,bcjhe->cbhde', kc * w_b[:, :, None], vc)
    dec_f = jnp.exp(lg_f * RET_CHUNK)[None, :, None, None]
    dec_b = jnp.exp(lg_b * RET_CHUNK)[None, :, None, None]
    zero = jnp.zeros((bsz, RET_HEADS, RET_DK, RET_DV), f32)
    _, state_f = lax.scan(lambda s, kv: (dec_f * s + kv, s), zero, kv_f)
    _, state_b = lax.scan(lambda s, kv: (dec_b * s + kv, s), zero, kv_b, reverse=True)
    q_f = qc * jnp.exp(lg_f[None, :] * (pos + 1.0)[:, None])[:, :, None]
    q_b = qc * jnp.exp(lg_b[None, :] * (RET_CHUNK - pos)[:, None])[:, :, None]
    out = (inner + jnp.einsum('bcihd,cbhde->bcihe', q_f, state_f)
           + jnp.einsum('bcihd,cbhde->bcihe', q_b, state_b))
    mu = jnp.mean(out, axis=-1, keepdims=True)
    var = jnp.mean(jnp.square(out - mu), axis=-1, keepdims=True)
    out = ((out - mu) * lax.rsqrt(var + LN_EPS)).reshape(bsz, seq, RET_WIDTH)
    return (jax.nn.silu(g.astype(f32)) * out).astype(q.dtype)


def _ssm_combine(e1, e2):
    a1, b1 = e1
    a2, b2 = e2
    return a2 * a1, a2 * b1 + b2


def s5_mixer(u, lam_re, lam_im, log_step, b_re, b_im, c_re, c_im, d, w_glu, b_glu):
    f32 = jnp.float32
    bsz, seq, _ = u.shape
    lam = lax.complex(lam_re.astype(f32), lam_im.astype(f32))
    step = jnp.exp(log_step.astype(f32))[..., None]
    lam_bar = jnp.exp(lam * step)
    b = lax.complex(b_re.astype(f32), b_im.astype(f32))
    b_bar = ((lam_bar - 1.0) / lam)[..., None] * b[None]
    uf = u.astype(f32)
    ug = uf.reshape(bsz, seq, SSM_GROUPS, SSM_CPG).astype(jnp.complex64)
    bu = jnp.einsum('blgc,rgpc->rblgp', ug, b_bar)
    bu = jnp.stack([bu[0], bu[1][:, ::-1]])
    a = jnp.broadcast_to(lam_bar[:, None, None], bu.shape)
    _, states = lax.associative_scan(_ssm_combine, (a, bu), axis=2)
    states = jnp.stack([states[0], states[1][:, ::-1]])
    c = lax.complex(c_re.astype(f32), c_im.astype(f32))
    y = jnp.einsum('rgcp,rblgp->blgc', c, states).real.reshape(bsz, seq, SSM_WIDTH)
    y = jax.nn.gelu(y + d.astype(f32) * uf)
    gate = jax.nn.sigmoid(y @ w_glu.astype(f32) + b_glu.astype(f32))
    return (y * gate).astype(u.dtype)


def window_sink_attention(q, k, v, sink):
    f32 = jnp.float32
    bsz, seq, _ = q.shape
    nc = seq // ATT_BLOCK
    qb = q.reshape(bsz, nc, ATT_BLOCK, ATT_KV_HEADS, ATT_GQ, ATT_HEAD_DIM) * ATT_HEAD_DIM ** -0.5
    pad = ((0, 0), (1, 1), (0, 0), (0, 0), (0, 0))
    kp = jnp.pad(k.reshape(bsz, nc, ATT_BLOCK, ATT_KV_HEADS, ATT_HEAD_DIM), pad)
    vp = jnp.pad(v.reshape(bsz, nc, ATT_BLOCK, ATT_KV_HEADS, ATT_HEAD_DIM), pad)
    kb = jnp.concatenate([kp[:, :-2], kp[:, 1:-1], kp[:, 2:]], axis=2)
    vb = jnp.concatenate([vp[:, :-2], vp[:, 1:-1], vp[:, 2:]], axis=2)
    scores = jnp.einsum('bcqkgd,bcskd->bckgqs', qb, kb).astype(f32)
    span = jnp.arange(3 * ATT_BLOCK)
    rel = span[None, :] - ATT_BLOCK - jnp.arange(ATT_BLOCK)[:, None]
    spos = jnp.arange(nc)[:, None] * ATT_BLOCK - ATT_BLOCK + span[None, :]
    valid = (jnp.abs(rel) <= ATT_WINDOW)[None] & ((spos >= 0) & (spos < seq))[:, None, :]
    slopes = jnp.exp2(-8.0 * jnp.arange(1, ATT_HEADS + 1, dtype=f32) / ATT_HEADS)
    slopes = slopes.reshape(ATT_KV_HEADS, ATT_GQ)[None, None, :, :, None, None]
    scores = scores - slopes * jnp.abs(rel).astype(f32)
    scores = jnp.where(valid[None, :, None, None], scores, NEG_INF)
    sink_f = sink.astype(f32).reshape(ATT_KV_HEADS, ATT_GQ)[None, None, :, :, None, None]
    m = jnp.maximum(jnp.max(scores, axis=-1, keepdims=True), sink_f)
    p = jnp.exp(scores - m)
    denom = jnp.sum(p, axis=-1, keepdims=True) + jnp.exp(sink_f - m)
    probs = (p / denom).astype(v.dtype)
    out = jnp.einsum('bckgqs,bcskd->bcqkgd', probs, vb)
    return out.reshape(bsz, seq, ATT_WIDTH)


def expert_choice_ffn(x, router_w, w_gate, w_up, w_down):
    bsz, seq, _ = x.shape
    cap = EC_FACTOR * seq // N_EXPERTS
    logits = jnp.einsum('bld,de->ble', x, router_w).astype(jnp.float32)
    aff = jax.nn.softmax(logits, axis=-1)
    gates, idx = lax.top_k(jnp.swapaxes(aff, 1, 2), cap)
    bidx = jnp.arange(bsz)[:, None, None]
    xs = x[bidx, idx]
    hdn = jax.nn.silu(jnp.einsum('becd,edf->becf', xs, w_gate)) * jnp.einsum('becd,edf->becf', xs, w_up)
    out = jnp.einsum('becf,efd->becd', hdn, w_down) * gates[..., None]
    return jnp.zeros_like(x).at[bidx, idx].add(out.astype(x.dtype))


def setup_inputs(seed: int = 0) -> dict:
    key = jax.random.key(seed)
    ks = jax.random.split(key, 26)
    f32 = jnp.float32

    def nrm(k, shape, scale):
        return scale * jax.random.normal(k, shape, f32)

    x = nrm(ks[0], (BATCH, SEQ, D_MODEL), 1.0)
    ln_in_g = 1.0 + nrm(ks[1], (D_MODEL,), 0.02)
    ln_in_b = nrm(ks[2], (D_MODEL,), 0.02)
    col_scale = jnp.concatenate([
        jnp.ones((2 * RET_WIDTH,), f32), jnp.full((RET_WIDTH,), DEEPNORM_BETA, f32),
        jnp.ones((RET_WIDTH + SSM_WIDTH + ATT_WIDTH + KV_WIDTH,), f32),
        jnp.full((KV_WIDTH,), DEEPNORM_BETA, f32)])
    w_in = nrm(ks[3], (DEPTH, D_MODEL, D_IN), D_MODEL ** -0.5) * col_scale
    p_decay = 1.0 - jnp.exp2(-5.0 - jnp.arange(RET_HEADS, dtype=f32))
    ret_theta = jnp.log(p_decay / (1.0 - p_decay))[None, None, :] + nrm(ks[4], (DEPTH, 2, RET_HEADS), 0.05)
    ssm_lambda_re = -0.5 + nrm(ks[5], (DEPTH, 2, SSM_GROUPS, SSM_STATE), 0.01)
    ssm_lambda_im = jnp.pi * jnp.arange(SSM_STATE, dtype=f32) + nrm(ks[6], (DEPTH, 2, SSM_GROUPS, SSM_STATE), 0.01)
    ssm_log_step = jax.random.uniform(ks[7], (DEPTH, 2, SSM_GROUPS), f32, math.log(1e-3), math.log(1e-1))
    ssm_b_re = nrm(ks[8], (DEPTH, SSM_GROUPS, SSM_STATE, SSM_CPG), (2.0 * SSM_CPG) ** -0.5)
    ssm_b_im = nrm(ks[9], (DEPTH, SSM_GROUPS, SSM_STATE, SSM_CPG), (2.0 * SSM_CPG) ** -0.5)
    ssm_c_re = nrm(ks[10], (DEPTH, 2, SSM_GROUPS, SSM_CPG, SSM_STATE), (4.0 * SSM_STATE) ** -0.5)
    ssm_c_im = nrm(ks[11], (DEPTH, 2, SSM_GROUPS, SSM_CPG, SSM_STATE), (4.0 * SSM_STATE) ** -0.5)
    ssm_d = nrm(ks[12], (DEPTH, SSM_WIDTH), 1.0)
    ssm_w_glu = nrm(ks[13], (DEPTH, SSM_WIDTH, SSM_WIDTH), SSM_WIDTH ** -0.5)
    ssm_b_glu = nrm(ks[14], (DEPTH, SSM_WIDTH), 0.02)
    attn_sink = nrm(ks[15], (DEPTH, ATT_HEADS), 1.0)
    w_out = nrm(ks[16], (DEPTH, D_MIX, D_MODEL), D_MIX ** -0.5 * DEEPNORM_BETA)
    ln1_g = 1.0 + nrm(ks[17], (DEPTH, D_MODEL), 0.02)
    ln1_b = nrm(ks[18], (DEPTH, D_MODEL), 0.02)
    router_w = nrm(ks[19], (DEPTH, D_MODEL, N_EXPERTS), D_MODEL ** -0.5)
    exp_w_gate = nrm(ks[20], (DEPTH, N_EXPERTS, D_MODEL, EXPERT_FF), D_MODEL ** -0.5)
    exp_w_up = nrm(ks[21], (DEPTH, N_EXPERTS, D_MODEL, EXPERT_FF), D_MODEL ** -0.5 * DEEPNORM_BETA)
    exp_w_down = nrm(ks[22], (DEPTH, N_EXPERTS, EXPERT_FF, D_MODEL), EXPERT_FF ** -0.5 * DEEPNORM_BETA)
    ln2_g = 1.0 + nrm(ks[23], (DEPTH, D_MODEL), 0.02)
    ln2_b = nrm(ks[24], (DEPTH, D_MODEL), 0.02)
    return {"x": x, "ln_in_g": ln_in_g, "ln_in_b": ln_in_b, "w_in": w_in, "ret_theta": ret_theta,
            "ssm_lambda_re": ssm_lambda_re, "ssm_lambda_im": ssm_lambda_im, "ssm_log_step": ssm_log_step,
            "ssm_b_re": ssm_b_re, "ssm_b_im": ssm_b_im, "ssm_c_re": ssm_c_re, "ssm_c_im": ssm_c_im,
            "ssm_d": ssm_d, "ssm_w_glu": ssm_w_glu, "ssm_b_glu": ssm_b_glu, "attn_sink": attn_sink,
            "w_out": w_out, "ln1_g": ln1_g, "ln1_b": ln1_b, "router_w": router_w,
            "exp_w_gate": exp_w_gate, "exp_w_up": exp_w_up, "exp_w_down": exp_w_down,
            "ln2_g": ln2_g, "ln2_b": ln2_b}


def reference(x, ln_in_g, ln_in_b, w_in, ret_theta, ssm_lambda_re, ssm_lambda_im, ssm_log_step,
              ssm_b_re, ssm_b_im, ssm_c_re, ssm_c_im, ssm_d, ssm_w_glu, ssm_b_glu, attn_sink,
              w_out, ln1_g, ln1_b, router_w, exp_w_gate, exp_w_up, exp_w_down, ln2_g, ln2_b):
    h = layer_norm(x, ln_in_g, ln_in_b)
    for l in range(DEPTH):
        proj = jnp.einsum('bld,de->ble', h, w_in[l])
        rq, rk, rv, rg, su, aq, ak, av = jnp.split(proj, _split_points(), axis=-1)
        y_ret = retention(rq, rk, rv, rg, ret_theta[l])
        y_ssm = s5_mixer(su, ssm_lambda_re[l], ssm_lambda_im[l], ssm_log_step[l], ssm_b_re[l], ssm_b_im[l],
                         ssm_c_re[l], ssm_c_im[l], ssm_d[l], ssm_w_glu[l], ssm_b_glu[l])
        y_att = window_sink_attention(aq, ak, av, attn_sink[l])
        mix = jnp.einsum('ble,ed->bld', jnp.concatenate([y_ret, y_ssm, y_att], axis=-1), w_out[l])
        h = layer_norm(DEEPNORM_ALPHA * h + mix, ln1_g[l], ln1_b[l])
        ffn = expert_choice_ffn(h, router_w[l], exp_w_gate[l], exp_w_up[l], exp_w_down[l])
        h = layer_norm(DEEPNORM_ALPHA * h + ffn, ln2_g[l], ln2_b[l])
    return h
```

```python
import math
import contextlib
import numpy as np
import concourse.bass as bass
import concourse.mybir as mybir
from concourse.bass_utils import run_bass_kernel_spmd

F32 = mybir.dt.float32
BF16 = mybir.dt.bfloat16
I32 = mybir.dt.int32
AF = mybir.ActivationFunctionType
ALU = mybir.AluOpType
AX = mybir.AxisListType

T = 4096
NT = T // 128
D = 1024
DEPTH = 2
ALPHA = (2.0 * DEPTH) ** 0.25
EPS = 1e-5
N_TM = 896
N_FM = 1408
NCORES = 4
NBA = 3

SAME_ENGINE_SYNC = True
DBG_O = 9
DBG_F = 9
DBG_CAST = 'mix'
DBG_NT = NT


class Prog:
    def __init__(self, nc, stack, ndma=32):
        self.nc = nc
        self.eng = {'pe': nc.tensor, 'act': nc.scalar, 'dve': nc.vector, 'pool': nc.gpsimd, 'sp': nc.sync}
        self.sem = {e: stack.enter_context(nc.semaphore('sem_' + e)) for e in ['pe', 'act', 'dve', 'pool']}
        self.cnt = {e: 0 for e in self.sem}
        self.nhw, self.nsw = ndma, 8
        self.dsem = [stack.enter_context(nc.semaphore('dsem%d' % i)) for i in range(self.nhw + self.nsw)]
        self.dcnt = [0] * (self.nhw + self.nsw)
        self.dnext = 0
        self.dnext_sw = 0
        self.waited = {e: {} for e in self.eng}
        self.res = {}
        self.nops = 0
        self.recent = {e: [] for e in self.eng}
        self.log = {e: [] for e in self.eng}

    def _semof(self, key):
        return self.sem[key] if isinstance(key, str) else self.dsem[key[1]]

    def _wait(self, e, tok):
        key, val = tok
        if self.waited[e].get(key, 0) >= val:
            return
        self.eng[e].wait_ge(self._semof(key), val)
        self.waited[e][key] = val
        self.log[e].append(('wait', key, val))

    def _deps(self, reads, writes):
        deps = []
        for r in reads:
            st = self.res.get(r)
            if st and st['w']:
                deps.append(st['w'])
        for w in writes:
            st = self.res.get(w)
            if st:
                if st['w']:
                    deps.append(st['w'])
                deps.extend(st['r'].items())
        return deps

    def _update(self, tok, reads, writes):
        for r in reads:
            st = self.res.setdefault(r, {'w': None, 'r': {}})
            if st['r'].get(tok[0], 0) < tok[1]:
                st['r'][tok[0]] = tok[1]
        for w in writes:
            self.res[w] = {'w': tok, 'r': {}}

    def op(self, e, fn, reads=(), writes=()):
        for tok in self._deps(reads, writes):
            if tok[0] == e and (e == 'pe' or not SAME_ENGINE_SYNC):
                continue
            self._wait(e, tok)
        inst = fn(self.eng[e])
        self.cnt[e] += 1
        inst.then_inc(self.sem[e], 1)
        self.log[e].append(('inc', e, 1))
        tok = (e, self.cnt[e])
        self._update(tok, reads, writes)
        self.nops += 1
        return tok

    def dma(self, q, fn, reads=(), writes=()):
        if q == 'pool':
            i = self.nhw + self.dnext_sw
            self.dnext_sw = (self.dnext_sw + 1) % self.nsw
        else:
            i = self.dnext
            self.dnext = (i + 1) % self.nhw
        deps = self._deps(reads, writes)
        if self.dcnt[i] > 0:
            deps.append((('d', i), self.dcnt[i]))
        rq = self.recent[q]
        if len(rq) >= 10:
            deps.append(rq.pop(0))
        for tok in deps:
            self._wait(q, tok)
        inst = fn(self.eng[q])
        self.dcnt[i] += 16
        inst.then_inc(self.dsem[i], 16)
        self.log[q].append(('inc', ('d', i), 16))
        tok = (('d', i), self.dcnt[i])
        self.recent[q].append(tok)
        self._update(tok, reads, writes)
        self.nops += 1
        return tok

    def barrier(self):
        toks = [(e, c) for e, c in self.cnt.items() if c > 0]
        toks += [(('d', i), c) for i, c in enumerate(self.dcnt) if c > 0]
        for e in self.eng:
            for tok in toks:
                self._wait(e, tok)

    def wait_all(self, e, keys):
        for r in keys:
            st = self.res.get(r)
            if st and st['w']:
                self._wait(e, st['w'])


def run_window(tile_gen, load, n, width):
    load(0)
    gens = []
    nxt = 0
    while nxt < n or gens:
        while len(gens) < width and nxt < n:
            if nxt + 1 < n:
                load(nxt + 1)
            gens.append(tile_gen(nxt))
            nxt += 1
        for g_ in list(gens):
            try:
                next(g_)
            except StopIteration:
                gens.remove(g_)


def dap(h, off, dims):
    return bass.AP(h, off, [list(d) for d in dims])


class K:
    pass


def build(debug=None, nlayers=DEPTH, stop_after=None, ext_in=None, phases=None):
    nc = bass.Bass("TRN2", target_bir_lowering=False)
    k = K()
    k.nc = nc
    k.debug = debug or []
    k.ext_in = ext_in or []
    phases = phases or ['a', 'r', 's', 't', 'o', 'f']

    def din(name, shape, dt=F32):
        return nc.dram_tensor(name, list(shape), dt, kind="ExternalInput")

    def dscr(name, shape, dt):
        kind = "ExternalOutput" if name in k.debug else ("ExternalInput" if name in k.ext_in else "Internal")
        return nc.dram_tensor(name, list(shape), dt, kind=kind)

    k.x = din("x", [T, D])
    k.ln_in_g = din("ln_in_g", [D]); k.ln_in_b = din("ln_in_b", [D])
    k.w_in = din("w_in", [DEPTH, D, 2304])
    k.out = nc.dram_tensor("out", [T, D], F32, kind="ExternalOutput")
    k.hA = dscr("hA", [T, D], F32)
    k.tm = dscr("tm", [T, N_TM], BF16)
    k.fm = dscr("fm", [N_FM, T], BF16)
    k.ycat = dscr("ycat", [T, 768], BF16)
    k.yssmT = dscr("yssmT", [256, T], BF16)
    k.ret_theta = din("ret_theta", [DEPTH, 8])
    k.attn_sink = din("attn_sink", [DEPTH, 8])
    k.w_out = din("w_out", [DEPTH, D, D])
    k.ln1_g = din("ln1_g", [DEPTH, D]); k.ln1_b = din("ln1_b", [DEPTH, D])
    k.ln2_g = din("ln2_g", [DEPTH, D]); k.ln2_b = din("ln2_b", [DEPTH, D])
    k.s_lre = din("s_lre", [DEPTH, 128, 16]); k.s_lim = din("s_lim", [DEPTH, 128, 16]); k.s_ls = din("s_ls", [DEPTH, 128, 16])
    k.s_bre = din("s_bre", [DEPTH, 128, 8, 16]); k.s_bim = din("s_bim", [DEPTH, 128, 8, 16])
    k.s_cre = din("s_cre", [DEPTH, 128, 16, 16]); k.s_cim = din("s_cim", [DEPTH, 128, 16, 16])
    k.s_d = din("s_d", [DEPTH, 128, 2]); k.s_bglu = din("s_bglu", [DEPTH, 128, 2])
    k.s_wglu = din("s_wglu", [DEPTH, 256, 256])
    k.yf = dscr("yf", [256, T], F32)
    k.router_w = din("router_w", [DEPTH, D, 16])
    k.w_gate = din("exp_w_gate", [DEPTH, 16, D, 2048])
    k.w_up = din("exp_w_up", [DEPTH, 16, D, 2048])
    k.w_down = din("exp_w_down", [DEPTH, 16, 2048, D])
    k.ffn = dscr("ffn", [T, D], F32)
    if DBG_F == 1:
        k.dbg_idx = nc.dram_tensor("dbg_idx", [128, 64], I32, kind="ExternalOutput")
        k.dbg_gate = nc.dram_tensor("dbg_gate", [128, 64], F32, kind="ExternalOutput")
        k.dbg_aff = nc.dram_tensor("dbg_aff", [128, 512], F32, kind="ExternalOutput")
        k.dbg_posm = nc.dram_tensor("dbg_posm", [128, 512], F32, kind="ExternalOutput")
    k.h1 = dscr("h1", [T, D], F32)
    k.h1b = dscr("h1b", [T, D], BF16)

    with contextlib.ExitStack() as st:
        P = Prog(nc, st)
        k.P = P
        k.st = st
        k.bank = [st.enter_context(nc.psum_tensor("bank%d" % i, [128, 512], F32)) for i in range(7)]
        k.bankT = st.enter_context(nc.psum_tensor("bankT", [128, 1024], BF16))
        setup_consts(k)
        for l in range(nlayers):
            if 'a' in phases:
                phase_a(k, l)
            if 'r' in phases:
                phase_r(k, l)
            if 't' in phases:
                phase_t(k, l)
            if 's' in phases:
                phase_s(k, l)
            if 'o' in phases:
                phase_o(k, l)
            if 'f' in phases:
                phase_f(k, l, last=(l == nlayers - 1))
        finish(k)
    nc._prog = P
    return nc


def sb(k, name, shape, dt, stack=None):
    k.nsb = getattr(k, 'nsb', 0) + 1
    return (stack or k.st).enter_context(k.nc.sbuf_tensor("%s_u%d" % (name, k.nsb), list(shape), dt))


def setup_consts(k):
    nc, P = k.nc, k.P
    k.identb = sb(k, "identb", [128, 128], BF16)
    k.identf = sb(k, "identf", [128, 128], F32)
    k.iota_i = sb(k, "iota_i", [128, 128], I32)
    k.dist = sb(k, "dist", [128, 128], F32)
    P.op('pool', lambda e: e.iota(k.iota_i[:], [[1, 128]], base=0, channel_multiplier=-1),
         writes=['iota_i'])
    P.op('dve', lambda e: e.tensor_copy(out=k.dist[:], in_=k.iota_i[:]), reads=['iota_i'], writes=['dist'])
    P.op('dve', lambda e: e.tensor_scalar(out=k.identf[:], in0=k.dist[:], scalar1=0.0, scalar2=None,
                                           op0=ALU.is_equal), reads=['dist'], writes=['identf'])
    P.op('dve', lambda e: e.tensor_copy(out=k.identb[:], in_=k.identf[:]), reads=['identf'], writes=['identb'])
    k.one_t = sb(k, "one_t", [128, 1], F32)
    P.op('dve', lambda e: e.memset(k.one_t[:], 1.0), writes=['one_t'])
    k.lnk_t = sb(k, "lnk_t", [128, 1], F32)
    P.op('dve', lambda e: e.memset(k.lnk_t[:], math.log(0.125)), writes=['lnk_t'])
    k.irow_i = sb(k, "irow_i", [128, 128], I32)
    k.irow = sb(k, "irow", [128, 128], F32)
    P.op('pool', lambda e: e.iota(k.irow_i[:], [[1, 128]], base=0, channel_multiplier=0), writes=['irow_i'])
    P.op('dve', lambda e: e.tensor_copy(out=k.irow[:], in_=k.irow_i[:]), reads=['irow_i'], writes=['irow'])
    k.relp = sb(k, "relp", [128, 128], F32)
    k.reln = sb(k, "reln", [128, 128], F32)
    P.op('dve', lambda e: e.tensor_scalar(out=k.relp[:], in0=k.dist[:], scalar1=0.0, scalar2=None, op0=ALU.max),
         reads=['dist'], writes=['relp'])
    P.op('dve', lambda e: e.tensor_scalar(out=k.reln[:], in0=k.dist[:], scalar1=-1.0, scalar2=0.0, op0=ALU.mult, op1=ALU.max),
         reads=['dist'], writes=['reln'])
    k.eps_t = sb(k, "eps_t", [128, 1], F32)
    P.op('dve', lambda e: e.memset(k.eps_t[:], EPS), writes=['eps_t'])


def layer_norm_tile(k, x_ap, xkey, out_ap, outkey, g_ap, gkey, b_ap, bkey, tag, scratch):
    P = k.P
    stats, mv, sd, rstd, nb, xn = (scratch[n] for n in ('stats', 'mv', 'sd', 'rstd', 'nb', 'xn'))
    s = tag
    for hh in range(2):
        P.op('dve', lambda e, hh=hh: e.bn_stats(out=stats[:, hh, :], in_=x_ap[:, hh * 512:(hh + 1) * 512]),
             reads=[xkey], writes=[s + 'stats%d' % hh])
        yield
    P.op('dve', lambda e: e.bn_aggr(out=mv[:], in_=stats[:].rearrange("p a b -> p (a b)")),
         reads=[s + 'stats0', s + 'stats1'], writes=[s + 'mv'])
    yield
    P.op('act', lambda e: e.activation(out=sd[:], in_=mv[:, 1:2], func=AF.Sqrt, bias=k.eps_t[:], scale=1.0),
         reads=[s + 'mv'], writes=[s + 'sd'])
    yield
    P.op('dve', lambda e: e.reciprocal(out=rstd[:], in_=sd[:]), reads=[s + 'sd'], writes=[s + 'rstd'])
    yield
    P.op('dve', lambda e: e.scalar_tensor_tensor(out=nb[:], in0=mv[:, 0:1], scalar=-1.0, in1=rstd[:],
                                                  op0=ALU.mult, op1=ALU.mult),
         reads=[s + 'mv', s + 'rstd'], writes=[s + 'nb'])
    yield
    P.op('act', lambda e: e.activation(out=xn[:], in_=x_ap, func=AF.Identity, bias=nb[:], scale=rstd[:]),
         reads=[xkey, s + 'nb', s + 'rstd'], writes=[s + 'xn'])
    yield
    P.op('pool', lambda e: e.tensor_tensor(out=xn[:], in0=xn[:], in1=g_ap, op=ALU.mult),
         reads=[s + 'xn', gkey], writes=[s + 'xn'])
    yield
    P.op('dve', lambda e: e.tensor_tensor(out=out_ap, in0=xn[:], in1=b_ap, op=ALU.add),
         reads=[s + 'xn', bkey], writes=[outkey])
    yield


def phase_a(k, l):
    nc, P = k.nc, k.P
    P.barrier()
    with contextlib.ExitStack() as ps:
        if l == 0:
            k.g_in = sb(k, "g_in", [128, D], F32, ps)
            k.b_in = sb(k, "b_in", [128, D], F32, ps)
            P.dma('sp', lambda e: e.dma_start(out=k.g_in[:], in_=dap(k.ln_in_g, 0, [[0, 128], [1, D]])), writes=['g_in'])
            P.dma('sp', lambda e: e.dma_start(out=k.b_in[:], in_=dap(k.ln_in_b, 0, [[0, 128], [1, D]])), writes=['b_in'])
        Wb = sb(k, "a_Wb", [128, 8, 2304], BF16, ps)
        Wst = [sb(k, "a_Wst%d" % i, [128, 8, 256], F32, ps) for i in range(2)]
        xt = [sb(k, "a_x%d" % i, [128, D], F32, ps) for i in range(NBA)]
        ht = [sb(k, "a_h%d" % i, [128, D], F32, ps) for i in range(NBA)]
        hb = [sb(k, "a_hb%d" % i, [128, D], BF16, ps) for i in range(NBA)]
        hT = [sb(k, "a_hT%d" % i, [128, 8, 512], BF16, ps) for i in range(2)]
        tmst = [sb(k, "a_tmst%d" % i, [128, N_TM], BF16, ps) for i in range(NBA)]
        fmst = [sb(k, "a_fmst%d" % i, [128, 512], BF16, ps) for i in range(3)]
        scr = [dict(stats=sb(k, "a_stats%d" % i, [128, 2, 6], F32, ps), mv=sb(k, "a_mv%d" % i, [128, 2], F32, ps),
                    sd=sb(k, "a_sd%d" % i, [128, 1], F32, ps), rstd=sb(k, "a_rstd%d" % i, [128, 1], F32, ps),
                    nb=sb(k, "a_nb%d" % i, [128, 1], F32, ps), xn=sb(k, "a_xn%d" % i, [128, D], F32, ps))
               for i in range(NBA)]
        for c in range(9):
            w = Wst[c % 2]
            wk = 'a_Wst%d' % (c % 2)
            P.dma('sp', lambda e, w=w, c=c: e.dma_start(
                out=w[:], in_=dap(k.w_in, l * D * 2304 + c * 256, [[2304, 128], [128 * 2304, 8], [1, 256]])),
                writes=[wk])
            eng = 'act' if c % 2 == 0 else 'dve'
            if eng == 'act':
                P.op('act', lambda e, w=w, c=c: e.copy(out=Wb[:, :, c * 256:(c + 1) * 256], in_=w[:]),
                     reads=[wk], writes=['a_Wb%d' % c])
            else:
                P.op('dve', lambda e, w=w, c=c: e.tensor_copy(out=Wb[:, :, c * 256:(c + 1) * 256], in_=w[:]),
                     reads=[wk], writes=['a_Wb%d' % c])
        Wkeys = ['a_Wb%d' % c for c in range(9)]
        nb = 0

        def a_load(ti):
            b = ti % NBA
            if l == 0:
                P.dma('sp', lambda e: e.dma_start(out=xt[b][:], in_=dap(k.x, ti * 128 * D, [[D, 128], [1, D]])), writes=['a_x%d' % b])
            else:
                P.dma('sp', lambda e: e.dma_start(out=ht[b][:], in_=dap(k.hA, ti * 128 * D, [[D, 128], [1, D]])),
                      reads=['hA%d' % ti], writes=['a_h%d' % b])
        for stile in range(T // 512):
            hTs = hT[stile % 2]
            hTk = 'a_hT%d' % (stile % 2)
            for s in range(4):
                ti = stile * 4 + s
                b = ti % NBA
                tok0 = ti * 128
                if ti == 0:
                    a_load(0)
                if ti + 1 < NT:
                    a_load(ti + 1)
                if l == 0:
                    for _ in layer_norm_tile(k, xt[b][:], 'a_x%d' % b, ht[b][:], 'a_h%d' % b, k.g_in[:], 'g_in', k.b_in[:], 'b_in',
                                             'a%d_' % b, scr[b]):
                        pass
                    P.dma('sp', lambda e, b=b, tok0=tok0: e.dma_start(out=dap(k.hA, tok0 * D, [[D, 128], [1, D]]), in_=ht[b][:]),
                          reads=['a_h%d' % b], writes=['hA%d' % ti])
                P.op('act', lambda e, b=b: e.copy(out=hb[b][:], in_=ht[b][:]), reads=['a_h%d' % b], writes=['a_hb%d' % b])

                def tr(e, b=b):
                    r = None
                    for kk in range(8):
                        r = e.transpose(out=k.bankT[:, kk * 128:(kk + 1) * 128], in_=hb[b][:, kk * 128:(kk + 1) * 128],
                                        identity=k.identb[:])
                    return r
                P.op('pe', tr, reads=['a_hb%d' % b, 'identb'], writes=['bankT'])
                P.op('dve', lambda e, s=s, hTs=hTs: e.tensor_copy(
                    out=hTs[:, :, s * 128:(s + 1) * 128], in_=k.bankT[:].rearrange("p (a b) -> p a b", a=8)),
                    reads=['bankT'], writes=[hTk + '_%d' % s])
                for gi, (c0, c1) in enumerate([(0, 512), (512, N_TM)]):
                    bk = nb % 6
                    nb += 1

                    def mm(e, bk=bk, c0=c0, c1=c1, s=s, hTs=hTs):
                        r = None
                        for kk in range(8):
                            r = e.matmul(k.bank[bk][:, 0:c1 - c0], lhsT=hTs[:, kk, s * 128:(s + 1) * 128],
                                         rhs=Wb[:, kk, c0:c1], start=(kk == 0), stop=(kk == 7))
                        return r
                    P.op('pe', mm, reads=[hTk + '_%d' % s] + Wkeys, writes=['bank%d' % bk])
                    P.op('act', lambda e, bk=bk, c0=c0, c1=c1, b=b: e.copy(out=tmst[b][:, c0:c1], in_=k.bank[bk][:, 0:c1 - c0]),
                         reads=['bank%d' % bk], writes=['a_tmst%d_%d' % (b, gi)])
                P.dma('sp', lambda e, b=b, tok0=tok0: e.dma_start(out=dap(k.tm, tok0 * N_TM, [[N_TM, 128], [1, N_TM]]), in_=tmst[b][:]),
                      reads=['a_tmst%d_0' % b, 'a_tmst%d_1' % b], writes=['tm%d' % ti])
                P._update(P.res['tm%d' % ti]['w'], ['a_tmst%d_0' % b, 'a_tmst%d_1' % b], [])
            for fg in range(11):
                bk = nb % 6
                nb += 1
                fb = fg % 3

                def mmf(e, bk=bk, fg=fg, hTs=hTs):
                    r = None
                    for kk in range(8):
                        r = e.matmul(k.bank[bk][:, :], lhsT=Wb[:, kk, N_TM + fg * 128:N_TM + (fg + 1) * 128],
                                     rhs=hTs[:, kk, :], start=(kk == 0), stop=(kk == 7))
                    return r
                P.op('pe', mmf, reads=[hTk + '_%d' % s for s in range(4)] + Wkeys, writes=['bank%d' % bk])
                eng = 'dve' if fg % 2 == 0 else 'act'
                if eng == 'dve':
                    P.op('dve', lambda e, bk=bk, fb=fb: e.tensor_copy(out=fmst[fb][:], in_=k.bank[bk][:, :]),
                         reads=['bank%d' % bk], writes=['a_fmst%d' % fb])
                else:
                    P.op('act', lambda e, bk=bk, fb=fb: e.copy(out=fmst[fb][:], in_=k.bank[bk][:, :]),
                         reads=['bank%d' % bk], writes=['a_fmst%d' % fb])
                P.dma('sp', lambda e, fb=fb, fg=fg, stile=stile: e.dma_start(
                    out=dap(k.fm, fg * 128 * T + stile * 512, [[T, 128], [1, 512]]), in_=fmst[fb][:]),
                    reads=['a_fmst%d' % fb], writes=['fm%d_%d' % (fg, stile)])
                P._update(P.res['fm%d_%d' % (fg, stile)]['w'], ['a_fmst%d' % fb], [])


def bc(tile, col, n, pstride, parts=128):
    return dap(tile, col, [[pstride, parts], [0, n]])


def phase_r(k, l):
    nc, P = k.nc, k.P
    P.barrier()
    with contextlib.ExitStack() as ps:
        ktm = sb(k, "r_ktm", [128, NT, 256], BF16, ps)
        vtm = sb(k, "r_vtm", [128, NT, 256], BF16, ps)
        gtm = sb(k, "r_gtm", [128, NT, 256], BF16, ps)
        Sf = sb(k, "r_Sf", [64, NT, 256], BF16, ps)
        Sb_ = sb(k, "r_Sb", [64, NT, 256], BF16, ps)
        stt = [sb(k, "r_st%d" % i, [64, 256], F32, ps) for i in range(2)]
        qT = [sb(k, "r_qT%d" % i, [64, 4, 512], BF16, ps) for i in range(2)]
        kT = [sb(k, "r_kT%d" % i, [64, 4, 512], BF16, ps) for i in range(2)]
        th = sb(k, "r_th", [128, 8], F32, ps)
        lg = sb(k, "r_lg", [128, 8], F32, ps)
        dec = sb(k, "r_dec", [128, 8], F32, ps)
        wcol = sb(k, "r_wcol", [128, 8], F32, ps)
        pidx = sb(k, "r_pidx", [128, 2], F32, ps)
        tmpa = sb(k, "r_tmpa", [128, 128], F32, ps)
        tmpb = sb(k, "r_tmpb", [128, 128], F32, ps)
        dmT = sb(k, "r_dmT", [128, 4, 128], F32, ps)
        WF = sb(k, "r_WF", [128, 256], F32, ps)
        WB = sb(k, "r_WB", [128, 256], F32, ps)
        qsf = sb(k, "r_qsf", [128, 4, 128], F32, ps)
        qsb = sb(k, "r_qsb", [128, 4, 128], F32, ps)
        irf = sb(k, "r_irf", [128, 128], F32, ps)
        irb = sb(k, "r_irb", [128, 128], F32, ps)
        kw = [sb(k, "r_kw%d" % i, [128, 256], BF16, ps) for i in range(2)]
        sTm = [sb(k, "r_sTm%d" % i, [128, 512], BF16, ps) for i in range(2)]
        qf = [sb(k, "r_qf%d" % i, [64, 4, 128], BF16, ps) for i in range(2)]
        qb = [sb(k, "r_qb%d" % i, [64, 4, 128], BF16, ps) for i in range(2)]
        o = [sb(k, "r_o%d" % i, [128, 256], F32, ps) for i in range(2)]
        sq = [sb(k, "r_sq%d" % i, [128, 256], F32, ps) for i in range(2)]
        sg = [sb(k, "r_sg%d" % i, [128, 256], F32, ps) for i in range(2)]
        on = [sb(k, "r_on%d" % i, [128, 256], F32, ps) for i in range(2)]
        yst = [sb(k, "r_yst%d" % i, [128, 256], BF16, ps) for i in range(2)]
        sm = [dict((n, sb(k, "r_%s%d" % (n, i), [128, 4], F32, ps)) for n in ('s1', 's2', 'mean', 'msq', 'var', 'sd', 'rstd'))
              for i in range(2)]

        for name, tl, c0 in (('r_ktm', ktm, 0), ('r_vtm', vtm, 256), ('r_gtm', gtm, 512)):
            for half in range(2):
                P.dma('sp', lambda e, tl=tl, c0=c0, half=half: e.dma_start(
                    out=tl[:, half * 16:(half + 1) * 16, :],
                    in_=dap(k.tm, half * 16 * 128 * N_TM + c0, [[N_TM, 128], [128 * N_TM, 16], [1, 256]])),
                    reads=['tm%d' % ti for ti in range(half * 16, half * 16 + 16)], writes=['%s_%d' % (name, half)])
        P.dma('sp', lambda e: e.dma_start(out=th[:], in_=dap(k.ret_theta, l * 8, [[0, 128], [1, 8]])), writes=['r_th'])
        P.op('act', lambda e: e.activation(out=lg[:], in_=th[:], func=AF.Exp, scale=-1.0), reads=['r_th'], writes=['r_lg'])
        P.op('act', lambda e: e.activation(out=lg[:], in_=lg[:], func=AF.Ln, bias=k.one_t[:], scale=1.0),
             reads=['r_lg', 'one_t'], writes=['r_lg'])
        P.op('dve', lambda e: e.tensor_scalar(out=lg[:], in0=lg[:], scalar1=-1.0, scalar2=None, op0=ALU.mult),
             reads=['r_lg'], writes=['r_lg'])
        P.op('act', lambda e: e.activation(out=dec[:], in_=lg[:], func=AF.Exp, scale=128.0), reads=['r_lg'], writes=['r_dec'])
        P.op('dve', lambda e: e.tensor_scalar(out=pidx[:, 0:1], in0=k.dist[:, 0:1], scalar1=127.0, scalar2=None, op0=ALU.add),
             reads=['dist'], writes=['r_pidx0'])
        P.op('dve', lambda e: e.tensor_scalar(out=pidx[:, 1:2], in0=k.dist[:, 0:1], scalar1=-1.0, scalar2=None, op0=ALU.mult),
             reads=['dist'], writes=['r_pidx1'])
        P.op('dve', lambda e: e.tensor_scalar(out=irf[:], in0=k.irow[:], scalar1=1.0, scalar2=None, op0=ALU.add),
             reads=['irow'], writes=['r_irf'])
        P.op('dve', lambda e: e.tensor_scalar(out=irb[:], in0=k.irow[:], scalar1=-1.0, scalar2=128.0, op0=ALU.mult, op1=ALU.add),
             reads=['irow'], writes=['r_irb'])
        for h in range(4):
            P.op('act', lambda e, h=h: e.activation(out=wcol[:, h:h + 1], in_=pidx[:, 0:1], func=AF.Exp, bias=k.lnk_t[:], scale=lg[:, h:h + 1]),
                 reads=['r_pidx0', 'r_lg', 'lnk_t'], writes=['r_wcol%d' % h])
            P.op('act', lambda e, h=h: e.activation(out=wcol[:, 4 + h:5 + h], in_=pidx[:, 1:2], func=AF.Exp, bias=k.lnk_t[:], scale=lg[:, 4 + h:5 + h]),
                 reads=['r_pidx1', 'r_lg', 'lnk_t'], writes=['r_wcol%d' % (4 + h)])
            P.op('dve', lambda e, h=h: e.tensor_copy(out=WF[:, h * 64:(h + 1) * 64], in_=bc(wcol, h, 64, 8)),
                 reads=['r_wcol%d' % h], writes=['r_WF%d' % h])
            P.op('dve', lambda e, h=h: e.tensor_copy(out=WB[:, h * 64:(h + 1) * 64], in_=bc(wcol, 4 + h, 64, 8)),
                 reads=['r_wcol%d' % (4 + h)], writes=['r_WB%d' % h])
            P.op('dve', lambda e, h=h: e.tensor_scalar(out=tmpa[:], in0=k.relp[:], scalar1=lg[:, h:h + 1], scalar2=None, op0=ALU.mult),
                 reads=['relp', 'r_lg'], writes=['r_tmpa'])
            P.op('dve', lambda e, h=h: e.scalar_tensor_tensor(out=tmpb[:], in0=k.reln[:], scalar=lg[:, 4 + h:5 + h], in1=tmpa[:],
                                                               op0=ALU.mult, op1=ALU.add),
                 reads=['reln', 'r_lg', 'r_tmpa'], writes=['r_tmpb'])
            P.op('act', lambda e, h=h: e.activation(out=dmT[:, h, :], in_=tmpb[:], func=AF.Exp, bias=k.lnk_t[:], scale=1.0),
                 reads=['r_tmpb', 'lnk_t'], writes=['r_dmT%d' % h])
            P.op('act', lambda e, h=h: e.activation(out=qsf[:, h, :], in_=irf[:], func=AF.Exp, scale=lg[:, h:h + 1]),
                 reads=['r_irf', 'r_lg'], writes=['r_qsf%d' % h])
            P.op('act', lambda e, h=h: e.activation(out=qsb[:, h, :], in_=irb[:], func=AF.Exp, scale=lg[:, 4 + h:5 + h]),
                 reads=['r_irb', 'r_lg'], writes=['r_qsb%d' % h])
        WFk = ['r_WF%d' % h for h in range(4)]
        WBk = ['r_WB%d' % h for h in range(4)]
        for d in range(2):
            W = WF if d == 0 else WB
            Wk = WFk if d == 0 else WBk
            Sall = Sf if d == 0 else Sb_
            stn = 'r_st%d' % d
            P.op('dve', lambda e, d=d: e.memset(stt[d][:], 0.0), writes=[stn])
            order = range(NT) if d == 0 else range(NT - 1, -1, -1)
            for n, c in enumerate(order):
                half = c // 16
                kb = n % 2
                P.op('act', lambda e, c=c, Sall=Sall, d=d: e.copy(out=Sall[:, c, :], in_=stt[d][:]),
                     reads=[stn], writes=['r_S%d_%d' % (d, c)])
                P.op('pool', lambda e, c=c, kb=kb, W=W: e.tensor_tensor(out=kw[kb][:], in0=ktm[:, c, :], in1=W[:], op=ALU.mult),
                     reads=['r_ktm_%d' % half] + Wk, writes=['r_kw%d' % kb])
                bk = n % 2

                def mmkv(e, c=c, kb=kb, bk=bk):
                    r = None
                    for h in range(4):
                        r = e.matmul(k.bank[bk][0:64, h * 64:(h + 1) * 64], lhsT=kw[kb][:, h * 64:(h + 1) * 64],
                                     rhs=vtm[:, c, h * 64:(h + 1) * 64], start=True, stop=True)
                    return r
                P.op('pe', mmkv, reads=['r_kw%d' % kb, 'r_vtm_%d' % half], writes=['bank%d' % bk])
                for h in range(4):
                    P.op('dve', lambda e, h=h, d=d, bk=bk: e.scalar_tensor_tensor(
                        out=stt[d][:, h * 64:(h + 1) * 64], in0=stt[d][:, h * 64:(h + 1) * 64], scalar=dec[0:64, 4 * d + h:4 * d + h + 1],
                        in1=k.bank[bk][0:64, h * 64:(h + 1) * 64], op0=ALU.mult, op1=ALU.add),
                        reads=[stn, 'r_dec', 'bank%d' % bk], writes=[stn])
        for c in range(NT):
            blk, cc = c // 4, c % 4
            half = c // 16
            qb_i = blk % 2
            b = c % 2
            if cc == 0:
                P.dma('sp', lambda e, blk=blk, qb_i=qb_i: e.dma_start(
                    out=qT[qb_i][:], in_=dap(k.fm, blk * 512, [[T, 64], [64 * T, 4], [1, 512]])),
                    reads=['fm%d_%d' % (fg, blk) for fg in (0, 1)], writes=['r_qT%d' % qb_i])
                P.dma('sp', lambda e, blk=blk, qb_i=qb_i: e.dma_start(
                    out=kT[qb_i][:], in_=dap(k.fm, 256 * T + blk * 512, [[T, 64], [64 * T, 4], [1, 512]])),
                    reads=['fm%d_%d' % (fg, blk) for fg in (2, 3)], writes=['r_kT%d' % qb_i])
            bs = 2 + (c % 2)

            def mms(e, qb_i=qb_i, cc=cc, bs=bs):
                r = None
                for h in range(4):
                    r = e.matmul(k.bank[bs][:, h * 128:(h + 1) * 128], lhsT=kT[qb_i][:, h, cc * 128:(cc + 1) * 128],
                                 rhs=qT[qb_i][:, h, cc * 128:(cc + 1) * 128], start=True, stop=True)
                return r
            P.op('pe', mms, reads=['r_qT%d' % qb_i, 'r_kT%d' % qb_i], writes=['bank%d' % bs])
            P.op('dve', lambda e, b=b, bs=bs: e.tensor_tensor(out=sTm[b][:], in0=k.bank[bs][:, :],
                                                            in1=dmT[:].rearrange("p a b -> p (a b)"), op=ALU.mult),
                 reads=['bank%d' % bs] + ['r_dmT%d' % h for h in range(4)], writes=['r_sTm%d' % b])
            P.op('pool', lambda e, b=b, qb_i=qb_i, cc=cc: e.tensor_tensor(out=qf[b][:], in0=qT[qb_i][:, :, cc * 128:(cc + 1) * 128],
                                                                         in1=qsf[0:64, :, :], op=ALU.mult),
                 reads=['r_qT%d' % qb_i] + ['r_qsf%d' % h for h in range(4)], writes=['r_qf%d' % b])
            P.op('pool', lambda e, b=b, qb_i=qb_i, cc=cc: e.tensor_tensor(out=qb[b][:], in0=qT[qb_i][:, :, cc * 128:(cc + 1) * 128],
                                                                         in1=qsb[0:64, :, :], op=ALU.mult),
                 reads=['r_qT%d' % qb_i] + ['r_qsb%d' % h for h in range(4)], writes=['r_qb%d' % b])
            bo = 4 + (c % 2)

            def mmo(e, b=b, c=c, bo=bo):
                r = None
                for h in range(4):
                    oap = k.bank[bo][:, h * 64:(h + 1) * 64]
                    e.matmul(oap, lhsT=sTm[b][:, h * 128:(h + 1) * 128], rhs=vtm[:, c, h * 64:(h + 1) * 64], start=True, stop=False)
                    e.matmul(oap, lhsT=qf[b][:, h, :], rhs=Sf[:, c, h * 64:(h + 1) * 64], start=False, stop=False)
                    r = e.matmul(oap, lhsT=qb[b][:, h, :], rhs=Sb_[:, c, h * 64:(h + 1) * 64], start=False, stop=True)
                return r
            P.op('pe', mmo, reads=['r_sTm%d' % b, 'r_qf%d' % b, 'r_qb%d' % b, 'r_vtm_%d' % half, 'r_S0_%d' % c, 'r_S1_%d' % c],
                 writes=['bank%d' % bo])
            m = sm[b]
            mk = lambda n: 'r_%s%d' % (n, b)
            P.op('act', lambda e, b=b, bo=bo: e.copy(out=o[b][:], in_=k.bank[bo][:, 0:256]), reads=['bank%d' % bo], writes=[mk('o')])
            P.op('act', lambda e, b=b: e.activation(out=sq[b][:], in_=o[b][:], func=AF.Square), reads=[mk('o')], writes=[mk('sq')])
            P.op('act', lambda e, b=b, c=c: e.activation(out=sg[b][:], in_=gtm[:, c, :], func=AF.Silu),
                 reads=['r_gtm_%d' % half], writes=[mk('sg')])
            P.op('dve', lambda e, b=b, m=m: e.tensor_reduce(out=m['s1'][:], in_=o[b][:].rearrange("p (a b) -> p a b", a=4), axis=AX.X, op=ALU.add),
                 reads=[mk('o')], writes=[mk('s1')])
            P.op('dve', lambda e, b=b, m=m: e.tensor_reduce(out=m['s2'][:], in_=sq[b][:].rearrange("p (a b) -> p a b", a=4), axis=AX.X, op=ALU.add),
                 reads=[mk('sq')], writes=[mk('s2')])
            P.op('dve', lambda e, m=m: e.tensor_scalar(out=m['mean'][:], in0=m['s1'][:], scalar1=1.0 / 64, scalar2=None, op0=ALU.mult),
                 reads=[mk('s1')], writes=[mk('mean')])
            P.op('dve', lambda e, m=m: e.tensor_tensor(out=m['msq'][:], in0=m['mean'][:], in1=m['mean'][:], op=ALU.mult),
                 reads=[mk('mean')], writes=[mk('msq')])
            P.op('dve', lambda e, m=m: e.scalar_tensor_tensor(out=m['var'][:], in0=m['s2'][:], scalar=1.0 / 64, in1=m['msq'][:],
                                                               op0=ALU.mult, op1=ALU.subtract),
                 reads=[mk('s2'), mk('msq')], writes=[mk('var')])
            P.op('act', lambda e, m=m: e.activation(out=m['sd'][:], in_=m['var'][:], func=AF.Sqrt, bias=k.eps_t[:], scale=1.0),
                 reads=[mk('var'), 'eps_t'], writes=[mk('sd')])
            P.op('dve', lambda e, m=m: e.reciprocal(out=m['rstd'][:], in_=m['sd'][:]), reads=[mk('sd')], writes=[mk('rstd')])
            for h in range(4):
                P.op('dve', lambda e, h=h, b=b, m=m: e.tensor_scalar(
                    out=on[b][:, h * 64:(h + 1) * 64], in0=o[b][:, h * 64:(h + 1) * 64], scalar1=m['mean'][:, h:h + 1],
                    scalar2=m['rstd'][:, h:h + 1], op0=ALU.subtract, op1=ALU.mult),
                    reads=[mk('o'), mk('mean'), mk('rstd')], writes=[mk('on') + '_%d' % h])
            P.op('pool', lambda e, b=b: e.tensor_tensor(out=yst[b][:], in0=on[b][:], in1=sg[b][:], op=ALU.mult),
                 reads=[mk('on') + '_%d' % h for h in range(4)] + [mk('sg')], writes=[mk('yst')])
            P.dma('sp', lambda e, b=b, c=c: e.dma_start(out=dap(k.ycat, c * 128 * 768, [[768, 128], [1, 256]]), in_=yst[b][:]),
                  reads=[mk('yst')], writes=['ycat_r%d' % c])
            P._update(P.res['ycat_r%d' % c]['w'], [mk('yst')], [])


def phase_t(k, l):
    nc, P = k.nc, k.P
    P.barrier()
    EBk = [['EB%d_%d' % (kb, h) for h in range(8)] for kb in range(3)]
    with contextlib.ExitStack() as ps:
        k.EB = [sb(k, "EB%d" % kb, [128, 8, 128], F32, ps) for kb in range(3)]
        k.t_abs = sb(k, "t_abs", [128, 128], F32, ps)
        k.t_msk = sb(k, "t_msk", [128, 128], F32, ps)
        for kb in range(3):
            if kb == 0:
                P.op('dve', lambda e: e.tensor_scalar(out=k.t_abs[:], in0=k.dist[:], scalar1=128.0, scalar2=None, op0=ALU.add),
                     reads=['dist'], writes=['t_abs'])
                P.op('dve', lambda e: e.tensor_scalar(out=k.t_msk[:], in0=k.dist[:], scalar1=0.0, scalar2=None, op0=ALU.is_le),
                     reads=['dist'], writes=['t_msk'])
            elif kb == 1:
                P.op('dve', lambda e: e.tensor_tensor(out=k.t_abs[:], in0=k.relp[:], in1=k.reln[:], op=ALU.add),
                     reads=['relp', 'reln'], writes=['t_abs'])
                P.op('dve', lambda e: e.memset(k.t_msk[:], 1.0), writes=['t_msk'])
            else:
                P.op('dve', lambda e: e.tensor_scalar(out=k.t_abs[:], in0=k.dist[:], scalar1=-1.0, scalar2=128.0, op0=ALU.mult, op1=ALU.add),
                     reads=['dist'], writes=['t_abs'])
                P.op('dve', lambda e: e.tensor_scalar(out=k.t_msk[:], in0=k.dist[:], scalar1=0.0, scalar2=None, op0=ALU.is_ge),
                     reads=['dist'], writes=['t_msk'])
            for h in range(8):
                P.op('act', lambda e, kb=kb, h=h: e.activation(out=k.EB[kb][:, h, :], in_=k.t_abs[:], func=AF.Exp, scale=-(2.0 ** -(h + 1))),
                     reads=['t_abs'], writes=['EB%d_%d' % (kb, h)])
                P.op('dve', lambda e, kb=kb, h=h: e.tensor_tensor(out=k.EB[kb][:, h, :], in0=k.EB[kb][:, h, :], in1=k.t_msk[:], op=ALU.mult),
                     reads=['EB%d_%d' % (kb, h), 't_msk'], writes=['EB%d_%d' % (kb, h)])
        kT = sb(k, "t_kT", [64, 2, T], BF16, ps)
        vA = sb(k, "t_vA", [128, NT, 2, 65], BF16, ps)
        qT = [sb(k, "t_qT%d" % i, [64, 8, 512], BF16, ps) for i in range(2)]
        snk = sb(k, "t_snk", [128, 8], F32, ps)
        ex = [sb(k, "t_ex%d" % i, [128, 512], F32, ps) for i in range(3)]
        pT = [sb(k, "t_pT%d" % i, [128, 512], BF16, ps) for i in range(6)]
        den = [sb(k, "t_den%d" % i, [128, 4], F32, ps) for i in range(2)]
        rec = [sb(k, "t_rec%d" % i, [128, 4], F32, ps) for i in range(2)]
        yst = [sb(k, "t_yst%d" % i, [128, 512], BF16, ps) for i in range(2)]
        for half in range(2):
            P.dma('sp', lambda e, half=half: e.dma_start(
                out=kT[:, :, half * 2048:(half + 1) * 2048], in_=dap(k.fm, 1280 * T + half * 2048, [[T, 64], [64 * T, 2], [1, 2048]])),
                reads=['fm10_%d' % b for b in range(half * 4, half * 4 + 4)], writes=['t_kT%d' % half])
            for kvh in range(2):
                P.dma('sp', lambda e, half=half, kvh=kvh: e.dma_start(
                    out=vA[:, half * 16:(half + 1) * 16, kvh, 0:64],
                    in_=dap(k.tm, half * 16 * 128 * N_TM + 768 + kvh * 64, [[N_TM, 128], [128 * N_TM, 16], [1, 64]])),
                    reads=['tm%d' % ti for ti in range(half * 16, half * 16 + 16)], writes=['t_vA%d_%d' % (half, kvh)])
        P.op('pool', lambda e: e.memset(vA[:, :, :, 64:65], 1.0), writes=['t_vA1s'])
        P.dma('sp', lambda e: e.dma_start(out=snk[:], in_=dap(k.attn_sink, l * 8, [[0, 128], [1, 8]])), writes=['t_snk'])
        P.op('act', lambda e: e.activation(out=snk[:], in_=snk[:], func=AF.Exp), reads=['t_snk'], writes=['t_snk'])
        npT = 0
        nex = 0
        for c in range(NT):
            blk, cc = c // 4, c % 4
            qi = blk % 2
            if cc == 0:
                P.dma('sp', lambda e, blk=blk, qi=qi: e.dma_start(
                    out=qT[qi][:], in_=dap(k.fm, 768 * T + blk * 512, [[T, 64], [64 * T, 8], [1, 512]])),
                    reads=['fm%d_%d' % (fg, blk) for fg in (6, 7, 8, 9)], writes=['t_qT%d' % qi])
            yb = c % 2
            for kvh in range(2):
                kbs = [kb for kb in range(3) if 0 <= c - 1 + kb < NT]
                pts = []
                for kb in kbs:
                    kblk = c - 1 + kb
                    bs = (nex % 3)
                    xi = nex % 3
                    nex += 1
                    pi = npT % 6
                    npT += 1
                    pts.append(pi)
                    P.op('pe', lambda e, bs=bs, kvh=kvh, kblk=kblk, qi=qi, cc=cc: e.matmul(
                        k.bank[bs][:, :], lhsT=kT[:, kvh, kblk * 128:(kblk + 1) * 128],
                        rhs=qT[qi][:, kvh * 4:(kvh + 1) * 4, cc * 128:(cc + 1) * 128], start=True, stop=True),
                        reads=['t_kT%d' % (kblk // 16), 't_qT%d' % qi], writes=['bank%d' % bs])
                    P.op('act', lambda e, bs=bs, xi=xi: e.activation(out=ex[xi][:], in_=k.bank[bs][:, :], func=AF.Exp, scale=0.125),
                         reads=['bank%d' % bs], writes=['t_ex%d' % xi])
                    P.op('dve', lambda e, xi=xi, pi=pi, kb=kb, kvh=kvh: e.tensor_tensor(
                        out=pT[pi][:], in0=ex[xi][:], in1=k.EB[kb][:, kvh * 4:(kvh + 1) * 4, :].rearrange("p a b -> p (a b)"), op=ALU.mult),
                        reads=['t_ex%d' % xi] + EBk[kb], writes=['t_pT%d' % pi])
                bo = 3 + kvh + 2 * (c % 2)

                def mmo(e, kbs=kbs, pts=pts, c=c, kvh=kvh, bo=bo):
                    r = None
                    for g in range(4):
                        for n, (kb, pi) in enumerate(zip(kbs, pts)):
                            kblk = c - 1 + kb
                            r = e.matmul(k.bank[bo][:, g * 65:(g + 1) * 65], lhsT=pT[pi][:, g * 128:(g + 1) * 128],
                                         rhs=vA[:, kblk, kvh, :], start=(n == 0), stop=(n == len(kbs) - 1))
                    return r
                P.op('pe', mmo, reads=['t_pT%d' % pi for pi in pts] + ['t_vA0_0', 't_vA0_1', 't_vA1_0', 't_vA1_1', 't_vA1s'], writes=['bank%d' % bo])
                dk = 't_den%d' % kvh
                P.op('dve', lambda e, bo=bo, kvh=kvh: e.tensor_tensor(
                    out=den[kvh][:], in0=dap(k.bank[bo], 64, [[512, 128], [65, 4]]), in1=snk[:, kvh * 4:(kvh + 1) * 4], op=ALU.add),
                    reads=['bank%d' % bo, 't_snk'], writes=[dk])
                P.op('dve', lambda e, kvh=kvh: e.reciprocal(out=rec[kvh][:], in_=den[kvh][:]), reads=[dk], writes=['t_rec%d' % kvh])
                P.op('dve', lambda e, bo=bo, kvh=kvh, yb=yb: e.tensor_tensor(
                    out=yst[yb][:, kvh * 256:(kvh + 1) * 256].rearrange("p (a b) -> p a b", a=4),
                    in0=dap(k.bank[bo], 0, [[512, 128], [65, 4], [1, 64]]),
                    in1=dap(rec[kvh], 0, [[4, 128], [1, 4], [0, 64]]), op=ALU.mult),
                    reads=['bank%d' % bo, 't_rec%d' % kvh], writes=['t_yst%d_%d' % (yb, kvh)])
            P.dma('sp', lambda e, yb=yb, c=c: e.dma_start(out=dap(k.ycat, c * 128 * 768 + 256, [[768, 128], [1, 512]]), in_=yst[yb][:]),
                  reads=['t_yst%d_0' % yb, 't_yst%d_1' % yb], writes=['ycat_t%d' % c])
            P._update(P.res['ycat_t%d' % c]['w'], ['t_yst%d_0' % yb, 't_yst%d_1' % yb], [])


def phase_o(k, l):
    nc, P = k.nc, k.P
    P.barrier()
    with contextlib.ExitStack() as ps:
        Wo = sb(k, "o_Wo", [128, 8, D], BF16, ps)
        Wst = [sb(k, "o_Wst%d" % i, [128, 8, 256], F32, ps) for i in range(2)]
        g1 = sb(k, "o_g1", [128, D], F32, ps)
        b1 = sb(k, "o_b1", [128, D], F32, ps)
        yc = [sb(k, "o_yc%d" % i, [128, 768], BF16, ps) for i in range(NBA)]
        yT = [sb(k, "o_yT%d" % i, [128, 8, 128], BF16, ps) for i in range(NBA)]
        ht = [sb(k, "o_h%d" % i, [128, D], F32, ps) for i in range(NBA)]
        rt = [sb(k, "o_r%d" % i, [128, D], F32, ps) for i in range(NBA)]
        h1t = [sb(k, "o_h1%d" % i, [128, D], F32, ps) for i in range(NBA)]
        h1bt = [sb(k, "o_h1b%d" % i, [128, D], BF16, ps) for i in range(NBA)]
        scr = [dict(stats=sb(k, "o_stats%d" % i, [128, 2, 6], F32, ps), mv=sb(k, "o_mv%d" % i, [128, 2], F32, ps),
                    sd=sb(k, "o_sd%d" % i, [128, 1], F32, ps), rstd=sb(k, "o_rstd%d" % i, [128, 1], F32, ps),
                    nb=sb(k, "o_nb%d" % i, [128, 1], F32, ps), xn=sb(k, "o_xn%d" % i, [128, D], F32, ps))
               for i in range(NBA)]
        P.dma('sp', lambda e: e.dma_start(out=g1[:], in_=dap(k.ln1_g, l * D, [[0, 128], [1, D]])), writes=['o_g1'])
        P.dma('sp', lambda e: e.dma_start(out=b1[:], in_=dap(k.ln1_b, l * D, [[0, 128], [1, D]])), writes=['o_b1'])
        for c in range(4):
            w = Wst[c % 2]
            wk = 'o_Wst%d' % (c % 2)
            P.dma('sp', lambda e, w=w, c=c: e.dma_start(
                out=w[:], in_=dap(k.w_out, l * D * D + c * 256, [[D, 128], [128 * D, 8], [1, 256]])), writes=[wk])
            if c % 2 == 0:
                P.op('act', lambda e, w=w, c=c: e.copy(out=Wo[:, :, c * 256:(c + 1) * 256], in_=w[:]), reads=[wk], writes=['o_Wo%d' % c])
            else:
                P.op('dve', lambda e, w=w, c=c: e.tensor_copy(out=Wo[:, :, c * 256:(c + 1) * 256], in_=w[:]), reads=[wk], writes=['o_Wo%d' % c])
        Wkeys = ['o_Wo%d' % c for c in range(4)]

        def o_load(ti):
            b = ti % NBA
            tok0 = ti * 128
            P.dma('sp', lambda e: e.dma_start(out=yc[b][:], in_=dap(k.ycat, tok0 * 768, [[768, 128], [1, 768]])),
                  reads=['ycat_r%d' % ti, 'ycat_t%d' % ti], writes=['o_yc%d' % b])
            P.dma('sp', lambda e: e.dma_start(out=yT[b][:, 2:4, :], in_=dap(k.yssmT, tok0, [[T, 128], [128 * T, 2], [1, 128]])),
                  reads=['yssmT%d' % ti], writes=['o_yT%d_s' % b])
            P.dma('sp', lambda e: e.dma_start(out=ht[b][:], in_=dap(k.hA, tok0 * D, [[D, 128], [1, D]])),
                  reads=['hA%d' % ti], writes=['o_h%d' % b])
        def o_tile(ti):
            b = ti % NBA
            tok0 = ti * 128

            def tr(e, b=b):
                r = None
                for kk in range(6):
                    r = e.transpose(out=k.bankT[:, kk * 128:(kk + 1) * 128], in_=yc[b][:, kk * 128:(kk + 1) * 128], identity=k.identb[:])
                return r
            P.op('pe', tr, reads=['o_yc%d' % b, 'identb'], writes=['bankT'])
            P.op('dve', lambda e, b=b: e.tensor_copy(out=yT[b][:, 0:2, :], in_=k.bankT[:, 0:256].rearrange("p (a b) -> p a b", a=2)),
                 reads=['bankT'], writes=['o_yT%d_r' % b])
            P.op('dve', lambda e, b=b: e.tensor_copy(out=yT[b][:, 4:8, :], in_=k.bankT[:, 256:768].rearrange("p (a b) -> p a b", a=4)),
                 reads=['bankT'], writes=['o_yT%d_t' % b])
            yield
            for half in range(2):
                bk = (2 * ti + half) % 4

                def mm(e, b=b, half=half, bk=bk):
                    r = None
                    for kk in range(8):
                        r = e.matmul(k.bank[bk][:, :], lhsT=yT[b][:, kk, :], rhs=Wo[:, kk, half * 512:(half + 1) * 512],
                                     start=(kk == 0), stop=(kk == 7))
                    return r
                P.op('pe', mm, reads=['o_yT%d_r' % b, 'o_yT%d_s' % b, 'o_yT%d_t' % b] + Wkeys, writes=['bank%d' % bk])
                yield
                P.op('dve', lambda e, b=b, half=half, bk=bk: e.scalar_tensor_tensor(
                    out=rt[b][:, half * 512:(half + 1) * 512], in0=ht[b][:, half * 512:(half + 1) * 512], scalar=ALPHA,
                    in1=k.bank[bk][:, :], op0=ALU.mult, op1=ALU.add),
                    reads=['o_h%d' % b, 'bank%d' % bk], writes=['o_r%d_%d' % (b, half)])
                yield
            P.res['o_r%d' % b] = P.res['o_r%d_1' % b]
            yield from layer_norm_tile(k, rt[b][:], 'o_r%d' % b, h1t[b][:], 'o_h1%d' % b, g1[:], 'o_g1', b1[:], 'o_b1', 'o%d_' % b, scr[b])
            P._update(P.res['o_h1%d' % b]['w'], ['o_r%d_0' % b, 'o_r%d_1' % b], [])
            P.op('act', lambda e, b=b: e.copy(out=h1bt[b][:], in_=h1t[b][:]), reads=['o_h1%d' % b], writes=['o_h1b%d' % b])
            yield
            P.dma('sp', lambda e, b=b, tok0=tok0: e.dma_start(out=dap(k.h1, tok0 * D, [[D, 128], [1, D]]), in_=h1t[b][:]),
                  reads=['o_h1%d' % b], writes=['h1_%d' % ti])
            P.dma('sp', lambda e, b=b, tok0=tok0: e.dma_start(out=dap(k.h1b, tok0 * D, [[D, 128], [1, D]]), in_=h1bt[b][:]),
                  reads=['o_h1b%d' % b], writes=['h1b_%d' % ti])

        run_window(o_tile, o_load, NT, NBA - 1)


TL = 256
SW = 4
NSET = 4
NCH = T // TL
TWO_PI = 2.0 * math.pi
CW1 = 6.28125
CW2 = TWO_PI - CW1
PI_LO = 3.1415925


def sincos(k, arg, argkey, out_sin, out_cos, outkey, n, scr, tag):
    P = k.P
    x, kf, ki = scr['x'], scr['kf'], scr['ki']
    for out, shift, nm in ((out_sin, 0.0, 's'), (out_cos, 0.5 * math.pi, 'c')):
        t = tag + nm
        P.op('dve', lambda e, shift=shift: e.tensor_scalar(out=x[:, 0:n], in0=arg, scalar1=shift, scalar2=None, op0=ALU.add),
             reads=[argkey], writes=[tag + 'x'])
        P.op('dve', lambda e: e.tensor_scalar(out=kf[:, 0:n], in0=x[:, 0:n], scalar1=1.0 / TWO_PI, scalar2=None, op0=ALU.mult),
             reads=[tag + 'x'], writes=[tag + 'kf'])
        P.op('dve', lambda e: e.tensor_copy(out=ki[:, 0:n], in_=kf[:, 0:n]), reads=[tag + 'kf'], writes=[tag + 'ki'])
        P.op('dve', lambda e: e.tensor_copy(out=kf[:, 0:n], in_=ki[:, 0:n]), reads=[tag + 'ki'], writes=[tag + 'kf'])
        P.op('dve', lambda e: e.scalar_tensor_tensor(out=x[:, 0:n], in0=kf[:, 0:n], scalar=-CW1, in1=x[:, 0:n], op0=ALU.mult, op1=ALU.add),
             reads=[tag + 'kf', tag + 'x'], writes=[tag + 'x'])
        P.op('dve', lambda e: e.scalar_tensor_tensor(out=x[:, 0:n], in0=kf[:, 0:n], scalar=-CW2, in1=x[:, 0:n], op0=ALU.mult, op1=ALU.add),
             reads=[tag + 'kf', tag + 'x'], writes=[tag + 'x'])
        P.op('dve', lambda e: e.tensor_scalar(out=x[:, 0:n], in0=x[:, 0:n], scalar1=-PI_LO, scalar2=PI_LO, op0=ALU.max, op1=ALU.min),
             reads=[tag + 'x'], writes=[tag + 'x'])
        P.op('act', lambda e, out=out: e.activation(out=out, in_=x[:, 0:n], func=AF.Sin), reads=[tag + 'x'], writes=[outkey + nm])


def phase_s(k, l):
    nc, P = k.nc, k.P
    P.barrier()
    with contextlib.ExitStack() as ps:
        def t(name, shape, dt=F32):
            return sb(k, "s_" + name, shape, dt, ps)
        lre, lim, ls = t("lre", [128, 16]), t("lim", [128, 16]), t("ls", [128, 16])
        bre, bim = t("bre", [128, 8, 16]), t("bim", [128, 8, 16])
        cre, cim = t("cre", [128, 16, 16]), t("cim", [128, 16, 16])
        dcol, bglu = t("dcol", [128, 2]), t("bglu", [128, 2])
        wst = t("wst", [128, 2, 256]); wglu = t("wglu", [128, 2, 256], BF16)
        step, aa, th, rr = t("step", [128, 16]), t("aa", [128, 16]), t("th", [128, 16]), t("rr", [128, 16])
        sn, cs, thT, sT, cT = (t(n, [128, 16]) for n in ("sn", "cs", "thT", "sT", "cT"))
        nsT = t("nsT", [128, 16])
        lbr, lbi, nr, d2, inv, cr, ci, tq = (t(n, [128, 16]) for n in ("lbr", "lbi", "nr", "d2", "inv", "cr", "ci", "tq"))
        scr = dict(x=t("scx", [128, TL]), kf=t("sckf", [128, TL]), ki=t("scki", [128, TL], I32))
        irow_i = t("irow_i", [128, TL], I32); irowT = t("irowT", [128, TL])
        arg = t("arg", [128, TL])
        t1, t2, bbr, bbi = (t(n, [128, 16]) for n in ("t1", "t2", "bbr", "bbi"))
        Bpad = [t("Bpad%d" % i, [128, 128]) for i in range(2)]
        LB = [[t("LB%d_%d" % (i, ri), [128, 128], BF16) for ri in range(2)] for i in range(16)]
        LC = [[t("LC%d_%d" % (i, ri), [128, 128], BF16) for ri in range(3)] for i in range(16)]
        SIN = [t("SIN%d" % i, [128, TL]) for i in range(16)]
        COS = [t("COS%d" % i, [128, TL]) for i in range(16)]
        Rt = [t("Rt%d" % i, [128, TL]) for i in range(16)]
        init = [t("init%d" % i, [128, 2]) for i in range(8)]
        tc_ = [t("tc%d" % i, [128, 2]) for i in range(8)]
        uraw = [t("uraw%d" % i, [128, 2, TL], BF16) for i in range(2)]
        uc = [t("uc%d" % i, [128, 2, TL], BF16) for i in range(2)]
        mm_ = [[t("m%d_%d" % (j, i), [128, TL]) for j in range(4)] for i in range(NSET)]
        zin = [t("zin%d" % i, [128, 2, TL]) for i in range(NSET)]
        zz = [t("z%d" % i, [128, 2, TL]) for i in range(NSET)]
        qq = [[t("q%d_%d" % (j, i), [128, TL], BF16) for j in range(4)] for i in range(NSET)]
        yfs = [t("yfs%d" % i, [128, 2, TL]) for i in range(2)]
        ys = [t("ys%d" % i, [128, 2, TL]) for i in range(2)]
        x2 = [t("x2%d" % i, [128, 2, TL]) for i in range(2)]
        gg = [t("g%d" % i, [128, 2, TL], BF16) for i in range(2)]
        sig = [t("sig%d" % i, [128, 2, TL]) for i in range(2)]
        yo = [t("yo%d" % i, [128, 2, TL], BF16) for i in range(2)]

        for tl, src, n, key in ((lre, k.s_lre, 16, 'lre'), (lim, k.s_lim, 16, 'lim'), (ls, k.s_ls, 16, 'ls'),
                                (bre, k.s_bre, 128, 'bre'), (bim, k.s_bim, 128, 'bim'),
                                (cre, k.s_cre, 256, 'cre'), (cim, k.s_cim, 256, 'cim'),
                                (dcol, k.s_d, 2, 'dcol'), (bglu, k.s_bglu, 2, 'bglu')):
            P.dma('sp', lambda e, tl=tl, src=src, n=n: e.dma_start(
                out=tl[:].rearrange("p a b -> p (a b)") if len(tl.shape) == 3 else tl[:],
                in_=dap(src, l * 128 * n, [[n, 128], [1, n]])), writes=['s_' + key])
        P.dma('sp', lambda e: e.dma_start(out=wst[:], in_=dap(k.s_wglu, l * 65536, [[256, 128], [128 * 256, 2], [1, 256]])), writes=['s_wst'])
        P.op('act', lambda e: e.copy(out=wglu[:], in_=wst[:]), reads=['s_wst'], writes=['s_wglu'])
        P.op('pool', lambda e: e.iota(irow_i[:], [[1, TL]], base=0, channel_multiplier=0), writes=['s_irow_i'])
        P.op('dve', lambda e: e.tensor_copy(out=irowT[:], in_=irow_i[:]), reads=['s_irow_i'], writes=['s_irowT'])
        P.op('act', lambda e: e.activation(out=step[:], in_=ls[:], func=AF.Exp), reads=['s_ls'], writes=['s_step'])
        P.op('dve', lambda e: e.tensor_tensor(out=aa[:], in0=lre[:], in1=step[:], op=ALU.mult), reads=['s_lre', 's_step'], writes=['s_aa'])
        P.op('dve', lambda e: e.tensor_tensor(out=th[:], in0=lim[:], in1=step[:], op=ALU.mult), reads=['s_lim', 's_step'], writes=['s_th'])
        P.op('act', lambda e: e.activation(out=rr[:], in_=aa[:], func=AF.Exp), reads=['s_aa'], writes=['s_rr'])
        sincos(k, th[:], 's_th', sn[:], cs[:], 's_th_', 16, scr, 's_sc_')
        P.op('dve', lambda e: e.tensor_scalar(out=thT[:], in0=th[:], scalar1=float(TL), scalar2=None, op0=ALU.mult), reads=['s_th'], writes=['s_thT'])
        sincos(k, thT[:], 's_thT', sT[:], cT[:], 's_thT_', 16, scr, 's_sc_')
        P.op('dve', lambda e: e.tensor_scalar(out=nsT[:], in0=sT[:], scalar1=-1.0, scalar2=None, op0=ALU.mult), reads=['s_thT_s'], writes=['s_nsT'])
        P.op('dve', lambda e: e.tensor_tensor(out=lbr[:], in0=rr[:], in1=cs[:], op=ALU.mult), reads=['s_rr', 's_th_c'], writes=['s_lbr'])
        P.op('dve', lambda e: e.tensor_tensor(out=lbi[:], in0=rr[:], in1=sn[:], op=ALU.mult), reads=['s_rr', 's_th_s'], writes=['s_lbi'])
        P.op('dve', lambda e: e.tensor_scalar(out=nr[:], in0=lbr[:], scalar1=-1.0, scalar2=None, op0=ALU.add), reads=['s_lbr'], writes=['s_nr'])
        P.op('dve', lambda e: e.tensor_tensor(out=d2[:], in0=lre[:], in1=lre[:], op=ALU.mult), reads=['s_lre'], writes=['s_d2'])
        P.op('dve', lambda e: e.tensor_tensor(out=tq[:], in0=lim[:], in1=lim[:], op=ALU.mult), reads=['s_lim'], writes=['s_tq'])
        P.op('dve', lambda e: e.tensor_tensor(out=d2[:], in0=d2[:], in1=tq[:], op=ALU.add), reads=['s_d2', 's_tq'], writes=['s_d2'])
        P.op('dve', lambda e: e.reciprocal(out=inv[:], in_=d2[:]), reads=['s_d2'], writes=['s_inv'])
        P.op('dve', lambda e: e.tensor_tensor(out=cr[:], in0=nr[:], in1=lre[:], op=ALU.mult), reads=['s_nr', 's_lre'], writes=['s_cr'])
        P.op('dve', lambda e: e.tensor_tensor(out=tq[:], in0=lbi[:], in1=lim[:], op=ALU.mult), reads=['s_lbi', 's_lim', 's_d2'], writes=['s_tq'])
        P.op('dve', lambda e: e.tensor_tensor(out=cr[:], in0=cr[:], in1=tq[:], op=ALU.add), reads=['s_cr', 's_tq'], writes=['s_cr'])
        P.op('dve', lambda e: e.tensor_tensor(out=cr[:], in0=cr[:], in1=inv[:], op=ALU.mult), reads=['s_cr', 's_inv'], writes=['s_cr'])
        P.op('dve', lambda e: e.tensor_tensor(out=ci[:], in0=lbi[:], in1=lre[:], op=ALU.mult), reads=['s_lbi', 's_lre'], writes=['s_ci'])
        P.op('dve', lambda e: e.tensor_tensor(out=tq[:], in0=nr[:], in1=lim[:], op=ALU.mult), reads=['s_nr', 's_lim', 's_cr'], writes=['s_tq'])
        P.op('dve', lambda e: e.tensor_tensor(out=ci[:], in0=ci[:], in1=tq[:], op=ALU.subtract), reads=['s_ci', 's_tq'], writes=['s_ci'])
        P.op('dve', lambda e: e.tensor_tensor(out=ci[:], in0=ci[:], in1=inv[:], op=ALU.mult), reads=['s_ci', 's_inv'], writes=['s_ci'])
        for idx in range(16):
            pair = idx // 2
            c0 = 32 * (pair % 4)
            ic = lambda tl, idx=idx: tl[:, idx:idx + 1]
            P.op('dve', lambda e, pair=pair, idx=idx: e.tensor_scalar(out=t1[:], in0=bim[:, pair, :], scalar1=ci[:, idx:idx + 1], scalar2=None, op0=ALU.mult),
                 reads=['s_bim', 's_ci'], writes=['s_t1'])
            P.op('dve', lambda e, pair=pair, idx=idx: e.scalar_tensor_tensor(out=bbr[:], in0=bre[:, pair, :], scalar=cr[:, idx:idx + 1], in1=t1[:],
                                                                              op0=ALU.mult, op1=ALU.subtract),
                 reads=['s_bre', 's_cr', 's_t1'], writes=['s_bbr'])
            P.op('dve', lambda e, pair=pair, idx=idx: e.tensor_scalar(out=t2[:], in0=bre[:, pair, :], scalar1=ci[:, idx:idx + 1], scalar2=None, op0=ALU.mult),
                 reads=['s_bre', 's_ci'], writes=['s_t2'])
            P.op('dve', lambda e, pair=pair, idx=idx: e.scalar_tensor_tensor(out=bbi[:], in0=bim[:, pair, :], scalar=cr[:, idx:idx + 1], in1=t2[:],
                                                                              op0=ALU.mult, op1=ALU.add),
                 reads=['s_bim', 's_cr', 's_t2'], writes=['s_bbi'])
            for ri, (bb, bk_) in enumerate(((bbr, 's_bbr'), (bbi, 's_bbi'))):
                bp = Bpad[ri]
                bpk = 's_Bpad%d' % ri
                P.op('pool', lambda e, bp=bp: e.memset(bp[:], 0.0), writes=[bpk])
                P.op('dve', lambda e, bp=bp, bb=bb, c0=c0: e.tensor_copy(out=bp[0:64, c0:c0 + 16], in_=bb[0:64, :]), reads=[bk_, bpk], writes=[bpk + 'a'])
                P.op('dve', lambda e, bp=bp, bb=bb, c0=c0: e.tensor_copy(out=bp[64:128, c0 + 16:c0 + 32], in_=bb[64:128, :]), reads=[bk_, bpk], writes=[bpk + 'b'])
                bkk = ri
                P.op('pe', lambda e, bp=bp, bkk=bkk: e.transpose(out=k.bank[bkk][:, 0:128], in_=bp[:], identity=k.identf[:]),
                     reads=[bpk, bpk + 'a', bpk + 'b', 'identf'], writes=['bank%d' % bkk])
                P._update(P.res['bank%d' % bkk]['w'], [bpk], [])
                P.op('act', lambda e, idx=idx, ri=ri, bkk=bkk: e.copy(out=LB[idx][ri][:], in_=k.bank[bkk][:, 0:128]),
                     reads=['bank%d' % bkk], writes=['s_LB%d' % idx + '_%d' % ri])
            for ri, (cc_, ck, sgn) in enumerate(((cre, 's_cre', 1.0), (cim, 's_cim', -1.0), (cre, 's_cre', -1.0))):
                lc = LC[idx][ri]
                lck = 's_LC%d_%d' % (idx, ri)
                P.op('pool', lambda e, lc=lc: e.memset(lc[:], 0.0), writes=[lck])
                P.op('dve', lambda e, lc=lc, cc_=cc_, c0=c0, sgn=sgn, idx=idx: e.tensor_scalar(
                    out=lc[0:64, c0:c0 + 16], in0=cc_[0:64, idx, :], scalar1=sgn, scalar2=None, op0=ALU.mult), reads=[ck, lck], writes=[lck + 'a'])
                P.op('dve', lambda e, lc=lc, cc_=cc_, c0=c0, sgn=sgn, idx=idx: e.tensor_scalar(
                    out=lc[64:128, c0 + 16:c0 + 32], in0=cc_[64:128, idx, :], scalar1=sgn, scalar2=None, op0=ALU.mult), reads=[ck, lck], writes=[lck + 'b'])
            P.op('dve', lambda e, idx=idx: e.tensor_scalar(out=arg[:], in0=irowT[:], scalar1=th[:, idx:idx + 1], scalar2=None, op0=ALU.mult),
                 reads=['s_irowT', 's_th', 's_tab%d_s' % (idx - 1), 's_tab%d_c' % (idx - 1)], writes=['s_arg'])
            sincos(k, arg[:], 's_arg', SIN[idx][:], COS[idx][:], 's_tab%d_' % idx, TL, scr, 's_sc_')
            P.op('pool', lambda e, idx=idx: e.tensor_copy(out=Rt[idx][:], in_=bc(rr, idx, TL, 16)), reads=['s_rr'], writes=['s_Rt%d' % idx])
        nu = 0
        for d in range(2):
            for pr in range(8):
                P.op('dve', lambda e, pr=pr: e.memset(init[pr][:], 0.0), writes=['s_init%d' % pr])
            for n in range(NCH):
                cf = n if d == 0 else NCH - 1 - n
                ub = n % 2
                P.dma('sp', lambda e, ub=ub, cf=cf: e.dma_start(out=uraw[ub][:], in_=dap(k.fm, 512 * T + cf * TL, [[T, 128], [128 * T, 2], [1, TL]])),
                      reads=['fm4_%d' % (cf * TL // 512), 'fm5_%d' % (cf * TL // 512)], writes=['s_uraw%d' % ub])
                if d == 0:
                    ucur, uck = uraw[ub], 's_uraw%d' % ub
                else:
                    for ft in range(2):
                        P.op('pool', lambda e, ub=ub, ft=ft: e.tensor_copy(out=uc[ub][:, ft, :], in_=dap(uraw[ub], ft * TL + TL - 1, [[2 * TL, 128], [-1, TL]])),
                             reads=['s_uraw%d' % ub], writes=['s_uc%d_%d' % (ub, ft)])
                    P.res['s_uc%d' % ub] = P.res['s_uc%d_1' % ub]
                    ucur, uck = uc[ub], 's_uc%d' % ub
                    P.dma('sp', lambda e, ub=ub, cf=cf: e.dma_start(out=yfs[ub][:], in_=dap(k.yf, cf * TL, [[T, 128], [128 * T, 2], [1, TL]])),
                          reads=['yf_%d' % cf], writes=['s_yfs%d' % ub])
                def unit(pr, nu_, n=n, d=d, ub=ub, ucur=ucur, uck=uck):
                    idx = pr * 2 + d
                    ft = pr // 4
                    u2 = nu_ % NSET
                    u3 = nu_ % NSET
                    bb = nu_ % 4
                    ucks = [uck] if d == 0 else ['s_uc%d_0' % ub, 's_uc%d_1' % ub]

                    def mmb(e, idx=idx, ft=ft, bb=bb, ucur=ucur):
                        e.matmul(k.bank[bb][:, 0:TL], lhsT=LB[idx][0][:], rhs=ucur[:, ft, :], start=True, stop=True)
                        return e.matmul(k.bank[bb][:, TL:2 * TL], lhsT=LB[idx][1][:], rhs=ucur[:, ft, :], start=True, stop=True)
                    P.op('pe', mmb, reads=ucks + ['s_LB%d_0' % idx, 's_LB%d_1' % idx], writes=['bank%d' % bb])
                    yield
                    m = mm_[u2]
                    mk = lambda j: 's_m%d_%d' % (j, u2)
                    tabs = ['s_tab%d_s' % idx, 's_tab%d_c' % idx]
                    br_, bi_ = k.bank[bb][:, 0:TL], k.bank[bb][:, TL:2 * TL]
                    P.op('dve', lambda e, m=m, br_=br_, idx=idx: e.tensor_tensor(out=m[0][:], in0=br_, in1=COS[idx][:], op=ALU.mult),
                         reads=['bank%d' % bb] + tabs, writes=[mk(0)])
                    yield
                    P.op('dve', lambda e, m=m, bi_=bi_, idx=idx: e.tensor_tensor(out=m[1][:], in0=bi_, in1=SIN[idx][:], op=ALU.mult),
                         reads=['bank%d' % bb] + tabs, writes=[mk(1)])
                    yield
                    P.op('dve', lambda e, m=m, bi_=bi_, idx=idx: e.tensor_tensor(out=m[2][:], in0=bi_, in1=COS[idx][:], op=ALU.mult),
                         reads=['bank%d' % bb] + tabs, writes=[mk(2)])
                    yield
                    P.op('dve', lambda e, m=m, br_=br_, idx=idx: e.tensor_tensor(out=m[3][:], in0=br_, in1=SIN[idx][:], op=ALU.mult),
                         reads=['bank%d' % bb] + tabs, writes=[mk(3)])
                    yield
                    zk = 's_zin%d' % u2
                    P.op('pool', lambda e, m=m, u2=u2: e.tensor_tensor(out=zin[u2][:, 0, :], in0=m[0][:], in1=m[1][:], op=ALU.add),
                         reads=[mk(0), mk(1)], writes=[zk + 'r'])
                    yield
                    P.op('pool', lambda e, m=m, u2=u2: e.tensor_tensor(out=zin[u2][:, 1, :], in0=m[2][:], in1=m[3][:], op=ALU.subtract),
                         reads=[mk(2), mk(3)], writes=[zk + 'i'])
                    yield
                    zt = zz[u3]
                    ztk = 's_z%d' % u3
                    ik = 's_init%d' % pr
                    P.op('dve', lambda e, zt=zt, u2=u2, idx=idx, pr=pr: e.tensor_tensor_scan(
                        out=zt[:, 0, :], data0=Rt[idx][:], data1=zin[u2][:, 0, :], initial=init[pr][:, 0:1], op0=ALU.mult, op1=ALU.add),
                        reads=[zk + 'r', 's_Rt%d' % idx, ik, ik + 'r', ik + 'i'], writes=[ztk + 'r'])
                    yield
                    P.op('dve', lambda e, zt=zt, u2=u2, idx=idx, pr=pr: e.tensor_tensor_scan(
                        out=zt[:, 1, :], data0=Rt[idx][:], data1=zin[u2][:, 1, :], initial=init[pr][:, 1:2], op0=ALU.mult, op1=ALU.add),
                        reads=[zk + 'i', 's_Rt%d' % idx, ik, ik + 'r', ik + 'i'], writes=[ztk + 'i'])
                    yield
                    if n < NCH - 1:
                        zl_r, zl_i = zt[:, 0, TL - 1:TL], zt[:, 1, TL - 1:TL]
                        tk = 's_tc%d' % pr
                        P.op('act', lambda e, pr=pr, idx=idx, zl_i=zl_i: e.activation(out=tc_[pr][:, 0:1], in_=zl_i, func=AF.Identity, scale=nsT[:, idx:idx + 1]),
                             reads=[ztk + 'i', 's_nsT'], writes=[tk + 'a'])
                        yield
                        P.op('act', lambda e, pr=pr, idx=idx, zl_r=zl_r: e.activation(out=tc_[pr][:, 1:2], in_=zl_r, func=AF.Identity, scale=sT[:, idx:idx + 1]),
                             reads=[ztk + 'r', 's_thT_s'], writes=[tk + 'b'])
                        yield
                        P.op('act', lambda e, pr=pr, idx=idx, zl_r=zl_r: e.activation(out=init[pr][:, 0:1], in_=zl_r, func=AF.Identity,
                                                                                     scale=cT[:, idx:idx + 1], bias=tc_[pr][:, 0:1]),
                             reads=[ztk + 'r', 's_thT_c', tk + 'a', ik], writes=[ik + 'r'])
                        yield
                        P.op('act', lambda e, pr=pr, idx=idx, zl_i=zl_i: e.activation(out=init[pr][:, 1:2], in_=zl_i, func=AF.Identity,
                                                                                     scale=cT[:, idx:idx + 1], bias=tc_[pr][:, 1:2]),
                             reads=[ztk + 'i', 's_thT_c', tk + 'b', ik], writes=[ik + 'i'])
                        P.res[ik] = P.res[ik + 'i']
                        P._update(P.res[ik]['w'], [], [])
                        yield
                    q = qq[u3]
                    qk = lambda j: 's_q%d_%d' % (j, u3)
                    P.op('pool', lambda e, q=q, zt=zt, idx=idx: e.tensor_tensor(out=q[0][:], in0=zt[:, 0, :], in1=COS[idx][:], op=ALU.mult),
                         reads=[ztk + 'r'] + tabs, writes=[qk(0)])
                    yield
                    P.op('dve', lambda e, q=q, zt=zt, idx=idx: e.tensor_tensor(out=q[1][:], in0=zt[:, 1, :], in1=SIN[idx][:], op=ALU.mult),
                         reads=[ztk + 'i'] + tabs, writes=[qk(1)])
                    yield
                    P.op('dve', lambda e, q=q, zt=zt, idx=idx: e.tensor_tensor(out=q[2][:], in0=zt[:, 0, :], in1=SIN[idx][:], op=ALU.mult),
                         reads=[ztk + 'r'] + tabs, writes=[qk(2)])
                    yield
                    P.op('dve', lambda e, q=q, zt=zt, idx=idx: e.tensor_tensor(out=q[3][:], in0=zt[:, 1, :], in1=COS[idx][:], op=ALU.mult),
                         reads=[ztk + 'i'] + tabs, writes=[qk(3)])
                    yield
                    yb = 4 + ft
                    first = (pr % 4 == 0)
                    last = (pr % 4 == 3)

                    def mmc(e, idx=idx, q=q, yb=yb, first=first, last=last):
                        e.matmul(k.bank[yb][:, 0:TL], lhsT=LC[idx][0][:], rhs=q[0][:], start=first, stop=False)
                        e.matmul(k.bank[yb][:, 0:TL], lhsT=LC[idx][2][:], rhs=q[1][:], start=False, stop=False)
                        e.matmul(k.bank[yb][:, 0:TL], lhsT=LC[idx][1][:], rhs=q[2][:], start=False, stop=False)
                        return e.matmul(k.bank[yb][:, 0:TL], lhsT=LC[idx][1][:], rhs=q[3][:], start=False, stop=last)
                    lckeys = ['s_LC%d_%d%s' % (idx, ri, sfx) for ri in range(3) for sfx in ('a', 'b')]
                    qkeys = [qk(j) for j in range(4)]
                    if first:
                        P.op('pe', mmc, reads=qkeys + lckeys, writes=['bank%d' % yb])
                        yield
                    else:
                        P.op('pe', mmc, reads=qkeys + ['bank%d' % yb] + lckeys, writes=['bank%d_acc' % yb])
                        yield
                        P.res['bank%d' % yb]['w'] = P.res['bank%d_acc' % yb]['w']
                    if last:
                        if d == 0:
                            P.op('act', lambda e, ub=ub, ft=ft, yb=yb: e.copy(out=ys[ub][:, ft, :], in_=k.bank[yb][:, 0:TL]),
                                 reads=['bank%d' % yb], writes=['s_ys%d_%d' % (ub, ft)])
                            yield
                        else:
                            P.op('dve', lambda e, ub=ub, ft=ft, yb=yb: e.tensor_tensor(
                                out=ys[ub][:, ft, :], in0=yfs[ub][:, ft, :], in1=dap(k.bank[yb], TL - 1, [[512, 128], [-1, TL]]), op=ALU.add),
                                reads=['bank%d' % yb, 's_yfs%d' % ub], writes=['s_ys%d_%d' % (ub, ft)])
                            yield

                for p0 in range(0, 8, SW):
                    gens = [unit(p0 + i_, nu + i_) for i_ in range(SW)]
                    nu += SW
                    while gens:
                        for g_ in list(gens):
                            try:
                                next(g_)
                            except StopIteration:
                                gens.remove(g_)
                if d == 0:
                    P.dma('sp', lambda e, ub=ub, cf=cf: e.dma_start(out=dap(k.yf, cf * TL, [[T, 128], [128 * T, 2], [1, TL]]), in_=ys[ub][:]),
                          reads=['s_ys%d_0' % ub, 's_ys%d_1' % ub], writes=['yf_%d' % cf])
                    P._update(P.res['yf_%d' % cf]['w'], ['s_ys%d_0' % ub, 's_ys%d_1' % ub], [])
                    continue
                ysk = ['s_ys%d_0' % ub, 's_ys%d_1' % ub]
                for ft in range(2):
                    P.op('dve', lambda e, ub=ub, ft=ft: e.scalar_tensor_tensor(
                        out=ys[ub][:, ft, :], in0=uraw[ub][:, ft, :], scalar=dcol[:, ft:ft + 1], in1=ys[ub][:, ft, :], op0=ALU.mult, op1=ALU.add),
                        reads=['s_uraw%d' % ub, 's_dcol', ysk[ft]], writes=[ysk[ft]])
                P.op('pool', lambda e, ub=ub: e.tensor_tensor(out=x2[ub][:], in0=ys[ub][:], in1=ys[ub][:], op=ALU.mult), reads=ysk, writes=['s_x2%d' % ub])
                P.op('pool', lambda e, ub=ub: e.tensor_scalar(out=x2[ub][:], in0=x2[ub][:], scalar1=0.044715, scalar2=1.0, op0=ALU.mult, op1=ALU.add),
                     reads=['s_x2%d' % ub], writes=['s_x2%d' % ub])
                P.op('pool', lambda e, ub=ub: e.tensor_tensor(out=x2[ub][:], in0=x2[ub][:], in1=ys[ub][:], op=ALU.mult), reads=['s_x2%d' % ub] + ysk, writes=['s_x2%d' % ub])
                P.op('act', lambda e, ub=ub: e.activation(out=x2[ub][:], in_=x2[ub][:], func=AF.Tanh, scale=math.sqrt(2.0 / math.pi)),
                     reads=['s_x2%d' % ub], writes=['s_x2%d' % ub])
                P.op('pool', lambda e, ub=ub: e.tensor_scalar(out=x2[ub][:], in0=x2[ub][:], scalar1=0.5, scalar2=0.5, op0=ALU.mult, op1=ALU.add),
                     reads=['s_x2%d' % ub], writes=['s_x2%d' % ub])
                P.op('pool', lambda e, ub=ub: e.tensor_tensor(out=ys[ub][:], in0=x2[ub][:], in1=ys[ub][:], op=ALU.mult), reads=['s_x2%d' % ub] + ysk, writes=['s_yg%d' % ub])
                P._update(P.res['s_yg%d' % ub]['w'], [], ysk)
                P.op('act', lambda e, ub=ub: e.copy(out=gg[ub][:], in_=ys[ub][:]), reads=['s_yg%d' % ub] + ysk, writes=['s_gg%d' % ub])
                for fo in range(2):
                    gb = 6

                    def mmg(e, ub=ub, fo=fo, gb=gb):
                        e.matmul(k.bank[gb][:, 0:TL], lhsT=wglu[:, 0, fo * 128:(fo + 1) * 128], rhs=gg[ub][:, 0, :], start=True, stop=False)
                        return e.matmul(k.bank[gb][:, 0:TL], lhsT=wglu[:, 1, fo * 128:(fo + 1) * 128], rhs=gg[ub][:, 1, :], start=False, stop=True)
                    P.op('pe', mmg, reads=['s_gg%d' % ub, 's_wglu'], writes=['bank%d' % gb])
                    P.op('act', lambda e, ub=ub, fo=fo, gb=gb: e.activation(out=sig[ub][:, fo, :], in_=k.bank[gb][:, 0:TL], func=AF.Sigmoid,
                                                                           bias=bglu[:, fo:fo + 1], scale=1.0),
                         reads=['bank%d' % gb, 's_bglu'], writes=['s_sig%d_%d' % (ub, fo)])
                P.op('dve', lambda e, ub=ub: e.tensor_tensor(out=yo[ub][:], in0=ys[ub][:], in1=sig[ub][:], op=ALU.mult),
                     reads=['s_yg%d' % ub, 's_sig%d_0' % ub, 's_sig%d_1' % ub] + ysk, writes=['s_yo%d' % ub])
                P.dma('sp', lambda e, ub=ub, cf=cf: e.dma_start(out=dap(k.yssmT, cf * TL, [[T, 128], [128 * T, 2], [1, TL]]), in_=yo[ub][:]),
                      reads=['s_yo%d' % ub], writes=['yssmT_c%d' % cf])
                P._update(P.res['yssmT_c%d' % cf]['w'], ['s_yo%d' % ub], [])
        for ti in range(NT):
            P.res['yssmT%d' % ti] = P.res['yssmT_c%d' % (ti * 128 // TL)]


CAP = 512
NEXP = 16
NBIS = 30


def phase_f(k, l, last):
    nc, P = k.nc, k.P
    P.barrier()
    dst = k.out if last else k.hA
    with contextlib.ExitStack() as ps:
        def tp(name, shape, dt=F32):
            return sb(k, "f_" + name, shape, dt, ps)
        idx_t = [[tp("idx%d_%d" % (e_, cb), [128, 1], I32) for cb in range(4)] for e_ in range(16)]
        gate_all = tp("gate_all", [128, 16, 4])
        p2 = contextlib.ExitStack()

        def t(name, shape, dt=F32):
            return sb(k, "f_" + name, shape, dt, p2)
        rw = t("rw", [128, 8, 16])
        aff = t("aff", [128, NT, 16])
        ones = t("ones", [128, 128])
        ustr = t("ustr", [128, 128])
        lo, thr, pc, gew = t("lo", [128, 16]), t("thr", [128, 16]), t("pc", [128, 16]), t("gew", [128, 16])
        cmpb = t("cmpb", [128, NT, 16])
        met = t("met", [128, 16, NT])
        incl = t("incl", [128, 16, NT])
        rpat_i = t("rpat_i", [128, 16, NT], I32)
        rpat = t("rpat", [128, 16, NT])
        posm = t("posm", [128, 16, NT])
        io_i = t("io_i", [128, 512], I32)
        io512 = t("io512", [128, 512], mybir.dt.float16)
        tvi = t("tvi", [128, NT, 16], I32)
        r1 = t("r1", [128, NT, 16])
        tv = t("tv", [128, NT, 16, 5], BF16)
        idf = t("idf", [128, 4])
        slot = t("slot", [128, 4, 5])
        oh = [t("oh%d" % i, [128, 512], BF16) for i in range(3)]
        with contextlib.ExitStack() as p1:
            h1t = [sb(k, "f_h1r%d" % i, [128, D], F32, p1) for i in range(2)]
            h1T = [sb(k, "f_h1T%d" % i, [128, 8, 128], F32, p1) for i in range(2)]
            ex = [sb(k, "f_ex%d" % i, [128, 16], F32, p1) for i in range(2)]
            sm = [dict((n, sb(k, "f_%s%d" % (n, i), [128, 1], F32, p1)) for n in ('mx', 'sum', 'rs')) for i in range(2)]
            P.dma('sp', lambda e: e.dma_start(out=rw[:], in_=dap(k.router_w, l * D * 16, [[16, 128], [128 * 16, 8], [1, 16]])), writes=['f_rw'])
            def f1_load(ti):
                b = ti % 2
                P.dma('sp', lambda e: e.dma_start(out=h1t[b][:], in_=dap(k.h1, ti * 128 * D, [[D, 128], [1, D]])),
                      reads=['h1_%d' % ti], writes=['f_h1r%d' % b])
            for ti in range(NT):
                b = ti % 2
                if ti == 0:
                    f1_load(0)
                if ti + 1 < NT:
                    f1_load(ti + 1)
                for hh in range(2):
                    bk = (2 * ti + hh) % 4

                    def tr(e, b=b, hh=hh, bk=bk):
                        r = None
                        for j in range(4):
                            kk = hh * 4 + j
                            r = e.transpose(out=k.bank[bk][:, j * 128:(j + 1) * 128], in_=h1t[b][:, kk * 128:(kk + 1) * 128], identity=k.identf[:])
                        return r
                    P.op('pe', tr, reads=['f_h1r%d' % b, 'identf'], writes=['bank%d' % bk])
                    P.op('dve' if hh == 0 else 'act', lambda e, b=b, hh=hh, bk=bk: (e.tensor_copy if hh == 0 else e.copy)(
                        out=h1T[b][:, hh * 4:(hh + 1) * 4, :], in_=k.bank[bk][:, :].rearrange("p (a b) -> p a b", a=4)),
                        reads=['bank%d' % bk], writes=['f_h1T%d_%d' % (b, hh)])
                lb = 4 + (ti % 2)

                def mml(e, b=b, lb=lb):
                    r = None
                    for kk in range(8):
                        r = e.matmul(k.bank[lb][:, 0:16], lhsT=h1T[b][:, kk, :], rhs=rw[:, kk, :], start=(kk == 0), stop=(kk == 7))
                    return r
                P.op('pe', mml, reads=['f_h1T%d_0' % b, 'f_h1T%d_1' % b, 'f_rw'], writes=['bank%d' % lb])
                m = sm[b]
                P.op('dve', lambda e, m=m, lb=lb: e.tensor_reduce(out=m['mx'][:], in_=k.bank[lb][:, 0:16], axis=AX.X, op=ALU.max, negate=True),
                     reads=['bank%d' % lb], writes=['f_mx%d' % b])
                P.op('act', lambda e, m=m, lb=lb, b=b: e.activation(out=ex[b][:], in_=k.bank[lb][:, 0:16], func=AF.Exp, bias=m['mx'][:], scale=1.0,
                                                                   accum_out=m['sum'][:]),
                     reads=['bank%d' % lb, 'f_mx%d' % b], writes=['f_ex%d' % b, 'f_sum%d' % b])
                P.op('dve', lambda e, m=m: e.reciprocal(out=m['rs'][:], in_=m['sum'][:]), reads=['f_sum%d' % b], writes=['f_rs%d' % b])
                P.op('dve', lambda e, m=m, b=b, ti=ti: e.tensor_scalar(out=aff[:, ti, :], in0=ex[b][:], scalar1=m['rs'][:], scalar2=None, op0=ALU.mult),
                     reads=['f_ex%d' % b, 'f_rs%d' % b], writes=['f_aff%d' % ti])
        affk = ['f_aff%d' % ti for ti in range(NT)]
        P.op('dve', lambda e: e.memset(ones[:], 1.0), writes=['f_ones'])
        P.op('dve', lambda e: e.tensor_scalar(out=ustr[:], in0=k.dist[:], scalar1=0.0, scalar2=None, op0=ALU.is_gt), reads=['dist'], writes=['f_ustr'])
        P.op('dve', lambda e: e.memset(lo[:], 0.0), writes=['f_lo'])
        for it in range(NBIS):
            w = 0.5 ** (it + 1)
            P.op('dve', lambda e, w=w: e.tensor_scalar(out=thr[:], in0=lo[:], scalar1=w, scalar2=None, op0=ALU.add), reads=['f_lo'], writes=['f_thr'])
            P.op('dve', lambda e: e.tensor_tensor(out=cmpb[:], in0=aff[:], in1=dap(thr, 0, [[16, 128], [0, NT], [1, 16]]), op=ALU.is_ge),
                 reads=affk + ['f_thr'], writes=['f_cmpb'])
            P.op('dve', lambda e: e.tensor_reduce(out=pc[:], in_=cmpb[:].rearrange("p t e -> p e t"), axis=AX.X, op=ALU.add),
                 reads=['f_cmpb'], writes=['f_pc'])
            P.op('pe', lambda e: e.matmul(k.bank[0][:, 0:16], lhsT=ones[:], rhs=pc[:], start=True, stop=True),
                 reads=['f_ones', 'f_pc'], writes=['bank0'])
            P.op('dve', lambda e, w=w: e.tensor_scalar(out=gew[:], in0=k.bank[0][:, 0:16], scalar1=CAP - 0.5, scalar2=w, op0=ALU.is_ge, op1=ALU.mult),
                 reads=['bank0'], writes=['f_gew'])
            P.op('dve', lambda e: e.tensor_tensor(out=lo[:], in0=lo[:], in1=gew[:], op=ALU.add), reads=['f_lo', 'f_gew'], writes=['f_lo'])
        P.op('dve', lambda e: e.tensor_tensor(out=met[:], in0=aff[:].rearrange("p t e -> p e t"), in1=dap(lo, 0, [[16, 128], [1, 16], [0, NT]]), op=ALU.is_ge),
             reads=affk + ['f_lo'], writes=['f_met'])
        P.op('pool', lambda e: e.iota(rpat_i[:].rearrange("p a b -> p (a b)"), [[0, 16], [1, NT]], base=0, channel_multiplier=0), writes=['f_rpat_i'])
        P.op('dve', lambda e: e.tensor_copy(out=rpat[:], in_=rpat_i[:]), reads=['f_rpat_i'], writes=['f_rpat'])
        P.op('dve', lambda e: e.tensor_scalar(out=rpat[:], in0=rpat[:], scalar1=0.0, scalar2=None, op0=ALU.is_gt), reads=['f_rpat'], writes=['f_rpat'])
        P.op('dve', lambda e: e.tensor_tensor_scan(out=incl[:].rearrange("p a b -> p (a b)"), data0=rpat[:].rearrange("p a b -> p (a b)"),
                                                    data1=met[:].rearrange("p a b -> p (a b)"), initial=0.0, op0=ALU.mult, op1=ALU.add),
             reads=['f_rpat', 'f_met'], writes=['f_incl'])
        P.op('dve', lambda e: e.tensor_copy(out=pc[:], in_=incl[:, :, NT - 1]), reads=['f_incl'], writes=['f_pc'])
        P.op('pe', lambda e: e.matmul(k.bank[0][:, 0:16], lhsT=ustr[:], rhs=pc[:], start=True, stop=True), reads=['f_ustr', 'f_pc'], writes=['bank0'])
        P.op('dve', lambda e: e.tensor_tensor(out=posm[:], in0=incl[:], in1=met[:], op=ALU.subtract), reads=['f_incl', 'f_met'], writes=['f_posm'])
        P.op('dve', lambda e: e.tensor_tensor(out=posm[:], in0=posm[:], in1=dap(k.bank[0], 0, [[512, 128], [1, 16], [0, NT]]), op=ALU.add),
             reads=['f_posm', 'bank0'], writes=['f_posm'])
        P.op('dve', lambda e: e.scalar_tensor_tensor(out=posm[:], in0=posm[:], scalar=1.0, in1=met[:], op0=ALU.add, op1=ALU.mult),
             reads=['f_posm', 'f_met'], writes=['f_posm'])
        P.op('dve', lambda e: e.tensor_scalar(out=posm[:], in0=posm[:], scalar1=-1.0, scalar2=None, op0=ALU.add), reads=['f_posm'], writes=['f_posm'])
        P.op('pool', lambda e: e.iota(io_i[:], [[1, 512]], base=0, channel_multiplier=0), writes=['f_io_i'])
        P.op('dve', lambda e: e.tensor_copy(out=io512[:], in_=io_i[:]), reads=['f_io_i'], writes=['f_io512'])
        P.op('pool', lambda e: e.iota(tvi[:].rearrange("p a b -> p (a b)"), [[1, NT], [0, 16]], base=0, channel_multiplier=0), writes=['f_tvi'])
        P.op('dve', lambda e: e.tensor_copy(out=tv[:, :, :, 0], in_=tvi[:]), reads=['f_tvi'], writes=['f_tv0'])
        P.op('pool', lambda e: e.iota(tvi[:].rearrange("p a b -> p (a b)"), [[0, NT], [0, 16]], base=0, channel_multiplier=1), reads=['f_tv0'], writes=['f_tvi'])
        P.op('dve', lambda e: e.tensor_copy(out=tv[:, :, :, 1], in_=tvi[:]), reads=['f_tvi'], writes=['f_tv1'])
        P.op('dve', lambda e: e.tensor_copy(out=tv[:, :, :, 2], in_=aff[:]), reads=affk, writes=['f_tv2'])
        P.op('dve', lambda e: e.tensor_tensor(out=r1[:], in0=aff[:], in1=tv[:, :, :, 2], op=ALU.subtract), reads=affk + ['f_tv2'], writes=['f_r1'])
        P.op('dve', lambda e: e.tensor_copy(out=tv[:, :, :, 3], in_=r1[:]), reads=['f_r1'], writes=['f_tv3'])
        P.op('dve', lambda e: e.tensor_tensor(out=r1[:], in0=r1[:], in1=tv[:, :, :, 3], op=ALU.subtract), reads=['f_r1', 'f_tv3'], writes=['f_r1'])
        P.op('dve', lambda e: e.tensor_copy(out=tv[:, :, :, 4], in_=r1[:]), reads=['f_r1'], writes=['f_tv4'])
        tvk = ['f_tv%d' % i for i in range(5)]
        noh = 0
        for ex_ in range(NEXP):
            for ti in range(NT):
                o = noh % 3
                noh += 1
                P.op('dve', lambda e, o=o, ex_=ex_, ti=ti: e.tensor_scalar(out=oh[o][:], in0=io512[:], scalar1=posm[:, ex_, ti:ti + 1], scalar2=None, op0=ALU.is_equal),
                     reads=['f_io512', 'f_posm'], writes=['f_oh%d' % o])

                def mms(e, o=o, ex_=ex_, ti=ti):
                    r = None
                    for cb in range(4):
                        r = e.matmul(k.bank[cb][:, 0:5], lhsT=oh[o][:, cb * 128:(cb + 1) * 128], rhs=tv[:, ti, ex_, :],
                                     start=(ti == 0), stop=(ti == NT - 1))
                    return r
                P.op('pe', mms, reads=['f_oh%d' % o] + tvk, writes=['bank0_3'] if ti else ['bank0', 'bank1', 'bank2', 'bank3', 'bank0_3'])
            for cb in range(4):
                P.op('act', lambda e, cb=cb: e.copy(out=slot[:, cb, :], in_=k.bank[cb][:, 0:5]), reads=['bank0_3'], writes=['f_slot%d' % cb])
                P._update(P.res['f_slot%d' % cb]['w'], ['bank%d' % cb], [])
            slk = ['f_slot%d' % cb for cb in range(4)]
            P.op('dve', lambda e: e.scalar_tensor_tensor(out=idf[:], in0=slot[:, :, 0], scalar=128.0, in1=slot[:, :, 1], op0=ALU.mult, op1=ALU.add),
                 reads=slk, writes=['f_idf'])
            P.op('dve', lambda e, ex_=ex_: e.tensor_reduce(out=gate_all[:, ex_, :], in_=slot[:, :, 2:5], axis=AX.X, op=ALU.add),
                 reads=slk, writes=['f_gate%d' % ex_])
            for cb in range(4):
                P.op('dve', lambda e, ex_=ex_, cb=cb: e.tensor_copy(out=idx_t[ex_][cb][:], in_=idf[:, cb:cb + 1]), reads=['f_idf'], writes=['f_idx%d_%d' % (ex_, cb)])
            P.res['f_idx%d' % ex_] = P.res['f_idx%d_3' % ex_]
            P._update(P.res['f_idx%d' % ex_]['w'], slk + ['f_idf'], [])
        if DBG_F == 1:
            P.dma('sp', lambda e: e.dma_start(out=dap(k.dbg_gate, 0, [[64, 128], [1, 64]]), in_=gate_all[:].rearrange("p a b -> p (a b)")),
                  reads=['f_gate%d' % i for i in range(16)], writes=['dbg_gate'])
            P.dma('sp', lambda e: e.dma_start(out=dap(k.dbg_aff, 0, [[512, 128], [1, 512]]), in_=aff[:].rearrange("p a b -> p (a b)")),
                  reads=affk, writes=['dbg_aff'])
            P.dma('sp', lambda e: e.dma_start(out=dap(k.dbg_posm, 0, [[512, 128], [1, 512]]), in_=posm[:].rearrange("p a b -> p (a b)")),
                  reads=['f_posm'], writes=['dbg_posm'])
            finish(k)
            p2.close()
            return
        P.barrier()
        p2.close()
        t = tp
        zt = t("zt", [128, D])
        P.op('pool', lambda e: e.memset(zt[:], 0.0), writes=['f_zt'])
        for ti in range(NT):
            P.dma('sp', lambda e, ti=ti: e.dma_start(out=dap(k.ffn, ti * 128 * D, [[D, 128], [1, D]]), in_=zt[:]), reads=['f_zt'], writes=['ffn_z%d' % ti])
        zero_toks = [P.res['ffn_z%d' % ti]['w'] for ti in range(NT)]
        with contextlib.ExitStack() as p5:
            def t5(name, shape, dt=F32):
                return sb(k, "f_" + name, shape, dt, p5)
            xs = t5("xs", [128, 4, D], BF16)
            xsT = [t5("xsT%d" % i, [128, 8, 512], BF16) for i in range(2)]
            stgA = [t5("stgA%d" % i, [128, 8, 256]) for i in range(3)]
            stgB = [t5("stgB%d" % i, [128, 2, D]) for i in range(2)]
            wgb = [t5("wgb%d" % i, [128, 8, 512], BF16) for i in range(2)]
            wub = [t5("wub%d" % i, [128, 8, 512], BF16) for i in range(2)]
            wdb = t5("wdb", [128, 16, D], BF16)
            hdn = t5("hdn", [128, 16, 512], BF16)
            sl = [t5("sl%d" % i, [128, 512]) for i in range(2)]
            ost = [t5("ost%d" % i, [128, D]) for i in range(2)]
            cnt = dict(stg=0, stgB=0, cast=0, psg=0, ps2=0, ost=0)

            def gather(ex_):
                xb = ex_ % 2
                for cb in range(4):
                    P.dma('pool', lambda e, cb=cb, ex_=ex_: e.indirect_dma_start(
                        out=xs[:, cb, :], out_offset=None, in_=k.h1b[:, :],
                        in_offset=bass.IndirectOffsetOnAxis(ap=idx_t[ex_][cb][:, :], axis=0)),
                        reads=['f_idx%d' % ex_] + ['h1b_%d' % ti for ti in range(NT)], writes=['f_xs%d' % cb])

                    def trx(e, cb=cb):
                        r = None
                        for kk in range(8):
                            r = e.transpose(out=k.bankT[:, kk * 128:(kk + 1) * 128], in_=xs[:, cb, kk * 128:(kk + 1) * 128], identity=k.identb[:])
                        return r
                    P.op('pe', trx, reads=['f_xs%d' % cb, 'identb'], writes=['bankT'])
                    P.op('dve', lambda e, cb=cb, xb=xb: e.tensor_copy(out=xsT[xb][:, :, cb * 128:(cb + 1) * 128],
                                                                    in_=k.bankT[:].rearrange("p (a b) -> p a b", a=8)),
                         reads=['bankT'], writes=['f_xsT%d_%d' % (xb, cb)])

            def cast_engine():
                ce = 'act' if (cnt['cast'] % 3 != 2) else 'dve'
                cnt['cast'] += 1
                return ce

            def load_gu_piece(g, pi):
                ex_, j = g // 4, g % 4
                wb = g % 2
                wsrc, wdst, wkey = ((k.w_gate, wgb[wb], 'f_wgb%d' % wb), (k.w_up, wub[wb], 'f_wub%d' % wb))[pi // 2]
                hh = pi % 2
                sg = cnt['stg'] % 3
                cnt['stg'] += 1
                off = ((l * NEXP + ex_) * D) * 2048 + j * 512 + hh * 256
                P.dma('sp', lambda e: e.dma_start(out=stgA[sg][:], in_=dap(wsrc, off, [[2048, 128], [128 * 2048, 8], [1, 256]])),
                      writes=['f_stg%d' % sg])
                ce = cast_engine()
                P.op(ce, lambda e: (e.copy if ce == 'act' else e.tensor_copy)(out=wdst[:, :, hh * 256:(hh + 1) * 256], in_=stgA[sg][:]),
                     reads=['f_stg%d' % sg], writes=[wkey + '_%d' % hh])

            def load_d(g):
                ex_, j = g // 4, g % 4
                for hh in range(2):
                    sg = cnt['stgB'] % 2
                    cnt['stgB'] += 1
                    off = ((l * NEXP + ex_) * 2048 + j * 512 + hh * 256) * D
                    P.dma('sp', lambda e, sg=sg, off=off: e.dma_start(
                        out=stgB[sg][:], in_=dap(k.w_down, off, [[D, 128], [128 * D, 2], [1, D]])),
                        writes=['f_stgB%d' % sg])
                    ce = cast_engine()
                    kt0 = j * 4 + hh * 2
                    P.op(ce, lambda e, sg=sg, kt0=kt0, ce=ce: (e.copy if ce == 'act' else e.tensor_copy)(
                        out=wdb[:, kt0:kt0 + 2, :], in_=stgB[sg][:]),
                        reads=['f_stgB%d' % sg], writes=['f_wdb%d' % (kt0 // 2)])

            def phase1(g, fis):
                ex_, j = g // 4, g % 4
                wb = g % 2
                xb = ex_ % 2
                xk = ['f_xsT%d_%d' % (xb, cb) for cb in range(4)]
                wgk = ['f_wgb%d_0' % wb, 'f_wgb%d_1' % wb]
                wuk = ['f_wub%d_0' % wb, 'f_wub%d_1' % wb]
                for fi in fis:
                    ftile = j * 4 + fi
                    g_b = 3 + (cnt['psg'] % 2)
                    u_b = 5 + (cnt['psg'] % 2)
                    sb_ = cnt['psg'] % 2
                    cnt['psg'] += 1

                    def mg(e, g_b=g_b, wb=wb, fi=fi, xb=xb):
                        r = None
                        for kk in range(8):
                            r = e.matmul(k.bank[g_b][:, :], lhsT=wgb[wb][:, kk, fi * 128:(fi + 1) * 128], rhs=xsT[xb][:, kk, :], start=(kk == 0), stop=(kk == 7))
                        return r

                    def mu(e, u_b=u_b, wb=wb, fi=fi, xb=xb):
                        r = None
                        for kk in range(8):
                            r = e.matmul(k.bank[u_b][:, :], lhsT=wub[wb][:, kk, fi * 128:(fi + 1) * 128], rhs=xsT[xb][:, kk, :], start=(kk == 0), stop=(kk == 7))
                        return r
                    P.op('pe', mg, reads=wgk + xk, writes=['bank%d' % g_b])
                    P.op('pe', mu, reads=wuk + xk, writes=['bank%d' % u_b])
                    P.op('act', lambda e, sb_=sb_, g_b=g_b: e.activation(out=sl[sb_][:], in_=k.bank[g_b][:, :], func=AF.Silu),
                         reads=['bank%d' % g_b], writes=['f_sl%d' % sb_])
                    P.op('dve', lambda e, sb_=sb_, u_b=u_b, ftile=ftile: e.tensor_tensor(out=hdn[:, ftile, :], in0=k.bank[u_b][:, :], in1=sl[sb_][:], op=ALU.mult),
                         reads=['bank%d' % u_b, 'f_sl%d' % sb_], writes=['f_hdn%d' % ftile])

            def phase2(ex_):
                hk = ['f_hdn%d' % i for i in range(16)]
                wdk = ['f_wdb%d' % i for i in range(8)]
                for cb in range(4):
                    ob = cnt['ost'] % 2
                    cnt['ost'] += 1
                    for half in range(2):
                        pb = cnt['ps2'] % 3
                        cnt['ps2'] += 1

                        def md(e, pb=pb, cb=cb, half=half):
                            r = None
                            for kk in range(16):
                                r = e.matmul(k.bank[pb][:, :], lhsT=hdn[:, kk, cb * 128:(cb + 1) * 128], rhs=wdb[:, kk, half * 512:(half + 1) * 512],
                                             start=(kk == 0), stop=(kk == 15))
                            return r
                        P.op('pe', md, reads=hk + wdk, writes=['bank%d' % pb])
                        P.op('dve', lambda e, pb=pb, ob=ob, half=half, cb=cb, ex_=ex_: e.tensor_scalar(
                            out=ost[ob][:, half * 512:(half + 1) * 512], in0=k.bank[pb][:, :], scalar1=gate_all[:, ex_, cb:cb + 1], scalar2=None, op0=ALU.mult),
                            reads=['bank%d' % pb, 'f_gate%d' % ex_], writes=['f_ost%d_%d' % (ob, half)])
                    P.dma('pool', lambda e, ob=ob, cb=cb, ex_=ex_: e.indirect_dma_start(
                        out=k.ffn[:, :], out_offset=bass.IndirectOffsetOnAxis(ap=idx_t[ex_][cb][:, :], axis=0), in_=ost[ob][:], in_offset=None,
                        compute_op=ALU.add),
                        reads=['f_ost%d_0' % ob, 'f_ost%d_1' % ob, 'f_idx%d' % ex_] + (['ffn_z%d' % ti for ti in range(NT)] if ex_ == 0 and cb == 0 else []),
                        writes=['ffn_acc'])
                    P._update(P.res['ffn_acc']['w'], ['f_ost%d_0' % ob, 'f_ost%d_1' % ob], [])

            NG = NEXP * 4
            gather(0)
            for pi in range(4):
                load_gu_piece(0, pi)
            load_d(0)
            for g in range(NG):
                ex_, j = g // 4, g % 4
                if j == 2 and ex_ + 1 < NEXP:
                    gather(ex_ + 1)
                for fi in range(4):
                    if g + 1 < NG:
                        load_gu_piece(g + 1, fi)
                    phase1(g, [fi])
                if j == 3:
                    phase2(ex_)
                if g + 1 < NG:
                    load_d(g + 1)
        P.barrier()
        g2 = t("g2", [128, D]); b2 = t("b2", [128, D])
        h1t = [t("h1%d" % i, [128, D]) for i in range(3)]
        ft_ = [t("ft%d" % i, [128, D]) for i in range(3)]
        rt = [t("r%d" % i, [128, D]) for i in range(3)]
        h2t = [t("h2%d" % i, [128, D]) for i in range(3)]
        scr = [dict(stats=t("stats%d" % i, [128, 2, 6]), mv=t("mv%d" % i, [128, 2]), sd=t("sd%d" % i, [128, 1]), rstd=t("rstd%d" % i, [128, 1]),
                    nb=t("nb%d" % i, [128, 1]), xn=t("xn%d" % i, [128, D])) for i in range(3)]
        P.dma('sp', lambda e: e.dma_start(out=g2[:], in_=dap(k.ln2_g, l * D, [[0, 128], [1, D]])), writes=['f_g2'])
        P.dma('sp', lambda e: e.dma_start(out=b2[:], in_=dap(k.ln2_b, l * D, [[0, 128], [1, D]])), writes=['f_b2'])

        def f6_load(ti):
            b = ti % 3
            P.dma('sp', lambda e: e.dma_start(out=h1t[b][:], in_=dap(k.h1, ti * 128 * D, [[D, 128], [1, D]])),
                  reads=['h1_%d' % ti], writes=['f_h1%d' % b])
            P.dma('sp', lambda e: e.dma_start(out=ft_[b][:], in_=dap(k.ffn, ti * 128 * D, [[D, 128], [1, D]])),
                  reads=['ffn_acc'], writes=['f_ft%d' % b])
        def f6_tile(ti):
            b = ti % 3
            tok0 = ti * 128
            P.op('dve', lambda e, b=b: e.scalar_tensor_tensor(out=rt[b][:], in0=h1t[b][:], scalar=ALPHA, in1=ft_[b][:], op0=ALU.mult, op1=ALU.add),
                 reads=['f_h1%d' % b, 'f_ft%d' % b], writes=['f_r%d' % b])
            yield
            yield from layer_norm_tile(k, rt[b][:], 'f_r%d' % b, h2t[b][:], 'f_h2%d' % b, g2[:], 'f_g2', b2[:], 'f_b2', 'f%d_' % b, scr[b])
            P.dma('sp', lambda e, b=b, tok0=tok0: e.dma_start(out=dap(dst, tok0 * D, [[D, 128], [1, D]]), in_=h2t[b][:]),
                  reads=['f_h2%d' % b], writes=['hA%d' % ti])

        run_window(f6_tile, f6_load, NT, 2)


def finish(k):
    P = k.P
    for i, c in enumerate(P.dcnt):
        if c > 0:
            P._wait('sp', (('d', i), c))


def prep_inputs(inputs):
    w_in = np.asarray(inputs["w_in"])
    sl = lambda a, b: list(range(a, b))
    tm_cols = sl(256, 512) + sl(512, 768) + sl(768, 1024) + sl(1920, 2048)
    fm_cols = sl(0, 256) + sl(256, 512) + sl(1024, 1280) + sl(1280, 1792) + sl(1792, 1920)
    w_perm = np.ascontiguousarray(w_in[:, :, tm_cols + fm_cols])
    shared = {"ln_in_g": np.ascontiguousarray(inputs["ln_in_g"]), "ln_in_b": np.ascontiguousarray(inputs["ln_in_b"]),
              "w_in": w_perm,
              "ret_theta": np.ascontiguousarray(np.asarray(inputs["ret_theta"]).reshape(DEPTH, 8)),
              "attn_sink": np.ascontiguousarray(inputs["attn_sink"]),
              "w_out": np.ascontiguousarray(inputs["w_out"]),
              "ln1_g": np.ascontiguousarray(inputs["ln1_g"]), "ln1_b": np.ascontiguousarray(inputs["ln1_b"]),
              "ln2_g": np.ascontiguousarray(inputs["ln2_g"]), "ln2_b": np.ascontiguousarray(inputs["ln2_b"])}
    L = DEPTH
    A = lambda n: np.asarray(inputs[n], dtype=np.float32)
    rl = lambda x: np.ascontiguousarray(x.reshape(L, 2, 8, 2, 64).transpose(0, 3, 4, 2, 1).reshape(L, 128, 16))
    shared["s_lre"] = rl(A("ssm_lambda_re")); shared["s_lim"] = rl(A("ssm_lambda_im"))
    lsx = A("ssm_log_step").reshape(L, 2, 8, 2).transpose(0, 3, 2, 1)
    shared["s_ls"] = np.ascontiguousarray(np.broadcast_to(lsx[:, :, None, :, :], (L, 2, 64, 8, 2)).reshape(L, 128, 16))
    rb = lambda x: np.ascontiguousarray(x.reshape(L, 8, 2, 64, 16).transpose(0, 2, 3, 1, 4).reshape(L, 128, 8, 16))
    shared["s_bre"] = rb(A("ssm_b_re")); shared["s_bim"] = rb(A("ssm_b_im"))
    rc = lambda x: np.ascontiguousarray(x.reshape(L, 2, 8, 2, 16, 64).transpose(0, 3, 5, 2, 1, 4).reshape(L, 128, 16, 16))
    shared["s_cre"] = rc(A("ssm_c_re")); shared["s_cim"] = rc(A("ssm_c_im"))
    rd = lambda x: np.ascontiguousarray(x.reshape(L, 2, 128).transpose(0, 2, 1))
    shared["s_d"] = rd(A("ssm_d")); shared["s_bglu"] = rd(A("ssm_b_glu"))
    shared["s_wglu"] = np.ascontiguousarray(A("ssm_w_glu"))
    shared["router_w"] = np.ascontiguousarray(A("router_w"))
    shared["exp_w_gate"] = np.ascontiguousarray(A("exp_w_gate"))
    shared["exp_w_up"] = np.ascontiguousarray(A("exp_w_up"))
    shared["exp_w_down"] = np.ascontiguousarray(A("exp_w_down"))
    return shared


def kernel(**inputs):
    shared = prep_inputs(inputs)
    nc = build()
    x = np.asarray(inputs["x"])
    in_maps = []
    for c in range(NCORES):
        m = dict(shared)
        m["x"] = np.ascontiguousarray(x[c])
        in_maps.append(m)
    res = run_bass_kernel_spmd(nc, in_maps, core_ids=list(range(NCORES)))
    return np.stack([res.results[c]["out"] for c in range(NCORES)], axis=0)
```

```python
import math
import contextlib
import numpy as np
import concourse.bass as bass
import concourse.mybir as mybir
from concourse.bass_utils import run_bass_kernel_spmd

F32 = mybir.dt.float32
BF16 = mybir.dt.bfloat16
I32 = mybir.dt.int32
AF = mybir.ActivationFunctionType
ALU = mybir.AluOpType
AX = mybir.AxisListType

T = 4096
NT = T // 128
D = 1024
DEPTH = 2
ALPHA = (2.0 * DEPTH) ** 0.25
EPS = 1e-5
N_TM = 896
N_FM = 1408
NCORES = 4
NBA = 3

SAME_ENGINE_SYNC = True
DBG_O = 9
DBG_F = 9
DBG_CAST = 'mix'
DBG_NT = NT


class Prog:
    def __init__(self, nc, stack, ndma=32):
        self.nc = nc
        self.eng = {'pe': nc.tensor, 'act': nc.scalar, 'dve': nc.vector, 'pool': nc.gpsimd, 'sp': nc.sync}
        self.sem = {e: stack.enter_context(nc.semaphore('sem_' + e)) for e in ['pe', 'act', 'dve', 'pool']}
        self.cnt = {e: 0 for e in self.sem}
        self.nhw, self.nsw = ndma, 8
        self.dsem = [stack.enter_context(nc.semaphore('dsem%d' % i)) for i in range(self.nhw + self.nsw)]
        self.dcnt = [0] * (self.nhw + self.nsw)
        self.dnext = 0
        self.dnext_sw = 0
        self.waited = {e: {} for e in self.eng}
        self.res = {}
        self.nops = 0
        self.recent = {e: [] for e in self.eng}
        self.log = {e: [] for e in self.eng}

    def _semof(self, key):
        return self.sem[key] if isinstance(key, str) else self.dsem[key[1]]

    def _wait(self, e, tok):
        key, val = tok
        if self.waited[e].get(key, 0) >= val:
            return
        self.eng[e].wait_ge(self._semof(key), val)
        self.waited[e][key] = val
        self.log[e].append(('wait', key, val))

    def _deps(self, reads, writes):
        deps = []
        for r in reads:
            st = self.res.get(r)
            if st and st['w']:
                deps.append(st['w'])
        for w in writes:
            st = self.res.get(w)
            if st:
                if st['w']:
                    deps.append(st['w'])
                deps.extend(st['r'].items())
        return deps

    def _update(self, tok, reads, writes):
        for r in reads:
            st = self.res.setdefault(r, {'w': None, 'r': {}})
            if st['r'].get(tok[0], 0) < tok[1]:
                st['r'][tok[0]] = tok[1]
        for w in writes:
            self.res[w] = {'w': tok, 'r': {}}

    def op(self, e, fn, reads=(), writes=()):
        for tok in self._deps(reads, writes):
            if tok[0] == e and (e == 'pe' or not SAME_ENGINE_SYNC):
                continue
            self._wait(e, tok)
        inst = fn(self.eng[e])
        self.cnt[e] += 1
        inst.then_inc(self.sem[e], 1)
        self.log[e].append(('inc', e, 1))
        tok = (e, self.cnt[e])
        self._update(tok, reads, writes)
        self.nops += 1
        return tok

    def dma(self, q, fn, reads=(), writes=()):
        if q == 'pool':
            i = self.nhw + self.dnext_sw
            self.dnext_sw = (self.dnext_sw + 1) % self.nsw
        else:
            i = self.dnext
            self.dnext = (i + 1) % self.nhw
        deps = self._deps(reads, writes)
        if self.dcnt[i] > 0:
            deps.append((('d', i), self.dcnt[i]))
        rq = self.recent[q]
        if len(rq) >= 10:
            deps.append(rq.pop(0))
        for tok in deps:
            self._wait(q, tok)
        inst = fn(self.eng[q])
        self.dcnt[i] += 16
        inst.then_inc(self.dsem[i], 16)
        self.log[q].append(('inc', ('d', i), 16))
        tok = (('d', i), self.dcnt[i])
        self.recent[q].append(tok)
        self._update(tok, reads, writes)
        self.nops += 1
        return tok

    def barrier(self):
        toks = [(e, c) for e, c in self.cnt.items() if c > 0]
        toks += [(('d', i), c) for i, c in enumerate(self.dcnt) if c > 0]
        for e in self.eng:
            for tok in toks:
                self._wait(e, tok)

    def wait_all(self, e, keys):
        for r in keys:
            st = self.res.get(r)
            if st and st['w']:
                self._wait(e, st['w'])


def run_window(tile_gen, load, n, width):
    load(0)
    gens = []
    nxt = 0
    while nxt < n or gens:
        while len(gens) < width and nxt < n:
            if nxt + 1 < n:
                load(nxt + 1)
            gens.append(tile_gen(nxt))
            nxt += 1
        for g_ in list(gens):
            try:
                next(g_)
            except StopIteration:
                gens.remove(g_)


def dap(h, off, dims):
    return bass.AP(h, off, [list(d) for d in dims])


class K:
    pass


def build(debug=None, nlayers=DEPTH, stop_after=None, ext_in=None, phases=None):
    nc = bass.Bass("TRN2", target_bir_lowering=False)
    k = K()
    k.nc = nc
    k.debug = debug or []
    k.ext_in = ext_in or []
    phases = phases or ['a', 'r', 's', 't', 'o', 'f']

    def din(name, shape, dt=F32):
        return nc.dram_tensor(name, list(shape), dt, kind="ExternalInput")

    def dscr(name, shape, dt):
        kind = "ExternalOutput" if name in k.debug else ("ExternalInput" if name in k.ext_in else "Internal")
        return nc.dram_tensor(name, list(shape), dt, kind=kind)

    k.x = din("x", [T, D])
    k.ln_in_g = din("ln_in_g", [D]); k.ln_in_b = din("ln_in_b", [D])
    k.w_in = din("w_in", [DEPTH, D, 2304])
    k.out = nc.dram_tensor("out", [T, D], F32, kind="ExternalOutput")
    k.hA = dscr("hA", [T, D], F32)
    k.tm = dscr("tm", [T, N_TM], BF16)
    k.fm = dscr("fm", [N_FM, T], BF16)
    k.ycat = dscr("ycat", [T, 768], BF16)
    k.yssmT = dscr("yssmT", [256, T], BF16)
    k.ret_theta = din("ret_theta", [DEPTH, 8])
    k.attn_sink = din("attn_sink", [DEPTH, 8])
    k.w_out = din("w_out", [DEPTH, D, D])
    k.ln1_g = din("ln1_g", [DEPTH, D]); k.ln1_b = din("ln1_b", [DEPTH, D])
    k.ln2_g = din("ln2_g", [DEPTH, D]); k.ln2_b = din("ln2_b", [DEPTH, D])
    k.s_lre = din("s_lre", [DEPTH, 128, 16]); k.s_lim = din("s_lim", [DEPTH, 128, 16]); k.s_ls = din("s_ls", [DEPTH, 128, 16])
    k.s_bre = din("s_bre", [DEPTH, 128, 8, 16]); k.s_bim = din("s_bim", [DEPTH, 128, 8, 16])
    k.s_cre = din("s_cre", [DEPTH, 128, 16, 16]); k.s_cim = din("s_cim", [DEPTH, 128, 16, 16])
    k.s_d = din("s_d", [DEPTH, 128, 2]); k.s_bglu = din("s_bglu", [DEPTH, 128, 2])
    k.s_wglu = din("s_wglu", [DEPTH, 256, 256])
    k.yf = dscr("yf", [256, T], F32)
    k.router_w = din("router_w", [DEPTH, D, 16])
    k.w_gate = din("exp_w_gate", [DEPTH, 16, D, 2048])
    k.w_up = din("exp_w_up", [DEPTH, 16, D, 2048])
    k.w_down = din("exp_w_down", [DEPTH, 16, 2048, D])
    k.ffn = dscr("ffn", [T, D], F32)
    if DBG_F == 1:
        k.dbg_idx = nc.dram_tensor("dbg_idx", [128, 64], I32, kind="ExternalOutput")
        k.dbg_gate = nc.dram_tensor("dbg_gate", [128, 64], F32, kind="ExternalOutput")
        k.dbg_aff = nc.dram_tensor("dbg_aff", [128, 512], F32, kind="ExternalOutput")
        k.dbg_posm = nc.dram_tensor("dbg_posm", [128, 512], F32, kind="ExternalOutput")
    k.h1 = dscr("h1", [T, D], F32)
    k.h1b = dscr("h1b", [T, D], BF16)

    with contextlib.ExitStack() as st:
        P = Prog(nc, st)
        k.P = P
        k.st = st
        k.bank = [st.enter_context(nc.psum_tensor("bank%d" % i, [128, 512], F32)) for i in range(7)]
        k.bankT = st.enter_context(nc.psum_tensor("bankT", [128, 1024], BF16))
        setup_consts(k)
        for l in range(nlayers):
            if 'a' in phases:
                phase_a(k, l)
            if 'r' in phases:
                phase_r(k, l)
            if 't' in phases:
                phase_t(k, l)
            if 's' in phases:
                phase_s(k, l)
            if 'o' in phases:
                phase_o(k, l)
            if 'f' in phases:
                phase_f(k, l, last=(l == nlayers - 1))
        finish(k)
    nc._prog = P
    return nc


def sb(k, name, shape, dt, stack=None):
    k.nsb = getattr(k, 'nsb', 0) + 1
    return (stack or k.st).enter_context(k.nc.sbuf_tensor("%s_u%d" % (name, k.nsb), list(shape), dt))


def setup_consts(k):
    nc, P = k.nc, k.P
    k.identb = sb(k, "identb", [128, 128], BF16)
    k.identf = sb(k, "identf", [128, 128], F32)
    k.iota_i = sb(k, "iota_i", [128, 128], I32)
    k.dist = sb(k, "dist", [128, 128], F32)
    P.op('pool', lambda e: e.iota(k.iota_i[:], [[1, 128]], base=0, channel_multiplier=-1),
         writes=['iota_i'])
    P.op('dve', lambda e: e.tensor_copy(out=k.dist[:], in_=k.iota_i[:]), reads=['iota_i'], writes=['dist'])
    P.op('dve', lambda e: e.tensor_scalar(out=k.identf[:], in0=k.dist[:], scalar1=0.0, scalar2=None,
                                           op0=ALU.is_equal), reads=['dist'], writes=['identf'])
    P.op('dve', lambda e: e.tensor_copy(out=k.identb[:], in_=k.identf[:]), reads=['identf'], writes=['identb'])
    k.one_t = sb(k, "one_t", [128, 1], F32)
    P.op('dve', lambda e: e.memset(k.one_t[:], 1.0), writes=['one_t'])
    k.lnk_t = sb(k, "lnk_t", [128, 1], F32)
    P.op('dve', lambda e: e.memset(k.lnk_t[:], math.log(0.125)), writes=['lnk_t'])
    k.irow_i = sb(k, "irow_i", [128, 128], I32)
    k.irow = sb(k, "irow", [128, 128], F32)
    P.op('pool', lambda e: e.iota(k.irow_i[:], [[1, 128]], base=0, channel_multiplier=0), writes=['irow_i'])
    P.op('dve', lambda e: e.tensor_copy(out=k.irow[:], in_=k.irow_i[:]), reads=['irow_i'], writes=['irow'])
    k.relp = sb(k, "relp", [128, 128], F32)
    k.reln = sb(k, "reln", [128, 128], F32)
    P.op('dve', lambda e: e.tensor_scalar(out=k.relp[:], in0=k.dist[:], scalar1=0.0, scalar2=None, op0=ALU.max),
         reads=['dist'], writes=['relp'])
    P.op('dve', lambda e: e.tensor_scalar(out=k.reln[:], in0=k.dist[:], scalar1=-1.0, scalar2=0.0, op0=ALU.mult, op1=ALU.max),
         reads=['dist'], writes=['reln'])
    k.eps_t = sb(k, "eps_t", [128, 1], F32)
    P.op('dve', lambda e: e.memset(k.eps_t[:], EPS), writes=['eps_t'])


def layer_norm_tile(k, x_ap, xkey, out_ap, outkey, g_ap, gkey, b_ap, bkey, tag, scratch):
    P = k.P
    stats, mv, sd, rstd, nb, xn = (scratch[n] for n in ('stats', 'mv', 'sd', 'rstd', 'nb', 'xn'))
    s = tag
    for hh in range(2):
        P.op('dve', lambda e, hh=hh: e.bn_stats(out=stats[:, hh, :], in_=x_ap[:, hh * 512:(hh + 1) * 512]),
             reads=[xkey], writes=[s + 'stats%d' % hh])
        yield
    P.op('dve', lambda e: e.bn_aggr(out=mv[:], in_=stats[:].rearrange("p a b -> p (a b)")),
         reads=[s + 'stats0', s + 'stats1'], writes=[s + 'mv'])
    yield
    P.op('act', lambda e: e.activation(out=sd[:], in_=mv[:, 1:2], func=AF.Sqrt, bias=k.eps_t[:], scale=1.0),
         reads=[s + 'mv'], writes=[s + 'sd'])
    yield
    P.op('dve', lambda e: e.reciprocal(out=rstd[:], in_=sd[:]), reads=[s + 'sd'], writes=[s + 'rstd'])
    yield
    P.op('dve', lambda e: e.scalar_tensor_tensor(out=nb[:], in0=mv[:, 0:1], scalar=-1.0, in1=rstd[:],
                                                  op0=ALU.mult, op1=ALU.mult),
         reads=[s + 'mv', s + 'rstd'], writes=[s + 'nb'])
    yield
    P.op('act', lambda e: e.activation(out=xn[:], in_=x_ap, func=AF.Identity, bias=nb[:], scale=rstd[:]),
         reads=[xkey, s + 'nb', s + 'rstd'], writes=[s + 'xn'])
    yield
    P.op('pool', lambda e: e.tensor_tensor(out=xn[:], in0=xn[:], in1=g_ap, op=ALU.mult),
         reads=[s + 'xn', gkey], writes=[s + 'xn'])
    yield
    P.op('dve', lambda e: e.tensor_tensor(out=out_ap, in0=xn[:], in1=b_ap, op=ALU.add),
         reads=[s + 'xn', bkey], writes=[outkey])
    yield


def phase_a(k, l):
    nc, P = k.nc, k.P
    P.barrier()
    with contextlib.ExitStack() as ps:
        if l == 0:
            k.g_in = sb(k, "g_in", [128, D], F32, ps)
            k.b_in = sb(k, "b_in", [128, D], F32, ps)
            P.dma('sp', lambda e: e.dma_start(out=k.g_in[:], in_=dap(k.ln_in_g, 0, [[0, 128], [1, D]])), writes=['g_in'])
            P.dma('sp', lambda e: e.dma_start(out=k.b_in[:], in_=dap(k.ln_in_b, 0, [[0, 128], [1, D]])), writes=['b_in'])
        Wb = sb(k, "a_Wb", [128, 8, 2304], BF16, ps)
        Wst = [sb(k, "a_Wst%d" % i, [128, 8, 256], F32, ps) for i in range(2)]
        xt = [sb(k, "a_x%d" % i, [128, D], F32, ps) for i in range(NBA)]
        ht = [sb(k, "a_h%d" % i, [128, D], F32, ps) for i in range(NBA)]
        hb = [sb(k, "a_hb%d" % i, [128, D], BF16, ps) for i in range(NBA)]
        hT = [sb(k, "a_hT%d" % i, [128, 8, 512], BF16, ps) for i in range(2)]
        tmst = [sb(k, "a_tmst%d" % i, [128, N_TM], BF16, ps) for i in range(NBA)]
        fmst = [sb(k, "a_fmst%d" % i, [128, 512], BF16, ps) for i in range(3)]
        scr = [dict(stats=sb(k, "a_stats%d" % i, [128, 2, 6], F32, ps), mv=sb(k, "a_mv%d" % i, [128, 2], F32, ps),
                    sd=sb(k, "a_sd%d" % i, [128, 1], F32, ps), rstd=sb(k, "a_rstd%d" % i, [128, 1], F32, ps),
                    nb=sb(k, "a_nb%d" % i, [128, 1], F32, ps), xn=sb(k, "a_xn%d" % i, [128, D], F32, ps))
               for i in range(NBA)]
        for c in range(9):
            w = Wst[c % 2]
            wk = 'a_Wst%d' % (c % 2)
            P.dma('sp', lambda e, w=w, c=c: e.dma_start(
                out=w[:], in_=dap(k.w_in, l * D * 2304 + c * 256, [[2304, 128], [128 * 2304, 8], [1, 256]])),
                writes=[wk])
            eng = 'act' if c % 2 == 0 else 'dve'
            if eng == 'act':
                P.op('act', lambda e, w=w, c=c: e.copy(out=Wb[:, :, c * 256:(c + 1) * 256], in_=w[:]),
                     reads=[wk], writes=['a_Wb%d' % c])
            else:
                P.op('dve', lambda e, w=w, c=c: e.tensor_copy(out=Wb[:, :, c * 256:(c + 1) * 256], in_=w[:]),
                     reads=[wk], writes=['a_Wb%d' % c])
        Wkeys = ['a_Wb%d' % c for c in range(9)]
        nb = 0

        def a_load(ti):
            b = ti % NBA
            if l == 0:
                P.dma('sp', lambda e: e.dma_start(out=xt[b][:], in_=dap(k.x, ti * 128 * D, [[D, 128], [1, D]])), writes=['a_x%d' % b])
            else:
                P.dma('sp', lambda e: e.dma_start(out=ht[b][:], in_=dap(k.hA, ti * 128 * D, [[D, 128], [1, D]])),
                      reads=['hA%d' % ti], writes=['a_h%d' % b])
        for stile in range(T // 512):
            hTs = hT[stile % 2]
            hTk = 'a_hT%d' % (stile % 2)
            for s in range(4):
                ti = stile * 4 + s
                b = ti % NBA
                tok0 = ti * 128
                if ti == 0:
                    a_load(0)
                if ti + 1 < NT:
                    a_load(ti + 1)
                if l == 0:
                    for _ in layer_norm_tile(k, xt[b][:], 'a_x%d' % b, ht[b][:], 'a_h%d' % b, k.g_in[:], 'g_in', k.b_in[:], 'b_in',
                                             'a%d_' % b, scr[b]):
                        pass
                    P.dma('sp', lambda e, b=b, tok0=tok0: e.dma_start(out=dap(k.hA, tok0 * D, [[D, 128], [1, D]]), in_=ht[b][:]),
                          reads=['a_h%d' % b], writes=['hA%d' % ti])
                P.op('act', lambda e, b=b: e.copy(out=hb[b][:], in_=ht[b][:]), reads=['a_h%d' % b], writes=['a_hb%d' % b])

                def tr(e, b=b):
                    r = None
                    for kk in range(8):
                        r = e.transpose(out=k.bankT[:, kk * 128:(kk + 1) * 128], in_=hb[b][:, kk * 128:(kk + 1) * 128],
                                        identity=k.identb[:])
                    return r
                P.op('pe', tr, reads=['a_hb%d' % b, 'identb'], writes=['bankT'])
                P.op('dve', lambda e, s=s, hTs=hTs: e.tensor_copy(
                    out=hTs[:, :, s * 128:(s + 1) * 128], in_=k.bankT[:].rearrange("p (a b) -> p a b", a=8)),
                    reads=['bankT'], writes=[hTk + '_%d' % s])
                for gi, (c0, c1) in enumerate([(0, 512), (512, N_TM)]):
                    bk = nb % 6
                    nb += 1

                    def mm(e, bk=bk, c0=c0, c1=c1, s=s, hTs=hTs):
                        r = None
                        for kk in range(8):
                            r = e.matmul(k.bank[bk][:, 0:c1 - c0], lhsT=hTs[:, kk, s * 128:(s + 1) * 128],
                                         rhs=Wb[:, kk, c0:c1], start=(kk == 0), stop=(kk == 7))
                        return r
                    P.op('pe', mm, reads=[hTk + '_%d' % s] + Wkeys, writes=['bank%d' % bk])
                    P.op('act', lambda e, bk=bk, c0=c0, c1=c1, b=b: e.copy(out=tmst[b][:, c0:c1], in_=k.bank[bk][:, 0:c1 - c0]),
                         reads=['bank%d' % bk], writes=['a_tmst%d_%d' % (b, gi)])
                P.dma('sp', lambda e, b=b, tok0=tok0: e.dma_start(out=dap(k.tm, tok0 * N_TM, [[N_TM, 128], [1, N_TM]]), in_=tmst[b][:]),
                      reads=['a_tmst%d_0' % b, 'a_tmst%d_1' % b], writes=['tm%d' % ti])
                P._update(P.res['tm%d' % ti]['w'], ['a_tmst%d_0' % b, 'a_tmst%d_1' % b], [])
            for fg in range(11):
                bk = nb % 6
                nb += 1
                fb = fg % 3

                def mmf(e, bk=bk, fg=fg, hTs=hTs):
                    r = None
                    for kk in range(8):
                        r = e.matmul(k.bank[bk][:, :], lhsT=Wb[:, kk, N_TM + fg * 128:N_TM + (fg + 1) * 128],
                                     rhs=hTs[:, kk, :], start=(kk == 0), stop=(kk == 7))
                    return r
                P.op('pe', mmf, reads=[hTk + '_%d' % s for s in range(4)] + Wkeys, writes=['bank%d' % bk])
                eng = 'dve' if fg % 2 == 0 else 'act'
                if eng == 'dve':
                    P.op('dve', lambda e, bk=bk, fb=fb: e.tensor_copy(out=fmst[fb][:], in_=k.bank[bk][:, :]),
                         reads=['bank%d' % bk], writes=['a_fmst%d' % fb])
                else:
                    P.op('act', lambda e, bk=bk, fb=fb: e.copy(out=fmst[fb][:], in_=k.bank[bk][:, :]),
                         reads=['bank%d' % bk], writes=['a_fmst%d' % fb])
                P.dma('sp', lambda e, fb=fb, fg=fg, stile=stile: e.dma_start(
                    out=dap(k.fm, fg * 128 * T + stile * 512, [[T, 128], [1, 512]]), in_=fmst[fb][:]),
                    reads=['a_fmst%d' % fb], writes=['fm%d_%d' % (fg, stile)])
                P._update(P.res['fm%d_%d' % (fg, stile)]['w'], ['a_fmst%d' % fb], [])


def bc(tile, col, n, pstride, parts=128):
    return dap(tile, col, [[pstride, parts], [0, n]])


def phase_r(k, l):
    nc, P = k.nc, k.P
    P.barrier()
    with contextlib.ExitStack() as ps:
        ktm = sb(k, "r_ktm", [128, NT, 256], BF16, ps)
        vtm = sb(k, "r_vtm", [128, NT, 256], BF16, ps)
        gtm = sb(k, "r_gtm", [128, NT, 256], BF16, ps)
        Sf = sb(k, "r_Sf", [64, NT, 256], BF16, ps)
        Sb_ = sb(k, "r_Sb", [64, NT, 256], BF16, ps)
        stt = [sb(k, "r_st%d" % i, [64, 256], F32, ps) for i in range(2)]
        qT = [sb(k, "r_qT%d" % i, [64, 4, 512], BF16, ps) for i in range(2)]
        kT = [sb(k, "r_kT%d" % i, [64, 4, 512], BF16, ps) for i in range(2)]
        th = sb(k, "r_th", [128, 8], F32, ps)
        lg = sb(k, "r_lg", [128, 8], F32, ps)
        dec = sb(k, "r_dec", [128, 8], F32, ps)
        wcol = sb(k, "r_wcol", [128, 8], F32, ps)
        pidx = sb(k, "r_pidx", [128, 2], F32, ps)
        tmpa = sb(k, "r_tmpa", [128, 128], F32, ps)
        tmpb = sb(k, "r_tmpb", [128, 128], F32, ps)
        dmT = sb(k, "r_dmT", [128, 4, 128], F32, ps)
        WF = sb(k, "r_WF", [128, 256], F32, ps)
        WB = sb(k, "r_WB", [128, 256], F32, ps)
        qsf = sb(k, "r_qsf", [128, 4, 128], F32, ps)
        qsb = sb(k, "r_qsb", [128, 4, 128], F32, ps)
        irf = sb(k, "r_irf", [128, 128], F32, ps)
        irb = sb(k, "r_irb", [128, 128], F32, ps)
        kw = [sb(k, "r_kw%d" % i, [128, 256], BF16, ps) for i in range(2)]
        sTm = [sb(k, "r_sTm%d" % i, [128, 512], BF16, ps) for i in range(2)]
        qf = [sb(k, "r_qf%d" % i, [64, 4, 128], BF16, ps) for i in range(2)]
        qb = [sb(k, "r_qb%d" % i, [64, 4, 128], BF16, ps) for i in range(2)]
        o = [sb(k, "r_o%d" % i, [128, 256], F32, ps) for i in range(2)]
        sq = [sb(k, "r_sq%d" % i, [128, 256], F32, ps) for i in range(2)]
        sg = [sb(k, "r_sg%d" % i, [128, 256], F32, ps) for i in range(2)]
        on = [sb(k, "r_on%d" % i, [128, 256], F32, ps) for i in range(2)]
        yst = [sb(k, "r_yst%d" % i, [128, 256], BF16, ps) for i in range(2)]
        sm = [dict((n, sb(k, "r_%s%d" % (n, i), [128, 4], F32, ps)) for n in ('s1', 's2', 'mean', 'msq', 'var', 'sd', 'rstd'))
              for i in range(2)]

        for name, tl, c0 in (('r_ktm', ktm, 0), ('r_vtm', vtm, 256), ('r_gtm', gtm, 512)):
            for half in range(2):
                P.dma('sp', lambda e, tl=tl, c0=c0, half=half: e.dma_start(
                    out=tl[:, half * 16:(half + 1) * 16, :],
                    in_=dap(k.tm, half * 16 * 128 * N_TM + c0, [[N_TM, 128], [128 * N_TM, 16], [1, 256]])),
                    reads=['tm%d' % ti for ti in range(half * 16, half * 16 + 16)], writes=['%s_%d' % (name, half)])
        P.dma('sp', lambda e: e.dma_start(out=th[:], in_=dap(k.ret_theta, l * 8, [[0, 128], [1, 8]])), writes=['r_th'])
        P.op('act', lambda e: e.activation(out=lg[:], in_=th[:], func=AF.Exp, scale=-1.0), reads=['r_th'], writes=['r_lg'])
        P.op('act', lambda e: e.activation(out=lg[:], in_=lg[:], func=AF.Ln, bias=k.one_t[:], scale=1.0),
             reads=['r_lg', 'one_t'], writes=['r_lg'])
        P.op('dve', lambda e: e.tensor_scalar(out=lg[:], in0=lg[:], scalar1=-1.0, scalar2=None, op0=ALU.mult),
             reads=['r_lg'], writes=['r_lg'])
        P.op('act', lambda e: e.activation(out=dec[:], in_=lg[:], func=AF.Exp, scale=128.0), reads=['r_lg'], writes=['r_dec'])
        P.op('dve', lambda e: e.tensor_scalar(out=pidx[:, 0:1], in0=k.dist[:, 0:1], scalar1=127.0, scalar2=None, op0=ALU.add),
             reads=['dist'], writes=['r_pidx0'])
        P.op('dve', lambda e: e.tensor_scalar(out=pidx[:, 1:2], in0=k.dist[:, 0:1], scalar1=-1.0, scalar2=None, op0=ALU.mult),
             reads=['dist'], writes=['r_pidx1'])
        P.op('dve', lambda e: e.tensor_scalar(out=irf[:], in0=k.irow[:], scalar1=1.0, scalar2=None, op0=ALU.add),
             reads=['irow'], writes=['r_irf'])
        P.op('dve', lambda e: e.tensor_scalar(out=irb[:], in0=k.irow[:], scalar1=-1.0, scalar2=128.0, op0=ALU.mult, op1=ALU.add),
             reads=['irow'], writes=['r_irb'])
        for h in range(4):
            P.op('act', lambda e, h=h: e.activation(out=wcol[:, h:h + 1], in_=pidx[:, 0:1], func=AF.Exp, bias=k.lnk_t[:], scale=lg[:, h:h + 1]),
                 reads=['r_pidx0', 'r_lg', 'lnk_t'], writes=['r_wcol%d' % h])
            P.op('act', lambda e, h=h: e.activation(out=wcol[:, 4 + h:5 + h], in_=pidx[:, 1:2], func=AF.Exp, bias=k.lnk_t[:], scale=lg[:, 4 + h:5 + h]),
                 reads=['r_pidx1', 'r_lg', 'lnk_t'], writes=['r_wcol%d' % (4 + h)])
            P.op('dve', lambda e, h=h: e.tensor_copy(out=WF[:, h * 64:(h + 1) * 64], in_=bc(wcol, h, 64, 8)),
                 reads=['r_wcol%d' % h], writes=['r_WF%d' % h])
            P.op('dve', lambda e, h=h: e.tensor_copy(out=WB[:, h * 64:(h + 1) * 64], in_=bc(wcol, 4 + h, 64, 8)),
                 reads=['r_wcol%d' % (4 + h)], writes=['r_WB%d' % h])
            P.op('dve', lambda e, h=h: e.tensor_scalar(out=tmpa[:], in0=k.relp[:], scalar1=lg[:, h:h + 1], scalar2=None, op0=ALU.mult),
                 reads=['relp', 'r_lg'], writes=['r_tmpa'])
            P.op('dve', lambda e, h=h: e.scalar_tensor_tensor(out=tmpb[:], in0=k.reln[:], scalar=lg[:, 4 + h:5 + h], in1=tmpa[:],
                                                               op0=ALU.mult, op1=ALU.add),
                 reads=['reln', 'r_lg', 'r_tmpa'], writes=['r_tmpb'])
            P.op('act', lambda e, h=h: e.activation(out=dmT[:, h, :], in_=tmpb[:], func=AF.Exp, bias=k.lnk_t[:], scale=1.0),
                 reads=['r_tmpb', 'lnk_t'], writes=['r_dmT%d' % h])
            P.op('act', lambda e, h=h: e.activation(out=qsf[:, h, :], in_=irf[:], func=AF.Exp, scale=lg[:, h:h + 1]),
                 reads=['r_irf', 'r_lg'], writes=['r_qsf%d' % h])
            P.op('act', lambda e, h=h: e.activation(out=qsb[:, h, :], in_=irb[:], func=AF.Exp, scale=lg[:, 4 + h:5 + h]),
                 reads=['r_irb', 'r_lg'], writes=['r_qsb%d' % h])
        WFk = ['r_WF%d' % h for h in range(4)]
        WBk = ['r_WB%d' % h for h in range(4)]
        def r_pass1(d):
            W = WF if d == 0 else WB
            Wk = WFk if d == 0 else WBk
            Sall = Sf if d == 0 else Sb_
            stn = 'r_st%d' % d
            P.op('dve', lambda e, d=d: e.memset(stt[d][:], 0.0), writes=[stn])
            yield
            order = range(NT) if d == 0 else range(NT - 1, -1, -1)
            kb = d
            bk = d
            for n, c in enumerate(order):
                half = c // 16
                P.op('act', lambda e, c=c, Sall=Sall, d=d: e.copy(out=Sall[:, c, :], in_=stt[d][:]),
                     reads=[stn], writes=['r_S%d_%d' % (d, c)])
                yield
                P.op('pool', lambda e, c=c, kb=kb, W=W: e.tensor_tensor(out=kw[kb][:], in0=ktm[:, c, :], in1=W[:], op=ALU.mult),
                     reads=['r_ktm_%d' % half] + Wk, writes=['r_kw%d' % kb])
                yield

                def mmkv(e, c=c, kb=kb, bk=bk):
                    r = None
                    for h in range(4):
                        r = e.matmul(k.bank[bk][0:64, h * 64:(h + 1) * 64], lhsT=kw[kb][:, h * 64:(h + 1) * 64],
                                     rhs=vtm[:, c, h * 64:(h + 1) * 64], start=True, stop=True)
                    return r
                P.op('pe', mmkv, reads=['r_kw%d' % kb, 'r_vtm_%d' % half], writes=['bank%d' % bk])
                yield
                for h in range(4):
                    P.op('dve', lambda e, h=h, d=d, bk=bk: e.scalar_tensor_tensor(
                        out=stt[d][:, h * 64:(h + 1) * 64], in0=stt[d][:, h * 64:(h + 1) * 64], scalar=dec[0:64, 4 * d + h:4 * d + h + 1],
                        in1=k.bank[bk][0:64, h * 64:(h + 1) * 64], op0=ALU.mult, op1=ALU.add),
                        reads=[stn, 'r_dec', 'bank%d' % bk], writes=[stn])
                    yield
        gens = [r_pass1(0), r_pass1(1)]
        while gens:
            for g_ in list(gens):
                try:
                    next(g_)
                except StopIteration:
                    gens.remove(g_)
        for c in range(NT):
            blk, cc = c // 4, c % 4
            half = c // 16
            qb_i = blk % 2
            b = c % 2
            if cc == 0:
                P.dma('sp', lambda e, blk=blk, qb_i=qb_i: e.dma_start(
                    out=qT[qb_i][:], in_=dap(k.fm, blk * 512, [[T, 64], [64 * T, 4], [1, 512]])),
                    reads=['fm%d_%d' % (fg, blk) for fg in (0, 1)], writes=['r_qT%d' % qb_i])
                P.dma('sp', lambda e, blk=blk, qb_i=qb_i: e.dma_start(
                    out=kT[qb_i][:], in_=dap(k.fm, 256 * T + blk * 512, [[T, 64], [64 * T, 4], [1, 512]])),
                    reads=['fm%d_%d' % (fg, blk) for fg in (2, 3)], writes=['r_kT%d' % qb_i])
            bs = 2 + (c % 2)

            def mms(e, qb_i=qb_i, cc=cc, bs=bs):
                r = None
                for h in range(4):
                    r = e.matmul(k.bank[bs][:, h * 128:(h + 1) * 128], lhsT=kT[qb_i][:, h, cc * 128:(cc + 1) * 128],
                                 rhs=qT[qb_i][:, h, cc * 128:(cc + 1) * 128], start=True, stop=True)
                return r
            P.op('pe', mms, reads=['r_qT%d' % qb_i, 'r_kT%d' % qb_i], writes=['bank%d' % bs])
            P.op('dve', lambda e, b=b, bs=bs: e.tensor_tensor(out=sTm[b][:], in0=k.bank[bs][:, :],
                                                            in1=dmT[:].rearrange("p a b -> p (a b)"), op=ALU.mult),
                 reads=['bank%d' % bs] + ['r_dmT%d' % h for h in range(4)], writes=['r_sTm%d' % b])
            P.op('pool', lambda e, b=b, qb_i=qb_i, cc=cc: e.tensor_tensor(out=qf[b][:], in0=qT[qb_i][:, :, cc * 128:(cc + 1) * 128],
                                                                         in1=qsf[0:64, :, :], op=ALU.mult),
                 reads=['r_qT%d' % qb_i] + ['r_qsf%d' % h for h in range(4)], writes=['r_qf%d' % b])
            P.op('pool', lambda e, b=b, qb_i=qb_i, cc=cc: e.tensor_tensor(out=qb[b][:], in0=qT[qb_i][:, :, cc * 128:(cc + 1) * 128],
                                                                         in1=qsb[0:64, :, :], op=ALU.mult),
                 reads=['r_qT%d' % qb_i] + ['r_qsb%d' % h for h in range(4)], writes=['r_qb%d' % b])
            bo = 4 + (c % 2)

            def mmo(e, b=b, c=c, bo=bo):
                r = None
                for h in range(4):
                    oap = k.bank[bo][:, h * 64:(h + 1) * 64]
                    e.matmul(oap, lhsT=sTm[b][:, h * 128:(h + 1) * 128], rhs=vtm[:, c, h * 64:(h + 1) * 64], start=True, stop=False)
                    e.matmul(oap, lhsT=qf[b][:, h, :], rhs=Sf[:, c, h * 64:(h + 1) * 64], start=False, stop=False)
                    r = e.matmul(oap, lhsT=qb[b][:, h, :], rhs=Sb_[:, c, h * 64:(h + 1) * 64], start=False, stop=True)
                return r
            P.op('pe', mmo, reads=['r_sTm%d' % b, 'r_qf%d' % b, 'r_qb%d' % b, 'r_vtm_%d' % half, 'r_S0_%d' % c, 'r_S1_%d' % c],
                 writes=['bank%d' % bo])
            m = sm[b]
            mk = lambda n: 'r_%s%d' % (n, b)
            P.op('act', lambda e, b=b, bo=bo: e.copy(out=o[b][:], in_=k.bank[bo][:, 0:256]), reads=['bank%d' % bo], writes=[mk('o')])
            P.op('act', lambda e, b=b: e.activation(out=sq[b][:], in_=o[b][:], func=AF.Square), reads=[mk('o')], writes=[mk('sq')])
            P.op('act', lambda e, b=b, c=c: e.activation(out=sg[b][:], in_=gtm[:, c, :], func=AF.Silu),
                 reads=['r_gtm_%d' % half], writes=[mk('sg')])
            P.op('dve', lambda e, b=b, m=m: e.tensor_reduce(out=m['s1'][:], in_=o[b][:].rearrange("p (a b) -> p a b", a=4), axis=AX.X, op=ALU.add),
                 reads=[mk('o')], writes=[mk('s1')])
            P.op('dve', lambda e, b=b, m=m: e.tensor_reduce(out=m['s2'][:], in_=sq[b][:].rearrange("p (a b) -> p a b", a=4), axis=AX.X, op=ALU.add),
                 reads=[mk('sq')], writes=[mk('s2')])
            P.op('dve', lambda e, m=m: e.tensor_scalar(out=m['mean'][:], in0=m['s1'][:], scalar1=1.0 / 64, scalar2=None, op0=ALU.mult),
                 reads=[mk('s1')], writes=[mk('mean')])
            P.op('dve', lambda e, m=m: e.tensor_tensor(out=m['msq'][:], in0=m['mean'][:], in1=m['mean'][:], op=ALU.mult),
                 reads=[mk('mean')], writes=[mk('msq')])
            P.op('dve', lambda e, m=m: e.scalar_tensor_tensor(out=m['var'][:], in0=m['s2'][:], scalar=1.0 / 64, in1=m['msq'][:],
                                                               op0=ALU.mult, op1=ALU.subtract),
                 reads=[mk('s2'), mk('msq')], writes=[mk('var')])
            P.op('act', lambda e, m=m: e.activation(out=m['sd'][:], in_=m['var'][:], func=AF.Sqrt, bias=k.eps_t[:], scale=1.0),
                 reads=[mk('var'), 'eps_t'], writes=[mk('sd')])
            P.op('dve', lambda e, m=m: e.reciprocal(out=m['rstd'][:], in_=m['sd'][:]), reads=[mk('sd')], writes=[mk('rstd')])
            for h in range(4):
                P.op('dve', lambda e, h=h, b=b, m=m: e.tensor_scalar(
                    out=on[b][:, h * 64:(h + 1) * 64], in0=o[b][:, h * 64:(h + 1) * 64], scalar1=m['mean'][:, h:h + 1],
                    scalar2=m['rstd'][:, h:h + 1], op0=ALU.subtract, op1=ALU.mult),
                    reads=[mk('o'), mk('mean'), mk('rstd')], writes=[mk('on') + '_%d' % h])
            P.op('pool', lambda e, b=b: e.tensor_tensor(out=yst[b][:], in0=on[b][:], in1=sg[b][:], op=ALU.mult),
                 reads=[mk('on') + '_%d' % h for h in range(4)] + [mk('sg')], writes=[mk('yst')])
            P.dma('sp', lambda e, b=b, c=c: e.dma_start(out=dap(k.ycat, c * 128 * 768, [[768, 128], [1, 256]]), in_=yst[b][:]),
                  reads=[mk('yst')], writes=['ycat_r%d' % c])
            P._update(P.res['ycat_r%d' % c]['w'], [mk('yst')], [])


def phase_t(k, l):
    nc, P = k.nc, k.P
    P.barrier()
    EBk = [['EB%d_%d' % (kb, h) for h in range(8)] for kb in range(3)]
    with contextlib.ExitStack() as ps:
        k.EB = [sb(k, "EB%d" % kb, [128, 8, 128], F32, ps) for kb in range(3)]
        k.t_abs = sb(k, "t_abs", [128, 128], F32, ps)
        k.t_msk = sb(k, "t_msk", [128, 128], F32, ps)
        for kb in range(3):
            if kb == 0:
                P.op('dve', lambda e: e.tensor_scalar(out=k.t_abs[:], in0=k.dist[:], scalar1=128.0, scalar2=None, op0=ALU.add),
                     reads=['dist'], writes=['t_abs'])
                P.op('dve', lambda e: e.tensor_scalar(out=k.t_msk[:], in0=k.dist[:], scalar1=0.0, scalar2=None, op0=ALU.is_le),
                     reads=['dist'], writes=['t_msk'])
            elif kb == 1:
                P.op('dve', lambda e: e.tensor_tensor(out=k.t_abs[:], in0=k.relp[:], in1=k.reln[:], op=ALU.add),
                     reads=['relp', 'reln'], writes=['t_abs'])
                P.op('dve', lambda e: e.memset(k.t_msk[:], 1.0), writes=['t_msk'])
            else:
                P.op('dve', lambda e: e.tensor_scalar(out=k.t_abs[:], in0=k.dist[:], scalar1=-1.0, scalar2=128.0, op0=ALU.mult, op1=ALU.add),
                     reads=['dist'], writes=['t_abs'])
                P.op('dve', lambda e: e.tensor_scalar(out=k.t_msk[:], in0=k.dist[:], scalar1=0.0, scalar2=None, op0=ALU.is_ge),
                     reads=['dist'], writes=['t_msk'])
            for h in range(8):
                P.op('act', lambda e, kb=kb, h=h: e.activation(out=k.EB[kb][:, h, :], in_=k.t_abs[:], func=AF.Exp, scale=-(2.0 ** -(h + 1))),
                     reads=['t_abs'], writes=['EB%d_%d' % (kb, h)])
                P.op('dve', lambda e, kb=kb, h=h: e.tensor_tensor(out=k.EB[kb][:, h, :], in0=k.EB[kb][:, h, :], in1=k.t_msk[:], op=ALU.mult),
                     reads=['EB%d_%d' % (kb, h), 't_msk'], writes=['EB%d_%d' % (kb, h)])
        kT = sb(k, "t_kT", [64, 2, T], BF16, ps)
        vA = sb(k, "t_vA", [128, NT, 2, 65], BF16, ps)
        qT = [sb(k, "t_qT%d" % i, [64, 8, 512], BF16, ps) for i in range(2)]
        snk = sb(k, "t_snk", [128, 8], F32, ps)
        ex = [sb(k, "t_ex%d" % i, [128, 512], F32, ps) for i in range(3)]
        pT = [sb(k, "t_pT%d" % i, [128, 512], BF16, ps) for i in range(6)]
        den = [sb(k, "t_den%d" % i, [128, 4], F32, ps) for i in range(2)]
        rec = [sb(k, "t_rec%d" % i, [128, 4], F32, ps) for i in range(2)]
        yst = [sb(k, "t_yst%d" % i, [128, 512], BF16, ps) for i in range(2)]
        for half in range(2):
            P.dma('sp', lambda e, half=half: e.dma_start(
                out=kT[:, :, half * 2048:(half + 1) * 2048], in_=dap(k.fm, 1280 * T + half * 2048, [[T, 64], [64 * T, 2], [1, 2048]])),
                reads=['fm10_%d' % b for b in range(half * 4, half * 4 + 4)], writes=['t_kT%d' % half])
            for kvh in range(2):
                P.dma('sp', lambda e, half=half, kvh=kvh: e.dma_start(
                    out=vA[:, half * 16:(half + 1) * 16, kvh, 0:64],
                    in_=dap(k.tm, half * 16 * 128 * N_TM + 768 + kvh * 64, [[N_TM, 128], [128 * N_TM, 16], [1, 64]])),
                    reads=['tm%d' % ti for ti in range(half * 16, half * 16 + 16)], writes=['t_vA%d_%d' % (half, kvh)])
        P.op('pool', lambda e: e.memset(vA[:, :, :, 64:65], 1.0), writes=['t_vA1s'])
        P.dma('sp', lambda e: e.dma_start(out=snk[:], in_=dap(k.attn_sink, l * 8, [[0, 128], [1, 8]])), writes=['t_snk'])
        P.op('act', lambda e: e.activation(out=snk[:], in_=snk[:], func=AF.Exp), reads=['t_snk'], writes=['t_snk'])
        npT = 0
        nex = 0
        for c in range(NT):
            blk, cc = c // 4, c % 4
            qi = blk % 2
            if cc == 0:
                P.dma('sp', lambda e, blk=blk, qi=qi: e.dma_start(
                    out=qT[qi][:], in_=dap(k.fm, 768 * T + blk * 512, [[T, 64], [64 * T, 8], [1, 512]])),
                    reads=['fm%d_%d' % (fg, blk) for fg in (6, 7, 8, 9)], writes=['t_qT%d' % qi])
            yb = c % 2
            for kvh in range(2):
                kbs = [kb for kb in range(3) if 0 <= c - 1 + kb < NT]
                pts = []
                for kb in kbs:
                    kblk = c - 1 + kb
                    bs = (nex % 3)
                    xi = nex % 3
                    nex += 1
                    pi = npT % 6
                    npT += 1
                    pts.append(pi)
                    P.op('pe', lambda e, bs=bs, kvh=kvh, kblk=kblk, qi=qi, cc=cc: e.matmul(
                        k.bank[bs][:, :], lhsT=kT[:, kvh, kblk * 128:(kblk + 1) * 128],
                        rhs=qT[qi][:, kvh * 4:(kvh + 1) * 4, cc * 128:(cc + 1) * 128], start=True, stop=True),
                        reads=['t_kT%d' % (kblk // 16), 't_qT%d' % qi], writes=['bank%d' % bs])
                    P.op('act', lambda e, bs=bs, xi=xi: e.activation(out=ex[xi][:], in_=k.bank[bs][:, :], func=AF.Exp, scale=0.125),
                         reads=['bank%d' % bs], writes=['t_ex%d' % xi])
                    P.op('dve', lambda e, xi=xi, pi=pi, kb=kb, kvh=kvh: e.tensor_tensor(
                        out=pT[pi][:], in0=ex[xi][:], in1=k.EB[kb][:, kvh * 4:(kvh + 1) * 4, :].rearrange("p a b -> p (a b)"), op=ALU.mult),
                        reads=['t_ex%d' % xi] + EBk[kb], writes=['t_pT%d' % pi])
                bo = 3 + kvh + 2 * (c % 2)

                def mmo(e, kbs=kbs, pts=pts, c=c, kvh=kvh, bo=bo):
                    r = None
                    for g in range(4):
                        for n, (kb, pi) in enumerate(zip(kbs, pts)):
                            kblk = c - 1 + kb
                            r = e.matmul(k.bank[bo][:, g * 65:(g + 1) * 65], lhsT=pT[pi][:, g * 128:(g + 1) * 128],
                                         rhs=vA[:, kblk, kvh, :], start=(n == 0), stop=(n == len(kbs) - 1))
                    return r
                P.op('pe', mmo, reads=['t_pT%d' % pi for pi in pts] + ['t_vA0_0', 't_vA0_1', 't_vA1_0', 't_vA1_1', 't_vA1s'], writes=['bank%d' % bo])
                dk = 't_den%d' % kvh
                P.op('dve', lambda e, bo=bo, kvh=kvh: e.tensor_tensor(
                    out=den[kvh][:], in0=dap(k.bank[bo], 64, [[512, 128], [65, 4]]), in1=snk[:, kvh * 4:(kvh + 1) * 4], op=ALU.add),
                    reads=['bank%d' % bo, 't_snk'], writes=[dk])
                P.op('dve', lambda e, kvh=kvh: e.reciprocal(out=rec[kvh][:], in_=den[kvh][:]), reads=[dk], writes=['t_rec%d' % kvh])
                P.op('dve', lambda e, bo=bo, kvh=kvh, yb=yb: e.tensor_tensor(
                    out=yst[yb][:, kvh * 256:(kvh + 1) * 256].rearrange("p (a b) -> p a b", a=4),
                    in0=dap(k.bank[bo], 0, [[512, 128], [65, 4], [1, 64]]),
                    in1=dap(rec[kvh], 0, [[4, 128], [1, 4], [0, 64]]), op=ALU.mult),
                    reads=['bank%d' % bo, 't_rec%d' % kvh], writes=['t_yst%d_%d' % (yb, kvh)])
            P.dma('sp', lambda e, yb=yb, c=c: e.dma_start(out=dap(k.ycat, c * 128 * 768 + 256, [[768, 128], [1, 512]]), in_=yst[yb][:]),
                  reads=['t_yst%d_0' % yb, 't_yst%d_1' % yb], writes=['ycat_t%d' % c])
            P._update(P.res['ycat_t%d' % c]['w'], ['t_yst%d_0' % yb, 't_yst%d_1' % yb], [])


def phase_o(k, l):
    nc, P = k.nc, k.P
    P.barrier()
    with contextlib.ExitStack() as ps:
        Wo = sb(k, "o_Wo", [128, 8, D], BF16, ps)
        Wst = [sb(k, "o_Wst%d" % i, [128, 8, 256], F32, ps) for i in range(2)]
        g1 = sb(k, "o_g1", [128, D], F32, ps)
        b1 = sb(k, "o_b1", [128, D], F32, ps)
        yc = [sb(k, "o_yc%d" % i, [128, 768], BF16, ps) for i in range(NBA)]
        yT = [sb(k, "o_yT%d" % i, [128, 8, 128], BF16, ps) for i in range(NBA)]
        ht = [sb(k, "o_h%d" % i, [128, D], F32, ps) for i in range(NBA)]
        rt = [sb(k, "o_r%d" % i, [128, D], F32, ps) for i in range(NBA)]
        h1t = [sb(k, "o_h1%d" % i, [128, D], F32, ps) for i in range(NBA)]
        h1bt = [sb(k, "o_h1b%d" % i, [128, D], BF16, ps) for i in range(NBA)]
        scr = [dict(stats=sb(k, "o_stats%d" % i, [128, 2, 6], F32, ps), mv=sb(k, "o_mv%d" % i, [128, 2], F32, ps),
                    sd=sb(k, "o_sd%d" % i, [128, 1], F32, ps), rstd=sb(k, "o_rstd%d" % i, [128, 1], F32, ps),
                    nb=sb(k, "o_nb%d" % i, [128, 1], F32, ps), xn=sb(k, "o_xn%d" % i, [128, D], F32, ps))
               for i in range(NBA)]
        P.dma('sp', lambda e: e.dma_start(out=g1[:], in_=dap(k.ln1_g, l * D, [[0, 128], [1, D]])), writes=['o_g1'])
        P.dma('sp', lambda e: e.dma_start(out=b1[:], in_=dap(k.ln1_b, l * D, [[0, 128], [1, D]])), writes=['o_b1'])
        for c in range(4):
            w = Wst[c % 2]
            wk = 'o_Wst%d' % (c % 2)
            P.dma('sp', lambda e, w=w, c=c: e.dma_start(
                out=w[:], in_=dap(k.w_out, l * D * D + c * 256, [[D, 128], [128 * D, 8], [1, 256]])), writes=[wk])
            if c % 2 == 0:
                P.op('act', lambda e, w=w, c=c: e.copy(out=Wo[:, :, c * 256:(c + 1) * 256], in_=w[:]), reads=[wk], writes=['o_Wo%d' % c])
            else:
                P.op('dve', lambda e, w=w, c=c: e.tensor_copy(out=Wo[:, :, c * 256:(c + 1) * 256], in_=w[:]), reads=[wk], writes=['o_Wo%d' % c])
        Wkeys = ['o_Wo%d' % c for c in range(4)]

        def o_load(ti):
            b = ti % NBA
            tok0 = ti * 128
            P.dma('sp', lambda e: e.dma_start(out=yc[b][:], in_=dap(k.ycat, tok0 * 768, [[768, 128], [1, 768]])),
                  reads=['ycat_r%d' % ti, 'ycat_t%d' % ti], writes=['o_yc%d' % b])
            P.dma('sp', lambda e: e.dma_start(out=yT[b][:, 2:4, :], in_=dap(k.yssmT, tok0, [[T, 128], [128 * T, 2], [1, 128]])),
                  reads=['yssmT%d' % ti], writes=['o_yT%d_s' % b])
            P.dma('sp', lambda e: e.dma_start(out=ht[b][:], in_=dap(k.hA, tok0 * D, [[D, 128], [1, D]])),
                  reads=['hA%d' % ti], writes=['o_h%d' % b])
        def o_tile(ti):
            b = ti % NBA
            tok0 = ti * 128

            def tr(e, b=b):
                r = None
                for kk in range(6):
                    r = e.transpose(out=k.bankT[:, kk * 128:(kk + 1) * 128], in_=yc[b][:, kk * 128:(kk + 1) * 128], identity=k.identb[:])
                return r
            P.op('pe', tr, reads=['o_yc%d' % b, 'identb'], writes=['bankT'])
            P.op('dve', lambda e, b=b: e.tensor_copy(out=yT[b][:, 0:2, :], in_=k.bankT[:, 0:256].rearrange("p (a b) -> p a b", a=2)),
                 reads=['bankT'], writes=['o_yT%d_r' % b])
            P.op('dve', lambda e, b=b: e.tensor_copy(out=yT[b][:, 4:8, :], in_=k.bankT[:, 256:768].rearrange("p (a b) -> p a b", a=4)),
                 reads=['bankT'], writes=['o_yT%d_t' % b])
            yield
            for half in range(2):
                bk = (2 * ti + half) % 4

                def mm(e, b=b, half=half, bk=bk):
                    r = None
                    for kk in range(8):
                        r = e.matmul(k.bank[bk][:, :], lhsT=yT[b][:, kk, :], rhs=Wo[:, kk, half * 512:(half + 1) * 512],
                                     start=(kk == 0), stop=(kk == 7))
                    return r
                P.op('pe', mm, reads=['o_yT%d_r' % b, 'o_yT%d_s' % b, 'o_yT%d_t' % b] + Wkeys, writes=['bank%d' % bk])
                yield
                P.op('dve', lambda e, b=b, half=half, bk=bk: e.scalar_tensor_tensor(
                    out=rt[b][:, half * 512:(half + 1) * 512], in0=ht[b][:, half * 512:(half + 1) * 512], scalar=ALPHA,
                    in1=k.bank[bk][:, :], op0=ALU.mult, op1=ALU.add),
                    reads=['o_h%d' % b, 'bank%d' % bk], writes=['o_r%d_%d' % (b, half)])
                yield
            P.res['o_r%d' % b] = P.res['o_r%d_1' % b]
            yield from layer_norm_tile(k, rt[b][:], 'o_r%d' % b, h1t[b][:], 'o_h1%d' % b, g1[:], 'o_g1', b1[:], 'o_b1', 'o%d_' % b, scr[b])
            P._update(P.res['o_h1%d' % b]['w'], ['o_r%d_0' % b, 'o_r%d_1' % b], [])
            P.op('act', lambda e, b=b: e.copy(out=h1bt[b][:], in_=h1t[b][:]), reads=['o_h1%d' % b], writes=['o_h1b%d' % b])
            yield
            P.dma('sp', lambda e, b=b, tok0=tok0: e.dma_start(out=dap(k.h1, tok0 * D, [[D, 128], [1, D]]), in_=h1t[b][:]),
                  reads=['o_h1%d' % b], writes=['h1_%d' % ti])
            P.dma('sp', lambda e, b=b, tok0=tok0: e.dma_start(out=dap(k.h1b, tok0 * D, [[D, 128], [1, D]]), in_=h1bt[b][:]),
                  reads=['o_h1b%d' % b], writes=['h1b_%d' % ti])

        run_window(o_tile, o_load, NT, NBA - 1)


TL = 256
SW = 4
NSET = 4
NCH = T // TL
TWO_PI = 2.0 * math.pi
CW1 = 6.28125
CW2 = TWO_PI - CW1
PI_LO = 3.1415925


def sincos(k, arg, argkey, out_sin, out_cos, outkey, n, scr, tag):
    P = k.P
    x, kf, ki = scr['x'], scr['kf'], scr['ki']
    for out, shift, nm in ((out_sin, 0.0, 's'), (out_cos, 0.5 * math.pi, 'c')):
        t = tag + nm
        P.op('dve', lambda e, shift=shift: e.tensor_scalar(out=x[:, 0:n], in0=arg, scalar1=shift, scalar2=None, op0=ALU.add),
             reads=[argkey], writes=[tag + 'x'])
        P.op('dve', lambda e: e.tensor_scalar(out=kf[:, 0:n], in0=x[:, 0:n], scalar1=1.0 / TWO_PI, scalar2=None, op0=ALU.mult),
             reads=[tag + 'x'], writes=[tag + 'kf'])
        P.op('dve', lambda e: e.tensor_copy(out=ki[:, 0:n], in_=kf[:, 0:n]), reads=[tag + 'kf'], writes=[tag + 'ki'])
        P.op('dve', lambda e: e.tensor_copy(out=kf[:, 0:n], in_=ki[:, 0:n]), reads=[tag + 'ki'], writes=[tag + 'kf'])
        P.op('dve', lambda e: e.scalar_tensor_tensor(out=x[:, 0:n], in0=kf[:, 0:n], scalar=-CW1, in1=x[:, 0:n], op0=ALU.mult, op1=ALU.add),
             reads=[tag + 'kf', tag + 'x'], writes=[tag + 'x'])
        P.op('dve', lambda e: e.scalar_tensor_tensor(out=x[:, 0:n], in0=kf[:, 0:n], scalar=-CW2, in1=x[:, 0:n], op0=ALU.mult, op1=ALU.add),
             reads=[tag + 'kf', tag + 'x'], writes=[tag + 'x'])
        P.op('dve', lambda e: e.tensor_scalar(out=x[:, 0:n], in0=x[:, 0:n], scalar1=-PI_LO, scalar2=PI_LO, op0=ALU.max, op1=ALU.min),
             reads=[tag + 'x'], writes=[tag + 'x'])
        P.op('act', lambda e, out=out: e.activation(out=out, in_=x[:, 0:n], func=AF.Sin), reads=[tag + 'x'], writes=[outkey + nm])


def phase_s(k, l):
    nc, P = k.nc, k.P
    P.barrier()
    with contextlib.ExitStack() as ps:
        def t(name, shape, dt=F32):
            return sb(k, "s_" + name, shape, dt, ps)
        lre, lim, ls = t("lre", [128, 16]), t("lim", [128, 16]), t("ls", [128, 16])
        bre, bim = t("bre", [128, 8, 16]), t("bim", [128, 8, 16])
        cre, cim = t("cre", [128, 16, 16]), t("cim", [128, 16, 16])
        dcol, bglu = t("dcol", [128, 2]), t("bglu", [128, 2])
        wst = t("wst", [128, 2, 256]); wglu = t("wglu", [128, 2, 256], BF16)
        step, aa, th, rr = t("step", [128, 16]), t("aa", [128, 16]), t("th", [128, 16]), t("rr", [128, 16])
        sn, cs, thT, sT, cT = (t(n, [128, 16]) for n in ("sn", "cs", "thT", "sT", "cT"))
        nsT = t("nsT", [128, 16])
        lbr, lbi, nr, d2, inv, cr, ci, tq = (t(n, [128, 16]) for n in ("lbr", "lbi", "nr", "d2", "inv", "cr", "ci", "tq"))
        scr = dict(x=t("scx", [128, TL]), kf=t("sckf", [128, TL]), ki=t("scki", [128, TL], I32))
        irow_i = t("irow_i", [128, TL], I32); irowT = t("irowT", [128, TL])
        arg = t("arg", [128, TL])
        t1, t2, bbr, bbi = (t(n, [128, 16]) for n in ("t1", "t2", "bbr", "bbi"))
        Bpad = [t("Bpad%d" % i, [128, 128]) for i in range(2)]
        LB = [[t("LB%d_%d" % (i, ri), [128, 128], BF16) for ri in range(2)] for i in range(16)]
        LC = [[t("LC%d_%d" % (i, ri), [128, 128], BF16) for ri in range(3)] for i in range(16)]
        SIN = [t("SIN%d" % i, [128, TL]) for i in range(16)]
        COS = [t("COS%d" % i, [128, TL]) for i in range(16)]
        Rt = [t("Rt%d" % i, [128, TL]) for i in range(16)]
        init = [t("init%d" % i, [128, 2]) for i in range(8)]
        tc_ = [t("tc%d" % i, [128, 2]) for i in range(8)]
        uraw = [t("uraw%d" % i, [128, 2, TL], BF16) for i in range(2)]
        uc = [t("uc%d" % i, [128, 2, TL], BF16) for i in range(2)]
        mm_ = [[t("m%d_%d" % (j, i), [128, TL]) for j in range(4)] for i in range(NSET)]
        zin = [t("zin%d" % i, [128, 2, TL]) for i in range(NSET)]
        zz = [t("z%d" % i, [128, 2, TL]) for i in range(NSET)]
        qq = [[t("q%d_%d" % (j, i), [128, TL], BF16) for j in range(4)] for i in range(NSET)]
        yfs = [t("yfs%d" % i, [128, 2, TL]) for i in range(2)]
        ys = [t("ys%d" % i, [128, 2, TL]) for i in range(2)]
        x2 = [t("x2%d" % i, [128, 2, TL]) for i in range(2)]
        gg = [t("g%d" % i, [128, 2, TL], BF16) for i in range(2)]
        sig = [t("sig%d" % i, [128, 2, TL]) for i in range(2)]
        yo = [t("yo%d" % i, [128, 2, TL], BF16) for i in range(2)]

        for tl, src, n, key in ((lre, k.s_lre, 16, 'lre'), (lim, k.s_lim, 16, 'lim'), (ls, k.s_ls, 16, 'ls'),
                                (bre, k.s_bre, 128, 'bre'), (bim, k.s_bim, 128, 'bim'),
                                (cre, k.s_cre, 256, 'cre'), (cim, k.s_cim, 256, 'cim'),
                                (dcol, k.s_d, 2, 'dcol'), (bglu, k.s_bglu, 2, 'bglu')):
            P.dma('sp', lambda e, tl=tl, src=src, n=n: e.dma_start(
                out=tl[:].rearrange("p a b -> p (a b)") if len(tl.shape) == 3 else tl[:],
                in_=dap(src, l * 128 * n, [[n, 128], [1, n]])), writes=['s_' + key])
        P.dma('sp', lambda e: e.dma_start(out=wst[:], in_=dap(k.s_wglu, l * 65536, [[256, 128], [128 * 256, 2], [1, 256]])), writes=['s_wst'])
        P.op('act', lambda e: e.copy(out=wglu[:], in_=wst[:]), reads=['s_wst'], writes=['s_wglu'])
        P.op('pool', lambda e: e.iota(irow_i[:], [[1, TL]], base=0, channel_multiplier=0), writes=['s_irow_i'])
        P.op('dve', lambda e: e.tensor_copy(out=irowT[:], in_=irow_i[:]), reads=['s_irow_i'], writes=['s_irowT'])
        P.op('act', lambda e: e.activation(out=step[:], in_=ls[:], func=AF.Exp), reads=['s_ls'], writes=['s_step'])
        P.op('dve', lambda e: e.tensor_tensor(out=aa[:], in0=lre[:], in1=step[:], op=ALU.mult), reads=['s_lre', 's_step'], writes=['s_aa'])
        P.op('dve', lambda e: e.tensor_tensor(out=th[:], in0=lim[:], in1=step[:], op=ALU.mult), reads=['s_lim', 's_step'], writes=['s_th'])
        P.op('act', lambda e: e.activation(out=rr[:], in_=aa[:], func=AF.Exp), reads=['s_aa'], writes=['s_rr'])
        sincos(k, th[:], 's_th', sn[:], cs[:], 's_th_', 16, scr, 's_sc_')
        P.op('dve', lambda e: e.tensor_scalar(out=thT[:], in0=th[:], scalar1=float(TL), scalar2=None, op0=ALU.mult), reads=['s_th'], writes=['s_thT'])
        sincos(k, thT[:], 's_thT', sT[:], cT[:], 's_thT_', 16, scr, 's_sc_')
        P.op('dve', lambda e: e.tensor_scalar(out=nsT[:], in0=sT[:], scalar1=-1.0, scalar2=None, op0=ALU.mult), reads=['s_thT_s'], writes=['s_nsT'])
        P.op('dve', lambda e: e.tensor_tensor(out=lbr[:], in0=rr[:], in1=cs[:], op=ALU.mult), reads=['s_rr', 's_th_c'], writes=['s_lbr'])
        P.op('dve', lambda e: e.tensor_tensor(out=lbi[:], in0=rr[:], in1=sn[:], op=ALU.mult), reads=['s_rr', 's_th_s'], writes=['s_lbi'])
        P.op('dve', lambda e: e.tensor_scalar(out=nr[:], in0=lbr[:], scalar1=-1.0, scalar2=None, op0=ALU.add), reads=['s_lbr'], writes=['s_nr'])
        P.op('dve', lambda e: e.tensor_tensor(out=d2[:], in0=lre[:], in1=lre[:], op=ALU.mult), reads=['s_lre'], writes=['s_d2'])
        P.op('dve', lambda e: e.tensor_tensor(out=tq[:], in0=lim[:], in1=lim[:], op=ALU.mult), reads=['s_lim'], writes=['s_tq'])
        P.op('dve', lambda e: e.tensor_tensor(out=d2[:], in0=d2[:], in1=tq[:], op=ALU.add), reads=['s_d2', 's_tq'], writes=['s_d2'])
        P.op('dve', lambda e: e.reciprocal(out=inv[:], in_=d2[:]), reads=['s_d2'], writes=['s_inv'])
        P.op('dve', lambda e: e.tensor_tensor(out=cr[:], in0=nr[:], in1=lre[:], op=ALU.mult), reads=['s_nr', 's_lre'], writes=['s_cr'])
        P.op('dve', lambda e: e.tensor_tensor(out=tq[:], in0=lbi[:], in1=lim[:], op=ALU.mult), reads=['s_lbi', 's_lim', 's_d2'], writes=['s_tq'])
        P.op('dve', lambda e: e.tensor_tensor(out=cr[:], in0=cr[:], in1=tq[:], op=ALU.add), reads=['s_cr', 's_tq'], writes=['s_cr'])
        P.op('dve', lambda e: e.tensor_tensor(out=cr[:], in0=cr[:], in1=inv[:], op=ALU.mult), reads=['s_cr', 's_inv'], writes=['s_cr'])
        P.op('dve', lambda e: e.tensor_tensor(out=ci[:], in0=lbi[:], in1=lre[:], op=ALU.mult), reads=['s_lbi', 's_lre'], writes=['s_ci'])
        P.op('dve', lambda e: e.tensor_tensor(out=tq[:], in0=nr[:], in1=lim[:], op=ALU.mult), reads=['s_nr', 's_lim', 's_cr'], writes=['s_tq'])
        P.op('dve', lambda e: e.tensor_tensor(out=ci[:], in0=ci[:], in1=tq[:], op=ALU.subtract), reads=['s_ci', 's_tq'], writes=['s_ci'])
        P.op('dve', lambda e: e.tensor_tensor(out=ci[:], in0=ci[:], in1=inv[:], op=ALU.mult), reads=['s_ci', 's_inv'], writes=['s_ci'])
        for idx in range(16):
            pair = idx // 2
            c0 = 32 * (pair % 4)
            ic = lambda tl, idx=idx: tl[:, idx:idx + 1]
            P.op('dve', lambda e, pair=pair, idx=idx: e.tensor_scalar(out=t1[:], in0=bim[:, pair, :], scalar1=ci[:, idx:idx + 1], scalar2=None, op0=ALU.mult),
                 reads=['s_bim', 's_ci'], writes=['s_t1'])
            P.op('dve', lambda e, pair=pair, idx=idx: e.scalar_tensor_tensor(out=bbr[:], in0=bre[:, pair, :], scalar=cr[:, idx:idx + 1], in1=t1[:],
                                                                              op0=ALU.mult, op1=ALU.subtract),
                 reads=['s_bre', 's_cr', 's_t1'], writes=['s_bbr'])
            P.op('dve', lambda e, pair=pair, idx=idx: e.tensor_scalar(out=t2[:], in0=bre[:, pair, :], scalar1=ci[:, idx:idx + 1], scalar2=None, op0=ALU.mult),
                 reads=['s_bre', 's_ci'], writes=['s_t2'])
            P.op('dve', lambda e, pair=pair, idx=idx: e.scalar_tensor_tensor(out=bbi[:], in0=bim[:, pair, :], scalar=cr[:, idx:idx + 1], in1=t2[:],
                                                                              op0=ALU.mult, op1=ALU.add),
                 reads=['s_bim', 's_cr', 's_t2'], writes=['s_bbi'])
            for ri, (bb, bk_) in enumerate(((bbr, 's_bbr'), (bbi, 's_bbi'))):
                bp = Bpad[ri]
                bpk = 's_Bpad%d' % ri
                P.op('pool', lambda e, bp=bp: e.memset(bp[:], 0.0), writes=[bpk])
                P.op('dve', lambda e, bp=bp, bb=bb, c0=c0: e.tensor_copy(out=bp[0:64, c0:c0 + 16], in_=bb[0:64, :]), reads=[bk_, bpk], writes=[bpk + 'a'])
                P.op('dve', lambda e, bp=bp, bb=bb, c0=c0: e.tensor_copy(out=bp[64:128, c0 + 16:c0 + 32], in_=bb[64:128, :]), reads=[bk_, bpk], writes=[bpk + 'b'])
                bkk = ri
                P.op('pe', lambda e, bp=bp, bkk=bkk: e.transpose(out=k.bank[bkk][:, 0:128], in_=bp[:], identity=k.identf[:]),
                     reads=[bpk, bpk + 'a', bpk + 'b', 'identf'], writes=['bank%d' % bkk])
                P._update(P.res['bank%d' % bkk]['w'], [bpk], [])
                P.op('act', lambda e, idx=idx, ri=ri, bkk=bkk: e.copy(out=LB[idx][ri][:], in_=k.bank[bkk][:, 0:128]),
                     reads=['bank%d' % bkk], writes=['s_LB%d' % idx + '_%d' % ri])
            for ri, (cc_, ck, sgn) in enumerate(((cre, 's_cre', 1.0), (cim, 's_cim', -1.0), (cre, 's_cre', -1.0))):
                lc = LC[idx][ri]
                lck = 's_LC%d_%d' % (idx, ri)
                P.op('pool', lambda e, lc=lc: e.memset(lc[:], 0.0), writes=[lck])
                P.op('dve', lambda e, lc=lc, cc_=cc_, c0=c0, sgn=sgn, idx=idx: e.tensor_scalar(
                    out=lc[0:64, c0:c0 + 16], in0=cc_[0:64, idx, :], scalar1=sgn, scalar2=None, op0=ALU.mult), reads=[ck, lck], writes=[lck + 'a'])
                P.op('dve', lambda e, lc=lc, cc_=cc_, c0=c0, sgn=sgn, idx=idx: e.tensor_scalar(
                    out=lc[64:128, c0 + 16:c0 + 32], in0=cc_[64:128, idx, :], scalar1=sgn, scalar2=None, op0=ALU.mult), reads=[ck, lck], writes=[lck + 'b'])
            P.op('dve', lambda e, idx=idx: e.tensor_scalar(out=arg[:], in0=irowT[:], scalar1=th[:, idx:idx + 1], scalar2=None, op0=ALU.mult),
                 reads=['s_irowT', 's_th', 's_tab%d_s' % (idx - 1), 's_tab%d_c' % (idx - 1)], writes=['s_arg'])
            sincos(k, arg[:], 's_arg', SIN[idx][:], COS[idx][:], 's_tab%d_' % idx, TL, scr, 's_sc_')
            P.op('pool', lambda e, idx=idx: e.tensor_copy(out=Rt[idx][:], in_=bc(rr, idx, TL, 16)), reads=['s_rr'], writes=['s_Rt%d' % idx])
        nu = 0
        for d in range(2):
            for pr in range(8):
                P.op('dve', lambda e, pr=pr: e.memset(init[pr][:], 0.0), writes=['s_init%d' % pr])
            for n in range(NCH):
                cf = n if d == 0 else NCH - 1 - n
                ub = n % 2
                P.dma('sp', lambda e, ub=ub, cf=cf: e.dma_start(out=uraw[ub][:], in_=dap(k.fm, 512 * T + cf * TL, [[T, 128], [128 * T, 2], [1, TL]])),
                      reads=['fm4_%d' % (cf * TL // 512), 'fm5_%d' % (cf * TL // 512)], writes=['s_uraw%d' % ub])
                if d == 0:
                    ucur, uck = uraw[ub], 's_uraw%d' % ub
                else:
                    for ft in range(2):
                        P.op('pool', lambda e, ub=ub, ft=ft: e.tensor_copy(out=uc[ub][:, ft, :], in_=dap(uraw[ub], ft * TL + TL - 1, [[2 * TL, 128], [-1, TL]])),
                             reads=['s_uraw%d' % ub], writes=['s_uc%d_%d' % (ub, ft)])
                    P.res['s_uc%d' % ub] = P.res['s_uc%d_1' % ub]
                    ucur, uck = uc[ub], 's_uc%d' % ub
                    P.dma('sp', lambda e, ub=ub, cf=cf: e.dma_start(out=yfs[ub][:], in_=dap(k.yf, cf * TL, [[T, 128], [128 * T, 2], [1, TL]])),
                          reads=['yf_%d' % cf], writes=['s_yfs%d' % ub])
                def unit(pr, nu_, n=n, d=d, ub=ub, ucur=ucur, uck=uck):
                    idx = pr * 2 + d
                    ft = pr // 4
                    u2 = nu_ % NSET
                    u3 = nu_ % NSET
                    bb = nu_ % 4
                    ucks = [uck] if d == 0 else ['s_uc%d_0' % ub, 's_uc%d_1' % ub]

                    def mmb(e, idx=idx, ft=ft, bb=bb, ucur=ucur):
                        e.matmul(k.bank[bb][:, 0:TL], lhsT=LB[idx][0][:], rhs=ucur[:, ft, :], start=True, stop=True)
                        return e.matmul(k.bank[bb][:, TL:2 * TL], lhsT=LB[idx][1][:], rhs=ucur[:, ft, :], start=True, stop=True)
                    P.op('pe', mmb, reads=ucks + ['s_LB%d_0' % idx, 's_LB%d_1' % idx], writes=['bank%d' % bb])
                    yield
                    m = mm_[u2]
                    mk = lambda j: 's_m%d_%d' % (j, u2)
                    tabs = ['s_tab%d_s' % idx, 's_tab%d_c' % idx]
                    br_, bi_ = k.bank[bb][:, 0:TL], k.bank[bb][:, TL:2 * TL]
                    P.op('dve', lambda e, m=m, br_=br_, idx=idx: e.tensor_tensor(out=m[0][:], in0=br_, in1=COS[idx][:], op=ALU.mult),
                         reads=['bank%d' % bb] + tabs, writes=[mk(0)])
                    yield
                    P.op('dve', lambda e, m=m, bi_=bi_, idx=idx: e.tensor_tensor(out=m[1][:], in0=bi_, in1=SIN[idx][:], op=ALU.mult),
                         reads=['bank%d' % bb] + tabs, writes=[mk(1)])
                    yield
                    P.op('dve', lambda e, m=m, bi_=bi_, idx=idx: e.tensor_tensor(out=m[2][:], in0=bi_, in1=COS[idx][:], op=ALU.mult),
                         reads=['bank%d' % bb] + tabs, writes=[mk(2)])
                    yield
                    P.op('dve', lambda e, m=m, br_=br_, idx=idx: e.tensor_tensor(out=m[3][:], in0=br_, in1=SIN[idx][:], op=ALU.mult),
                         reads=['bank%d' % bb] + tabs, writes=[mk(3)])
                    yield
                    zk = 's_zin%d' % u2
                    P.op('pool', lambda e, m=m, u2=u2: e.tensor_tensor(out=zin[u2][:, 0, :], in0=m[0][:], in1=m[1][:], op=ALU.add),
                         reads=[mk(0), mk(1)], writes=[zk + 'r'])
                    yield
                    P.op('pool', lambda e, m=m, u2=u2: e.tensor_tensor(out=zin[u2][:, 1, :], in0=m[2][:], in1=m[3][:], op=ALU.subtract),
                         reads=[mk(2), mk(3)], writes=[zk + 'i'])
                    yield
                    zt = zz[u3]
                    ztk = 's_z%d' % u3
                    ik = 's_init%d' % pr
                    P.op('dve', lambda e, zt=zt, u2=u2, idx=idx, pr=pr: e.tensor_tensor_scan(
                        out=zt[:, 0, :], data0=Rt[idx][:], data1=zin[u2][:, 0, :], initial=init[pr][:, 0:1], op0=ALU.mult, op1=ALU.add),
                        reads=[zk + 'r', 's_Rt%d' % idx, ik, ik + 'r', ik + 'i'], writes=[ztk + 'r'])
                    yield
                    P.op('dve', lambda e, zt=zt, u2=u2, idx=idx, pr=pr: e.tensor_tensor_scan(
                        out=zt[:, 1, :], data0=Rt[idx][:], data1=zin[u2][:, 1, :], initial=init[pr][:, 1:2], op0=ALU.mult, op1=ALU.add),
                        reads=[zk + 'i', 's_Rt%d' % idx, ik, ik + 'r', ik + 'i'], writes=[ztk + 'i'])
                    yield
                    if n < NCH - 1:
                        zl_r, zl_i = zt[:, 0, TL - 1:TL], zt[:, 1, TL - 1:TL]
                        tk = 's_tc%d' % pr
                        P.op('act', lambda e, pr=pr, idx=idx, zl_i=zl_i: e.activation(out=tc_[pr][:, 0:1], in_=zl_i, func=AF.Identity, scale=nsT[:, idx:idx + 1]),
                             reads=[ztk + 'i', 's_nsT'], writes=[tk + 'a'])
                        yield
                        P.op('act', lambda e, pr=pr, idx=idx, zl_r=zl_r: e.activation(out=tc_[pr][:, 1:2], in_=zl_r, func=AF.Identity, scale=sT[:, idx:idx + 1]),
                             reads=[ztk + 'r', 's_thT_s'], writes=[tk + 'b'])
                        yield
                        P.op('act', lambda e, pr=pr, idx=idx, zl_r=zl_r: e.activation(out=init[pr][:, 0:1], in_=zl_r, func=AF.Identity,
                                                                                     scale=cT[:, idx:idx + 1], bias=tc_[pr][:, 0:1]),
                             reads=[ztk + 'r', 's_thT_c', tk + 'a', ik], writes=[ik + 'r'])
                        yield
                        P.op('act', lambda e, pr=pr, idx=idx, zl_i=zl_i: e.activation(out=init[pr][:, 1:2], in_=zl_i, func=AF.Identity,
                                                                                     scale=cT[:, idx:idx + 1], bias=tc_[pr][:, 1:2]),
                             reads=[ztk + 'i', 's_thT_c', tk + 'b', ik], writes=[ik + 'i'])
                        P.res[ik] = P.res[ik + 'i']
                        P._update(P.res[ik]['w'], [], [])
                        yield
                    q = qq[u3]
                    qk = lambda j: 's_q%d_%d' % (j, u3)
                    P.op('pool', lambda e, q=q, zt=zt, idx=idx: e.tensor_tensor(out=q[0][:], in0=zt[:, 0, :], in1=COS[idx][:], op=ALU.mult),
                         reads=[ztk + 'r'] + tabs, writes=[qk(0)])
                    yield
                    P.op('dve', lambda e, q=q, zt=zt, idx=idx: e.tensor_tensor(out=q[1][:], in0=zt[:, 1, :], in1=SIN[idx][:], op=ALU.mult),
                         reads=[ztk + 'i'] + tabs, writes=[qk(1)])
                    yield
                    P.op('dve', lambda e, q=q, zt=zt, idx=idx: e.tensor_tensor(out=q[2][:], in0=zt[:, 0, :], in1=SIN[idx][:], op=ALU.mult),
                         reads=[ztk + 'r'] + tabs, writes=[qk(2)])
                    yield
                    P.op('dve', lambda e, q=q, zt=zt, idx=idx: e.tensor_tensor(out=q[3][:], in0=zt[:, 1, :], in1=COS[idx][:], op=ALU.mult),
                         reads=[ztk + 'i'] + tabs, writes=[qk(3)])
                    yield
                    yb = 4 + ft
                    first = (pr % 4 == 0)
                    last = (pr % 4 == 3)

                    def mmc(e, idx=idx, q=q, yb=yb, first=first, last=last):
                        e.matmul(k.bank[yb][:, 0:TL], lhsT=LC[idx][0][:], rhs=q[0][:], start=first, stop=False)
                        e.matmul(k.bank[yb][:, 0:TL], lhsT=LC[idx][2][:], rhs=q[1][:], start=False, stop=False)
                        e.matmul(k.bank[yb][:, 0:TL], lhsT=LC[idx][1][:], rhs=q[2][:], start=False, stop=False)
                        return e.matmul(k.bank[yb][:, 0:TL], lhsT=LC[idx][1][:], rhs=q[3][:], start=False, stop=last)
                    lckeys = ['s_LC%d_%d%s' % (idx, ri, sfx) for ri in range(3) for sfx in ('a', 'b')]
                    qkeys = [qk(j) for j in range(4)]
                    if first:
                        P.op('pe', mmc, reads=qkeys + lckeys, writes=['bank%d' % yb])
                        yield
                    else:
                        P.op('pe', mmc, reads=qkeys + ['bank%d' % yb] + lckeys, writes=['bank%d_acc' % yb])
                        yield
                        P.res['bank%d' % yb]['w'] = P.res['bank%d_acc' % yb]['w']
                    if last:
                        if d == 0:
                            P.op('act', lambda e, ub=ub, ft=ft, yb=yb: e.copy(out=ys[ub][:, ft, :], in_=k.bank[yb][:, 0:TL]),
                                 reads=['bank%d' % yb], writes=['s_ys%d_%d' % (ub, ft)])
                            yield
                        else:
                            P.op('dve', lambda e, ub=ub, ft=ft, yb=yb: e.tensor_tensor(
                                out=ys[ub][:, ft, :], in0=yfs[ub][:, ft, :], in1=dap(k.bank[yb], TL - 1, [[512, 128], [-1, TL]]), op=ALU.add),
                                reads=['bank%d' % yb, 's_yfs%d' % ub], writes=['s_ys%d_%d' % (ub, ft)])
                            yield

                for p0 in range(0, 8, SW):
                    gens = [unit(p0 + i_, nu + i_) for i_ in range(SW)]
                    nu += SW
                    while gens:
                        for g_ in list(gens):
                            try:
                                next(g_)
                            except StopIteration:
                                gens.remove(g_)
                if d == 0:
                    P.dma('sp', lambda e, ub=ub, cf=cf: e.dma_start(out=dap(k.yf, cf * TL, [[T, 128], [128 * T, 2], [1, TL]]), in_=ys[ub][:]),
                          reads=['s_ys%d_0' % ub, 's_ys%d_1' % ub], writes=['yf_%d' % cf])
                    P._update(P.res['yf_%d' % cf]['w'], ['s_ys%d_0' % ub, 's_ys%d_1' % ub], [])
                    continue
                ysk = ['s_ys%d_0' % ub, 's_ys%d_1' % ub]
                for ft in range(2):
                    P.op('dve', lambda e, ub=ub, ft=ft: e.scalar_tensor_tensor(
                        out=ys[ub][:, ft, :], in0=uraw[ub][:, ft, :], scalar=dcol[:, ft:ft + 1], in1=ys[ub][:, ft, :], op0=ALU.mult, op1=ALU.add),
                        reads=['s_uraw%d' % ub, 's_dcol', ysk[ft]], writes=[ysk[ft]])
                P.op('pool', lambda e, ub=ub: e.tensor_tensor(out=x2[ub][:], in0=ys[ub][:], in1=ys[ub][:], op=ALU.mult), reads=ysk, writes=['s_x2%d' % ub])
                P.op('pool', lambda e, ub=ub: e.tensor_scalar(out=x2[ub][:], in0=x2[ub][:], scalar1=0.044715, scalar2=1.0, op0=ALU.mult, op1=ALU.add),
                     reads=['s_x2%d' % ub], writes=['s_x2%d' % ub])
                P.op('pool', lambda e, ub=ub: e.tensor_tensor(out=x2[ub][:], in0=x2[ub][:], in1=ys[ub][:], op=ALU.mult), reads=['s_x2%d' % ub] + ysk, writes=['s_x2%d' % ub])
                P.op('act', lambda e, ub=ub: e.activation(out=x2[ub][:], in_=x2[ub][:], func=AF.Tanh, scale=math.sqrt(2.0 / math.pi)),
                     reads=['s_x2%d' % ub], writes=['s_x2%d' % ub])
                P.op('pool', lambda e, ub=ub: e.tensor_scalar(out=x2[ub][:], in0=x2[ub][:], scalar1=0.5, scalar2=0.5, op0=ALU.mult, op1=ALU.add),
                     reads=['s_x2%d' % ub], writes=['s_x2%d' % ub])
                P.op('pool', lambda e, ub=ub: e.tensor_tensor(out=ys[ub][:], in0=x2[ub][:], in1=ys[ub][:], op=ALU.mult), reads=['s_x2%d' % ub] + ysk, writes=['s_yg%d' % ub])
                P._update(P.res['s_yg%d' % ub]['w'], [], ysk)
                P.op('act', lambda e, ub=ub: e.copy(out=gg[ub][:], in_=ys[ub][:]), reads=['s_yg%d' % ub] + ysk, writes=['s_gg%d' % ub])
                for fo in range(2):
                    gb = 6

                    def mmg(e, ub=ub, fo=fo, gb=gb):
                        e.matmul(k.bank[gb][:, 0:TL], lhsT=wglu[:, 0, fo * 128:(fo + 1) * 128], rhs=gg[ub][:, 0, :], start=True, stop=False)
                        return e.matmul(k.bank[gb][:, 0:TL], lhsT=wglu[:, 1, fo * 128:(fo + 1) * 128], rhs=gg[ub][:, 1, :], start=False, stop=True)
                    P.op('pe', mmg, reads=['s_gg%d' % ub, 's_wglu'], writes=['bank%d' % gb])
                    P.op('act', lambda e, ub=ub, fo=fo, gb=gb: e.activation(out=sig[ub][:, fo, :], in_=k.bank[gb][:, 0:TL], func=AF.Sigmoid,
                                                                           bias=bglu[:, fo:fo + 1], scale=1.0),
                         reads=['bank%d' % gb, 's_bglu'], writes=['s_sig%d_%d' % (ub, fo)])
                P.op('dve', lambda e, ub=ub: e.tensor_tensor(out=yo[ub][:], in0=ys[ub][:], in1=sig[ub][:], op=ALU.mult),
                     reads=['s_yg%d' % ub, 's_sig%d_0' % ub, 's_sig%d_1' % ub] + ysk, writes=['s_yo%d' % ub])
                P.dma('sp', lambda e, ub=ub, cf=cf: e.dma_start(out=dap(k.yssmT, cf * TL, [[T, 128], [128 * T, 2], [1, TL]]), in_=yo[ub][:]),
                      reads=['s_yo%d' % ub], writes=['yssmT_c%d' % cf])
                P._update(P.res['yssmT_c%d' % cf]['w'], ['s_yo%d' % ub], [])
        for ti in range(NT):
            P.res['yssmT%d' % ti] = P.res['yssmT_c%d' % (ti * 128 // TL)]


CAP = 512
NEXP = 16
NBIS = 30


def phase_f(k, l, last):
    nc, P = k.nc, k.P
    P.barrier()
    dst = k.out if last else k.hA
    with contextlib.ExitStack() as ps:
        def tp(name, shape, dt=F32):
            return sb(k, "f_" + name, shape, dt, ps)
        idx_t = [[tp("idx%d_%d" % (e_, cb), [128, 1], I32) for cb in range(4)] for e_ in range(16)]
        gate_all = tp("gate_all", [128, 16, 4])
        p2 = contextlib.ExitStack()

        def t(name, shape, dt=F32):
            return sb(k, "f_" + name, shape, dt, p2)
        rw = t("rw", [128, 8, 16])
        aff = t("aff", [128, NT, 16])
        ones = t("ones", [128, 128])
        ustr = t("ustr", [128, 128])
        lo, thr, pc, gew = t("lo", [128, 16]), t("thr", [128, 16]), t("pc", [128, 16]), t("gew", [128, 16])
        cmpb = t("cmpb", [128, NT, 16])
        met = t("met", [128, 16, NT])
        incl = t("incl", [128, 16, NT])
        rpat_i = t("rpat_i", [128, 16, NT], I32)
        rpat = t("rpat", [128, 16, NT])
        posm = t("posm", [128, 16, NT])
        io_i = t("io_i", [128, 512], I32)
        io512 = t("io512", [128, 512], mybir.dt.float16)
        tvi = t("tvi", [128, NT, 16], I32)
        r1 = t("r1", [128, NT, 16])
        tv = t("tv", [128, NT, 16, 5], BF16)
        idf = t("idf", [128, 4])
        slot = t("slot", [128, 4, 5])
        oh = [t("oh%d" % i, [128, 512], BF16) for i in range(3)]
        with contextlib.ExitStack() as p1:
            h1t = [sb(k, "f_h1r%d" % i, [128, D], F32, p1) for i in range(2)]
            h1T = [sb(k, "f_h1T%d" % i, [128, 8, 128], F32, p1) for i in range(2)]
            ex = [sb(k, "f_ex%d" % i, [128, 16], F32, p1) for i in range(2)]
            sm = [dict((n, sb(k, "f_%s%d" % (n, i), [128, 1], F32, p1)) for n in ('mx', 'sum', 'rs')) for i in range(2)]
            P.dma('sp', lambda e: e.dma_start(out=rw[:], in_=dap(k.router_w, l * D * 16, [[16, 128], [128 * 16, 8], [1, 16]])), writes=['f_rw'])
            def f1_load(ti):
                b = ti % 2
                P.dma('sp', lambda e: e.dma_start(out=h1t[b][:], in_=dap(k.h1, ti * 128 * D, [[D, 128], [1, D]])),
                      reads=['h1_%d' % ti], writes=['f_h1r%d' % b])
            for ti in range(NT):
                b = ti % 2
                if ti == 0:
                    f1_load(0)
                if ti + 1 < NT:
                    f1_load(ti + 1)
                for hh in range(2):
                    bk = (2 * ti + hh) % 4

                    def tr(e, b=b, hh=hh, bk=bk):
                        r = None
                        for j in range(4):
                            kk = hh * 4 + j
                            r = e.transpose(out=k.bank[bk][:, j * 128:(j + 1) * 128], in_=h1t[b][:, kk * 128:(kk + 1) * 128], identity=k.identf[:])
                        return r
                    P.op('pe', tr, reads=['f_h1r%d' % b, 'identf'], writes=['bank%d' % bk])
                    P.op('dve' if hh == 0 else 'act', lambda e, b=b, hh=hh, bk=bk: (e.tensor_copy if hh == 0 else e.copy)(
                        out=h1T[b][:, hh * 4:(hh + 1) * 4, :], in_=k.bank[bk][:, :].rearrange("p (a b) -> p a b", a=4)),
                        reads=['bank%d' % bk], writes=['f_h1T%d_%d' % (b, hh)])
                lb = 4 + (ti % 2)

                def mml(e, b=b, lb=lb):
                    r = None
                    for kk in range(8):
                        r = e.matmul(k.bank[lb][:, 0:16], lhsT=h1T[b][:, kk, :], rhs=rw[:, kk, :], start=(kk == 0), stop=(kk == 7))
                    return r
                P.op('pe', mml, reads=['f_h1T%d_0' % b, 'f_h1T%d_1' % b, 'f_rw'], writes=['bank%d' % lb])
                m = sm[b]
                P.op('dve', lambda e, m=m, lb=lb: e.tensor_reduce(out=m['mx'][:], in_=k.bank[lb][:, 0:16], axis=AX.X, op=ALU.max, negate=True),
                     reads=['bank%d' % lb], writes=['f_mx%d' % b])
                P.op('act', lambda e, m=m, lb=lb, b=b: e.activation(out=ex[b][:], in_=k.bank[lb][:, 0:16], func=AF.Exp, bias=m['mx'][:], scale=1.0,
                                                                   accum_out=m['sum'][:]),
                     reads=['bank%d' % lb, 'f_mx%d' % b], writes=['f_ex%d' % b, 'f_sum%d' % b])
                P.op('dve', lambda e, m=m: e.reciprocal(out=m['rs'][:], in_=m['sum'][:]), reads=['f_sum%d' % b], writes=['f_rs%d' % b])
                P.op('dve', lambda e, m=m, b=b, ti=ti: e.tensor_scalar(out=aff[:, ti, :], in0=ex[b][:], scalar1=m['rs'][:], scalar2=None, op0=ALU.mult),
                     reads=['f_ex%d' % b, 'f_rs%d' % b], writes=['f_aff%d' % ti])
        affk = ['f_aff%d' % ti for ti in range(NT)]
        P.op('dve', lambda e: e.memset(ones[:], 1.0), writes=['f_ones'])
        P.op('dve', lambda e: e.tensor_scalar(out=ustr[:], in0=k.dist[:], scalar1=0.0, scalar2=None, op0=ALU.is_gt), reads=['dist'], writes=['f_ustr'])
        P.op('dve', lambda e: e.memset(lo[:], 0.0), writes=['f_lo'])
        for it in range(NBIS):
            w = 0.5 ** (it + 1)
            P.op('dve', lambda e, w=w: e.tensor_scalar(out=thr[:], in0=lo[:], scalar1=w, scalar2=None, op0=ALU.add), reads=['f_lo'], writes=['f_thr'])
            P.op('dve', lambda e: e.tensor_tensor(out=cmpb[:], in0=aff[:], in1=dap(thr, 0, [[16, 128], [0, NT], [1, 16]]), op=ALU.is_ge),
                 reads=affk + ['f_thr'], writes=['f_cmpb'])
            P.op('dve', lambda e: e.tensor_reduce(out=pc[:], in_=cmpb[:].rearrange("p t e -> p e t"), axis=AX.X, op=ALU.add),
                 reads=['f_cmpb'], writes=['f_pc'])
            P.op('pe', lambda e: e.matmul(k.bank[0][:, 0:16], lhsT=ones[:], rhs=pc[:], start=True, stop=True),
                 reads=['f_ones', 'f_pc'], writes=['bank0'])
            P.op('dve', lambda e, w=w: e.tensor_scalar(out=gew[:], in0=k.bank[0][:, 0:16], scalar1=CAP - 0.5, scalar2=w, op0=ALU.is_ge, op1=ALU.mult),
                 reads=['bank0'], writes=['f_gew'])
            P.op('dve', lambda e: e.tensor_tensor(out=lo[:], in0=lo[:], in1=gew[:], op=ALU.add), reads=['f_lo', 'f_gew'], writes=['f_lo'])
        P.op('dve', lambda e: e.tensor_tensor(out=met[:], in0=aff[:].rearrange("p t e -> p e t"), in1=dap(lo, 0, [[16, 128], [1, 16], [0, NT]]), op=ALU.is_ge),
             reads=affk + ['f_lo'], writes=['f_met'])
        P.op('pool', lambda e: e.iota(rpat_i[:].rearrange("p a b -> p (a b)"), [[0, 16], [1, NT]], base=0, channel_multiplier=0), writes=['f_rpat_i'])
        P.op('dve', lambda e: e.tensor_copy(out=rpat[:], in_=rpat_i[:]), reads=['f_rpat_i'], writes=['f_rpat'])
        P.op('dve', lambda e: e.tensor_scalar(out=rpat[:], in0=rpat[:], scalar1=0.0, scalar2=None, op0=ALU.is_gt), reads=['f_rpat'], writes=['f_rpat'])
        P.op('dve', lambda e: e.tensor_tensor_scan(out=incl[:].rearrange("p a b -> p (a b)"), data0=rpat[:].rearrange("p a b -> p (a b)"),
                                                    data1=met[:].rearrange("p a b -> p (a b)"), initial=0.0, op0=ALU.mult, op1=ALU.add),
             reads=['f_rpat', 'f_met'], writes=['f_incl'])
        P.op('dve', lambda e: e.tensor_copy(out=pc[:], in_=incl[:, :, NT - 1]), reads=['f_incl'], writes=['f_pc'])
        P.op('pe', lambda e: e.matmul(k.bank[0][:, 0:16], lhsT=ustr[:], rhs=pc[:], start=True, stop=True), reads=['f_ustr', 'f_pc'], writes=['bank0'])
        P.op('dve', lambda e: e.tensor_tensor(out=posm[:], in0=incl[:], in1=met[:], op=ALU.subtract), reads=['f_incl', 'f_met'], writes=['f_posm'])
        P.op('dve', lambda e: e.tensor_tensor(out=posm[:], in0=posm[:], in1=dap(k.bank[0], 0, [[512, 128], [1, 16], [0, NT]]), op=ALU.add),
             reads=['f_posm', 'bank0'], writes=['f_posm'])
        P.op('dve', lambda e: e.scalar_tensor_tensor(out=posm[:], in0=posm[:], scalar=1.0, in1=met[:], op0=ALU.add, op1=ALU.mult),
             reads=['f_posm', 'f_met'], writes=['f_posm'])
        P.op('dve', lambda e: e.tensor_scalar(out=posm[:], in0=posm[:], scalar1=-1.0, scalar2=None, op0=ALU.add), reads=['f_posm'], writes=['f_posm'])
        P.op('pool', lambda e: e.iota(io_i[:], [[1, 512]], base=0, channel_multiplier=0), writes=['f_io_i'])
        P.op('dve', lambda e: e.tensor_copy(out=io512[:], in_=io_i[:]), reads=['f_io_i'], writes=['f_io512'])
        P.op('pool', lambda e: e.iota(tvi[:].rearrange("p a b -> p (a b)"), [[1, NT], [0, 16]], base=0, channel_multiplier=0), writes=['f_tvi'])
        P.op('dve', lambda e: e.tensor_copy(out=tv[:, :, :, 0], in_=tvi[:]), reads=['f_tvi'], writes=['f_tv0'])
        P.op('pool', lambda e: e.iota(tvi[:].rearrange("p a b -> p (a b)"), [[0, NT], [0, 16]], base=0, channel_multiplier=1), reads=['f_tv0'], writes=['f_tvi'])
        P.op('dve', lambda e: e.tensor_copy(out=tv[:, :, :, 1], in_=tvi[:]), reads=['f_tvi'], writes=['f_tv1'])
        P.op('dve', lambda e: e.tensor_copy(out=tv[:, :, :, 2], in_=aff[:]), reads=affk, writes=['f_tv2'])
        P.op('dve', lambda e: e.tensor_tensor(out=r1[:], in0=aff[:], in1=tv[:, :, :, 2], op=ALU.subtract), reads=affk + ['f_tv2'], writes=['f_r1'])
        P.op('dve', lambda e: e.tensor_copy(out=tv[:, :, :, 3], in_=r1[:]), reads=['f_r1'], writes=['f_tv3'])
        P.op('dve', lambda e: e.tensor_tensor(out=r1[:], in0=r1[:], in1=tv[:, :, :, 3], op=ALU.subtract), reads=['f_r1', 'f_tv3'], writes=['f_r1'])
        P.op('dve', lambda e: e.tensor_copy(out=tv[:, :, :, 4], in_=r1[:]), reads=['f_r1'], writes=['f_tv4'])
        tvk = ['f_tv%d' % i for i in range(5)]
        noh = 0
        for ex_ in range(NEXP):
            for ti in range(NT):
                o = noh % 3
                noh += 1
                P.op('dve', lambda e, o=o, ex_=ex_, ti=ti: e.tensor_scalar(out=oh[o][:], in0=io512[:], scalar1=posm[:, ex_, ti:ti + 1], scalar2=None, op0=ALU.is_equal),
                     reads=['f_io512', 'f_posm'], writes=['f_oh%d' % o])

                def mms(e, o=o, ex_=ex_, ti=ti):
                    r = None
                    for cb in range(4):
                        r = e.matmul(k.bank[cb][:, 0:5], lhsT=oh[o][:, cb * 128:(cb + 1) * 128], rhs=tv[:, ti, ex_, :],
                                     start=(ti == 0), stop=(ti == NT - 1))
                    return r
                P.op('pe', mms, reads=['f_oh%d' % o] + tvk, writes=['bank0_3'] if ti else ['bank0', 'bank1', 'bank2', 'bank3', 'bank0_3'])
            for cb in range(4):
                P.op('act', lambda e, cb=cb: e.copy(out=slot[:, cb, :], in_=k.bank[cb][:, 0:5]), reads=['bank0_3'], writes=['f_slot%d' % cb])
                P._update(P.res['f_slot%d' % cb]['w'], ['bank%d' % cb], [])
            slk = ['f_slot%d' % cb for cb in range(4)]
            P.op('dve', lambda e: e.scalar_tensor_tensor(out=idf[:], in0=slot[:, :, 0], scalar=128.0, in1=slot[:, :, 1], op0=ALU.mult, op1=ALU.add),
                 reads=slk, writes=['f_idf'])
            P.op('dve', lambda e, ex_=ex_: e.tensor_reduce(out=gate_all[:, ex_, :], in_=slot[:, :, 2:5], axis=AX.X, op=ALU.add),
                 reads=slk, writes=['f_gate%d' % ex_])
            for cb in range(4):
                P.op('dve', lambda e, ex_=ex_, cb=cb: e.tensor_copy(out=idx_t[ex_][cb][:], in_=idf[:, cb:cb + 1]), reads=['f_idf'], writes=['f_idx%d_%d' % (ex_, cb)])
            P.res['f_idx%d' % ex_] = P.res['f_idx%d_3' % ex_]
            P._update(P.res['f_idx%d' % ex_]['w'], slk + ['f_idf'], [])
        if DBG_F == 1:
            P.dma('sp', lambda e: e.dma_start(out=dap(k.dbg_gate, 0, [[64, 128], [1, 64]]), in_=gate_all[:].rearrange("p a b -> p (a b)")),
                  reads=['f_gate%d' % i for i in range(16)], writes=['dbg_gate'])
            P.dma('sp', lambda e: e.dma_start(out=dap(k.dbg_aff, 0, [[512, 128], [1, 512]]), in_=aff[:].rearrange("p a b -> p (a b)")),
                  reads=affk, writes=['dbg_aff'])
            P.dma('sp', lambda e: e.dma_start(out=dap(k.dbg_posm, 0, [[512, 128], [1, 512]]), in_=posm[:].rearrange("p a b -> p (a b)")),
                  reads=['f_posm'], writes=['dbg_posm'])
            finish(k)
            p2.close()
            return
        P.barrier()
        p2.close()
        t = tp
        zt = t("zt", [128, D])
        P.op('pool', lambda e: e.memset(zt[:], 0.0), writes=['f_zt'])
        for ti in range(NT):
            P.dma('sp', lambda e, ti=ti: e.dma_start(out=dap(k.ffn, ti * 128 * D, [[D, 128], [1, D]]), in_=zt[:]), reads=['f_zt'], writes=['ffn_z%d' % ti])
        zero_toks = [P.res['ffn_z%d' % ti]['w'] for ti in range(NT)]
        with contextlib.ExitStack() as p5:
            def t5(name, shape, dt=F32):
                return sb(k, "f_" + name, shape, dt, p5)
            xs = t5("xs", [128, 4, D], BF16)
            xsT = [t5("xsT%d" % i, [128, 8, 512], BF16) for i in range(2)]
            stgA = [t5("stgA%d" % i, [128, 8, 256]) for i in range(3)]
            stgB = [t5("stgB%d" % i, [128, 2, D]) for i in range(2)]
            wgb = [t5("wgb%d" % i, [128, 8, 512], BF16) for i in range(2)]
            wub = [t5("wub%d" % i, [128, 8, 512], BF16) for i in range(2)]
            wdb = t5("wdb", [128, 16, D], BF16)
            hdn = t5("hdn", [128, 16, 512], BF16)
            sl = [t5("sl%d" % i, [128, 512]) for i in range(2)]
            ost = [t5("ost%d" % i, [128, D]) for i in range(2)]
            cnt = dict(stg=0, stgB=0, cast=0, psg=0, ps2=0, ost=0)

            def gather(ex_):
                xb = ex_ % 2
                for cb in range(4):
                    P.dma('pool', lambda e, cb=cb, ex_=ex_: e.indirect_dma_start(
                        out=xs[:, cb, :], out_offset=None, in_=k.h1b[:, :],
                        in_offset=bass.IndirectOffsetOnAxis(ap=idx_t[ex_][cb][:, :], axis=0)),
                        reads=['f_idx%d' % ex_] + ['h1b_%d' % ti for ti in range(NT)], writes=['f_xs%d' % cb])

                    def trx(e, cb=cb):
                        r = None
                        for kk in range(8):
                            r = e.transpose(out=k.bankT[:, kk * 128:(kk + 1) * 128], in_=xs[:, cb, kk * 128:(kk + 1) * 128], identity=k.identb[:])
                        return r
                    P.op('pe', trx, reads=['f_xs%d' % cb, 'identb'], writes=['bankT'])
                    P.op('dve', lambda e, cb=cb, xb=xb: e.tensor_copy(out=xsT[xb][:, :, cb * 128:(cb + 1) * 128],
                                                                    in_=k.bankT[:].rearrange("p (a b) -> p a b", a=8)),
                         reads=['bankT'], writes=['f_xsT%d_%d' % (xb, cb)])

            def cast_engine():
                ce = 'act' if (cnt['cast'] % 3 != 2) else 'dve'
                cnt['cast'] += 1
                return ce

            def load_gu_piece(g, pi):
                ex_, j = g // 4, g % 4
                wb = g % 2
                wsrc, wdst, wkey = ((k.w_gate, wgb[wb], 'f_wgb%d' % wb), (k.w_up, wub[wb], 'f_wub%d' % wb))[pi // 2]
                hh = pi % 2
                sg = cnt['stg'] % 3
                cnt['stg'] += 1
                off = ((l * NEXP + ex_) * D) * 2048 + j * 512 + hh * 256
                P.dma('sp', lambda e: e.dma_start(out=stgA[sg][:], in_=dap(wsrc, off, [[2048, 128], [128 * 2048, 8], [1, 256]])),
                      writes=['f_stg%d' % sg])
                ce = cast_engine()
                P.op(ce, lambda e: (e.copy if ce == 'act' else e.tensor_copy)(out=wdst[:, :, hh * 256:(hh + 1) * 256], in_=stgA[sg][:]),
                     reads=['f_stg%d' % sg], writes=[wkey + '_%d' % hh])

            def load_d(g):
                ex_, j = g // 4, g % 4
                for hh in range(2):
                    sg = cnt['stgB'] % 2
                    cnt['stgB'] += 1
                    off = ((l * NEXP + ex_) * 2048 + j * 512 + hh * 256) * D
                    P.dma('sp', lambda e, sg=sg, off=off: e.dma_start(
                        out=stgB[sg][:], in_=dap(k.w_down, off, [[D, 128], [128 * D, 2], [1, D]])),
                        writes=['f_stgB%d' % sg])
                    ce = cast_engine()
                    kt0 = j * 4 + hh * 2
                    P.op(ce, lambda e, sg=sg, kt0=kt0, ce=ce: (e.copy if ce == 'act' else e.tensor_copy)(
                        out=wdb[:, kt0:kt0 + 2, :], in_=stgB[sg][:]),
                        reads=['f_stgB%d' % sg], writes=['f_wdb%d' % (kt0 // 2)])

            def phase1(g, fis):
                ex_, j = g // 4, g % 4
                wb = g % 2
                xb = ex_ % 2
                xk = ['f_xsT%d_%d' % (xb, cb) for cb in range(4)]
                wgk = ['f_wgb%d_0' % wb, 'f_wgb%d_1' % wb]
                wuk = ['f_wub%d_0' % wb, 'f_wub%d_1' % wb]
                for fi in fis:
                    ftile = j * 4 + fi
                    g_b = 3 + (cnt['psg'] % 2)
                    u_b = 5 + (cnt['psg'] % 2)
                    sb_ = cnt['psg'] % 2
                    cnt['psg'] += 1

                    def mg(e, g_b=g_b, wb=wb, fi=fi, xb=xb):
                        r = None
                        for kk in range(8):
                            r = e.matmul(k.bank[g_b][:, :], lhsT=wgb[wb][:, kk, fi * 128:(fi + 1) * 128], rhs=xsT[xb][:, kk, :], start=(kk == 0), stop=(kk == 7))
                        return r

                    def mu(e, u_b=u_b, wb=wb, fi=fi, xb=xb):
                        r = None
                        for kk in range(8):
                            r = e.matmul(k.bank[u_b][:, :], lhsT=wub[wb][:, kk, fi * 128:(fi + 1) * 128], rhs=xsT[xb][:, kk, :], start=(kk == 0), stop=(kk == 7))
                        return r
                    P.op('pe', mg, reads=wgk + xk, writes=['bank%d' % g_b])
                    P.op('pe', mu, reads=wuk + xk, writes=['bank%d' % u_b])
                    P.op('act', lambda e, sb_=sb_, g_b=g_b: e.activation(out=sl[sb_][:], in_=k.bank[g_b][:, :], func=AF.Silu),
                         reads=['bank%d' % g_b], writes=['f_sl%d' % sb_])
                    P.op('dve', lambda e, sb_=sb_, u_b=u_b, ftile=ftile: e.tensor_tensor(out=hdn[:, ftile, :], in0=k.bank[u_b][:, :], in1=sl[sb_][:], op=ALU.mult),
                         reads=['bank%d' % u_b, 'f_sl%d' % sb_], writes=['f_hdn%d' % ftile])

            def phase2(ex_):
                hk = ['f_hdn%d' % i for i in range(16)]
                wdk = ['f_wdb%d' % i for i in range(8)]
                for cb in range(4):
                    ob = cnt['ost'] % 2
                    cnt['ost'] += 1
                    for half in range(2):
                        pb = cnt['ps2'] % 3
                        cnt['ps2'] += 1

                        def md(e, pb=pb, cb=cb, half=half):
                            r = None
                            for kk in range(16):
                                r = e.matmul(k.bank[pb][:, :], lhsT=hdn[:, kk, cb * 128:(cb + 1) * 128], rhs=wdb[:, kk, half * 512:(half + 1) * 512],
                                             start=(kk == 0), stop=(kk == 15))
                            return r
                        P.op('pe', md, reads=hk + wdk, writes=['bank%d' % pb])
                        P.op('dve', lambda e, pb=pb, ob=ob, half=half, cb=cb, ex_=ex_: e.tensor_scalar(
                            out=ost[ob][:, half * 512:(half + 1) * 512], in0=k.bank[pb][:, :], scalar1=gate_all[:, ex_, cb:cb + 1], scalar2=None, op0=ALU.mult),
                            reads=['bank%d' % pb, 'f_gate%d' % ex_], writes=['f_ost%d_%d' % (ob, half)])
                    P.dma('pool', lambda e, ob=ob, cb=cb, ex_=ex_: e.indirect_dma_start(
                        out=k.ffn[:, :], out_offset=bass.IndirectOffsetOnAxis(ap=idx_t[ex_][cb][:, :], axis=0), in_=ost[ob][:], in_offset=None,
                        compute_op=ALU.add),
                        reads=['f_ost%d_0' % ob, 'f_ost%d_1' % ob, 'f_idx%d' % ex_] + (['ffn_z%d' % ti for ti in range(NT)] if ex_ == 0 and cb == 0 else []),
                        writes=['ffn_acc'])
                    P._update(P.res['ffn_acc']['w'], ['f_ost%d_0' % ob, 'f_ost%d_1' % ob], [])

            NG = NEXP * 4
            gather(0)
            for pi in range(4):
                load_gu_piece(0, pi)
            load_d(0)
            for g in range(NG):
                ex_, j = g // 4, g % 4
                if j == 2 and ex_ + 1 < NEXP:
                    gather(ex_ + 1)
                for fi in range(4):
                    if g + 1 < NG:
                        load_gu_piece(g + 1, fi)
                    phase1(g, [fi])
                if j == 3:
                    phase2(ex_)
                if g + 1 < NG:
                    load_d(g + 1)
        P.barrier()
        g2 = t("g2", [128, D]); b2 = t("b2", [128, D])
        h1t = [t("h1%d" % i, [128, D]) for i in range(3)]
        ft_ = [t("ft%d" % i, [128, D]) for i in range(3)]
        rt = [t("r%d" % i, [128, D]) for i in range(3)]
        h2t = [t("h2%d" % i, [128, D]) for i in range(3)]
        scr = [dict(stats=t("stats%d" % i, [128, 2, 6]), mv=t("mv%d" % i, [128, 2]), sd=t("sd%d" % i, [128, 1]), rstd=t("rstd%d" % i, [128, 1]),
                    nb=t("nb%d" % i, [128, 1]), xn=t("xn%d" % i, [128, D])) for i in range(3)]
        P.dma('sp', lambda e: e.dma_start(out=g2[:], in_=dap(k.ln2_g, l * D, [[0, 128], [1, D]])), writes=['f_g2'])
        P.dma('sp', lambda e: e.dma_start(out=b2[:], in_=dap(k.ln2_b, l * D, [[0, 128], [1, D]])), writes=['f_b2'])

        def f6_load(ti):
            b = ti % 3
            P.dma('sp', lambda e: e.dma_start(out=h1t[b][:], in_=dap(k.h1, ti * 128 * D, [[D, 128], [1, D]])),
                  reads=['h1_%d' % ti], writes=['f_h1%d' % b])
            P.dma('sp', lambda e: e.dma_start(out=ft_[b][:], in_=dap(k.ffn, ti * 128 * D, [[D, 128], [1, D]])),
                  reads=['ffn_acc'], writes=['f_ft%d' % b])
        def f6_tile(ti):
            b = ti % 3
            tok0 = ti * 128
            P.op('dve', lambda e, b=b: e.scalar_tensor_tensor(out=rt[b][:], in0=h1t[b][:], scalar=ALPHA, in1=ft_[b][:], op0=ALU.mult, op1=ALU.add),
                 reads=['f_h1%d' % b, 'f_ft%d' % b], writes=['f_r%d' % b])
            yield
            yield from layer_norm_tile(k, rt[b][:], 'f_r%d' % b, h2t[b][:], 'f_h2%d' % b, g2[:], 'f_g2', b2[:], 'f_b2', 'f%d_' % b, scr[b])
            P.dma('sp', lambda e, b=b, tok0=tok0: e.dma_start(out=dap(dst, tok0 * D, [[D, 128], [1, D]]), in_=h2t[b][:]),
                  reads=['f_h2%d' % b], writes=['hA%d' % ti])

        run_window(f6_tile, f6_load, NT, 2)


def finish(k):
    P = k.P
    for i, c in enumerate(P.dcnt):
        if c > 0:
            P._wait('sp', (('d', i), c))


def prep_inputs(inputs):
    w_in = np.asarray(inputs["w_in"])
    sl = lambda a, b: list(range(a, b))
    tm_cols = sl(256, 512) + sl(512, 768) + sl(768, 1024) + sl(1920, 2048)
    fm_cols = sl(0, 256) + sl(256, 512) + sl(1024, 1280) + sl(1280, 1792) + sl(1792, 1920)
    w_perm = np.ascontiguousarray(w_in[:, :, tm_cols + fm_cols])
    shared = {"ln_in_g": np.ascontiguousarray(inputs["ln_in_g"]), "ln_in_b": np.ascontiguousarray(inputs["ln_in_b"]),
              "w_in": w_perm,
              "ret_theta": np.ascontiguousarray(np.asarray(inputs["ret_theta"]).reshape(DEPTH, 8)),
              "attn_sink": np.ascontiguousarray(inputs["attn_sink"]),
              "w_out": np.ascontiguousarray(inputs["w_out"]),
              "ln1_g": np.ascontiguousarray(inputs["ln1_g"]), "ln1_b": np.ascontiguousarray(inputs["ln1_b"]),
              "ln2_g": np.ascontiguousarray(inputs["ln2_g"]), "ln2_b": np.ascontiguousarray(inputs["ln2_b"])}
    L = DEPTH
    A = lambda n: np.asarray(inputs[n], dtype=np.float32)
    rl = lambda x: np.ascontiguousarray(x.reshape(L, 2, 8, 2, 64).transpose(0, 3, 4, 2, 1).reshape(L, 128, 16))
    shared["s_lre"] = rl(A("ssm_lambda_re")); shared["s_lim"] = rl(A("ssm_lambda_im"))
    lsx = A("ssm_log_step").reshape(L, 2, 8, 2).transpose(0, 3, 2, 1)
    shared["s_ls"] = np.ascontiguousarray(np.broadcast_to(lsx[:, :, None, :, :], (L, 2, 64, 8, 2)).reshape(L, 128, 16))
    rb = lambda x: np.ascontiguousarray(x.reshape(L, 8, 2, 64, 16).transpose(0, 2, 3, 1, 4).reshape(L, 128, 8, 16))
    shared["s_bre"] = rb(A("ssm_b_re")); shared["s_bim"] = rb(A("ssm_b_im"))
    rc = lambda x: np.ascontiguousarray(x.reshape(L, 2, 8, 2, 16, 64).transpose(0, 3, 5, 2, 1, 4).reshape(L, 128, 16, 16))
    shared["s_cre"] = rc(A("ssm_c_re")); shared["s_cim"] = rc(A("ssm_c_im"))
    rd = lambda x: np.ascontiguousarray(x.reshape(L, 2, 128).transpose(0, 2, 1))
    shared["s_d"] = rd(A("ssm_d")); shared["s_bglu"] = rd(A("ssm_b_glu"))
    shared["s_wglu"] = np.ascontiguousarray(A("ssm_w_glu"))
    shared["router_w"] = np.ascontiguousarray(A("router_w"))
    shared["exp_w_gate"] = np.ascontiguousarray(A("exp_w_gate"))
    shared["exp_w_up"] = np.ascontiguousarray(A("exp_w_up"))
    shared["exp_w_down"] = np.ascontiguousarray(A("exp_w_down"))
    return shared


def kernel(**inputs):
    shared = prep_inputs(inputs)
    nc = build()
    x = np.asarray(inputs["x"])
    in_maps = []
    for c in range(NCORES):
        m = dict(shared)
        m["x"] = np.ascontiguousarray(x[c])
        in_maps.append(m)
    res = run_bass_kernel_spmd(nc, in_maps, core_ids=list(range(NCORES)))
    return np.stack([res.results[c]["out"] for c in range(NCORES)], axis=0)
```

```python
import math
import contextlib
import numpy as np
import concourse.bass as bass
import concourse.mybir as mybir
from concourse.bass_utils import run_bass_kernel_spmd

F32 = mybir.dt.float32
BF16 = mybir.dt.bfloat16
I32 = mybir.dt.int32
AF = mybir.ActivationFunctionType
ALU = mybir.AluOpType
AX = mybir.AxisListType

T = 4096
NT = T // 128
D = 1024
DEPTH = 2
ALPHA = (2.0 * DEPTH) ** 0.25
EPS = 1e-5
N_TM = 896
N_FM = 1408
NCORES = 4
NBA = 3

SAME_ENGINE_SYNC = True
DBG_O = 9
DBG_F = 9
DBG_CAST = 'mix'
DBG_NT = NT


class Prog:
    def __init__(self, nc, stack, ndma=32):
        self.nc = nc
        self.eng = {'pe': nc.tensor, 'act': nc.scalar, 'dve': nc.vector, 'pool': nc.gpsimd, 'sp': nc.sync}
        self.sem = {e: stack.enter_context(nc.semaphore('sem_' + e)) for e in ['pe', 'act', 'dve', 'pool']}
        self.cnt = {e: 0 for e in self.sem}
        self.nhw, self.nsw = ndma, 8
        self.dsem = [stack.enter_context(nc.semaphore('dsem%d' % i)) for i in range(self.nhw + self.nsw)]
        self.dcnt = [0] * (self.nhw + self.nsw)
        self.dnext = 0
        self.dnext_sw = 0
        self.waited = {e: {} for e in self.eng}
        self.res = {}
        self.nops = 0
        self.recent = {e: [] for e in self.eng}
        self.log = {e: [] for e in self.eng}

    def _semof(self, key):
        return self.sem[key] if isinstance(key, str) else self.dsem[key[1]]

    def _wait(self, e, tok):
        key, val = tok
        if self.waited[e].get(key, 0) >= val:
            return
        self.eng[e].wait_ge(self._semof(key), val)
        self.waited[e][key] = val
        self.log[e].append(('wait', key, val))

    def _deps(self, reads, writes):
        deps = []
        for r in reads:
            st = self.res.get(r)
            if st and st['w']:
                deps.append(st['w'])
        for w in writes:
            st = self.res.get(w)
            if st:
                if st['w']:
                    deps.append(st['w'])
                deps.extend(st['r'].items())
        return deps

    def _update(self, tok, reads, writes):
        for r in reads:
            st = self.res.setdefault(r, {'w': None, 'r': {}})
            if st['r'].get(tok[0], 0) < tok[1]:
                st['r'][tok[0]] = tok[1]
        for w in writes:
            self.res[w] = {'w': tok, 'r': {}}

    def op(self, e, fn, reads=(), writes=()):
        for tok in self._deps(reads, writes):
            if tok[0] == e and (e == 'pe' or not SAME_ENGINE_SYNC):
                continue
            self._wait(e, tok)
        inst = fn(self.eng[e])
        self.cnt[e] += 1
        inst.then_inc(self.sem[e], 1)
        self.log[e].append(('inc', e, 1))
        tok = (e, self.cnt[e])
        self._update(tok, reads, writes)
        self.nops += 1
        return tok

    def dma(self, q, fn, reads=(), writes=()):
        if q == 'pool':
            i = self.nhw + self.dnext_sw
            self.dnext_sw = (self.dnext_sw + 1) % self.nsw
        else:
            i = self.dnext
            self.dnext = (i + 1) % self.nhw
        deps = self._deps(reads, writes)
        if self.dcnt[i] > 0:
            deps.append((('d', i), self.dcnt[i]))
        rq = self.recent[q]
        if len(rq) >= 10:
            deps.append(rq.pop(0))
        for tok in deps:
            self._wait(q, tok)
        inst = fn(self.eng[q])
        self.dcnt[i] += 16
        inst.then_inc(self.dsem[i], 16)
        self.log[q].append(('inc', ('d', i), 16))
        tok = (('d', i), self.dcnt[i])
        self.recent[q].append(tok)
        self._update(tok, reads, writes)
        self.nops += 1
        return tok

    def barrier(self):
        toks = [(e, c) for e, c in self.cnt.items() if c > 0]
        toks += [(('d', i), c) for i, c in enumerate(self.dcnt) if c > 0]
        for e in self.eng:
            for tok in toks:
                self._wait(e, tok)

    def wait_all(self, e, keys):
        for r in keys:
            st = self.res.get(r)
            if st and st['w']:
                self._wait(e, st['w'])


def run_window(tile_gen, load, n, width):
    load(0)
    gens = []
    nxt = 0
    while nxt < n or gens:
        while len(gens) < width and nxt < n:
            if nxt + 1 < n:
                load(nxt + 1)
            gens.append(tile_gen(nxt))
            nxt += 1
        for g_ in list(gens):
            try:
                next(g_)
            except StopIteration:
                gens.remove(g_)


def dap(h, off, dims):
    return bass.AP(h, off, [list(d) for d in dims])


class K:
    pass


def build(debug=None, nlayers=DEPTH, stop_after=None, ext_in=None, phases=None):
    nc = bass.Bass("TRN2", target_bir_lowering=False)
    k = K()
    k.nc = nc
    k.debug = debug or []
    k.ext_in = ext_in or []
    phases = phases or ['a', 'r', 's', 't', 'o', 'f']

    def din(name, shape, dt=F32):
        return nc.dram_tensor(name, list(shape), dt, kind="ExternalInput")

    def dscr(name, shape, dt):
        kind = "ExternalOutput" if name in k.debug else ("ExternalInput" if name in k.ext_in else "Internal")
        return nc.dram_tensor(name, list(shape), dt, kind=kind)

    k.x = din("x", [T, D])
    k.ln_in_g = din("ln_in_g", [D]); k.ln_in_b = din("ln_in_b", [D])
    k.w_in = din("w_in", [DEPTH, D, 2304])
    k.out = nc.dram_tensor("out", [T, D], F32, kind="ExternalOutput")
    k.hA = dscr("hA", [T, D], F32)
    k.tm = dscr("tm", [T, N_TM], BF16)
    k.fm = dscr("fm", [N_FM, T], BF16)
    k.ycat = dscr("ycat", [T, 768], BF16)
    k.yssmT = dscr("yssmT", [256, T], BF16)
    k.ret_theta = din("ret_theta", [DEPTH, 8])
    k.attn_sink = din("attn_sink", [DEPTH, 8])
    k.w_out = din("w_out", [DEPTH, D, D])
    k.ln1_g = din("ln1_g", [DEPTH, D]); k.ln1_b = din("ln1_b", [DEPTH, D])
    k.ln2_g = din("ln2_g", [DEPTH, D]); k.ln2_b = din("ln2_b", [DEPTH, D])
    k.s_lre = din("s_lre", [DEPTH, 128, 16]); k.s_lim = din("s_lim", [DEPTH, 128, 16]); k.s_ls = din("s_ls", [DEPTH, 128, 16])
    k.s_bre = din("s_bre", [DEPTH, 128, 8, 16]); k.s_bim = din("s_bim", [DEPTH, 128, 8, 16])
    k.s_cre = din("s_cre", [DEPTH, 128, 16, 16]); k.s_cim = din("s_cim", [DEPTH, 128, 16, 16])
    k.s_d = din("s_d", [DEPTH, 128, 2]); k.s_bglu = din("s_bglu", [DEPTH, 128, 2])
    k.s_wglu = din("s_wglu", [DEPTH, 256, 256])
    k.yf = dscr("yf", [256, T], F32)
    k.router_w = din("router_w", [DEPTH, D, 16])
    k.w_gate = din("exp_w_gate", [DEPTH, 16, D, 2048])
    k.w_up = din("exp_w_up", [DEPTH, 16, D, 2048])
    k.w_down = din("exp_w_down", [DEPTH, 16, 2048, D])
    k.ffn = dscr("ffn", [T, D], F32)
    if DBG_F == 1:
        k.dbg_idx = nc.dram_tensor("dbg_idx", [128, 64], I32, kind="ExternalOutput")
        k.dbg_gate = nc.dram_tensor("dbg_gate", [128, 64], F32, kind="ExternalOutput")
        k.dbg_aff = nc.dram_tensor("dbg_aff", [128, 512], F32, kind="ExternalOutput")
        k.dbg_posm = nc.dram_tensor("dbg_posm", [128, 512], F32, kind="ExternalOutput")
    k.h1 = dscr("h1", [T, D], F32)
    k.h1b = dscr("h1b", [T, D], BF16)

    with contextlib.ExitStack() as st:
        P = Prog(nc, st)
        k.P = P
        k.st = st
        k.bank = [st.enter_context(nc.psum_tensor("bank%d" % i, [128, 512], F32)) for i in range(7)]
        k.bankT = st.enter_context(nc.psum_tensor("bankT", [128, 1024], BF16))
        setup_consts(k)
        for l in range(nlayers):
            if 'a' in phases:
                phase_a(k, l)
            if 'r' in phases:
                phase_r(k, l)
            if 't' in phases:
                phase_t(k, l)
            if 's' in phases:
                phase_s(k, l)
            if 'o' in phases:
                phase_o(k, l)
            if 'f' in phases:
                phase_f(k, l, last=(l == nlayers - 1))
        finish(k)
    nc._prog = P
    return nc


def sb(k, name, shape, dt, stack=None):
    k.nsb = getattr(k, 'nsb', 0) + 1
    return (stack or k.st).enter_context(k.nc.sbuf_tensor("%s_u%d" % (name, k.nsb), list(shape), dt))


def setup_consts(k):
    nc, P = k.nc, k.P
    k.identb = sb(k, "identb", [128, 128], BF16)
    k.identf = sb(k, "identf", [128, 128], F32)
    k.iota_i = sb(k, "iota_i", [128, 128], I32)
    k.dist = sb(k, "dist", [128, 128], F32)
    P.op('pool', lambda e: e.iota(k.iota_i[:], [[1, 128]], base=0, channel_multiplier=-1),
         writes=['iota_i'])
    P.op('dve', lambda e: e.tensor_copy(out=k.dist[:], in_=k.iota_i[:]), reads=['iota_i'], writes=['dist'])
    P.op('dve', lambda e: e.tensor_scalar(out=k.identf[:], in0=k.dist[:], scalar1=0.0, scalar2=None,
                                           op0=ALU.is_equal), reads=['dist'], writes=['identf'])
    P.op('dve', lambda e: e.tensor_copy(out=k.identb[:], in_=k.identf[:]), reads=['identf'], writes=['identb'])
    k.one_t = sb(k, "one_t", [128, 1], F32)
    P.op('dve', lambda e: e.memset(k.one_t[:], 1.0), writes=['one_t'])
    k.lnk_t = sb(k, "lnk_t", [128, 1], F32)
    P.op('dve', lambda e: e.memset(k.lnk_t[:], math.log(0.125)), writes=['lnk_t'])
    k.irow_i = sb(k, "irow_i", [128, 128], I32)
    k.irow = sb(k, "irow", [128, 128], F32)
    P.op('pool', lambda e: e.iota(k.irow_i[:], [[1, 128]], base=0, channel_multiplier=0), writes=['irow_i'])
    P.op('dve', lambda e: e.tensor_copy(out=k.irow[:], in_=k.irow_i[:]), reads=['irow_i'], writes=['irow'])
    k.relp = sb(k, "relp", [128, 128], F32)
    k.reln = sb(k, "reln", [128, 128], F32)
    P.op('dve', lambda e: e.tensor_scalar(out=k.relp[:], in0=k.dist[:], scalar1=0.0, scalar2=None, op0=ALU.max),
         reads=['dist'], writes=['relp'])
    P.op('dve', lambda e: e.tensor_scalar(out=k.reln[:], in0=k.dist[:], scalar1=-1.0, scalar2=0.0, op0=ALU.mult, op1=ALU.max),
         reads=['dist'], writes=['reln'])
    k.eps_t = sb(k, "eps_t", [128, 1], F32)
    P.op('dve', lambda e: e.memset(k.eps_t[:], EPS), writes=['eps_t'])


def layer_norm_tile(k, x_ap, xkey, out_ap, outkey, g_ap, gkey, b_ap, bkey, tag, scratch):
    P = k.P
    stats, mv, sd, rstd, nb, xn = (scratch[n] for n in ('stats', 'mv', 'sd', 'rstd', 'nb', 'xn'))
    s = tag
    for hh in range(2):
        P.op('dve', lambda e, hh=hh: e.bn_stats(out=stats[:, hh, :], in_=x_ap[:, hh * 512:(hh + 1) * 512]),
             reads=[xkey], writes=[s + 'stats%d' % hh])
        yield
    P.op('dve', lambda e: e.bn_aggr(out=mv[:], in_=stats[:].rearrange("p a b -> p (a b)")),
         reads=[s + 'stats0', s + 'stats1'], writes=[s + 'mv'])
    yield
    P.op('act', lambda e: e.activation(out=sd[:], in_=mv[:, 1:2], func=AF.Sqrt, bias=k.eps_t[:], scale=1.0),
         reads=[s + 'mv'], writes=[s + 'sd'])
    yield
    P.op('dve', lambda e: e.reciprocal(out=rstd[:], in_=sd[:]), reads=[s + 'sd'], writes=[s + 'rstd'])
    yield
    P.op('dve', lambda e: e.scalar_tensor_tensor(out=nb[:], in0=mv[:, 0:1], scalar=-1.0, in1=rstd[:],
                                                  op0=ALU.mult, op1=ALU.mult),
         reads=[s + 'mv', s + 'rstd'], writes=[s + 'nb'])
    yield
    P.op('act', lambda e: e.activation(out=xn[:], in_=x_ap, func=AF.Identity, bias=nb[:], scale=rstd[:]),
         reads=[xkey, s + 'nb', s + 'rstd'], writes=[s + 'xn'])
    yield
    P.op('pool', lambda e: e.tensor_tensor(out=xn[:], in0=xn[:], in1=g_ap, op=ALU.mult),
         reads=[s + 'xn', gkey], writes=[s + 'xn'])
    yield
    P.op('dve', lambda e: e.tensor_tensor(out=out_ap, in0=xn[:], in1=b_ap, op=ALU.add),
         reads=[s + 'xn', bkey], writes=[outkey])
    yield


def phase_a(k, l):
    nc, P = k.nc, k.P
    P.barrier()
    with contextlib.ExitStack() as ps:
        if l == 0:
            k.g_in = sb(k, "g_in", [128, D], F32, ps)
            k.b_in = sb(k, "b_in", [128, D], F32, ps)
            P.dma('sp', lambda e: e.dma_start(out=k.g_in[:], in_=dap(k.ln_in_g, 0, [[0, 128], [1, D]])), writes=['g_in'])
            P.dma('sp', lambda e: e.dma_start(out=k.b_in[:], in_=dap(k.ln_in_b, 0, [[0, 128], [1, D]])), writes=['b_in'])
        Wb = sb(k, "a_Wb", [128, 8, 2304], BF16, ps)
        Wst = [sb(k, "a_Wst%d" % i, [128, 8, 256], F32, ps) for i in range(2)]
        xt = [sb(k, "a_x%d" % i, [128, D], F32, ps) for i in range(NBA)]
        ht = [sb(k, "a_h%d" % i, [128, D], F32, ps) for i in range(NBA)]
        hb = [sb(k, "a_hb%d" % i, [128, D], BF16, ps) for i in range(NBA)]
        hT = [sb(k, "a_hT%d" % i, [128, 8, 512], BF16, ps) for i in range(2)]
        tmst = [sb(k, "a_tmst%d" % i, [128, N_TM], BF16, ps) for i in range(NBA)]
        fmst = [sb(k, "a_fmst%d" % i, [128, 512], BF16, ps) for i in range(3)]
        scr = [dict(stats=sb(k, "a_stats%d" % i, [128, 2, 6], F32, ps), mv=sb(k, "a_mv%d" % i, [128, 2], F32, ps),
                    sd=sb(k, "a_sd%d" % i, [128, 1], F32, ps), rstd=sb(k, "a_rstd%d" % i, [128, 1], F32, ps),
                    nb=sb(k, "a_nb%d" % i, [128, 1], F32, ps), xn=sb(k, "a_xn%d" % i, [128, D], F32, ps))
               for i in range(NBA)]
        for c in range(9):
            w = Wst[c % 2]
            wk = 'a_Wst%d' % (c % 2)
            P.dma('sp', lambda e, w=w, c=c: e.dma_start(
                out=w[:], in_=dap(k.w_in, l * D * 2304 + c * 256, [[2304, 128], [128 * 2304, 8], [1, 256]])),
                writes=[wk])
            eng = 'act' if c % 2 == 0 else 'dve'
            if eng == 'act':
                P.op('act', lambda e, w=w, c=c: e.copy(out=Wb[:, :, c * 256:(c + 1) * 256], in_=w[:]),
                     reads=[wk], writes=['a_Wb%d' % c])
            else:
                P.op('dve', lambda e, w=w, c=c: e.tensor_copy(out=Wb[:, :, c * 256:(c + 1) * 256], in_=w[:]),
                     reads=[wk], writes=['a_Wb%d' % c])
        Wkeys = ['a_Wb%d' % c for c in range(9)]
        nb = 0

        def a_load(ti):
            b = ti % NBA
            if l == 0:
                P.dma('sp', lambda e: e.dma_start(out=xt[b][:], in_=dap(k.x, ti * 128 * D, [[D, 128], [1, D]])), writes=['a_x%d' % b])
            else:
                P.dma('sp', lambda e: e.dma_start(out=ht[b][:], in_=dap(k.hA, ti * 128 * D, [[D, 128], [1, D]])),
                      reads=['hA%d' % ti], writes=['a_h%d' % b])
        for stile in range(T // 512):
            hTs = hT[stile % 2]
            hTk = 'a_hT%d' % (stile % 2)
            for s in range(4):
                ti = stile * 4 + s
                b = ti % NBA
                tok0 = ti * 128
                if ti == 0:
                    a_load(0)
                if ti + 1 < NT:
                    a_load(ti + 1)
                if l == 0:
                    for _ in layer_norm_tile(k, xt[b][:], 'a_x%d' % b, ht[b][:], 'a_h%d' % b, k.g_in[:], 'g_in', k.b_in[:], 'b_in',
                                             'a%d_' % b, scr[b]):
                        pass
                    P.dma('sp', lambda e, b=b, tok0=tok0: e.dma_start(out=dap(k.hA, tok0 * D, [[D, 128], [1, D]]), in_=ht[b][:]),
                          reads=['a_h%d' % b], writes=['hA%d' % ti])
                P.op('act', lambda e, b=b: e.copy(out=hb[b][:], in_=ht[b][:]), reads=['a_h%d' % b], writes=['a_hb%d' % b])

                def tr(e, b=b):
                    r = None
                    for kk in range(8):
                        r = e.transpose(out=k.bankT[:, kk * 128:(kk + 1) * 128], in_=hb[b][:, kk * 128:(kk + 1) * 128],
                                        identity=k.identb[:])
                    return r
                P.op('pe', tr, reads=['a_hb%d' % b, 'identb'], writes=['bankT'])
                P.op('dve', lambda e, s=s, hTs=hTs: e.tensor_copy(
                    out=hTs[:, :, s * 128:(s + 1) * 128], in_=k.bankT[:].rearrange("p (a b) -> p a b", a=8)),
                    reads=['bankT'], writes=[hTk + '_%d' % s])
                for gi, (c0, c1) in enumerate([(0, 512), (512, N_TM)]):
                    bk = nb % 6
                    nb += 1

                    def mm(e, bk=bk, c0=c0, c1=c1, s=s, hTs=hTs):
                        r = None
                        for kk in range(8):
                            r = e.matmul(k.bank[bk][:, 0:c1 - c0], lhsT=hTs[:, kk, s * 128:(s + 1) * 128],
                                         rhs=Wb[:, kk, c0:c1], start=(kk == 0), stop=(kk == 7))
                        return r
                    P.op('pe', mm, reads=[hTk + '_%d' % s] + Wkeys, writes=['bank%d' % bk])
                    P.op('act', lambda e, bk=bk, c0=c0, c1=c1, b=b: e.copy(out=tmst[b][:, c0:c1], in_=k.bank[bk][:, 0:c1 - c0]),
                         reads=['bank%d' % bk], writes=['a_tmst%d_%d' % (b, gi)])
                P.dma('sp', lambda e, b=b, tok0=tok0: e.dma_start(out=dap(k.tm, tok0 * N_TM, [[N_TM, 128], [1, N_TM]]), in_=tmst[b][:]),
                      reads=['a_tmst%d_0' % b, 'a_tmst%d_1' % b], writes=['tm%d' % ti])
                P._update(P.res['tm%d' % ti]['w'], ['a_tmst%d_0' % b, 'a_tmst%d_1' % b], [])
            for fg in range(11):
                bk = nb % 6
                nb += 1
                fb = fg % 3

                def mmf(e, bk=bk, fg=fg, hTs=hTs):
                    r = None
                    for kk in range(8):
                        r = e.matmul(k.bank[bk][:, :], lhsT=Wb[:, kk, N_TM + fg * 128:N_TM + (fg + 1) * 128],
                                     rhs=hTs[:, kk, :], start=(kk == 0), stop=(kk == 7))
                    return r
                P.op('pe', mmf, reads=[hTk + '_%d' % s for s in range(4)] + Wkeys, writes=['bank%d' % bk])
                eng = 'dve' if fg % 2 == 0 else 'act'
                if eng == 'dve':
                    P.op('dve', lambda e, bk=bk, fb=fb: e.tensor_copy(out=fmst[fb][:], in_=k.bank[bk][:, :]),
                         reads=['bank%d' % bk], writes=['a_fmst%d' % fb])
                else:
                    P.op('act', lambda e, bk=bk, fb=fb: e.copy(out=fmst[fb][:], in_=k.bank[bk][:, :]),
                         reads=['bank%d' % bk], writes=['a_fmst%d' % fb])
                P.dma('sp', lambda e, fb=fb, fg=fg, stile=stile: e.dma_start(
                    out=dap(k.fm, fg * 128 * T + stile * 512, [[T, 128], [1, 512]]), in_=fmst[fb][:]),
                    reads=['a_fmst%d' % fb], writes=['fm%d_%d' % (fg, stile)])
                P._update(P.res['fm%d_%d' % (fg, stile)]['w'], ['a_fmst%d' % fb], [])


def bc(tile, col, n, pstride, parts=128):
    return dap(tile, col, [[pstride, parts], [0, n]])


def phase_r(k, l):
    nc, P = k.nc, k.P
    P.barrier()
    with contextlib.ExitStack() as ps:
        ktm = sb(k, "r_ktm", [128, NT, 256], BF16, ps)
        vtm = sb(k, "r_vtm", [128, NT, 256], BF16, ps)
        gtm = sb(k, "r_gtm", [128, NT, 256], BF16, ps)
        Sf = sb(k, "r_Sf", [64, NT, 256], BF16, ps)
        Sb_ = sb(k, "r_Sb", [64, NT, 256], BF16, ps)
        stt = [sb(k, "r_st%d" % i, [64, 256], F32, ps) for i in range(2)]
        qT = [sb(k, "r_qT%d" % i, [64, 4, 512], BF16, ps) for i in range(2)]
        kT = [sb(k, "r_kT%d" % i, [64, 4, 512], BF16, ps) for i in range(2)]
        th = sb(k, "r_th", [128, 8], F32, ps)
        lg = sb(k, "r_lg", [128, 8], F32, ps)
        dec = sb(k, "r_dec", [128, 8], F32, ps)
        wcol = sb(k, "r_wcol", [128, 8], F32, ps)
        pidx = sb(k, "r_pidx", [128, 2], F32, ps)
        tmpa = sb(k, "r_tmpa", [128, 128], F32, ps)
        tmpb = sb(k, "r_tmpb", [128, 128], F32, ps)
        dmT = sb(k, "r_dmT", [128, 4, 128], F32, ps)
        WF = sb(k, "r_WF", [128, 256], F32, ps)
        WB = sb(k, "r_WB", [128, 256], F32, ps)
        qsf = sb(k, "r_qsf", [128, 4, 128], F32, ps)
        qsb = sb(k, "r_qsb", [128, 4, 128], F32, ps)
        irf = sb(k, "r_irf", [128, 128], F32, ps)
        irb = sb(k, "r_irb", [128, 128], F32, ps)
        kw = [sb(k, "r_kw%d" % i, [128, 256], BF16, ps) for i in range(2)]
        sTm = [sb(k, "r_sTm%d" % i, [128, 512], BF16, ps) for i in range(2)]
        qf = [sb(k, "r_qf%d" % i, [64, 4, 128], BF16, ps) for i in range(2)]
        qb = [sb(k, "r_qb%d" % i, [64, 4, 128], BF16, ps) for i in range(2)]
        o = [sb(k, "r_o%d" % i, [128, 256], F32, ps) for i in range(2)]
        sq = [sb(k, "r_sq%d" % i, [128, 256], F32, ps) for i in range(2)]
        sg = [sb(k, "r_sg%d" % i, [128, 256], F32, ps) for i in range(2)]
        on = [sb(k, "r_on%d" % i, [128, 256], F32, ps) for i in range(2)]
        yst = [sb(k, "r_yst%d" % i, [128, 256], BF16, ps) for i in range(2)]
        sm = [dict((n, sb(k, "r_%s%d" % (n, i), [128, 4], F32, ps)) for n in ('s1', 's2', 'mean', 'msq', 'var', 'sd', 'rstd'))
              for i in range(2)]

        for name, tl, c0 in (('r_ktm', ktm, 0), ('r_vtm', vtm, 256), ('r_gtm', gtm, 512)):
            for half in range(2):
                P.dma('sp', lambda e, tl=tl, c0=c0, half=half: e.dma_start(
                    out=tl[:, half * 16:(half + 1) * 16, :],
                    in_=dap(k.tm, half * 16 * 128 * N_TM + c0, [[N_TM, 128], [128 * N_TM, 16], [1, 256]])),
                    reads=['tm%d' % ti for ti in range(half * 16, half * 16 + 16)], writes=['%s_%d' % (name, half)])
        P.dma('sp', lambda e: e.dma_start(out=th[:], in_=dap(k.ret_theta, l * 8, [[0, 128], [1, 8]])), writes=['r_th'])
        P.op('act', lambda e: e.activation(out=lg[:], in_=th[:], func=AF.Exp, scale=-1.0), reads=['r_th'], writes=['r_lg'])
        P.op('act', lambda e: e.activation(out=lg[:], in_=lg[:], func=AF.Ln, bias=k.one_t[:], scale=1.0),
             reads=['r_lg', 'one_t'], writes=['r_lg'])
        P.op('dve', lambda e: e.tensor_scalar(out=lg[:], in0=lg[:], scalar1=-1.0, scalar2=None, op0=ALU.mult),
             reads=['r_lg'], writes=['r_lg'])
        P.op('act', lambda e: e.activation(out=dec[:], in_=lg[:], func=AF.Exp, scale=128.0), reads=['r_lg'], writes=['r_dec'])
        P.op('dve', lambda e: e.tensor_scalar(out=pidx[:, 0:1], in0=k.dist[:, 0:1], scalar1=127.0, scalar2=None, op0=ALU.add),
             reads=['dist'], writes=['r_pidx0'])
        P.op('dve', lambda e: e.tensor_scalar(out=pidx[:, 1:2], in0=k.dist[:, 0:1], scalar1=-1.0, scalar2=None, op0=ALU.mult),
             reads=['dist'], writes=['r_pidx1'])
        P.op('dve', lambda e: e.tensor_scalar(out=irf[:], in0=k.irow[:], scalar1=1.0, scalar2=None, op0=ALU.add),
             reads=['irow'], writes=['r_irf'])
        P.op('dve', lambda e: e.tensor_scalar(out=irb[:], in0=k.irow[:], scalar1=-1.0, scalar2=128.0, op0=ALU.mult, op1=ALU.add),
             reads=['irow'], writes=['r_irb'])
        for h in range(4):
            P.op('act', lambda e, h=h: e.activation(out=wcol[:, h:h + 1], in_=pidx[:, 0:1], func=AF.Exp, bias=k.lnk_t[:], scale=lg[:, h:h + 1]),
                 reads=['r_pidx0', 'r_lg', 'lnk_t'], writes=['r_wcol%d' % h])
            P.op('act', lambda e, h=h: e.activation(out=wcol[:, 4 + h:5 + h], in_=pidx[:, 1:2], func=AF.Exp, bias=k.lnk_t[:], scale=lg[:, 4 + h:5 + h]),
                 reads=['r_pidx1', 'r_lg', 'lnk_t'], writes=['r_wcol%d' % (4 + h)])
            P.op('dve', lambda e, h=h: e.tensor_copy(out=WF[:, h * 64:(h + 1) * 64], in_=bc(wcol, h, 64, 8)),
                 reads=['r_wcol%d' % h], writes=['r_WF%d' % h])
            P.op('dve', lambda e, h=h: e.tensor_copy(out=WB[:, h * 64:(h + 1) * 64], in_=bc(wcol, 4 + h, 64, 8)),
                 reads=['r_wcol%d' % (4 + h)], writes=['r_WB%d' % h])
            P.op('dve', lambda e, h=h: e.tensor_scalar(out=tmpa[:], in0=k.relp[:], scalar1=lg[:, h:h + 1], scalar2=None, op0=ALU.mult),
                 reads=['relp', 'r_lg'], writes=['r_tmpa'])
            P.op('dve', lambda e, h=h: e.scalar_tensor_tensor(out=tmpb[:], in0=k.reln[:], scalar=lg[:, 4 + h:5 + h], in1=tmpa[:],
                                                               op0=ALU.mult, op1=ALU.add),
                 reads=['reln', 'r_lg', 'r_tmpa'], writes=['r_tmpb'])
            P.op('act', lambda e, h=h: e.activation(out=dmT[:, h, :], in_=tmpb[:], func=AF.Exp, bias=k.lnk_t[:], scale=1.0),
                 reads=['r_tmpb', 'lnk_t'], writes=['r_dmT%d' % h])
            P.op('act', lambda e, h=h: e.activation(out=qsf[:, h, :], in_=irf[:], func=AF.Exp, scale=lg[:, h:h + 1]),
                 reads=['r_irf', 'r_lg'], writes=['r_qsf%d' % h])
            P.op('act', lambda e, h=h: e.activation(out=qsb[:, h, :], in_=irb[:], func=AF.Exp, scale=lg[:, 4 + h:5 + h]),
                 reads=['r_irb', 'r_lg'], writes=['r_qsb%d' % h])
        WFk = ['r_WF%d' % h for h in range(4)]
        WBk = ['r_WB%d' % h for h in range(4)]
        def r_pass1(d):
            W = WF if d == 0 else WB
            Wk = WFk if d == 0 else WBk
            Sall = Sf if d == 0 else Sb_
            stn = 'r_st%d' % d
            P.op('dve', lambda e, d=d: e.memset(stt[d][:], 0.0), writes=[stn])
            yield
            order = range(NT) if d == 0 else range(NT - 1, -1, -1)
            kb = d
            bk = d
            for n, c in enumerate(order):
                half = c // 16
                P.op('act', lambda e, c=c, Sall=Sall, d=d: e.copy(out=Sall[:, c, :], in_=stt[d][:]),
                     reads=[stn], writes=['r_S%d_%d' % (d, c)])
                yield
                P.op('pool', lambda e, c=c, kb=kb, W=W: e.tensor_tensor(out=kw[kb][:], in0=ktm[:, c, :], in1=W[:], op=ALU.mult),
                     reads=['r_ktm_%d' % half] + Wk, writes=['r_kw%d' % kb])
                yield

                def mmkv(e, c=c, kb=kb, bk=bk):
                    r = None
                    for h in range(4):
                        r = e.matmul(k.bank[bk][0:64, h * 64:(h + 1) * 64], lhsT=kw[kb][:, h * 64:(h + 1) * 64],
                                     rhs=vtm[:, c, h * 64:(h + 1) * 64], start=True, stop=True)
                    return r
                P.op('pe', mmkv, reads=['r_kw%d' % kb, 'r_vtm_%d' % half], writes=['bank%d' % bk])
                yield
                for h in range(4):
                    P.op('dve', lambda e, h=h, d=d, bk=bk: e.scalar_tensor_tensor(
                        out=stt[d][:, h * 64:(h + 1) * 64], in0=stt[d][:, h * 64:(h + 1) * 64], scalar=dec[0:64, 4 * d + h:4 * d + h + 1],
                        in1=k.bank[bk][0:64, h * 64:(h + 1) * 64], op0=ALU.mult, op1=ALU.add),
                        reads=[stn, 'r_dec', 'bank%d' % bk], writes=[stn])
                    yield
        gens = [r_pass1(0), r_pass1(1)]
        while gens:
            for g_ in list(gens):
                try:
                    next(g_)
                except StopIteration:
                    gens.remove(g_)
        for c in range(NT):
            blk, cc = c // 4, c % 4
            half = c // 16
            qb_i = blk % 2
            b = c % 2
            if cc == 0:
                P.dma('sp', lambda e, blk=blk, qb_i=qb_i: e.dma_start(
                    out=qT[qb_i][:], in_=dap(k.fm, blk * 512, [[T, 64], [64 * T, 4], [1, 512]])),
                    reads=['fm%d_%d' % (fg, blk) for fg in (0, 1)], writes=['r_qT%d' % qb_i])
                P.dma('sp', lambda e, blk=blk, qb_i=qb_i: e.dma_start(
                    out=kT[qb_i][:], in_=dap(k.fm, 256 * T + blk * 512, [[T, 64], [64 * T, 4], [1, 512]])),
                    reads=['fm%d_%d' % (fg, blk) for fg in (2, 3)], writes=['r_kT%d' % qb_i])
            bs = 2 + (c % 2)

            def mms(e, qb_i=qb_i, cc=cc, bs=bs):
                r = None
                for h in range(4):
                    r = e.matmul(k.bank[bs][:, h * 128:(h + 1) * 128], lhsT=kT[qb_i][:, h, cc * 128:(cc + 1) * 128],
                                 rhs=qT[qb_i][:, h, cc * 128:(cc + 1) * 128], start=True, stop=True)
                return r
            P.op('pe', mms, reads=['r_qT%d' % qb_i, 'r_kT%d' % qb_i], writes=['bank%d' % bs])
            P.op('dve', lambda e, b=b, bs=bs: e.tensor_tensor(out=sTm[b][:], in0=k.bank[bs][:, :],
                                                            in1=dmT[:].rearrange("p a b -> p (a b)"), op=ALU.mult),
                 reads=['bank%d' % bs] + ['r_dmT%d' % h for h in range(4)], writes=['r_sTm%d' % b])
            P.op('pool', lambda e, b=b, qb_i=qb_i, cc=cc: e.tensor_tensor(out=qf[b][:], in0=qT[qb_i][:, :, cc * 128:(cc + 1) * 128],
                                                                         in1=qsf[0:64, :, :], op=ALU.mult),
                 reads=['r_qT%d' % qb_i] + ['r_qsf%d' % h for h in range(4)], writes=['r_qf%d' % b])
            P.op('pool', lambda e, b=b, qb_i=qb_i, cc=cc: e.tensor_tensor(out=qb[b][:], in0=qT[qb_i][:, :, cc * 128:(cc + 1) * 128],
                                                                         in1=qsb[0:64, :, :], op=ALU.mult),
                 reads=['r_qT%d' % qb_i] + ['r_qsb%d' % h for h in range(4)], writes=['r_qb%d' % b])
            bo = 4 + (c % 2)

            def mmo(e, b=b, c=c, bo=bo):
                r = None
                for h in range(4):
                    oap = k.bank[bo][:, h * 64:(h + 1) * 64]
                    e.matmul(oap, lhsT=sTm[b][:, h * 128:(h + 1) * 128], rhs=vtm[:, c, h * 64:(h + 1) * 64], start=True, stop=False)
                    e.matmul(oap, lhsT=qf[b][:, h, :], rhs=Sf[:, c, h * 64:(h + 1) * 64], start=False, stop=False)
                    r = e.matmul(oap, lhsT=qb[b][:, h, :], rhs=Sb_[:, c, h * 64:(h + 1) * 64], start=False, stop=True)
                return r
            P.op('pe', mmo, reads=['r_sTm%d' % b, 'r_qf%d' % b, 'r_qb%d' % b, 'r_vtm_%d' % half, 'r_S0_%d' % c, 'r_S1_%d' % c],
                 writes=['bank%d' % bo])
            m = sm[b]
            mk = lambda n: 'r_%s%d' % (n, b)
            P.op('act', lambda e, b=b, bo=bo: e.copy(out=o[b][:], in_=k.bank[bo][:, 0:256]), reads=['bank%d' % bo], writes=[mk('o')])
            P.op('act', lambda e, b=b: e.activation(out=sq[b][:], in_=o[b][:], func=AF.Square), reads=[mk('o')], writes=[mk('sq')])
            P.op('act', lambda e, b=b, c=c: e.activation(out=sg[b][:], in_=gtm[:, c, :], func=AF.Silu),
                 reads=['r_gtm_%d' % half], writes=[mk('sg')])
            P.op('dve', lambda e, b=b, m=m: e.tensor_reduce(out=m['s1'][:], in_=o[b][:].rearrange("p (a b) -> p a b", a=4), axis=AX.X, op=ALU.add),
                 reads=[mk('o')], writes=[mk('s1')])
            P.op('dve', lambda e, b=b, m=m: e.tensor_reduce(out=m['s2'][:], in_=sq[b][:].rearrange("p (a b) -> p a b", a=4), axis=AX.X, op=ALU.add),
                 reads=[mk('sq')], writes=[mk('s2')])
            P.op('dve', lambda e, m=m: e.tensor_scalar(out=m['mean'][:], in0=m['s1'][:], scalar1=1.0 / 64, scalar2=None, op0=ALU.mult),
                 reads=[mk('s1')], writes=[mk('mean')])
            P.op('dve', lambda e, m=m: e.tensor_tensor(out=m['msq'][:], in0=m['mean'][:], in1=m['mean'][:], op=ALU.mult),
                 reads=[mk('mean')], writes=[mk('msq')])
            P.op('dve', lambda e, m=m: e.scalar_tensor_tensor(out=m['var'][:], in0=m['s2'][:], scalar=1.0 / 64, in1=m['msq'][:],
                                                               op0=ALU.mult, op1=ALU.subtract),
                 reads=[mk('s2'), mk('msq')], writes=[mk('var')])
            P.op('act', lambda e, m=m: e.activation(out=m['sd'][:], in_=m['var'][:], func=AF.Sqrt, bias=k.eps_t[:], scale=1.0),
                 reads=[mk('var'), 'eps_t'], writes=[mk('sd')])
            P.op('dve', lambda e, m=m: e.reciprocal(out=m['rstd'][:], in_=m['sd'][:]), reads=[mk('sd')], writes=[mk('rstd')])
            for h in range(4):
                P.op('dve', lambda e, h=h, b=b, m=m: e.tensor_scalar(
                    out=on[b][:, h * 64:(h + 1) * 64], in0=o[b][:, h * 64:(h + 1) * 64], scalar1=m['mean'][:, h:h + 1],
                    scalar2=m['rstd'][:, h:h + 1], op0=ALU.subtract, op1=ALU.mult),
                    reads=[mk('o'), mk('mean'), mk('rstd')], writes=[mk('on') + '_%d' % h])
            P.op('pool', lambda e, b=b: e.tensor_tensor(out=yst[b][:], in0=on[b][:], in1=sg[b][:], op=ALU.mult),
                 reads=[mk('on') + '_%d' % h for h in range(4)] + [mk('sg')], writes=[mk('yst')])
            P.dma('sp', lambda e, b=b, c=c: e.dma_start(out=dap(k.ycat, c * 128 * 768, [[768, 128], [1, 256]]), in_=yst[b][:]),
                  reads=[mk('yst')], writes=['ycat_r%d' % c])
            P._update(P.res['ycat_r%d' % c]['w'], [mk('yst')], [])


def phase_t(k, l):
    nc, P = k.nc, k.P
    P.barrier()
    EBk = [['EB%d_%d' % (kb, h) for h in range(8)] for kb in range(3)]
    with contextlib.ExitStack() as ps:
        k.EB = [sb(k, "EB%d" % kb, [128, 8, 128], F32, ps) for kb in range(3)]
        k.t_abs = sb(k, "t_abs", [128, 128], F32, ps)
        k.t_msk = sb(k, "t_msk", [128, 128], F32, ps)
        for kb in range(3):
            if kb == 0:
                P.op('dve', lambda e: e.tensor_scalar(out=k.t_abs[:], in0=k.dist[:], scalar1=128.0, scalar2=None, op0=ALU.add),
                     reads=['dist'], writes=['t_abs'])
                P.op('dve', lambda e: e.tensor_scalar(out=k.t_msk[:], in0=k.dist[:], scalar1=0.0, scalar2=None, op0=ALU.is_le),
                     reads=['dist'], writes=['t_msk'])
            elif kb == 1:
                P.op('dve', lambda e: e.tensor_tensor(out=k.t_abs[:], in0=k.relp[:], in1=k.reln[:], op=ALU.add),
                     reads=['relp', 'reln'], writes=['t_abs'])
                P.op('dve', lambda e: e.memset(k.t_msk[:], 1.0), writes=['t_msk'])
            else:
                P.op('dve', lambda e: e.tensor_scalar(out=k.t_abs[:], in0=k.dist[:], scalar1=-1.0, scalar2=128.0, op0=ALU.mult, op1=ALU.add),
                     reads=['dist'], writes=['t_abs'])
                P.op('dve', lambda e: e.tensor_scalar(out=k.t_msk[:], in0=k.dist[:], scalar1=0.0, scalar2=None, op0=ALU.is_ge),
                     reads=['dist'], writes=['t_msk'])
            for h in range(8):
                P.op('act', lambda e, kb=kb, h=h: e.activation(out=k.EB[kb][:, h, :], in_=k.t_abs[:], func=AF.Exp, scale=-(2.0 ** -(h + 1))),
                     reads=['t_abs'], writes=['EB%d_%d' % (kb, h)])
                P.op('dve', lambda e, kb=kb, h=h: e.tensor_tensor(out=k.EB[kb][:, h, :], in0=k.EB[kb][:, h, :], in1=k.t_msk[:], op=ALU.mult),
                     reads=['EB%d_%d' % (kb, h), 't_msk'], writes=['EB%d_%d' % (kb, h)])
        kT = sb(k, "t_kT", [64, 2, T], BF16, ps)
        vA = sb(k, "t_vA", [128, NT, 2, 65], BF16, ps)
        qT = [sb(k, "t_qT%d" % i, [64, 8, 512], BF16, ps) for i in range(2)]
        snk = sb(k, "t_snk", [128, 8], F32, ps)
        ex = [sb(k, "t_ex%d" % i, [128, 512], F32, ps) for i in range(3)]
        pT = [sb(k, "t_pT%d" % i, [128, 512], BF16, ps) for i in range(6)]
        den = [sb(k, "t_den%d" % i, [128, 4], F32, ps) for i in range(2)]
        rec = [sb(k, "t_rec%d" % i, [128, 4], F32, ps) for i in range(2)]
        yst = [sb(k, "t_yst%d" % i, [128, 512], BF16, ps) for i in range(2)]
        for half in range(2):
            P.dma('sp', lambda e, half=half: e.dma_start(
                out=kT[:, :, half * 2048:(half + 1) * 2048], in_=dap(k.fm, 1280 * T + half * 2048, [[T, 64], [64 * T, 2], [1, 2048]])),
                reads=['fm10_%d' % b for b in range(half * 4, half * 4 + 4)], writes=['t_kT%d' % half])
            for kvh in range(2):
                P.dma('sp', lambda e, half=half, kvh=kvh: e.dma_start(
                    out=vA[:, half * 16:(half + 1) * 16, kvh, 0:64],
                    in_=dap(k.tm, half * 16 * 128 * N_TM + 768 + kvh * 64, [[N_TM, 128], [128 * N_TM, 16], [1, 64]])),
                    reads=['tm%d' % ti for ti in range(half * 16, half * 16 + 16)], writes=['t_vA%d_%d' % (half, kvh)])
        P.op('pool', lambda e: e.memset(vA[:, :, :, 64:65], 1.0), writes=['t_vA1s'])
        P.dma('sp', lambda e: e.dma_start(out=snk[:], in_=dap(k.attn_sink, l * 8, [[0, 128], [1, 8]])), writes=['t_snk'])
        P.op('act', lambda e: e.activation(out=snk[:], in_=snk[:], func=AF.Exp), reads=['t_snk'], writes=['t_snk'])
        npT = 0
        nex = 0
        for c in range(NT):
            blk, cc = c // 4, c % 4
            qi = blk % 2
            if cc == 0:
                P.dma('sp', lambda e, blk=blk, qi=qi: e.dma_start(
                    out=qT[qi][:], in_=dap(k.fm, 768 * T + blk * 512, [[T, 64], [64 * T, 8], [1, 512]])),
                    reads=['fm%d_%d' % (fg, blk) for fg in (6, 7, 8, 9)], writes=['t_qT%d' % qi])
            yb = c % 2
            for kvh in range(2):
                kbs = [kb for kb in range(3) if 0 <= c - 1 + kb < NT]
                pts = []
                for kb in kbs:
                    kblk = c - 1 + kb
                    bs = (nex % 3)
                    xi = nex % 3
                    nex += 1
                    pi = npT % 6
                    npT += 1
                    pts.append(pi)
                    P.op('pe', lambda e, bs=bs, kvh=kvh, kblk=kblk, qi=qi, cc=cc: e.matmul(
                        k.bank[bs][:, :], lhsT=kT[:, kvh, kblk * 128:(kblk + 1) * 128],
                        rhs=qT[qi][:, kvh * 4:(kvh + 1) * 4, cc * 128:(cc + 1) * 128], start=True, stop=True),
                        reads=['t_kT%d' % (kblk // 16), 't_qT%d' % qi], writes=['bank%d' % bs])
                    P.op('act', lambda e, bs=bs, xi=xi: e.activation(out=ex[xi][:], in_=k.bank[bs][:, :], func=AF.Exp, scale=0.125),
                         reads=['bank%d' % bs], writes=['t_ex%d' % xi])
                    P.op('dve', lambda e, xi=xi, pi=pi, kb=kb, kvh=kvh: e.tensor_tensor(
                        out=pT[pi][:], in0=ex[xi][:], in1=k.EB[kb][:, kvh * 4:(kvh + 1) * 4, :].rearrange("p a b -> p (a b)"), op=ALU.mult),
                        reads=['t_ex%d' % xi] + EBk[kb], writes=['t_pT%d' % pi])
                bo = 3 + kvh + 2 * (c % 2)

                def mmo(e, kbs=kbs, pts=pts, c=c, kvh=kvh, bo=bo):
                    r = None
                    for g in range(4):
                        for n, (kb, pi) in enumerate(zip(kbs, pts)):
                            kblk = c - 1 + kb
                            r = e.matmul(k.bank[bo][:, g * 65:(g + 1) * 65], lhsT=pT[pi][:, g * 128:(g + 1) * 128],
                                         rhs=vA[:, kblk, kvh, :], start=(n == 0), stop=(n == len(kbs) - 1))
                    return r
                P.op('pe', mmo, reads=['t_pT%d' % pi for pi in pts] + ['t_vA0_0', 't_vA0_1', 't_vA1_0', 't_vA1_1', 't_vA1s'], writes=['bank%d' % bo])
                dk = 't_den%d' % kvh
                P.op('dve', lambda e, bo=bo, kvh=kvh: e.tensor_tensor(
                    out=den[kvh][:], in0=dap(k.bank[bo], 64, [[512, 128], [65, 4]]), in1=snk[:, kvh * 4:(kvh + 1) * 4], op=ALU.add),
                    reads=['bank%d' % bo, 't_snk'], writes=[dk])
                P.op('dve', lambda e, kvh=kvh: e.reciprocal(out=rec[kvh][:], in_=den[kvh][:]), reads=[dk], writes=['t_rec%d' % kvh])
                P.op('dve', lambda e, bo=bo, kvh=kvh, yb=yb: e.tensor_tensor(
                    out=yst[yb][:, kvh * 256:(kvh + 1) * 256].rearrange("p (a b) -> p a b", a=4),
                    in0=dap(k.bank[bo], 0, [[512, 128], [65, 4], [1, 64]]),
                    in1=dap(rec[kvh], 0, [[4, 128], [1, 4], [0, 64]]), op=ALU.mult),
                    reads=['bank%d' % bo, 't_rec%d' % kvh], writes=['t_yst%d_%d' % (yb, kvh)])
            P.dma('sp', lambda e, yb=yb, c=c: e.dma_start(out=dap(k.ycat, c * 128 * 768 + 256, [[768, 128], [1, 512]]), in_=yst[yb][:]),
                  reads=['t_yst%d_0' % yb, 't_yst%d_1' % yb], writes=['ycat_t%d' % c])
            P._update(P.res['ycat_t%d' % c]['w'], ['t_yst%d_0' % yb, 't_yst%d_1' % yb], [])


def phase_o(k, l):
    nc, P = k.nc, k.P
    P.barrier()
    with contextlib.ExitStack() as ps:
        Wo = sb(k, "o_Wo", [128, 8, D], BF16, ps)
        Wst = [sb(k, "o_Wst%d" % i, [128, 8, 256], F32, ps) for i in range(2)]
        g1 = sb(k, "o_g1", [128, D], F32, ps)
        b1 = sb(k, "o_b1", [128, D], F32, ps)
        yc = [sb(k, "o_yc%d" % i, [128, 768], BF16, ps) for i in range(NBA)]
        yT = [sb(k, "o_yT%d" % i, [128, 8, 128], BF16, ps) for i in range(NBA)]
        ht = [sb(k, "o_h%d" % i, [128, D], F32, ps) for i in range(NBA)]
        rt = [sb(k, "o_r%d" % i, [128, D], F32, ps) for i in range(NBA)]
        h1t = [sb(k, "o_h1%d" % i, [128, D], F32, ps) for i in range(NBA)]
        h1bt = [sb(k, "o_h1b%d" % i, [128, D], BF16, ps) for i in range(NBA)]
        scr = [dict(stats=sb(k, "o_stats%d" % i, [128, 2, 6], F32, ps), mv=sb(k, "o_mv%d" % i, [128, 2], F32, ps),
                    sd=sb(k, "o_sd%d" % i, [128, 1], F32, ps), rstd=sb(k, "o_rstd%d" % i, [128, 1], F32, ps),
                    nb=sb(k, "o_nb%d" % i, [128, 1], F32, ps), xn=sb(k, "o_xn%d" % i, [128, D], F32, ps))
               for i in range(NBA)]
        P.dma('sp', lambda e: e.dma_start(out=g1[:], in_=dap(k.ln1_g, l * D, [[0, 128], [1, D]])), writes=['o_g1'])
        P.dma('sp', lambda e: e.dma_start(out=b1[:], in_=dap(k.ln1_b, l * D, [[0, 128], [1, D]])), writes=['o_b1'])
        for c in range(4):
            w = Wst[c % 2]
            wk = 'o_Wst%d' % (c % 2)
            P.dma('sp', lambda e, w=w, c=c: e.dma_start(
                out=w[:], in_=dap(k.w_out, l * D * D + c * 256, [[D, 128], [128 * D, 8], [1, 256]])), writes=[wk])
            if c % 2 == 0:
                P.op('act', lambda e, w=w, c=c: e.copy(out=Wo[:, :, c * 256:(c + 1) * 256], in_=w[:]), reads=[wk], writes=['o_Wo%d' % c])
            else:
                P.op('dve', lambda e, w=w, c=c: e.tensor_copy(out=Wo[:, :, c * 256:(c + 1) * 256], in_=w[:]), reads=[wk], writes=['o_Wo%d' % c])
        Wkeys = ['o_Wo%d' % c for c in range(4)]

        def o_load(ti):
            b = ti % NBA
            tok0 = ti * 128
            P.dma('sp', lambda e: e.dma_start(out=yc[b][:], in_=dap(k.ycat, tok0 * 768, [[768, 128], [1, 768]])),
                  reads=['ycat_r%d' % ti, 'ycat_t%d' % ti], writes=['o_yc%d' % b])
            P.dma('sp', lambda e: e.dma_start(out=yT[b][:, 2:4, :], in_=dap(k.yssmT, tok0, [[T, 128], [128 * T, 2], [1, 128]])),
                  reads=['yssmT%d' % ti], writes=['o_yT%d_s' % b])
            P.dma('sp', lambda e: e.dma_start(out=ht[b][:], in_=dap(k.hA, tok0 * D, [[D, 128], [1, D]])),
                  reads=['hA%d' % ti], writes=['o_h%d' % b])
        def o_tile(ti):
            b = ti % NBA
            tok0 = ti * 128

            def tr(e, b=b):
                r = None
                for kk in range(6):
                    r = e.transpose(out=k.bankT[:, kk * 128:(kk + 1) * 128], in_=yc[b][:, kk * 128:(kk + 1) * 128], identity=k.identb[:])
                return r
            P.op('pe', tr, reads=['o_yc%d' % b, 'identb'], writes=['bankT'])
            P.op('dve', lambda e, b=b: e.tensor_copy(out=yT[b][:, 0:2, :], in_=k.bankT[:, 0:256].rearrange("p (a b) -> p a b", a=2)),
                 reads=['bankT'], writes=['o_yT%d_r' % b])
            P.op('dve', lambda e, b=b: e.tensor_copy(out=yT[b][:, 4:8, :], in_=k.bankT[:, 256:768].rearrange("p (a b) -> p a b", a=4)),
                 reads=['bankT'], writes=['o_yT%d_t' % b])
            yield
            for half in range(2):
                bk = (2 * ti + half) % 4

                def mm(e, b=b, half=half, bk=bk):
                    r = None
                    for kk in range(8):
                        r = e.matmul(k.bank[bk][:, :], lhsT=yT[b][:, kk, :], rhs=Wo[:, kk, half * 512:(half + 1) * 512],
                                     start=(kk == 0), stop=(kk == 7))
                    return r
                P.op('pe', mm, reads=['o_yT%d_r' % b, 'o_yT%d_s' % b, 'o_yT%d_t' % b] + Wkeys, writes=['bank%d' % bk])
                yield
                P.op('dve', lambda e, b=b, half=half, bk=bk: e.scalar_tensor_tensor(
                    out=rt[b][:, half * 512:(half + 1) * 512], in0=ht[b][:, half * 512:(half + 1) * 512], scalar=ALPHA,
                    in1=k.bank[bk][:, :], op0=ALU.mult, op1=ALU.add),
                    reads=['o_h%d' % b, 'bank%d' % bk], writes=['o_r%d_%d' % (b, half)])
                yield
            P.res['o_r%d' % b] = P.res['o_r%d_1' % b]
            yield from layer_norm_tile(k, rt[b][:], 'o_r%d' % b, h1t[b][:], 'o_h1%d' % b, g1[:], 'o_g1', b1[:], 'o_b1', 'o%d_' % b, scr[b])
            P._update(P.res['o_h1%d' % b]['w'], ['o_r%d_0' % b, 'o_r%d_1' % b], [])
            P.op('act', lambda e, b=b: e.copy(out=h1bt[b][:], in_=h1t[b][:]), reads=['o_h1%d' % b], writes=['o_h1b%d' % b])
            yield
            P.dma('sp', lambda e, b=b, tok0=tok0: e.dma_start(out=dap(k.h1, tok0 * D, [[D, 128], [1, D]]), in_=h1t[b][:]),
                  reads=['o_h1%d' % b], writes=['h1_%d' % ti])
            P.dma('sp', lambda e, b=b, tok0=tok0: e.dma_start(out=dap(k.h1b, tok0 * D, [[D, 128], [1, D]]), in_=h1bt[b][:]),
                  reads=['o_h1b%d' % b], writes=['h1b_%d' % ti])

        run_window(o_tile, o_load, NT, NBA - 1)


TL = 256
SW = 4
NSET = 4
NCH = T // TL
TWO_PI = 2.0 * math.pi
CW1 = 6.28125
CW2 = TWO_PI - CW1
PI_LO = 3.1415925


def sincos(k, arg, argkey, out_sin, out_cos, outkey, n, scr, tag):
    P = k.P
    x, kf, ki = scr['x'], scr['kf'], scr['ki']
    for out, shift, nm in ((out_sin, 0.0, 's'), (out_cos, 0.5 * math.pi, 'c')):
        t = tag + nm
        P.op('dve', lambda e, shift=shift: e.tensor_scalar(out=x[:, 0:n], in0=arg, scalar1=shift, scalar2=None, op0=ALU.add),
             reads=[argkey], writes=[tag + 'x'])
        P.op('dve', lambda e: e.tensor_scalar(out=kf[:, 0:n], in0=x[:, 0:n], scalar1=1.0 / TWO_PI, scalar2=None, op0=ALU.mult),
             reads=[tag + 'x'], writes=[tag + 'kf'])
        P.op('dve', lambda e: e.tensor_copy(out=ki[:, 0:n], in_=kf[:, 0:n]), reads=[tag + 'kf'], writes=[tag + 'ki'])
        P.op('dve', lambda e: e.tensor_copy(out=kf[:, 0:n], in_=ki[:, 0:n]), reads=[tag + 'ki'], writes=[tag + 'kf'])
        P.op('dve', lambda e: e.scalar_tensor_tensor(out=x[:, 0:n], in0=kf[:, 0:n], scalar=-CW1, in1=x[:, 0:n], op0=ALU.mult, op1=ALU.add),
             reads=[tag + 'kf', tag + 'x'], writes=[tag + 'x'])
        P.op('dve', lambda e: e.scalar_tensor_tensor(out=x[:, 0:n], in0=kf[:, 0:n], scalar=-CW2, in1=x[:, 0:n], op0=ALU.mult, op1=ALU.add),
             reads=[tag + 'kf', tag + 'x'], writes=[tag + 'x'])
        P.op('dve', lambda e: e.tensor_scalar(out=x[:, 0:n], in0=x[:, 0:n], scalar1=-PI_LO, scalar2=PI_LO, op0=ALU.max, op1=ALU.min),
             reads=[tag + 'x'], writes=[tag + 'x'])
        P.op('act', lambda e, out=out: e.activation(out=out, in_=x[:, 0:n], func=AF.Sin), reads=[tag + 'x'], writes=[outkey + nm])


def phase_s(k, l):
    nc, P = k.nc, k.P
    P.barrier()
    with contextlib.ExitStack() as ps:
        def t(name, shape, dt=F32):
            return sb(k, "s_" + name, shape, dt, ps)
        lre, lim, ls = t("lre", [128, 16]), t("lim", [128, 16]), t("ls", [128, 16])
        bre, bim = t("bre", [128, 8, 16]), t("bim", [128, 8, 16])
        cre, cim = t("cre", [128, 16, 16]), t("cim", [128, 16, 16])
        dcol, bglu = t("dcol", [128, 2]), t("bglu", [128, 2])
        wst = t("wst", [128, 2, 256]); wglu = t("wglu", [128, 2, 256], BF16)
        step, aa, th, rr = t("step", [128, 16]), t("aa", [128, 16]), t("th", [128, 16]), t("rr", [128, 16])
        sn, cs, thT, sT, cT = (t(n, [128, 16]) for n in ("sn", "cs", "thT", "sT", "cT"))
        nsT = t("nsT", [128, 16])
        lbr, lbi, nr, d2, inv, cr, ci, tq = (t(n, [128, 16]) for n in ("lbr", "lbi", "nr", "d2", "inv", "cr", "ci", "tq"))
        scr = dict(x=t("scx", [128, TL]), kf=t("sckf", [128, TL]), ki=t("scki", [128, TL], I32))
        irow_i = t("irow_i", [128, TL], I32); irowT = t("irowT", [128, TL])
        arg = t("arg", [128, TL])
        t1, t2, bbr, bbi = (t(n, [128, 16]) for n in ("t1", "t2", "bbr", "bbi"))
        Bpad = [t("Bpad%d" % i, [128, 128]) for i in range(2)]
        LB = [[t("LB%d_%d" % (i, ri), [128, 128], BF16) for ri in range(2)] for i in range(16)]
        LC = [[t("LC%d_%d" % (i, ri), [128, 128], BF16) for ri in range(3)] for i in range(16)]
        SIN = [t("SIN%d" % i, [128, TL]) for i in range(16)]
        COS = [t("COS%d" % i, [128, TL]) for i in range(16)]
        Rt = [t("Rt%d" % i, [128, TL]) for i in range(16)]
        init = [t("init%d" % i, [128, 2]) for i in range(8)]
        tc_ = [t("tc%d" % i, [128, 2]) for i in range(8)]
        uraw = [t("uraw%d" % i, [128, 2, TL], BF16) for i in range(2)]
        uc = [t("uc%d" % i, [128, 2, TL], BF16) for i in range(2)]
        mm_ = [[t("m%d_%d" % (j, i), [128, TL]) for j in range(4)] for i in range(NSET)]
        zin = [t("zin%d" % i, [128, 2, TL]) for i in range(NSET)]
        zz = [t("z%d" % i, [128, 2, TL]) for i in range(NSET)]
        qq = [[t("q%d_%d" % (j, i), [128, TL], BF16) for j in range(4)] for i in range(NSET)]
        yfs = [t("yfs%d" % i, [128, 2, TL]) for i in range(2)]
        ys = [t("ys%d" % i, [128, 2, TL]) for i in range(2)]
        x2 = [t("x2%d" % i, [128, 2, TL]) for i in range(2)]
        gg = [t("g%d" % i, [128, 2, TL], BF16) for i in range(2)]
        sig = [t("sig%d" % i, [128, 2, TL]) for i in range(2)]
        yo = [t("yo%d" % i, [128, 2, TL], BF16) for i in range(2)]

        for tl, src, n, key in ((lre, k.s_lre, 16, 'lre'), (lim, k.s_lim, 16, 'lim'), (ls, k.s_ls, 16, 'ls'),
                                (bre, k.s_bre, 128, 'bre'), (bim, k.s_bim, 128, 'bim'),
                                (cre, k.s_cre, 256, 'cre'), (cim, k.s_cim, 256, 'cim'),
                                (dcol, k.s_d, 2, 'dcol'), (bglu, k.s_bglu, 2, 'bglu')):
            P.dma('sp', lambda e, tl=tl, src=src, n=n: e.dma_start(
                out=tl[:].rearrange("p a b -> p (a b)") if len(tl.shape) == 3 else tl[:],
                in_=dap(src, l * 128 * n, [[n, 128], [1, n]])), writes=['s_' + key])
        P.dma('sp', lambda e: e.dma_start(out=wst[:], in_=dap(k.s_wglu, l * 65536, [[256, 128], [128 * 256, 2], [1, 256]])), writes=['s_wst'])
        P.op('act', lambda e: e.copy(out=wglu[:], in_=wst[:]), reads=['s_wst'], writes=['s_wglu'])
        P.op('pool', lambda e: e.iota(irow_i[:], [[1, TL]], base=0, channel_multiplier=0), writes=['s_irow_i'])
        P.op('dve', lambda e: e.tensor_copy(out=irowT[:], in_=irow_i[:]), reads=['s_irow_i'], writes=['s_irowT'])
        P.op('act', lambda e: e.activation(out=step[:], in_=ls[:], func=AF.Exp), reads=['s_ls'], writes=['s_step'])
        P.op('dve', lambda e: e.tensor_tensor(out=aa[:], in0=lre[:], in1=step[:], op=ALU.mult), reads=['s_lre', 's_step'], writes=['s_aa'])
        P.op('dve', lambda e: e.tensor_tensor(out=th[:], in0=lim[:], in1=step[:], op=ALU.mult), reads=['s_lim', 's_step'], writes=['s_th'])
        P.op('act', lambda e: e.activation(out=rr[:], in_=aa[:], func=AF.Exp), reads=['s_aa'], writes=['s_rr'])
        sincos(k, th[:], 's_th', sn[:], cs[:], 's_th_', 16, scr, 's_sc_')
        P.op('dve', lambda e: e.tensor_scalar(out=thT[:], in0=th[:], scalar1=float(TL), scalar2=None, op0=ALU.mult), reads=['s_th'], writes=['s_thT'])
        sincos(k, thT[:], 's_thT', sT[:], cT[:], 's_thT_', 16, scr, 's_sc_')
        P.op('dve', lambda e: e.tensor_scalar(out=nsT[:], in0=sT[:], scalar1=-1.0, scalar2=None, op0=ALU.mult), reads=['s_thT_s'], writes=['s_nsT'])
        P.op('dve', lambda e: e.tensor_tensor(out=lbr[:], in0=rr[:], in1=cs[:], op=ALU.mult), reads=['s_rr', 's_th_c'], writes=['s_lbr'])
        P.op('dve', lambda e: e.tensor_tensor(out=lbi[:], in0=rr[:], in1=sn[:], op=ALU.mult), reads=['s_rr', 's_th_s'], writes=['s_lbi'])
        P.op('dve', lambda e: e.tensor_scalar(out=nr[:], in0=lbr[:], scalar1=-1.0, scalar2=None, op0=ALU.add), reads=['s_lbr'], writes=['s_nr'])
        P.op('dve', lambda e: e.tensor_tensor(out=d2[:], in0=lre[:], in1=lre[:], op=ALU.mult), reads=['s_lre'], writes=['s_d2'])
        P.op('dve', lambda e: e.tensor_tensor(out=tq[:], in0=lim[:], in1=lim[:], op=ALU.mult), reads=['s_lim'], writes=['s_tq'])
        P.op('dve', lambda e: e.tensor_tensor(out=d2[:], in0=d2[:], in1=tq[:], op=ALU.add), reads=['s_d2', 's_tq'], writes=['s_d2'])
        P.op('dve', lambda e: e.reciprocal(out=inv[:], in_=d2[:]), reads=['s_d2'], writes=['s_inv'])
        P.op('dve', lambda e: e.tensor_tensor(out=cr[:], in0=nr[:], in1=lre[:], op=ALU.mult), reads=['s_nr', 's_lre'], writes=['s_cr'])
        P.op('dve', lambda e: e.tensor_tensor(out=tq[:], in0=lbi[:], in1=lim[:], op=ALU.mult), reads=['s_lbi', 's_lim', 's_d2'], writes=['s_tq'])
        P.op('dve', lambda e: e.tensor_tensor(out=cr[:], in0=cr[:], in1=tq[:], op=ALU.add), reads=['s_cr', 's_tq'], writes=['s_cr'])
        P.op('dve', lambda e: e.tensor_tensor(out=cr[:], in0=cr[:], in1=inv[:], op=ALU.mult), reads=['s_cr', 's_inv'], writes=['s_cr'])
        P.op('dve', lambda e: e.tensor_tensor(out=ci[:], in0=lbi[:], in1=lre[:], op=ALU.mult), reads=['s_lbi', 's_lre'], writes=['s_ci'])
        P.op('dve', lambda e: e.tensor_tensor(out=tq[:], in0=nr[:], in1=lim[:], op=ALU.mult), reads=['s_nr', 's_lim', 's_cr'], writes=['s_tq'])
        P.op('dve', lambda e: e.tensor_tensor(out=ci[:], in0=ci[:], in1=tq[:], op=ALU.subtract), reads=['s_ci', 's_tq'], writes=['s_ci'])
        P.op('dve', lambda e: e.tensor_tensor(out=ci[:], in0=ci[:], in1=inv[:], op=ALU.mult), reads=['s_ci', 's_inv'], writes=['s_ci'])
        for idx in range(16):
            pair = idx // 2
            c0 = 32 * (pair % 4)
            ic = lambda tl, idx=idx: tl[:, idx:idx + 1]
            P.op('dve', lambda e, pair=pair, idx=idx: e.tensor_scalar(out=t1[:], in0=bim[:, pair, :], scalar1=ci[:, idx:idx + 1], scalar2=None, op0=ALU.mult),
                 reads=['s_bim', 's_ci'], writes=['s_t1'])
            P.op('dve', lambda e, pair=pair, idx=idx: e.scalar_tensor_tensor(out=bbr[:], in0=bre[:, pair, :], scalar=cr[:, idx:idx + 1], in1=t1[:],
                                                                              op0=ALU.mult, op1=ALU.subtract),
                 reads=['s_bre', 's_cr', 's_t1'], writes=['s_bbr'])
            P.op('dve', lambda e, pair=pair, idx=idx: e.tensor_scalar(out=t2[:], in0=bre[:, pair, :], scalar1=ci[:, idx:idx + 1], scalar2=None, op0=ALU.mult),
                 reads=['s_bre', 's_ci'], writes=['s_t2'])
            P.op('dve', lambda e, pair=pair, idx=idx: e.scalar_tensor_tensor(out=bbi[:], in0=bim[:, pair, :], scalar=cr[:, idx:idx + 1], in1=t2[:],
                                                                              op0=ALU.mult, op1=ALU.add),
                 reads=['s_bim', 's_cr', 's_t2'], writes=['s_bbi'])
            for ri, (bb, bk_) in enumerate(((bbr, 's_bbr'), (bbi, 's_bbi'))):
                bp = Bpad[ri]
                bpk = 's_Bpad%d' % ri
                P.op('pool', lambda e, bp=bp: e.memset(bp[:], 0.0), writes=[bpk])
                P.op('dve', lambda e, bp=bp, bb=bb, c0=c0: e.tensor_copy(out=bp[0:64, c0:c0 + 16], in_=bb[0:64, :]), reads=[bk_, bpk], writes=[bpk + 'a'])
                P.op('dve', lambda e, bp=bp, bb=bb, c0=c0: e.tensor_copy(out=bp[64:128, c0 + 16:c0 + 32], in_=bb[64:128, :]), reads=[bk_, bpk], writes=[bpk + 'b'])
                bkk = ri
                P.op('pe', lambda e, bp=bp, bkk=bkk: e.transpose(out=k.bank[bkk][:, 0:128], in_=bp[:], identity=k.identf[:]),
                     reads=[bpk, bpk + 'a', bpk + 'b', 'identf'], writes=['bank%d' % bkk])
                P._update(P.res['bank%d' % bkk]['w'], [bpk], [])
                P.op('act', lambda e, idx=idx, ri=ri, bkk=bkk: e.copy(out=LB[idx][ri][:], in_=k.bank[bkk][:, 0:128]),
                     reads=['bank%d' % bkk], writes=['s_LB%d' % idx + '_%d' % ri])
            for ri, (cc_, ck, sgn) in enumerate(((cre, 's_cre', 1.0), (cim, 's_cim', -1.0), (cre, 's_cre', -1.0))):
                lc = LC[idx][ri]
                lck = 's_LC%d_%d' % (idx, ri)
                P.op('pool', lambda e, lc=lc: e.memset(lc[:], 0.0), writes=[lck])
                P.op('dve', lambda e, lc=lc, cc_=cc_, c0=c0, sgn=sgn, idx=idx: e.tensor_scalar(
                    out=lc[0:64, c0:c0 + 16], in0=cc_[0:64, idx, :], scalar1=sgn, scalar2=None, op0=ALU.mult), reads=[ck, lck], writes=[lck + 'a'])
                P.op('dve', lambda e, lc=lc, cc_=cc_, c0=c0, sgn=sgn, idx=idx: e.tensor_scalar(
                    out=lc[64:128, c0 + 16:c0 + 32], in0=cc_[64:128, idx, :], scalar1=sgn, scalar2=None, op0=ALU.mult), reads=[ck, lck], writes=[lck + 'b'])
            P.op('dve', lambda e, idx=idx: e.tensor_scalar(out=arg[:], in0=irowT[:], scalar1=th[:, idx:idx + 1], scalar2=None, op0=ALU.mult),
                 reads=['s_irowT', 's_th', 's_tab%d_s' % (idx - 1), 's_tab%d_c' % (idx - 1)], writes=['s_arg'])
            sincos(k, arg[:], 's_arg', SIN[idx][:], COS[idx][:], 's_tab%d_' % idx, TL, scr, 's_sc_')
            P.op('pool', lambda e, idx=idx: e.tensor_copy(out=Rt[idx][:], in_=bc(rr, idx, TL, 16)), reads=['s_rr'], writes=['s_Rt%d' % idx])
        nu = 0
        for d in range(2):
            for pr in range(8):
                P.op('dve', lambda e, pr=pr: e.memset(init[pr][:], 0.0), writes=['s_init%d' % pr])
            for n in range(NCH):
                cf = n if d == 0 else NCH - 1 - n
                ub = n % 2
                P.dma('sp', lambda e, ub=ub, cf=cf: e.dma_start(out=uraw[ub][:], in_=dap(k.fm, 512 * T + cf * TL, [[T, 128], [128 * T, 2], [1, TL]])),
                      reads=['fm4_%d' % (cf * TL // 512), 'fm5_%d' % (cf * TL // 512)], writes=['s_uraw%d' % ub])
                if d == 0:
                    ucur, uck = uraw[ub], 's_uraw%d' % ub
                else:
                    for ft in range(2):
                        P.op('pool', lambda e, ub=ub, ft=ft: e.tensor_copy(out=uc[ub][:, ft, :], in_=dap(uraw[ub], ft * TL + TL - 1, [[2 * TL, 128], [-1, TL]])),
                             reads=['s_uraw%d' % ub], writes=['s_uc%d_%d' % (ub, ft)])
                    P.res['s_uc%d' % ub] = P.res['s_uc%d_1' % ub]
                    ucur, uck = uc[ub], 's_uc%d' % ub
                    P.dma('sp', lambda e, ub=ub, cf=cf: e.dma_start(out=yfs[ub][:], in_=dap(k.yf, cf * TL, [[T, 128], [128 * T, 2], [1, TL]])),
                          reads=['yf_%d' % cf], writes=['s_yfs%d' % ub])
                def unit(pr, nu_, n=n, d=d, ub=ub, ucur=ucur, uck=uck):
                    idx = pr * 2 + d
                    ft = pr // 4
                    u2 = nu_ % NSET
                    u3 = nu_ % NSET
                    bb = nu_ % 4
                    ucks = [uck] if d == 0 else ['s_uc%d_0' % ub, 's_uc%d_1' % ub]

                    def mmb(e, idx=idx, ft=ft, bb=bb, ucur=ucur):
                        e.matmul(k.bank[bb][:, 0:TL], lhsT=LB[idx][0][:], rhs=ucur[:, ft, :], start=True, stop=True)
                        return e.matmul(k.bank[bb][:, TL:2 * TL], lhsT=LB[idx][1][:], rhs=ucur[:, ft, :], start=True, stop=True)
                    P.op('pe', mmb, reads=ucks + ['s_LB%d_0' % idx, 's_LB%d_1' % idx], writes=['bank%d' % bb])
                    yield
                    m = mm_[u2]
                    mk = lambda j: 's_m%d_%d' % (j, u2)
                    tabs = ['s_tab%d_s' % idx, 's_tab%d_c' % idx]
                    br_, bi_ = k.bank[bb][:, 0:TL], k.bank[bb][:, TL:2 * TL]
                    P.op('dve', lambda e, m=m, br_=br_, idx=idx: e.tensor_tensor(out=m[0][:], in0=br_, in1=COS[idx][:], op=ALU.mult),
                         reads=['bank%d' % bb] + tabs, writes=[mk(0)])
                    yield
                    P.op('dve', lambda e, m=m, bi_=bi_, idx=idx: e.tensor_tensor(out=m[1][:], in0=bi_, in1=SIN[idx][:], op=ALU.mult),
                         reads=['bank%d' % bb] + tabs, writes=[mk(1)])
                    yield
                    P.op('dve', lambda e, m=m, bi_=bi_, idx=idx: e.tensor_tensor(out=m[2][:], in0=bi_, in1=COS[idx][:], op=ALU.mult),
                         reads=['bank%d' % bb] + tabs, writes=[mk(2)])
                    yield
                    P.op('dve', lambda e, m=m, br_=br_, idx=idx: e.tensor_tensor(out=m[3][:], in0=br_, in1=SIN[idx][:], op=ALU.mult),
                         reads=['bank%d' % bb] + tabs, writes=[mk(3)])
                    yield
                    zk = 's_zin%d' % u2
                    P.op('pool', lambda e, m=m, u2=u2: e.tensor_tensor(out=zin[u2][:, 0, :], in0=m[0][:], in1=m[1][:], op=ALU.add),
                         reads=[mk(0), mk(1)], writes=[zk + 'r'])
                    yield
                    P.op('pool', lambda e, m=m, u2=u2: e.tensor_tensor(out=zin[u2][:, 1, :], in0=m[2][:], in1=m[3][:], op=ALU.subtract),
                         reads=[mk(2), mk(3)], writes=[zk + 'i'])
                    yield
                    zt = zz[u3]
                    ztk = 's_z%d' % u3
                    ik = 's_init%d' % pr
                    P.op('dve', lambda e, zt=zt, u2=u2, idx=idx, pr=pr: e.tensor_tensor_scan(
                        out=zt[:, 0, :], data0=Rt[idx][:], data1=zin[u2][:, 0, :], initial=init[pr][:, 0:1], op0=ALU.mult, op1=ALU.add),
                        reads=[zk + 'r', 's_Rt%d' % idx, ik, ik + 'r', ik + 'i'], writes=[ztk + 'r'])
                    yield
                    P.op('dve', lambda e, zt=zt, u2=u2, idx=idx, pr=pr: e.tensor_tensor_scan(
                        out=zt[:, 1, :], data0=Rt[idx][:], data1=zin[u2][:, 1, :], initial=init[pr][:, 1:2], op0=ALU.mult, op1=ALU.add),
                        reads=[zk + 'i', 's_Rt%d' % idx, ik, ik + 'r', ik + 'i'], writes=[ztk + 'i'])
                    yield
                    if n < NCH - 1:
                        zl_r, zl_i = zt[:, 0, TL - 1:TL], zt[:, 1, TL - 1:TL]
                        tk = 's_tc%d' % pr
                        P.op('act', lambda e, pr=pr, idx=idx, zl_i=zl_i: e.activation(out=tc_[pr][:, 0:1], in_=zl_i, func=AF.Identity, scale=nsT[:, idx:idx + 1]),
                             reads=[ztk + 'i', 's_nsT'], writes=[tk + 'a'])
                        yield
                        P.op('act', lambda e, pr=pr, idx=idx, zl_r=zl_r: e.activation(out=tc_[pr][:, 1:2], in_=zl_r, func=AF.Identity, scale=sT[:, idx:idx + 1]),
                             reads=[ztk + 'r', 's_thT_s'], writes=[tk + 'b'])
                        yield
                        P.op('act', lambda e, pr=pr, idx=idx, zl_r=zl_r: e.activation(out=init[pr][:, 0:1], in_=zl_r, func=AF.Identity,
                                                                                     scale=cT[:, idx:idx + 1], bias=tc_[pr][:, 0:1]),
                             reads=[ztk + 'r', 's_thT_c', tk + 'a', ik], writes=[ik + 'r'])
                        yield
                        P.op('act', lambda e, pr=pr, idx=idx, zl_i=zl_i: e.activation(out=init[pr][:, 1:2], in_=zl_i, func=AF.Identity,
                                                                                     scale=cT[:, idx:idx + 1], bias=tc_[pr][:, 1:2]),
                             reads=[ztk + 'i', 's_thT_c', tk + 'b', ik], writes=[ik + 'i'])
                        P.res[ik] = P.res[ik + 'i']
                        P._update(P.res[ik]['w'], [], [])
                        yield
                    q = qq[u3]
                    qk = lambda j: 's_q%d_%d' % (j, u3)
                    P.op('pool', lambda e, q=q, zt=zt, idx=idx: e.tensor_tensor(out=q[0][:], in0=zt[:, 0, :], in1=COS[idx][:], op=ALU.mult),
                         reads=[ztk + 'r'] + tabs, writes=[qk(0)])
                    yield
                    P.op('dve', lambda e, q=q, zt=zt, idx=idx: e.tensor_tensor(out=q[1][:], in0=zt[:, 1, :], in1=SIN[idx][:], op=ALU.mult),
                         reads=[ztk + 'i'] + tabs, writes=[qk(1)])
                    yield
                    P.op('dve', lambda e, q=q, zt=zt, idx=idx: e.tensor_tensor(out=q[2][:], in0=zt[:, 0, :], in1=SIN[idx][:], op=ALU.mult),
                         reads=[ztk + 'r'] + tabs, writes=[qk(2)])
                    yield
                    P.op('dve', lambda e, q=q, zt=zt, idx=idx: e.tensor_tensor(out=q[3][:], in0=zt[:, 1, :], in1=COS[idx][:], op=ALU.mult),
                         reads=[ztk + 'i'] + tabs, writes=[qk(3)])
                    yield
                    yb = 4 + ft
                    first = (pr % 4 == 0)
                    last = (pr % 4 == 3)

                    def mmc(e, idx=idx, q=q, yb=yb, first=first, last=last):
                        e.matmul(k.bank[yb][:, 0:TL], lhsT=LC[idx][0][:], rhs=q[0][:], start=first, stop=False)
                        e.matmul(k.bank[yb][:, 0:TL], lhsT=LC[idx][2][:], rhs=q[1][:], start=False, stop=False)
                        e.matmul(k.bank[yb][:, 0:TL], lhsT=LC[idx][1][:], rhs=q[2][:], start=False, stop=False)
                        return e.matmul(k.bank[yb][:, 0:TL], lhsT=LC[idx][1][:], rhs=q[3][:], start=False, stop=last)
                    lckeys = ['s_LC%d_%d%s' % (idx, ri, sfx) for ri in range(3) for sfx in ('a', 'b')]
                    qkeys = [qk(j) for j in range(4)]
                    if first:
                        P.op('pe', mmc, reads=qkeys + lckeys, writes=['bank%d' % yb])
                        yield
                    else:
                        P.op('pe', mmc, reads=qkeys + ['bank%d' % yb] + lckeys, writes=['bank%d_acc' % yb])
                        yield
                        P.res['bank%d' % yb]['w'] = P.res['bank%d_acc' % yb]['w']
                    if last:
                        if d == 0:
                            P.op('act', lambda e, ub=ub, ft=ft, yb=yb: e.copy(out=ys[ub][:, ft, :], in_=k.bank[yb][:, 0:TL]),
                                 reads=['bank%d' % yb], writes=['s_ys%d_%d' % (ub, ft)])
                            yield
                        else:
                            P.op('dve', lambda e, ub=ub, ft=ft, yb=yb: e.tensor_tensor(
                                out=ys[ub][:, ft, :], in0=yfs[ub][:, ft, :], in1=dap(k.bank[yb], TL - 1, [[512, 128], [-1, TL]]), op=ALU.add),
                                reads=['bank%d' % yb, 's_yfs%d' % ub], writes=['s_ys%d_%d' % (ub, ft)])
                            yield

                for p0 in range(0, 8, SW):
                    gens = [unit(p0 + i_, nu + i_) for i_ in range(SW)]
                    nu += SW
                    while gens:
                        for g_ in list(gens):
                            try:
                                next(g_)
                            except StopIteration:
                                gens.remove(g_)
                if d == 0:
                    P.dma('sp', lambda e, ub=ub, cf=cf: e.dma_start(out=dap(k.yf, cf * TL, [[T, 128], [128 * T, 2], [1, TL]]), in_=ys[ub][:]),
                          reads=['s_ys%d_0' % ub, 's_ys%d_1' % ub], writes=['yf_%d' % cf])
                    P._update(P.res['yf_%d' % cf]['w'], ['s_ys%d_0' % ub, 's_ys%d_1' % ub], [])
                    continue
                ysk = ['s_ys%d_0' % ub, 's_ys%d_1' % ub]
                for ft in range(2):
                    P.op('dve', lambda e, ub=ub, ft=ft: e.scalar_tensor_tensor(
                        out=ys[ub][:, ft, :], in0=uraw[ub][:, ft, :], scalar=dcol[:, ft:ft + 1], in1=ys[ub][:, ft, :], op0=ALU.mult, op1=ALU.add),
                        reads=['s_uraw%d' % ub, 's_dcol', ysk[ft]], writes=[ysk[ft]])
                P.op('pool', lambda e, ub=ub: e.tensor_tensor(out=x2[ub][:], in0=ys[ub][:], in1=ys[ub][:], op=ALU.mult), reads=ysk, writes=['s_x2%d' % ub])
                P.op('pool', lambda e, ub=ub: e.tensor_scalar(out=x2[ub][:], in0=x2[ub][:], scalar1=0.044715, scalar2=1.0, op0=ALU.mult, op1=ALU.add),
                     reads=['s_x2%d' % ub], writes=['s_x2%d' % ub])
                P.op('pool', lambda e, ub=ub: e.tensor_tensor(out=x2[ub][:], in0=x2[ub][:], in1=ys[ub][:], op=ALU.mult), reads=['s_x2%d' % ub] + ysk, writes=['s_x2%d' % ub])
                P.op('act', lambda e, ub=ub: e.activation(out=x2[ub][:], in_=x2[ub][:], func=AF.Tanh, scale=math.sqrt(2.0 / math.pi)),
                     reads=['s_x2%d' % ub], writes=['s_x2%d' % ub])
                P.op('pool', lambda e, ub=ub: e.tensor_scalar(out=x2[ub][:], in0=x2[ub][:], scalar1=0.5, scalar2=0.5, op0=ALU.mult, op1=ALU.add),
                     reads=['s_x2%d' % ub], writes=['s_x2%d' % ub])
                P.op('pool', lambda e, ub=ub: e.tensor_tensor(out=ys[ub][:], in0=x2[ub][:], in1=ys[ub][:], op=ALU.mult), reads=['s_x2%d' % ub] + ysk, writes=['s_yg%d' % ub])
                P._update(P.res['s_yg%d' % ub]['w'], [], ysk)
                P.op('act', lambda e, ub=ub: e.copy(out=gg[ub][:], in_=ys[ub][:]), reads=['s_yg%d' % ub] + ysk, writes=['s_gg%d' % ub])
                for fo in range(2):
                    gb = 6

                    def mmg(e, ub=ub, fo=fo, gb=gb):
                        e.matmul(k.bank[gb][:, 0:TL], lhsT=wglu[:, 0, fo * 128:(fo + 1) * 128], rhs=gg[ub][:, 0, :], start=True, stop=False)
                        return e.matmul(k.bank[gb][:, 0:TL], lhsT=wglu[:, 1, fo * 128:(fo + 1) * 128], rhs=gg[ub][:, 1, :], start=False, stop=True)
                    P.op('pe', mmg, reads=['s_gg%d' % ub, 's_wglu'], writes=['bank%d' % gb])
                    P.op('act', lambda e, ub=ub, fo=fo, gb=gb: e.activation(out=sig[ub][:, fo, :], in_=k.bank[gb][:, 0:TL], func=AF.Sigmoid,
                                                                           bias=bglu[:, fo:fo + 1], scale=1.0),
                         reads=['bank%d' % gb, 's_bglu'], writes=['s_sig%d_%d' % (ub, fo)])
                P.op('dve', lambda e, ub=ub: e.tensor_tensor(out=yo[ub][:], in0=ys[ub][:], in1=sig[ub][:], op=ALU.mult),
                     reads=['s_yg%d' % ub, 's_sig%d_0' % ub, 's_sig%d_1' % ub] + ysk, writes=['s_yo%d' % ub])
                P.dma('sp', lambda e, ub=ub, cf=cf: e.dma_start(out=dap(k.yssmT, cf * TL, [[T, 128], [128 * T, 2], [1, TL]]), in_=yo[ub][:]),
                      reads=['s_yo%d' % ub], writes=['yssmT_c%d' % cf])
                P._update(P.res['yssmT_c%d' % cf]['w'], ['s_yo%d' % ub], [])
        for ti in range(NT):
            P.res['yssmT%d' % ti] = P.res['yssmT_c%d' % (ti * 128 // TL)]


CAP = 512
NEXP = 16
NBIS = 30


def phase_f(k, l, last):
    nc, P = k.nc, k.P
    P.barrier()
    dst = k.out if last else k.hA
    with contextlib.ExitStack() as ps:
        def tp(name, shape, dt=F32):
            return sb(k, "f_" + name, shape, dt, ps)
        idx_t = [[tp("idx%d_%d" % (e_, cb), [128, 1], I32) for cb in range(4)] for e_ in range(16)]
        gate_all = tp("gate_all", [128, 16, 4])
        p2 = contextlib.ExitStack()

        def t(name, shape, dt=F32):
            return sb(k, "f_" + name, shape, dt, p2)
        rw = t("rw", [128, 8, 16])
        aff = t("aff", [128, NT, 16])
        ones = t("ones", [128, 128])
        ustr = t("ustr", [128, 128])
        lo, thr, pc, gew = t("lo", [128, 16]), t("thr", [128, 16]), t("pc", [128, 16]), t("gew", [128, 16])
        cmpb = t("cmpb", [128, NT, 16])
        met = t("met", [128, 16, NT])
        incl = t("incl", [128, 16, NT])
        rpat_i = t("rpat_i", [128, 16, NT], I32)
        rpat = t("rpat", [128, 16, NT])
        posm = t("posm", [128, 16, NT])
        io_i = t("io_i", [128, 512], I32)
        io512 = t("io512", [128, 512], mybir.dt.float16)
        tvi = t("tvi", [128, NT, 16], I32)
        r1 = t("r1", [128, NT, 16])
        tv = t("tv", [128, NT, 16, 5], BF16)
        idf = t("idf", [128, 4])
        slot = t("slot", [128, 4, 5])
        oh = [t("oh%d" % i, [128, 512], BF16) for i in range(3)]
        with contextlib.ExitStack() as p1:
            h1t = [sb(k, "f_h1r%d" % i, [128, D], F32, p1) for i in range(3)]
            h1T = [sb(k, "f_h1T%d" % i, [128, 8, 128], F32, p1) for i in range(2)]
            ex = [sb(k, "f_ex%d" % i, [128, 16], F32, p1) for i in range(2)]
            sm = [dict((n, sb(k, "f_%s%d" % (n, i), [128, 1], F32, p1)) for n in ('mx', 'sum', 'rs')) for i in range(2)]
            P.dma('sp', lambda e: e.dma_start(out=rw[:], in_=dap(k.router_w, l * D * 16, [[16, 128], [128 * 16, 8], [1, 16]])), writes=['f_rw'])
            def f1_load(ti):
                b = ti % 3
                P.dma('sp', lambda e: e.dma_start(out=h1t[b][:], in_=dap(k.h1, ti * 128 * D, [[D, 128], [1, D]])),
                      reads=['h1_%d' % ti], writes=['f_h1r%d' % b])
            def f1_tile(ti):
                b = ti % 2
                b3 = ti % 3
                for hh in range(2):
                    bk = (2 * ti + hh) % 4

                    def tr(e, b3=b3, hh=hh, bk=bk):
                        r = None
                        for j in range(4):
                            kk = hh * 4 + j
                            r = e.transpose(out=k.bank[bk][:, j * 128:(j + 1) * 128], in_=h1t[b3][:, kk * 128:(kk + 1) * 128], identity=k.identf[:])
                        return r
                    P.op('pe', tr, reads=['f_h1r%d' % b3, 'identf'], writes=['bank%d' % bk])
                    yield
                    P.op('dve' if hh == 0 else 'act', lambda e, b=b, hh=hh, bk=bk: (e.tensor_copy if hh == 0 else e.copy)(
                        out=h1T[b][:, hh * 4:(hh + 1) * 4, :], in_=k.bank[bk][:, :].rearrange("p (a b) -> p a b", a=4)),
                        reads=['bank%d' % bk], writes=['f_h1T%d_%d' % (b, hh)])
                    yield
                lb = 4 + (ti % 2)

                def mml(e, b=b, lb=lb):
                    r = None
                    for kk in range(8):
                        r = e.matmul(k.bank[lb][:, 0:16], lhsT=h1T[b][:, kk, :], rhs=rw[:, kk, :], start=(kk == 0), stop=(kk == 7))
                    return r
                P.op('pe', mml, reads=['f_h1T%d_0' % b, 'f_h1T%d_1' % b, 'f_rw'], writes=['bank%d' % lb])
                yield
                m = sm[b]
                P.op('dve', lambda e, m=m, lb=lb: e.tensor_reduce(out=m['mx'][:], in_=k.bank[lb][:, 0:16], axis=AX.X, op=ALU.max, negate=True),
                     reads=['bank%d' % lb], writes=['f_mx%d' % b])
                yield
                P.op('act', lambda e, m=m, lb=lb, b=b: e.activation(out=ex[b][:], in_=k.bank[lb][:, 0:16], func=AF.Exp, bias=m['mx'][:], scale=1.0,
                                                                   accum_out=m['sum'][:]),
                     reads=['bank%d' % lb, 'f_mx%d' % b], writes=['f_ex%d' % b, 'f_sum%d' % b])
                yield
                P.op('dve', lambda e, m=m: e.reciprocal(out=m['rs'][:], in_=m['sum'][:]), reads=['f_sum%d' % b], writes=['f_rs%d' % b])
                yield
                P.op('dve', lambda e, m=m, b=b, ti=ti: e.tensor_scalar(out=aff[:, ti, :], in0=ex[b][:], scalar1=m['rs'][:], scalar2=None, op0=ALU.mult),
                     reads=['f_ex%d' % b, 'f_rs%d' % b], writes=['f_aff%d' % ti])
                yield

            run_window(f1_tile, f1_load, NT, 2)
        affk = ['f_aff%d' % ti for ti in range(NT)]
        P.op('dve', lambda e: e.memset(ones[:], 1.0), writes=['f_ones'])
        P.op('dve', lambda e: e.tensor_scalar(out=ustr[:], in0=k.dist[:], scalar1=0.0, scalar2=None, op0=ALU.is_gt), reads=['dist'], writes=['f_ustr'])
        P.op('dve', lambda e: e.memset(lo[:], 0.0), writes=['f_lo'])
        for it in range(NBIS):
            w = 0.5 ** (it + 1)
            P.op('dve', lambda e, w=w: e.tensor_scalar(out=thr[:], in0=lo[:], scalar1=w, scalar2=None, op0=ALU.add), reads=['f_lo'], writes=['f_thr'])
            P.op('dve', lambda e: e.tensor_tensor(out=cmpb[:], in0=aff[:], in1=dap(thr, 0, [[16, 128], [0, NT], [1, 16]]), op=ALU.is_ge),
                 reads=affk + ['f_thr'], writes=['f_cmpb'])
            P.op('dve', lambda e: e.tensor_reduce(out=pc[:], in_=cmpb[:].rearrange("p t e -> p e t"), axis=AX.X, op=ALU.add),
                 reads=['f_cmpb'], writes=['f_pc'])
            P.op('pe', lambda e: e.matmul(k.bank[0][:, 0:16], lhsT=ones[:], rhs=pc[:], start=True, stop=True),
                 reads=['f_ones', 'f_pc'], writes=['bank0'])
            P.op('dve', lambda e, w=w: e.tensor_scalar(out=gew[:], in0=k.bank[0][:, 0:16], scalar1=CAP - 0.5, scalar2=w, op0=ALU.is_ge, op1=ALU.mult),
                 reads=['bank0'], writes=['f_gew'])
            P.op('dve', lambda e: e.tensor_tensor(out=lo[:], in0=lo[:], in1=gew[:], op=ALU.add), reads=['f_lo', 'f_gew'], writes=['f_lo'])
        P.op('dve', lambda e: e.tensor_tensor(out=met[:], in0=aff[:].rearrange("p t e -> p e t"), in1=dap(lo, 0, [[16, 128], [1, 16], [0, NT]]), op=ALU.is_ge),
             reads=affk + ['f_lo'], writes=['f_met'])
        P.op('pool', lambda e: e.iota(rpat_i[:].rearrange("p a b -> p (a b)"), [[0, 16], [1, NT]], base=0, channel_multiplier=0), writes=['f_rpat_i'])
        P.op('dve', lambda e: e.tensor_copy(out=rpat[:], in_=rpat_i[:]), reads=['f_rpat_i'], writes=['f_rpat'])
        P.op('dve', lambda e: e.tensor_scalar(out=rpat[:], in0=rpat[:], scalar1=0.0, scalar2=None, op0=ALU.is_gt), reads=['f_rpat'], writes=['f_rpat'])
        P.op('dve', lambda e: e.tensor_tensor_scan(out=incl[:].rearrange("p a b -> p (a b)"), data0=rpat[:].rearrange("p a b -> p (a b)"),
                                                    data1=met[:].rearrange("p a b -> p (a b)"), initial=0.0, op0=ALU.mult, op1=ALU.add),
             reads=['f_rpat', 'f_met'], writes=['f_incl'])
        P.op('dve', lambda e: e.tensor_copy(out=pc[:], in_=incl[:, :, NT - 1]), reads=['f_incl'], writes=['f_pc'])
        P.op('pe', lambda e: e.matmul(k.bank[0][:, 0:16], lhsT=ustr[:], rhs=pc[:], start=True, stop=True), reads=['f_ustr', 'f_pc'], writes=['bank0'])
        P.op('dve', lambda e: e.tensor_tensor(out=posm[:], in0=incl[:], in1=met[:], op=ALU.subtract), reads=['f_incl', 'f_met'], writes=['f_posm'])
        P.op('dve', lambda e: e.tensor_tensor(out=posm[:], in0=posm[:], in1=dap(k.bank[0], 0, [[512, 128], [1, 16], [0, NT]]), op=ALU.add),
             reads=['f_posm', 'bank0'], writes=['f_posm'])
        P.op('dve', lambda e: e.scalar_tensor_tensor(out=posm[:], in0=posm[:], scalar=1.0, in1=met[:], op0=ALU.add, op1=ALU.mult),
             reads=['f_posm', 'f_met'], writes=['f_posm'])
        P.op('dve', lambda e: e.tensor_scalar(out=posm[:], in0=posm[:], scalar1=-1.0, scalar2=None, op0=ALU.add), reads=['f_posm'], writes=['f_posm'])
        P.op('pool', lambda e: e.iota(io_i[:], [[1, 512]], base=0, channel_multiplier=0), writes=['f_io_i'])
        P.op('dve', lambda e: e.tensor_copy(out=io512[:], in_=io_i[:]), reads=['f_io_i'], writes=['f_io512'])
        P.op('pool', lambda e: e.iota(tvi[:].rearrange("p a b -> p (a b)"), [[1, NT], [0, 16]], base=0, channel_multiplier=0), writes=['f_tvi'])
        P.op('dve', lambda e: e.tensor_copy(out=tv[:, :, :, 0], in_=tvi[:]), reads=['f_tvi'], writes=['f_tv0'])
        P.op('pool', lambda e: e.iota(tvi[:].rearrange("p a b -> p (a b)"), [[0, NT], [0, 16]], base=0, channel_multiplier=1), reads=['f_tv0'], writes=['f_tvi'])
        P.op('dve', lambda e: e.tensor_copy(out=tv[:, :, :, 1], in_=tvi[:]), reads=['f_tvi'], writes=['f_tv1'])
        P.op('dve', lambda e: e.tensor_copy(out=tv[:, :, :, 2], in_=aff[:]), reads=affk, writes=['f_tv2'])
        P.op('dve', lambda e: e.tensor_tensor(out=r1[:], in0=aff[:], in1=tv[:, :, :, 2], op=ALU.subtract), reads=affk + ['f_tv2'], writes=['f_r1'])
        P.op('dve', lambda e: e.tensor_copy(out=tv[:, :, :, 3], in_=r1[:]), reads=['f_r1'], writes=['f_tv3'])
        P.op('dve', lambda e: e.tensor_tensor(out=r1[:], in0=r1[:], in1=tv[:, :, :, 3], op=ALU.subtract), reads=['f_r1', 'f_tv3'], writes=['f_r1'])
        P.op('dve', lambda e: e.tensor_copy(out=tv[:, :, :, 4], in_=r1[:]), reads=['f_r1'], writes=['f_tv4'])
        tvk = ['f_tv%d' % i for i in range(5)]
        noh = 0
        for ex_ in range(NEXP):
            for ti in range(NT):
                o = noh % 3
                noh += 1
                P.op('dve', lambda e, o=o, ex_=ex_, ti=ti: e.tensor_scalar(out=oh[o][:], in0=io512[:], scalar1=posm[:, ex_, ti:ti + 1], scalar2=None, op0=ALU.is_equal),
                     reads=['f_io512', 'f_posm'], writes=['f_oh%d' % o])

                def mms(e, o=o, ex_=ex_, ti=ti):
                    r = None
                    for cb in range(4):
                        r = e.matmul(k.bank[cb][:, 0:5], lhsT=oh[o][:, cb * 128:(cb + 1) * 128], rhs=tv[:, ti, ex_, :],
                                     start=(ti == 0), stop=(ti == NT - 1))
                    return r
                P.op('pe', mms, reads=['f_oh%d' % o] + tvk, writes=['bank0_3'] if ti else ['bank0', 'bank1', 'bank2', 'bank3', 'bank0_3'])
            for cb in range(4):
                P.op('act', lambda e, cb=cb: e.copy(out=slot[:, cb, :], in_=k.bank[cb][:, 0:5]), reads=['bank0_3'], writes=['f_slot%d' % cb])
                P._update(P.res['f_slot%d' % cb]['w'], ['bank%d' % cb], [])
            slk = ['f_slot%d' % cb for cb in range(4)]
            P.op('dve', lambda e: e.scalar_tensor_tensor(out=idf[:], in0=slot[:, :, 0], scalar=128.0, in1=slot[:, :, 1], op0=ALU.mult, op1=ALU.add),
                 reads=slk, writes=['f_idf'])
            P.op('dve', lambda e, ex_=ex_: e.tensor_reduce(out=gate_all[:, ex_, :], in_=slot[:, :, 2:5], axis=AX.X, op=ALU.add),
                 reads=slk, writes=['f_gate%d' % ex_])
            for cb in range(4):
                P.op('dve', lambda e, ex_=ex_, cb=cb: e.tensor_copy(out=idx_t[ex_][cb][:], in_=idf[:, cb:cb + 1]), reads=['f_idf'], writes=['f_idx%d_%d' % (ex_, cb)])
            P.res['f_idx%d' % ex_] = P.res['f_idx%d_3' % ex_]
            P._update(P.res['f_idx%d' % ex_]['w'], slk + ['f_idf'], [])
        if DBG_F == 1:
            P.dma('sp', lambda e: e.dma_start(out=dap(k.dbg_gate, 0, [[64, 128], [1, 64]]), in_=gate_all[:].rearrange("p a b -> p (a b)")),
                  reads=['f_gate%d' % i for i in range(16)], writes=['dbg_gate'])
            P.dma('sp', lambda e: e.dma_start(out=dap(k.dbg_aff, 0, [[512, 128], [1, 512]]), in_=aff[:].rearrange("p a b -> p (a b)")),
                  reads=affk, writes=['dbg_aff'])
            P.dma('sp', lambda e: e.dma_start(out=dap(k.dbg_posm, 0, [[512, 128], [1, 512]]), in_=posm[:].rearrange("p a b -> p (a b)")),
                  reads=['f_posm'], writes=['dbg_posm'])
            finish(k)
            p2.close()
            return
        P.barrier()
        p2.close()
        t = tp
        zt = t("zt", [128, D])
        P.op('pool', lambda e: e.memset(zt[:], 0.0), writes=['f_zt'])
        for ti in range(NT):
            P.dma('sp', lambda e, ti=ti: e.dma_start(out=dap(k.ffn, ti * 128 * D, [[D, 128], [1, D]]), in_=zt[:]), reads=['f_zt'], writes=['ffn_z%d' % ti])
        zero_toks = [P.res['ffn_z%d' % ti]['w'] for ti in range(NT)]
        with contextlib.ExitStack() as p5:
            def t5(name, shape, dt=F32):
                return sb(k, "f_" + name, shape, dt, p5)
            xs = t5("xs", [128, 4, D], BF16)
            xsT = [t5("xsT%d" % i, [128, 8, 512], BF16) for i in range(2)]
            stgA = [t5("stgA%d" % i, [128, 8, 256]) for i in range(3)]
            stgB = [t5("stgB%d" % i, [128, 2, D]) for i in range(2)]
            wgb = [t5("wgb%d" % i, [128, 8, 512], BF16) for i in range(2)]
            wub = [t5("wub%d" % i, [128, 8, 512], BF16) for i in range(2)]
            wdb = t5("wdb", [128, 16, D], BF16)
            hdn = t5("hdn", [128, 16, 512], BF16)
            sl = [t5("sl%d" % i, [128, 512]) for i in range(2)]
            ost = [t5("ost%d" % i, [128, D]) for i in range(2)]
            cnt = dict(stg=0, stgB=0, cast=0, psg=0, ps2=0, ost=0)

            def gather(ex_):
                xb = ex_ % 2
                for cb in range(4):
                    P.dma('pool', lambda e, cb=cb, ex_=ex_: e.indirect_dma_start(
                        out=xs[:, cb, :], out_offset=None, in_=k.h1b[:, :],
                        in_offset=bass.IndirectOffsetOnAxis(ap=idx_t[ex_][cb][:, :], axis=0)),
                        reads=['f_idx%d' % ex_] + ['h1b_%d' % ti for ti in range(NT)], writes=['f_xs%d' % cb])

                    def trx(e, cb=cb):
                        r = None
                        for kk in range(8):
                            r = e.transpose(out=k.bankT[:, kk * 128:(kk + 1) * 128], in_=xs[:, cb, kk * 128:(kk + 1) * 128], identity=k.identb[:])
                        return r
                    P.op('pe', trx, reads=['f_xs%d' % cb, 'identb'], writes=['bankT'])
                    P.op('dve', lambda e, cb=cb, xb=xb: e.tensor_copy(out=xsT[xb][:, :, cb * 128:(cb + 1) * 128],
                                                                    in_=k.bankT[:].rearrange("p (a b) -> p a b", a=8)),
                         reads=['bankT'], writes=['f_xsT%d_%d' % (xb, cb)])

            def cast_engine():
                ce = 'act' if (cnt['cast'] % 3 != 2) else 'dve'
                cnt['cast'] += 1
                return ce

            def load_gu_piece(g, pi):
                ex_, j = g // 4, g % 4
                wb = g % 2
                wsrc, wdst, wkey = ((k.w_gate, wgb[wb], 'f_wgb%d' % wb), (k.w_up, wub[wb], 'f_wub%d' % wb))[pi // 2]
                hh = pi % 2
                sg = cnt['stg'] % 3
                cnt['stg'] += 1
                off = ((l * NEXP + ex_) * D) * 2048 + j * 512 + hh * 256
                P.dma('sp', lambda e: e.dma_start(out=stgA[sg][:], in_=dap(wsrc, off, [[2048, 128], [128 * 2048, 8], [1, 256]])),
                      writes=['f_stg%d' % sg])
                ce = cast_engine()
                P.op(ce, lambda e: (e.copy if ce == 'act' else e.tensor_copy)(out=wdst[:, :, hh * 256:(hh + 1) * 256], in_=stgA[sg][:]),
                     reads=['f_stg%d' % sg], writes=[wkey + '_%d' % hh])

            def load_d(g):
                ex_, j = g // 4, g % 4
                for hh in range(2):
                    sg = cnt['stgB'] % 2
                    cnt['stgB'] += 1
                    off = ((l * NEXP + ex_) * 2048 + j * 512 + hh * 256) * D
                    P.dma('sp', lambda e, sg=sg, off=off: e.dma_start(
                        out=stgB[sg][:], in_=dap(k.w_down, off, [[D, 128], [128 * D, 2], [1, D]])),
                        writes=['f_stgB%d' % sg])
                    ce = cast_engine()
                    kt0 = j * 4 + hh * 2
                    P.op(ce, lambda e, sg=sg, kt0=kt0, ce=ce: (e.copy if ce == 'act' else e.tensor_copy)(
                        out=wdb[:, kt0:kt0 + 2, :], in_=stgB[sg][:]),
                        reads=['f_stgB%d' % sg], writes=['f_wdb%d' % (kt0 // 2)])

            def phase1(g, fis):
                ex_, j = g // 4, g % 4
                wb = g % 2
                xb = ex_ % 2
                xk = ['f_xsT%d_%d' % (xb, cb) for cb in range(4)]
                wgk = ['f_wgb%d_0' % wb, 'f_wgb%d_1' % wb]
                wuk = ['f_wub%d_0' % wb, 'f_wub%d_1' % wb]
                for fi in fis:
                    ftile = j * 4 + fi
                    g_b = 3 + (cnt['psg'] % 2)
                    u_b = 5 + (cnt['psg'] % 2)
                    sb_ = cnt['psg'] % 2
                    cnt['psg'] += 1

                    def mg(e, g_b=g_b, wb=wb, fi=fi, xb=xb):
                        r = None
                        for kk in range(8):
                            r = e.matmul(k.bank[g_b][:, :], lhsT=wgb[wb][:, kk, fi * 128:(fi + 1) * 128], rhs=xsT[xb][:, kk, :], start=(kk == 0), stop=(kk == 7))
                        return r

                    def mu(e, u_b=u_b, wb=wb, fi=fi, xb=xb):
                        r = None
                        for kk in range(8):
                            r = e.matmul(k.bank[u_b][:, :], lhsT=wub[wb][:, kk, fi * 128:(fi + 1) * 128], rhs=xsT[xb][:, kk, :], start=(kk == 0), stop=(kk == 7))
                        return r
                    P.op('pe', mg, reads=wgk + xk, writes=['bank%d' % g_b])
                    P.op('pe', mu, reads=wuk + xk, writes=['bank%d' % u_b])
                    P.op('act', lambda e, sb_=sb_, g_b=g_b: e.activation(out=sl[sb_][:], in_=k.bank[g_b][:, :], func=AF.Silu),
                         reads=['bank%d' % g_b], writes=['f_sl%d' % sb_])
                    P.op('dve', lambda e, sb_=sb_, u_b=u_b, ftile=ftile: e.tensor_tensor(out=hdn[:, ftile, :], in0=k.bank[u_b][:, :], in1=sl[sb_][:], op=ALU.mult),
                         reads=['bank%d' % u_b, 'f_sl%d' % sb_], writes=['f_hdn%d' % ftile])

            def phase2(ex_):
                hk = ['f_hdn%d' % i for i in range(16)]
                wdk = ['f_wdb%d' % i for i in range(8)]
                for cb in range(4):
                    ob = cnt['ost'] % 2
                    cnt['ost'] += 1
                    for half in range(2):
                        pb = cnt['ps2'] % 3
                        cnt['ps2'] += 1

                        def md(e, pb=pb, cb=cb, half=half):
                            r = None
                            for kk in range(16):
                                r = e.matmul(k.bank[pb][:, :], lhsT=hdn[:, kk, cb * 128:(cb + 1) * 128], rhs=wdb[:, kk, half * 512:(half + 1) * 512],
                                             start=(kk == 0), stop=(kk == 15))
                            return r
                        P.op('pe', md, reads=hk + wdk, writes=['bank%d' % pb])
                        P.op('dve', lambda e, pb=pb, ob=ob, half=half, cb=cb, ex_=ex_: e.tensor_scalar(
                            out=ost[ob][:, half * 512:(half + 1) * 512], in0=k.bank[pb][:, :], scalar1=gate_all[:, ex_, cb:cb + 1], scalar2=None, op0=ALU.mult),
                            reads=['bank%d' % pb, 'f_gate%d' % ex_], writes=['f_ost%d_%d' % (ob, half)])
                    P.dma('pool', lambda e, ob=ob, cb=cb, ex_=ex_: e.indirect_dma_start(
                        out=k.ffn[:, :], out_offset=bass.IndirectOffsetOnAxis(ap=idx_t[ex_][cb][:, :], axis=0), in_=ost[ob][:], in_offset=None,
                        compute_op=ALU.add),
                        reads=['f_ost%d_0' % ob, 'f_ost%d_1' % ob, 'f_idx%d' % ex_] + (['ffn_z%d' % ti for ti in range(NT)] if ex_ == 0 and cb == 0 else []),
                        writes=['ffn_acc'])
                    P._update(P.res['ffn_acc']['w'], ['f_ost%d_0' % ob, 'f_ost%d_1' % ob], [])

            NG = NEXP * 4
            gather(0)
            for pi in range(4):
                load_gu_piece(0, pi)
            load_d(0)
            for g in range(NG):
                ex_, j = g // 4, g % 4
                if j == 2 and ex_ + 1 < NEXP:
                    gather(ex_ + 1)
                for fi in range(4):
                    if g + 1 < NG:
                        load_gu_piece(g + 1, fi)
                    phase1(g, [fi])
                if j == 3:
                    phase2(ex_)
                if g + 1 < NG:
                    load_d(g + 1)
        P.barrier()
        g2 = t("g2", [128, D]); b2 = t("b2", [128, D])
        h1t = [t("h1%d" % i, [128, D]) for i in range(3)]
        ft_ = [t("ft%d" % i, [128, D]) for i in range(3)]
        rt = [t("r%d" % i, [128, D]) for i in range(3)]
        h2t = [t("h2%d" % i, [128, D]) for i in range(3)]
        scr = [dict(stats=t("stats%d" % i, [128, 2, 6]), mv=t("mv%d" % i, [128, 2]), sd=t("sd%d" % i, [128, 1]), rstd=t("rstd%d" % i, [128, 1]),
                    nb=t("nb%d" % i, [128, 1]), xn=t("xn%d" % i, [128, D])) for i in range(3)]
        P.dma('sp', lambda e: e.dma_start(out=g2[:], in_=dap(k.ln2_g, l * D, [[0, 128], [1, D]])), writes=['f_g2'])
        P.dma('sp', lambda e: e.dma_start(out=b2[:], in_=dap(k.ln2_b, l * D, [[0, 128], [1, D]])), writes=['f_b2'])

        def f6_load(ti):
            b = ti % 3
            P.dma('sp', lambda e: e.dma_start(out=h1t[b][:], in_=dap(k.h1, ti * 128 * D, [[D, 128], [1, D]])),
                  reads=['h1_%d' % ti], writes=['f_h1%d' % b])
            P.dma('sp', lambda e: e.dma_start(out=ft_[b][:], in_=dap(k.ffn, ti * 128 * D, [[D, 128], [1, D]])),
                  reads=['ffn_acc'], writes=['f_ft%d' % b])
        def f6_tile(ti):
            b = ti % 3
            tok0 = ti * 128
            P.op('dve', lambda e, b=b: e.scalar_tensor_tensor(out=rt[b][:], in0=h1t[b][:], scalar=ALPHA, in1=ft_[b][:], op0=ALU.mult, op1=ALU.add),
                 reads=['f_h1%d' % b, 'f_ft%d' % b], writes=['f_r%d' % b])
            yield
            yield from layer_norm_tile(k, rt[b][:], 'f_r%d' % b, h2t[b][:], 'f_h2%d' % b, g2[:], 'f_g2', b2[:], 'f_b2', 'f%d_' % b, scr[b])
            P.dma('sp', lambda e, b=b, tok0=tok0: e.dma_start(out=dap(dst, tok0 * D, [[D, 128], [1, D]]), in_=h2t[b][:]),
                  reads=['f_h2%d' % b], writes=['hA%d' % ti])

        run_window(f6_tile, f6_load, NT, 2)


def finish(k):
    P = k.P
    for i, c in enumerate(P.dcnt):
        if c > 0:
            P._wait('sp', (('d', i), c))


def prep_inputs(inputs):
    w_in = np.asarray(inputs["w_in"])
    sl = lambda a, b: list(range(a, b))
    tm_cols = sl(256, 512) + sl(512, 768) + sl(768, 1024) + sl(1920, 2048)
    fm_cols = sl(0, 256) + sl(256, 512) + sl(1024, 1280) + sl(1280, 1792) + sl(1792, 1920)
    w_perm = np.ascontiguousarray(w_in[:, :, tm_cols + fm_cols])
    shared = {"ln_in_g": np.ascontiguousarray(inputs["ln_in_g"]), "ln_in_b": np.ascontiguousarray(inputs["ln_in_b"]),
              "w_in": w_perm,
              "ret_theta": np.ascontiguousarray(np.asarray(inputs["ret_theta"]).reshape(DEPTH, 8)),
              "attn_sink": np.ascontiguousarray(inputs["attn_sink"]),
              "w_out": np.ascontiguousarray(inputs["w_out"]),
              "ln1_g": np.ascontiguousarray(inputs["ln1_g"]), "ln1_b": np.ascontiguousarray(inputs["ln1_b"]),
              "ln2_g": np.ascontiguousarray(inputs["ln2_g"]), "ln2_b": np.ascontiguousarray(inputs["ln2_b"])}
    L = DEPTH
    A = lambda n: np.asarray(inputs[n], dtype=np.float32)
    rl = lambda x: np.ascontiguousarray(x.reshape(L, 2, 8, 2, 64).transpose(0, 3, 4, 2, 1).reshape(L, 128, 16))
    shared["s_lre"] = rl(A("ssm_lambda_re")); shared["s_lim"] = rl(A("ssm_lambda_im"))
    lsx = A("ssm_log_step").reshape(L, 2, 8, 2).transpose(0, 3, 2, 1)
    shared["s_ls"] = np.ascontiguousarray(np.broadcast_to(lsx[:, :, None, :, :], (L, 2, 64, 8, 2)).reshape(L, 128, 16))
    rb = lambda x: np.ascontiguousarray(x.reshape(L, 8, 2, 64, 16).transpose(0, 2, 3, 1, 4).reshape(L, 128, 8, 16))
    shared["s_bre"] = rb(A("ssm_b_re")); shared["s_bim"] = rb(A("ssm_b_im"))
    rc = lambda x: np.ascontiguousarray(x.reshape(L, 2, 8, 2, 16, 64).transpose(0, 3, 5, 2, 1, 4).reshape(L, 128, 16, 16))
    shared["s_cre"] = rc(A("ssm_c_re")); shared["s_cim"] = rc(A("ssm_c_im"))
    rd = lambda x: np.ascontiguousarray(x.reshape(L, 2, 128).transpose(0, 2, 1))
    shared["s_d"] = rd(A("ssm_d")); shared["s_bglu"] = rd(A("ssm_b_glu"))
    shared["s_wglu"] = np.ascontiguousarray(A("ssm_w_glu"))
    shared["router_w"] = np.ascontiguousarray(A("router_w"))
    shared["exp_w_gate"] = np.ascontiguousarray(A("exp_w_gate"))
    shared["exp_w_up"] = np.ascontiguousarray(A("exp_w_up"))
    shared["exp_w_down"] = np.ascontiguousarray(A("exp_w_down"))
    return shared


def kernel(**inputs):
    shared = prep_inputs(inputs)
    nc = build()
    x = np.asarray(inputs["x"])
    in_maps = []
    for c in range(NCORES):
        m = dict(shared)
        m["x"] = np.ascontiguousarray(x[c])
        in_maps.append(m)
    res = run_bass_kernel_spmd(nc, in_maps, core_ids=list(range(NCORES)))
    return np.stack([res.results[c]["out"] for c in range(NCORES)], axis=0)
```

```python
import math
import contextlib
import numpy as np
import concourse.bass as bass
import concourse.mybir as mybir
from concourse.bass_utils import run_bass_kernel_spmd

F32 = mybir.dt.float32
BF16 = mybir.dt.bfloat16
I32 = mybir.dt.int32
AF = mybir.ActivationFunctionType
ALU = mybir.AluOpType
AX = mybir.AxisListType

T = 4096
NT = T // 128
D = 1024
DEPTH = 2
ALPHA = (2.0 * DEPTH) ** 0.25
EPS = 1e-5
N_TM = 896
N_FM = 1408
NCORES = 4
NBA = 3

SAME_ENGINE_SYNC = True
DBG_O = 9
DBG_F = 9
DBG_CAST = 'mix'
DBG_NT = NT


class Prog:
    def __init__(self, nc, stack, ndma=32):
        self.nc = nc
        self.eng = {'pe': nc.tensor, 'act': nc.scalar, 'dve': nc.vector, 'pool': nc.gpsimd, 'sp': nc.sync}
        self.sem = {e: stack.enter_context(nc.semaphore('sem_' + e)) for e in ['pe', 'act', 'dve', 'pool']}
        self.cnt = {e: 0 for e in self.sem}
        self.nhw, self.nsw = ndma, 8
        self.dsem = [stack.enter_context(nc.semaphore('dsem%d' % i)) for i in range(self.nhw + self.nsw)]
        self.dcnt = [0] * (self.nhw + self.nsw)
        self.dnext = 0
        self.dnext_sw = 0
        self.waited = {e: {} for e in self.eng}
        self.res = {}
        self.nops = 0
        self.recent = {e: [] for e in self.eng}
        self.log = {e: [] for e in self.eng}

    def _semof(self, key):
        return self.sem[key] if isinstance(key, str) else self.dsem[key[1]]

    def _wait(self, e, tok):
        key, val = tok
        if self.waited[e].get(key, 0) >= val:
            return
        self.eng[e].wait_ge(self._semof(key), val)
        self.waited[e][key] = val
        self.log[e].append(('wait', key, val))

    def _deps(self, reads, writes):
        deps = []
        for r in reads:
            st = self.res.get(r)
            if st and st['w']:
                deps.append(st['w'])
        for w in writes:
            st = self.res.get(w)
            if st:
                if st['w']:
                    deps.append(st['w'])
                deps.extend(st['r'].items())
        return deps

    def _update(self, tok, reads, writes):
        for r in reads:
            st = self.res.setdefault(r, {'w': None, 'r': {}})
            if st['r'].get(tok[0], 0) < tok[1]:
                st['r'][tok[0]] = tok[1]
        for w in writes:
            self.res[w] = {'w': tok, 'r': {}}

    def op(self, e, fn, reads=(), writes=()):
        for tok in self._deps(reads, writes):
            if tok[0] == e and (e == 'pe' or not SAME_ENGINE_SYNC):
                continue
            self._wait(e, tok)
        inst = fn(self.eng[e])
        self.cnt[e] += 1
        inst.then_inc(self.sem[e], 1)
        self.log[e].append(('inc', e, 1))
        tok = (e, self.cnt[e])
        self._update(tok, reads, writes)
        self.nops += 1
        return tok

    def dma(self, q, fn, reads=(), writes=()):
        if q == 'pool':
            i = self.nhw + self.dnext_sw
            self.dnext_sw = (self.dnext_sw + 1) % self.nsw
        else:
            i = self.dnext
            self.dnext = (i + 1) % self.nhw
        deps = self._deps(reads, writes)
        if self.dcnt[i] > 0:
            deps.append((('d', i), self.dcnt[i]))
        rq = self.recent[q]
        if len(rq) >= 10:
            deps.append(rq.pop(0))
        for tok in deps:
            self._wait(q, tok)
        inst = fn(self.eng[q])
        self.dcnt[i] += 16
        inst.then_inc(self.dsem[i], 16)
        self.log[q].append(('inc', ('d', i), 16))
        tok = (('d', i), self.dcnt[i])
        self.recent[q].append(tok)
        self._update(tok, reads, writes)
        self.nops += 1
        return tok

    def barrier(self):
        toks = [(e, c) for e, c in self.cnt.items() if c > 0]
        toks += [(('d', i), c) for i, c in enumerate(self.dcnt) if c > 0]
        for e in self.eng:
            for tok in toks:
                self._wait(e, tok)

    def wait_all(self, e, keys):
        for r in keys:
            st = self.res.get(r)
            if st and st['w']:
                self._wait(e, st['w'])


def run_window(tile_gen, load, n, width):
    load(0)
    gens = []
    nxt = 0
    while nxt < n or gens:
        while len(gens) < width and nxt < n:
            if nxt + 1 < n:
                load(nxt + 1)
            gens.append(tile_gen(nxt))
            nxt += 1
        for g_ in list(gens):
            try:
                next(g_)
            except StopIteration:
                gens.remove(g_)


def dap(h, off, dims):
    return bass.AP(h, off, [list(d) for d in dims])


class K:
    pass


def build(debug=None, nlayers=DEPTH, stop_after=None, ext_in=None, phases=None):
    nc = bass.Bass("TRN2", target_bir_lowering=False)
    k = K()
    k.nc = nc
    k.debug = debug or []
    k.ext_in = ext_in or []
    phases = phases or ['a', 'r', 's', 't', 'o', 'f']

    def din(name, shape, dt=F32):
        return nc.dram_tensor(name, list(shape), dt, kind="ExternalInput")

    def dscr(name, shape, dt):
        kind = "ExternalOutput" if name in k.debug else ("ExternalInput" if name in k.ext_in else "Internal")
        return nc.dram_tensor(name, list(shape), dt, kind=kind)

    k.x = din("x", [T, D])
    k.ln_in_g = din("ln_in_g", [D]); k.ln_in_b = din("ln_in_b", [D])
    k.w_in = din("w_in", [DEPTH, D, 2304])
    k.out = nc.dram_tensor("out", [T, D], F32, kind="ExternalOutput")
    k.hA = dscr("hA", [T, D], F32)
    k.tm = dscr("tm", [T, N_TM], BF16)
    k.fm = dscr("fm", [N_FM, T], BF16)
    k.ycat = dscr("ycat", [T, 768], BF16)
    k.yssmT = dscr("yssmT", [256, T], BF16)
    k.ret_theta = din("ret_theta", [DEPTH, 8])
    k.attn_sink = din("attn_sink", [DEPTH, 8])
    k.w_out = din("w_out", [DEPTH, D, D])
    k.ln1_g = din("ln1_g", [DEPTH, D]); k.ln1_b = din("ln1_b", [DEPTH, D])
    k.ln2_g = din("ln2_g", [DEPTH, D]); k.ln2_b = din("ln2_b", [DEPTH, D])
    k.s_lre = din("s_lre", [DEPTH, 128, 16]); k.s_lim = din("s_lim", [DEPTH, 128, 16]); k.s_ls = din("s_ls", [DEPTH, 128, 16])
    k.s_bre = din("s_bre", [DEPTH, 128, 8, 16]); k.s_bim = din("s_bim", [DEPTH, 128, 8, 16])
    k.s_cre = din("s_cre", [DEPTH, 128, 16, 16]); k.s_cim = din("s_cim", [DEPTH, 128, 16, 16])
    k.s_d = din("s_d", [DEPTH, 128, 2]); k.s_bglu = din("s_bglu", [DEPTH, 128, 2])
    k.s_wglu = din("s_wglu", [DEPTH, 256, 256])
    k.yf = dscr("yf", [256, T], F32)
    k.router_w = din("router_w", [DEPTH, D, 16])
    k.w_gate = din("exp_w_gate", [DEPTH, 16, D, 2048])
    k.w_up = din("exp_w_up", [DEPTH, 16, D, 2048])
    k.w_down = din("exp_w_down", [DEPTH, 16, 2048, D])
    k.ffn = dscr("ffn", [T, D], F32)
    if DBG_F == 1:
        k.dbg_idx = nc.dram_tensor("dbg_idx", [128, 64], I32, kind="ExternalOutput")
        k.dbg_gate = nc.dram_tensor("dbg_gate", [128, 64], F32, kind="ExternalOutput")
        k.dbg_aff = nc.dram_tensor("dbg_aff", [128, 512], F32, kind="ExternalOutput")
        k.dbg_posm = nc.dram_tensor("dbg_posm", [128, 512], F32, kind="ExternalOutput")
    k.h1 = dscr("h1", [T, D], F32)
    k.h1b = dscr("h1b", [T, D], BF16)

    with contextlib.ExitStack() as st:
        P = Prog(nc, st)
        k.P = P
        k.st = st
        k.bank = [st.enter_context(nc.psum_tensor("bank%d" % i, [128, 512], F32)) for i in range(7)]
        k.bankT = st.enter_context(nc.psum_tensor("bankT", [128, 1024], BF16))
        setup_consts(k)
        for l in range(nlayers):
            if 'a' in phases:
                phase_a(k, l)
            if 'r' in phases:
                phase_r(k, l)
            if 't' in phases:
                phase_t(k, l)
            if 's' in phases:
                phase_s(k, l)
            if 'o' in phases:
                phase_o(k, l)
            if 'f' in phases:
                phase_f(k, l, last=(l == nlayers - 1))
        finish(k)
    nc._prog = P
    return nc


def sb(k, name, shape, dt, stack=None):
    k.nsb = getattr(k, 'nsb', 0) + 1
    return (stack or k.st).enter_context(k.nc.sbuf_tensor("%s_u%d" % (name, k.nsb), list(shape), dt))


def setup_consts(k):
    nc, P = k.nc, k.P
    k.identb = sb(k, "identb", [128, 128], BF16)
    k.identf = sb(k, "identf", [128, 128], F32)
    k.iota_i = sb(k, "iota_i", [128, 128], I32)
    k.dist = sb(k, "dist", [128, 128], F32)
    P.op('pool', lambda e: e.iota(k.iota_i[:], [[1, 128]], base=0, channel_multiplier=-1),
         writes=['iota_i'])
    P.op('dve', lambda e: e.tensor_copy(out=k.dist[:], in_=k.iota_i[:]), reads=['iota_i'], writes=['dist'])
    P.op('dve', lambda e: e.tensor_scalar(out=k.identf[:], in0=k.dist[:], scalar1=0.0, scalar2=None,
                                           op0=ALU.is_equal), reads=['dist'], writes=['identf'])
    P.op('dve', lambda e: e.tensor_copy(out=k.identb[:], in_=k.identf[:]), reads=['identf'], writes=['identb'])
    k.one_t = sb(k, "one_t", [128, 1], F32)
    P.op('dve', lambda e: e.memset(k.one_t[:], 1.0), writes=['one_t'])
    k.lnk_t = sb(k, "lnk_t", [128, 1], F32)
    P.op('dve', lambda e: e.memset(k.lnk_t[:], math.log(0.125)), writes=['lnk_t'])
    k.irow_i = sb(k, "irow_i", [128, 128], I32)
    k.irow = sb(k, "irow", [128, 128], F32)
    P.op('pool', lambda e: e.iota(k.irow_i[:], [[1, 128]], base=0, channel_multiplier=0), writes=['irow_i'])
    P.op('dve', lambda e: e.tensor_copy(out=k.irow[:], in_=k.irow_i[:]), reads=['irow_i'], writes=['irow'])
    k.relp = sb(k, "relp", [128, 128], F32)
    k.reln = sb(k, "reln", [128, 128], F32)
    P.op('dve', lambda e: e.tensor_scalar(out=k.relp[:], in0=k.dist[:], scalar1=0.0, scalar2=None, op0=ALU.max),
         reads=['dist'], writes=['relp'])
    P.op('dve', lambda e: e.tensor_scalar(out=k.reln[:], in0=k.dist[:], scalar1=-1.0, scalar2=0.0, op0=ALU.mult, op1=ALU.max),
         reads=['dist'], writes=['reln'])
    k.eps_t = sb(k, "eps_t", [128, 1], F32)
    P.op('dve', lambda e: e.memset(k.eps_t[:], EPS), writes=['eps_t'])


def layer_norm_tile(k, x_ap, xkey, out_ap, outkey, g_ap, gkey, b_ap, bkey, tag, scratch):
    P = k.P
    stats, mv, sd, rstd, nb, xn = (scratch[n] for n in ('stats', 'mv', 'sd', 'rstd', 'nb', 'xn'))
    s = tag
    for hh in range(2):
        P.op('dve', lambda e, hh=hh: e.bn_stats(out=stats[:, hh, :], in_=x_ap[:, hh * 512:(hh + 1) * 512]),
             reads=[xkey], writes=[s + 'stats%d' % hh])
        yield
    P.op('dve', lambda e: e.bn_aggr(out=mv[:], in_=stats[:].rearrange("p a b -> p (a b)")),
         reads=[s + 'stats0', s + 'stats1'], writes=[s + 'mv'])
    yield
    P.op('act', lambda e: e.activation(out=sd[:], in_=mv[:, 1:2], func=AF.Sqrt, bias=k.eps_t[:], scale=1.0),
         reads=[s + 'mv'], writes=[s + 'sd'])
    yield
    P.op('dve', lambda e: e.reciprocal(out=rstd[:], in_=sd[:]), reads=[s + 'sd'], writes=[s + 'rstd'])
    yield
    P.op('dve', lambda e: e.scalar_tensor_tensor(out=nb[:], in0=mv[:, 0:1], scalar=-1.0, in1=rstd[:],
                                                  op0=ALU.mult, op1=ALU.mult),
         reads=[s + 'mv', s + 'rstd'], writes=[s + 'nb'])
    yield
    P.op('act', lambda e: e.activation(out=xn[:], in_=x_ap, func=AF.Identity, bias=nb[:], scale=rstd[:]),
         reads=[xkey, s + 'nb', s + 'rstd'], writes=[s + 'xn'])
    yield
    P.op('dve', lambda e: e.tensor_tensor(out=xn[:], in0=xn[:], in1=g_ap, op=ALU.mult),
         reads=[s + 'xn', gkey], writes=[s + 'xn'])
    yield
    P.op('dve', lambda e: e.tensor_tensor(out=out_ap, in0=xn[:], in1=b_ap, op=ALU.add),
         reads=[s + 'xn', bkey], writes=[outkey])
    yield


def phase_a(k, l):
    nc, P = k.nc, k.P
    P.barrier()
    with contextlib.ExitStack() as ps:
        if l == 0:
            k.g_in = sb(k, "g_in", [128, D], F32, ps)
            k.b_in = sb(k, "b_in", [128, D], F32, ps)
            P.dma('sp', lambda e: e.dma_start(out=k.g_in[:], in_=dap(k.ln_in_g, 0, [[0, 128], [1, D]])), writes=['g_in'])
            P.dma('sp', lambda e: e.dma_start(out=k.b_in[:], in_=dap(k.ln_in_b, 0, [[0, 128], [1, D]])), writes=['b_in'])
        Wb = sb(k, "a_Wb", [128, 8, 2304], BF16, ps)
        Wst = [sb(k, "a_Wst%d" % i, [128, 8, 256], F32, ps) for i in range(2)]
        xt = [sb(k, "a_x%d" % i, [128, D], F32, ps) for i in range(NBA)]
        ht = [sb(k, "a_h%d" % i, [128, D], F32, ps) for i in range(NBA)]
        hb = [sb(k, "a_hb%d" % i, [128, D], BF16, ps) for i in range(NBA)]
        hT = [sb(k, "a_hT%d" % i, [128, 8, 512], BF16, ps) for i in range(2)]
        tmst = [sb(k, "a_tmst%d" % i, [128, N_TM], BF16, ps) for i in range(NBA)]
        fmst = [sb(k, "a_fmst%d" % i, [128, 512], BF16, ps) for i in range(3)]
        scr = [dict(stats=sb(k, "a_stats%d" % i, [128, 2, 6], F32, ps), mv=sb(k, "a_mv%d" % i, [128, 2], F32, ps),
                    sd=sb(k, "a_sd%d" % i, [128, 1], F32, ps), rstd=sb(k, "a_rstd%d" % i, [128, 1], F32, ps),
                    nb=sb(k, "a_nb%d" % i, [128, 1], F32, ps), xn=sb(k, "a_xn%d" % i, [128, D], F32, ps))
               for i in range(NBA)]
        for c in range(9):
            w = Wst[c % 2]
            wk = 'a_Wst%d' % (c % 2)
            P.dma('sp', lambda e, w=w, c=c: e.dma_start(
                out=w[:], in_=dap(k.w_in, l * D * 2304 + c * 256, [[2304, 128], [128 * 2304, 8], [1, 256]])),
                writes=[wk])
            eng = 'act' if c % 2 == 0 else 'dve'
            if eng == 'act':
                P.op('act', lambda e, w=w, c=c: e.copy(out=Wb[:, :, c * 256:(c + 1) * 256], in_=w[:]),
                     reads=[wk], writes=['a_Wb%d' % c])
            else:
                P.op('dve', lambda e, w=w, c=c: e.tensor_copy(out=Wb[:, :, c * 256:(c + 1) * 256], in_=w[:]),
                     reads=[wk], writes=['a_Wb%d' % c])
        Wkeys = ['a_Wb%d' % c for c in range(9)]
        nb = 0

        def a_load(ti):
            b = ti % NBA
            if l == 0:
                P.dma('sp', lambda e: e.dma_start(out=xt[b][:], in_=dap(k.x, ti * 128 * D, [[D, 128], [1, D]])), writes=['a_x%d' % b])
            else:
                P.dma('sp', lambda e: e.dma_start(out=ht[b][:], in_=dap(k.hA, ti * 128 * D, [[D, 128], [1, D]])),
                      reads=['hA%d' % ti], writes=['a_h%d' % b])
        for stile in range(T // 512):
            hTs = hT[stile % 2]
            hTk = 'a_hT%d' % (stile % 2)
            for s in range(4):
                ti = stile * 4 + s
                b = ti % NBA
                tok0 = ti * 128
                if ti == 0:
                    a_load(0)
                if ti + 1 < NT:
                    a_load(ti + 1)
                if l == 0:
                    for _ in layer_norm_tile(k, xt[b][:], 'a_x%d' % b, ht[b][:], 'a_h%d' % b, k.g_in[:], 'g_in', k.b_in[:], 'b_in',
                                             'a%d_' % b, scr[b]):
                        pass
                    P.dma('sp', lambda e, b=b, tok0=tok0: e.dma_start(out=dap(k.hA, tok0 * D, [[D, 128], [1, D]]), in_=ht[b][:]),
                          reads=['a_h%d' % b], writes=['hA%d' % ti])
                P.op('act', lambda e, b=b: e.copy(out=hb[b][:], in_=ht[b][:]), reads=['a_h%d' % b], writes=['a_hb%d' % b])

                def tr(e, b=b):
                    r = None
                    for kk in range(8):
                        r = e.transpose(out=k.bankT[:, kk * 128:(kk + 1) * 128], in_=hb[b][:, kk * 128:(kk + 1) * 128],
                                        identity=k.identb[:])
                    return r
                P.op('pe', tr, reads=['a_hb%d' % b, 'identb'], writes=['bankT'])
                P.op('dve', lambda e, s=s, hTs=hTs: e.tensor_copy(
                    out=hTs[:, :, s * 128:(s + 1) * 128], in_=k.bankT[:].rearrange("p (a b) -> p a b", a=8)),
                    reads=['bankT'], writes=[hTk + '_%d' % s])
                for gi, (c0, c1) in enumerate([(0, 512), (512, N_TM)]):
                    bk = nb % 6
                    nb += 1

                    def mm(e, bk=bk, c0=c0, c1=c1, s=s, hTs=hTs):
                        r = None
                        for kk in range(8):
                            r = e.matmul(k.bank[bk][:, 0:c1 - c0], lhsT=hTs[:, kk, s * 128:(s + 1) * 128],
                                         rhs=Wb[:, kk, c0:c1], start=(kk == 0), stop=(kk == 7))
                        return r
                    P.op('pe', mm, reads=[hTk + '_%d' % s] + Wkeys, writes=['bank%d' % bk])
                    P.op('act', lambda e, bk=bk, c0=c0, c1=c1, b=b: e.copy(out=tmst[b][:, c0:c1], in_=k.bank[bk][:, 0:c1 - c0]),
                         reads=['bank%d' % bk], writes=['a_tmst%d_%d' % (b, gi)])
                P.dma('sp', lambda e, b=b, tok0=tok0: e.dma_start(out=dap(k.tm, tok0 * N_TM, [[N_TM, 128], [1, N_TM]]), in_=tmst[b][:]),
                      reads=['a_tmst%d_0' % b, 'a_tmst%d_1' % b], writes=['tm%d' % ti])
                P._update(P.res['tm%d' % ti]['w'], ['a_tmst%d_0' % b, 'a_tmst%d_1' % b], [])
            for fg in range(11):
                bk = nb % 6
                nb += 1
                fb = fg % 3

                def mmf(e, bk=bk, fg=fg, hTs=hTs):
                    r = None
                    for kk in range(8):
                        r = e.matmul(k.bank[bk][:, :], lhsT=Wb[:, kk, N_TM + fg * 128:N_TM + (fg + 1) * 128],
                                     rhs=hTs[:, kk, :], start=(kk == 0), stop=(kk == 7))
                    return r
                P.op('pe', mmf, reads=[hTk + '_%d' % s for s in range(4)] + Wkeys, writes=['bank%d' % bk])
                eng = 'dve' if fg % 2 == 0 else 'act'
                if eng == 'dve':
                    P.op('dve', lambda e, bk=bk, fb=fb: e.tensor_copy(out=fmst[fb][:], in_=k.bank[bk][:, :]),
                         reads=['bank%d' % bk], writes=['a_fmst%d' % fb])
                else:
                    P.op('act', lambda e, bk=bk, fb=fb: e.copy(out=fmst[fb][:], in_=k.bank[bk][:, :]),
                         reads=['bank%d' % bk], writes=['a_fmst%d' % fb])
                P.dma('sp', lambda e, fb=fb, fg=fg, stile=stile: e.dma_start(
                    out=dap(k.fm, fg * 128 * T + stile * 512, [[T, 128], [1, 512]]), in_=fmst[fb][:]),
                    reads=['a_fmst%d' % fb], writes=['fm%d_%d' % (fg, stile)])
                P._update(P.res['fm%d_%d' % (fg, stile)]['w'], ['a_fmst%d' % fb], [])


def bc(tile, col, n, pstride, parts=128):
    return dap(tile, col, [[pstride, parts], [0, n]])


def phase_r(k, l):
    nc, P = k.nc, k.P
    P.barrier()
    with contextlib.ExitStack() as ps:
        ktm = sb(k, "r_ktm", [128, NT, 256], BF16, ps)
        vtm = sb(k, "r_vtm", [128, NT, 256], BF16, ps)
        gtm = sb(k, "r_gtm", [128, NT, 256], BF16, ps)
        Sf = sb(k, "r_Sf", [64, NT, 256], BF16, ps)
        Sb_ = sb(k, "r_Sb", [64, NT, 256], BF16, ps)
        stt = [sb(k, "r_st%d" % i, [64, 256], F32, ps) for i in range(2)]
        qT = [sb(k, "r_qT%d" % i, [64, 4, 512], BF16, ps) for i in range(2)]
        kT = [sb(k, "r_kT%d" % i, [64, 4, 512], BF16, ps) for i in range(2)]
        th = sb(k, "r_th", [128, 8], F32, ps)
        lg = sb(k, "r_lg", [128, 8], F32, ps)
        dec = sb(k, "r_dec", [128, 8], F32, ps)
        wcol = sb(k, "r_wcol", [128, 8], F32, ps)
        pidx = sb(k, "r_pidx", [128, 2], F32, ps)
        tmpa = sb(k, "r_tmpa", [128, 128], F32, ps)
        tmpb = sb(k, "r_tmpb", [128, 128], F32, ps)
        dmT = sb(k, "r_dmT", [128, 4, 128], F32, ps)
        WF = sb(k, "r_WF", [128, 256], F32, ps)
        WB = sb(k, "r_WB", [128, 256], F32, ps)
        qsf = sb(k, "r_qsf", [128, 4, 128], F32, ps)
        qsb = sb(k, "r_qsb", [128, 4, 128], F32, ps)
        irf = sb(k, "r_irf", [128, 128], F32, ps)
        irb = sb(k, "r_irb", [128, 128], F32, ps)
        kw = [sb(k, "r_kw%d" % i, [128, 256], BF16, ps) for i in range(2)]
        sTm = [sb(k, "r_sTm%d" % i, [128, 512], BF16, ps) for i in range(2)]
        qf = [sb(k, "r_qf%d" % i, [64, 4, 128], BF16, ps) for i in range(2)]
        qb = [sb(k, "r_qb%d" % i, [64, 4, 128], BF16, ps) for i in range(2)]
        o = [sb(k, "r_o%d" % i, [128, 256], F32, ps) for i in range(2)]
        sq = [sb(k, "r_sq%d" % i, [128, 256], F32, ps) for i in range(2)]
        sg = [sb(k, "r_sg%d" % i, [128, 256], F32, ps) for i in range(2)]
        on = [sb(k, "r_on%d" % i, [128, 256], F32, ps) for i in range(2)]
        yst = [sb(k, "r_yst%d" % i, [128, 256], BF16, ps) for i in range(2)]
        sm = [dict((n, sb(k, "r_%s%d" % (n, i), [128, 4], F32, ps)) for n in ('s1', 's2', 'mean', 'msq', 'var', 'sd', 'rstd'))
              for i in range(2)]

        for name, tl, c0 in (('r_ktm', ktm, 0), ('r_vtm', vtm, 256), ('r_gtm', gtm, 512)):
            for half in range(2):
                P.dma('sp', lambda e, tl=tl, c0=c0, half=half: e.dma_start(
                    out=tl[:, half * 16:(half + 1) * 16, :],
                    in_=dap(k.tm, half * 16 * 128 * N_TM + c0, [[N_TM, 128], [128 * N_TM, 16], [1, 256]])),
                    reads=['tm%d' % ti for ti in range(half * 16, half * 16 + 16)], writes=['%s_%d' % (name, half)])
        P.dma('sp', lambda e: e.dma_start(out=th[:], in_=dap(k.ret_theta, l * 8, [[0, 128], [1, 8]])), writes=['r_th'])
        P.op('act', lambda e: e.activation(out=lg[:], in_=th[:], func=AF.Exp, scale=-1.0), reads=['r_th'], writes=['r_lg'])
        P.op('act', lambda e: e.activation(out=lg[:], in_=lg[:], func=AF.Ln, bias=k.one_t[:], scale=1.0),
             reads=['r_lg', 'one_t'], writes=['r_lg'])
        P.op('dve', lambda e: e.tensor_scalar(out=lg[:], in0=lg[:], scalar1=-1.0, scalar2=None, op0=ALU.mult),
             reads=['r_lg'], writes=['r_lg'])
        P.op('act', lambda e: e.activation(out=dec[:], in_=lg[:], func=AF.Exp, scale=128.0), reads=['r_lg'], writes=['r_dec'])
        P.op('dve', lambda e: e.tensor_scalar(out=pidx[:, 0:1], in0=k.dist[:, 0:1], scalar1=127.0, scalar2=None, op0=ALU.add),
             reads=['dist'], writes=['r_pidx0'])
        P.op('dve', lambda e: e.tensor_scalar(out=pidx[:, 1:2], in0=k.dist[:, 0:1], scalar1=-1.0, scalar2=None, op0=ALU.mult),
             reads=['dist'], writes=['r_pidx1'])
        P.op('dve', lambda e: e.tensor_scalar(out=irf[:], in0=k.irow[:], scalar1=1.0, scalar2=None, op0=ALU.add),
             reads=['irow'], writes=['r_irf'])
        P.op('dve', lambda e: e.tensor_scalar(out=irb[:], in0=k.irow[:], scalar1=-1.0, scalar2=128.0, op0=ALU.mult, op1=ALU.add),
             reads=['irow'], writes=['r_irb'])
        for h in range(4):
            P.op('act', lambda e, h=h: e.activation(out=wcol[:, h:h + 1], in_=pidx[:, 0:1], func=AF.Exp, bias=k.lnk_t[:], scale=lg[:, h:h + 1]),
                 reads=['r_pidx0', 'r_lg', 'lnk_t'], writes=['r_wcol%d' % h])
            P.op('act', lambda e, h=h: e.activation(out=wcol[:, 4 + h:5 + h], in_=pidx[:, 1:2], func=AF.Exp, bias=k.lnk_t[:], scale=lg[:, 4 + h:5 + h]),
                 reads=['r_pidx1', 'r_lg', 'lnk_t'], writes=['r_wcol%d' % (4 + h)])
            P.op('dve', lambda e, h=h: e.tensor_copy(out=WF[:, h * 64:(h + 1) * 64], in_=bc(wcol, h, 64, 8)),
                 reads=['r_wcol%d' % h], writes=['r_WF%d' % h])
            P.op('dve', lambda e, h=h: e.tensor_copy(out=WB[:, h * 64:(h + 1) * 64], in_=bc(wcol, 4 + h, 64, 8)),
                 reads=['r_wcol%d' % (4 + h)], writes=['r_WB%d' % h])
            P.op('dve', lambda e, h=h: e.tensor_scalar(out=tmpa[:], in0=k.relp[:], scalar1=lg[:, h:h + 1], scalar2=None, op0=ALU.mult),
                 reads=['relp', 'r_lg'], writes=['r_tmpa'])
            P.op('dve', lambda e, h=h: e.scalar_tensor_tensor(out=tmpb[:], in0=k.reln[:], scalar=lg[:, 4 + h:5 + h], in1=tmpa[:],
                                                               op0=ALU.mult, op1=ALU.add),
                 reads=['reln', 'r_lg', 'r_tmpa'], writes=['r_tmpb'])
            P.op('act', lambda e, h=h: e.activation(out=dmT[:, h, :], in_=tmpb[:], func=AF.Exp, bias=k.lnk_t[:], scale=1.0),
                 reads=['r_tmpb', 'lnk_t'], writes=['r_dmT%d' % h])
            P.op('act', lambda e, h=h: e.activation(out=qsf[:, h, :], in_=irf[:], func=AF.Exp, scale=lg[:, h:h + 1]),
                 reads=['r_irf', 'r_lg'], writes=['r_qsf%d' % h])
            P.op('act', lambda e, h=h: e.activation(out=qsb[:, h, :], in_=irb[:], func=AF.Exp, scale=lg[:, 4 + h:5 + h]),
                 reads=['r_irb', 'r_lg'], writes=['r_qsb%d' % h])
        WFk = ['r_WF%d' % h for h in range(4)]
        WBk = ['r_WB%d' % h for h in range(4)]
        def r_pass1(d):
            W = WF if d == 0 else WB
            Wk = WFk if d == 0 else WBk
            Sall = Sf if d == 0 else Sb_
            stn = 'r_st%d' % d
            P.op('dve', lambda e, d=d: e.memset(stt[d][:], 0.0), writes=[stn])
            yield
            order = range(NT) if d == 0 else range(NT - 1, -1, -1)
            kb = d
            bk = d
            for n, c in enumerate(order):
                half = c // 16
                P.op('act', lambda e, c=c, Sall=Sall, d=d: e.copy(out=Sall[:, c, :], in_=stt[d][:]),
                     reads=[stn], writes=['r_S%d_%d' % (d, c)])
                yield
                P.op('pool', lambda e, c=c, kb=kb, W=W: e.tensor_tensor(out=kw[kb][:], in0=ktm[:, c, :], in1=W[:], op=ALU.mult),
                     reads=['r_ktm_%d' % half] + Wk, writes=['r_kw%d' % kb])
                yield

                def mmkv(e, c=c, kb=kb, bk=bk):
                    r = None
                    for h in range(4):
                        r = e.matmul(k.bank[bk][0:64, h * 64:(h + 1) * 64], lhsT=kw[kb][:, h * 64:(h + 1) * 64],
                                     rhs=vtm[:, c, h * 64:(h + 1) * 64], start=True, stop=True)
                    return r
                P.op('pe', mmkv, reads=['r_kw%d' % kb, 'r_vtm_%d' % half], writes=['bank%d' % bk])
                yield
                for h in range(4):
                    P.op('dve', lambda e, h=h, d=d, bk=bk: e.scalar_tensor_tensor(
                        out=stt[d][:, h * 64:(h + 1) * 64], in0=stt[d][:, h * 64:(h + 1) * 64], scalar=dec[0:64, 4 * d + h:4 * d + h + 1],
                        in1=k.bank[bk][0:64, h * 64:(h + 1) * 64], op0=ALU.mult, op1=ALU.add),
                        reads=[stn, 'r_dec', 'bank%d' % bk], writes=[stn])
                    yield
        gens = [r_pass1(0), r_pass1(1)]
        while gens:
            for g_ in list(gens):
                try:
                    next(g_)
                except StopIteration:
                    gens.remove(g_)
        for c in range(NT):
            blk, cc = c // 4, c % 4
            half = c // 16
            qb_i = blk % 2
            b = c % 2
            if cc == 0:
                P.dma('sp', lambda e, blk=blk, qb_i=qb_i: e.dma_start(
                    out=qT[qb_i][:], in_=dap(k.fm, blk * 512, [[T, 64], [64 * T, 4], [1, 512]])),
                    reads=['fm%d_%d' % (fg, blk) for fg in (0, 1)], writes=['r_qT%d' % qb_i])
                P.dma('sp', lambda e, blk=blk, qb_i=qb_i: e.dma_start(
                    out=kT[qb_i][:], in_=dap(k.fm, 256 * T + blk * 512, [[T, 64], [64 * T, 4], [1, 512]])),
                    reads=['fm%d_%d' % (fg, blk) for fg in (2, 3)], writes=['r_kT%d' % qb_i])
            bs = 2 + (c % 2)

            def mms(e, qb_i=qb_i, cc=cc, bs=bs):
                r = None
                for h in range(4):
                    r = e.matmul(k.bank[bs][:, h * 128:(h + 1) * 128], lhsT=kT[qb_i][:, h, cc * 128:(cc + 1) * 128],
                                 rhs=qT[qb_i][:, h, cc * 128:(cc + 1) * 128], start=True, stop=True)
                return r
            P.op('pe', mms, reads=['r_qT%d' % qb_i, 'r_kT%d' % qb_i], writes=['bank%d' % bs])
            P.op('dve', lambda e, b=b, bs=bs: e.tensor_tensor(out=sTm[b][:], in0=k.bank[bs][:, :],
                                                            in1=dmT[:].rearrange("p a b -> p (a b)"), op=ALU.mult),
                 reads=['bank%d' % bs] + ['r_dmT%d' % h for h in range(4)], writes=['r_sTm%d' % b])
            P.op('pool', lambda e, b=b, qb_i=qb_i, cc=cc: e.tensor_tensor(out=qf[b][:], in0=qT[qb_i][:, :, cc * 128:(cc + 1) * 128],
                                                                         in1=qsf[0:64, :, :], op=ALU.mult),
                 reads=['r_qT%d' % qb_i] + ['r_qsf%d' % h for h in range(4)], writes=['r_qf%d' % b])
            P.op('pool', lambda e, b=b, qb_i=qb_i, cc=cc: e.tensor_tensor(out=qb[b][:], in0=qT[qb_i][:, :, cc * 128:(cc + 1) * 128],
                                                                         in1=qsb[0:64, :, :], op=ALU.mult),
                 reads=['r_qT%d' % qb_i] + ['r_qsb%d' % h for h in range(4)], writes=['r_qb%d' % b])
            bo = 4 + (c % 2)

            def mmo(e, b=b, c=c, bo=bo):
                r = None
                for h in range(4):
                    oap = k.bank[bo][:, h * 64:(h + 1) * 64]
                    e.matmul(oap, lhsT=sTm[b][:, h * 128:(h + 1) * 128], rhs=vtm[:, c, h * 64:(h + 1) * 64], start=True, stop=False)
                    e.matmul(oap, lhsT=qf[b][:, h, :], rhs=Sf[:, c, h * 64:(h + 1) * 64], start=False, stop=False)
                    r = e.matmul(oap, lhsT=qb[b][:, h, :], rhs=Sb_[:, c, h * 64:(h + 1) * 64], start=False, stop=True)
                return r
            P.op('pe', mmo, reads=['r_sTm%d' % b, 'r_qf%d' % b, 'r_qb%d' % b, 'r_vtm_%d' % half, 'r_S0_%d' % c, 'r_S1_%d' % c],
                 writes=['bank%d' % bo])
            m = sm[b]
            mk = lambda n: 'r_%s%d' % (n, b)
            P.op('act', lambda e, b=b, bo=bo: e.copy(out=o[b][:], in_=k.bank[bo][:, 0:256]), reads=['bank%d' % bo], writes=[mk('o')])
            P.op('act', lambda e, b=b: e.activation(out=sq[b][:], in_=o[b][:], func=AF.Square), reads=[mk('o')], writes=[mk('sq')])
            P.op('act', lambda e, b=b, c=c: e.activation(out=sg[b][:], in_=gtm[:, c, :], func=AF.Silu),
                 reads=['r_gtm_%d' % half], writes=[mk('sg')])
            P.op('dve', lambda e, b=b, m=m: e.tensor_reduce(out=m['s1'][:], in_=o[b][:].rearrange("p (a b) -> p a b", a=4), axis=AX.X, op=ALU.add),
                 reads=[mk('o')], writes=[mk('s1')])
            P.op('dve', lambda e, b=b, m=m: e.tensor_reduce(out=m['s2'][:], in_=sq[b][:].rearrange("p (a b) -> p a b", a=4), axis=AX.X, op=ALU.add),
                 reads=[mk('sq')], writes=[mk('s2')])
            P.op('dve', lambda e, m=m: e.tensor_scalar(out=m['mean'][:], in0=m['s1'][:], scalar1=1.0 / 64, scalar2=None, op0=ALU.mult),
                 reads=[mk('s1')], writes=[mk('mean')])
            P.op('dve', lambda e, m=m: e.tensor_tensor(out=m['msq'][:], in0=m['mean'][:], in1=m['mean'][:], op=ALU.mult),
                 reads=[mk('mean')], writes=[mk('msq')])
            P.op('dve', lambda e, m=m: e.scalar_tensor_tensor(out=m['var'][:], in0=m['s2'][:], scalar=1.0 / 64, in1=m['msq'][:],
                                                               op0=ALU.mult, op1=ALU.subtract),
                 reads=[mk('s2'), mk('msq')], writes=[mk('var')])
            P.op('act', lambda e, m=m: e.activation(out=m['sd'][:], in_=m['var'][:], func=AF.Sqrt, bias=k.eps_t[:], scale=1.0),
                 reads=[mk('var'), 'eps_t'], writes=[mk('sd')])
            P.op('dve', lambda e, m=m: e.reciprocal(out=m['rstd'][:], in_=m['sd'][:]), reads=[mk('sd')], writes=[mk('rstd')])
            for h in range(4):
                P.op('dve', lambda e, h=h, b=b, m=m: e.tensor_scalar(
                    out=on[b][:, h * 64:(h + 1) * 64], in0=o[b][:, h * 64:(h + 1) * 64], scalar1=m['mean'][:, h:h + 1],
                    scalar2=m['rstd'][:, h:h + 1], op0=ALU.subtract, op1=ALU.mult),
                    reads=[mk('o'), mk('mean'), mk('rstd')], writes=[mk('on') + '_%d' % h])
            P.op('pool', lambda e, b=b: e.tensor_tensor(out=yst[b][:], in0=on[b][:], in1=sg[b][:], op=ALU.mult),
                 reads=[mk('on') + '_%d' % h for h in range(4)] + [mk('sg')], writes=[mk('yst')])
            P.dma('sp', lambda e, b=b, c=c: e.dma_start(out=dap(k.ycat, c * 128 * 768, [[768, 128], [1, 256]]), in_=yst[b][:]),
                  reads=[mk('yst')], writes=['ycat_r%d' % c])
            P._update(P.res['ycat_r%d' % c]['w'], [mk('yst')], [])


def phase_t(k, l):
    nc, P = k.nc, k.P
    P.barrier()
    EBk = [['EB%d_%d' % (kb, h) for h in range(8)] for kb in range(3)]
    with contextlib.ExitStack() as ps:
        k.EB = [sb(k, "EB%d" % kb, [128, 8, 128], F32, ps) for kb in range(3)]
        k.t_abs = sb(k, "t_abs", [128, 128], F32, ps)
        k.t_msk = sb(k, "t_msk", [128, 128], F32, ps)
        for kb in range(3):
            if kb == 0:
                P.op('dve', lambda e: e.tensor_scalar(out=k.t_abs[:], in0=k.dist[:], scalar1=128.0, scalar2=None, op0=ALU.add),
                     reads=['dist'], writes=['t_abs'])
                P.op('dve', lambda e: e.tensor_scalar(out=k.t_msk[:], in0=k.dist[:], scalar1=0.0, scalar2=None, op0=ALU.is_le),
                     reads=['dist'], writes=['t_msk'])
            elif kb == 1:
                P.op('dve', lambda e: e.tensor_tensor(out=k.t_abs[:], in0=k.relp[:], in1=k.reln[:], op=ALU.add),
                     reads=['relp', 'reln'], writes=['t_abs'])
                P.op('dve', lambda e: e.memset(k.t_msk[:], 1.0), writes=['t_msk'])
            else:
                P.op('dve', lambda e: e.tensor_scalar(out=k.t_abs[:], in0=k.dist[:], scalar1=-1.0, scalar2=128.0, op0=ALU.mult, op1=ALU.add),
                     reads=['dist'], writes=['t_abs'])
                P.op('dve', lambda e: e.tensor_scalar(out=k.t_msk[:], in0=k.dist[:], scalar1=0.0, scalar2=None, op0=ALU.is_ge),
                     reads=['dist'], writes=['t_msk'])
            for h in range(8):
                P.op('act', lambda e, kb=kb, h=h: e.activation(out=k.EB[kb][:, h, :], in_=k.t_abs[:], func=AF.Exp, scale=-(2.0 ** -(h + 1))),
                     reads=['t_abs'], writes=['EB%d_%d' % (kb, h)])
                P.op('dve', lambda e, kb=kb, h=h: e.tensor_tensor(out=k.EB[kb][:, h, :], in0=k.EB[kb][:, h, :], in1=k.t_msk[:], op=ALU.mult),
                     reads=['EB%d_%d' % (kb, h), 't_msk'], writes=['EB%d_%d' % (kb, h)])
        kT = sb(k, "t_kT", [64, 2, T], BF16, ps)
        vA = sb(k, "t_vA", [128, NT, 2, 65], BF16, ps)
        qT = [sb(k, "t_qT%d" % i, [64, 8, 512], BF16, ps) for i in range(2)]
        snk = sb(k, "t_snk", [128, 8], F32, ps)
        ex = [sb(k, "t_ex%d" % i, [128, 512], F32, ps) for i in range(3)]
        pT = [sb(k, "t_pT%d" % i, [128, 512], BF16, ps) for i in range(6)]
        den = [sb(k, "t_den%d" % i, [128, 4], F32, ps) for i in range(2)]
        rec = [sb(k, "t_rec%d" % i, [128, 4], F32, ps) for i in range(2)]
        yst = [sb(k, "t_yst%d" % i, [128, 512], BF16, ps) for i in range(2)]
        for half in range(2):
            P.dma('sp', lambda e, half=half: e.dma_start(
                out=kT[:, :, half * 2048:(half + 1) * 2048], in_=dap(k.fm, 1280 * T + half * 2048, [[T, 64], [64 * T, 2], [1, 2048]])),
                reads=['fm10_%d' % b for b in range(half * 4, half * 4 + 4)], writes=['t_kT%d' % half])
            for kvh in range(2):
                P.dma('sp', lambda e, half=half, kvh=kvh: e.dma_start(
                    out=vA[:, half * 16:(half + 1) * 16, kvh, 0:64],
                    in_=dap(k.tm, half * 16 * 128 * N_TM + 768 + kvh * 64, [[N_TM, 128], [128 * N_TM, 16], [1, 64]])),
                    reads=['tm%d' % ti for ti in range(half * 16, half * 16 + 16)], writes=['t_vA%d_%d' % (half, kvh)])
        P.op('pool', lambda e: e.memset(vA[:, :, :, 64:65], 1.0), writes=['t_vA1s'])
        P.dma('sp', lambda e: e.dma_start(out=snk[:], in_=dap(k.attn_sink, l * 8, [[0, 128], [1, 8]])), writes=['t_snk'])
        P.op('act', lambda e: e.activation(out=snk[:], in_=snk[:], func=AF.Exp), reads=['t_snk'], writes=['t_snk'])
        npT = 0
        nex = 0
        for c in range(NT):
            blk, cc = c // 4, c % 4
            qi = blk % 2
            if cc == 0:
                P.dma('sp', lambda e, blk=blk, qi=qi: e.dma_start(
                    out=qT[qi][:], in_=dap(k.fm, 768 * T + blk * 512, [[T, 64], [64 * T, 8], [1, 512]])),
                    reads=['fm%d_%d' % (fg, blk) for fg in (6, 7, 8, 9)], writes=['t_qT%d' % qi])
            yb = c % 2
            for kvh in range(2):
                kbs = [kb for kb in range(3) if 0 <= c - 1 + kb < NT]
                pts = []
                for kb in kbs:
                    kblk = c - 1 + kb
                    bs = (nex % 3)
                    xi = nex % 3
                    nex += 1
                    pi = npT % 6
                    npT += 1
                    pts.append(pi)
                    P.op('pe', lambda e, bs=bs, kvh=kvh, kblk=kblk, qi=qi, cc=cc: e.matmul(
                        k.bank[bs][:, :], lhsT=kT[:, kvh, kblk * 128:(kblk + 1) * 128],
                        rhs=qT[qi][:, kvh * 4:(kvh + 1) * 4, cc * 128:(cc + 1) * 128], start=True, stop=True),
                        reads=['t_kT%d' % (kblk // 16), 't_qT%d' % qi], writes=['bank%d' % bs])
                    P.op('act', lambda e, bs=bs, xi=xi: e.activation(out=ex[xi][:], in_=k.bank[bs][:, :], func=AF.Exp, scale=0.125),
                         reads=['bank%d' % bs], writes=['t_ex%d' % xi])
                    P.op('dve', lambda e, xi=xi, pi=pi, kb=kb, kvh=kvh: e.tensor_tensor(
                        out=pT[pi][:], in0=ex[xi][:], in1=k.EB[kb][:, kvh * 4:(kvh + 1) * 4, :].rearrange("p a b -> p (a b)"), op=ALU.mult),
                        reads=['t_ex%d' % xi] + EBk[kb], writes=['t_pT%d' % pi])
                bo = 3 + kvh + 2 * (c % 2)

                def mmo(e, kbs=kbs, pts=pts, c=c, kvh=kvh, bo=bo):
                    r = None
                    for g in range(4):
                        for n, (kb, pi) in enumerate(zip(kbs, pts)):
                            kblk = c - 1 + kb
                            r = e.matmul(k.bank[bo][:, g * 65:(g + 1) * 65], lhsT=pT[pi][:, g * 128:(g + 1) * 128],
                                         rhs=vA[:, kblk, kvh, :], start=(n == 0), stop=(n == len(kbs) - 1))
                    return r
                P.op('pe', mmo, reads=['t_pT%d' % pi for pi in pts] + ['t_vA0_0', 't_vA0_1', 't_vA1_0', 't_vA1_1', 't_vA1s'], writes=['bank%d' % bo])
                dk = 't_den%d' % kvh
                P.op('dve', lambda e, bo=bo, kvh=kvh: e.tensor_tensor(
                    out=den[kvh][:], in0=dap(k.bank[bo], 64, [[512, 128], [65, 4]]), in1=snk[:, kvh * 4:(kvh + 1) * 4], op=ALU.add),
                    reads=['bank%d' % bo, 't_snk'], writes=[dk])
                P.op('dve', lambda e, kvh=kvh: e.reciprocal(out=rec[kvh][:], in_=den[kvh][:]), reads=[dk], writes=['t_rec%d' % kvh])
                P.op('dve', lambda e, bo=bo, kvh=kvh, yb=yb: e.tensor_tensor(
                    out=yst[yb][:, kvh * 256:(kvh + 1) * 256].rearrange("p (a b) -> p a b", a=4),
                    in0=dap(k.bank[bo], 0, [[512, 128], [65, 4], [1, 64]]),
                    in1=dap(rec[kvh], 0, [[4, 128], [1, 4], [0, 64]]), op=ALU.mult),
                    reads=['bank%d' % bo, 't_rec%d' % kvh], writes=['t_yst%d_%d' % (yb, kvh)])
            P.dma('sp', lambda e, yb=yb, c=c: e.dma_start(out=dap(k.ycat, c * 128 * 768 + 256, [[768, 128], [1, 512]]), in_=yst[yb][:]),
                  reads=['t_yst%d_0' % yb, 't_yst%d_1' % yb], writes=['ycat_t%d' % c])
            P._update(P.res['ycat_t%d' % c]['w'], ['t_yst%d_0' % yb, 't_yst%d_1' % yb], [])


def phase_o(k, l):
    nc, P = k.nc, k.P
    P.barrier()
    with contextlib.ExitStack() as ps:
        Wo = sb(k, "o_Wo", [128, 8, D], BF16, ps)
        Wst = [sb(k, "o_Wst%d" % i, [128, 8, 256], F32, ps) for i in range(2)]
        g1 = sb(k, "o_g1", [128, D], F32, ps)
        b1 = sb(k, "o_b1", [128, D], F32, ps)
        yc = [sb(k, "o_yc%d" % i, [128, 768], BF16, ps) for i in range(NBA)]
        yT = [sb(k, "o_yT%d" % i, [128, 8, 128], BF16, ps) for i in range(NBA)]
        ht = [sb(k, "o_h%d" % i, [128, D], F32, ps) for i in range(NBA)]
        rt = [sb(k, "o_r%d" % i, [128, D], F32, ps) for i in range(NBA)]
        h1t = [sb(k, "o_h1%d" % i, [128, D], F32, ps) for i in range(NBA)]
        h1bt = [sb(k, "o_h1b%d" % i, [128, D], BF16, ps) for i in range(NBA)]
        scr = [dict(stats=sb(k, "o_stats%d" % i, [128, 2, 6], F32, ps), mv=sb(k, "o_mv%d" % i, [128, 2], F32, ps),
                    sd=sb(k, "o_sd%d" % i, [128, 1], F32, ps), rstd=sb(k, "o_rstd%d" % i, [128, 1], F32, ps),
                    nb=sb(k, "o_nb%d" % i, [128, 1], F32, ps), xn=sb(k, "o_xn%d" % i, [128, D], F32, ps))
               for i in range(NBA)]
        P.dma('sp', lambda e: e.dma_start(out=g1[:], in_=dap(k.ln1_g, l * D, [[0, 128], [1, D]])), writes=['o_g1'])
        P.dma('sp', lambda e: e.dma_start(out=b1[:], in_=dap(k.ln1_b, l * D, [[0, 128], [1, D]])), writes=['o_b1'])
        for c in range(4):
            w = Wst[c % 2]
            wk = 'o_Wst%d' % (c % 2)
            P.dma('sp', lambda e, w=w, c=c: e.dma_start(
                out=w[:], in_=dap(k.w_out, l * D * D + c * 256, [[D, 128], [128 * D, 8], [1, 256]])), writes=[wk])
            if c % 2 == 0:
                P.op('act', lambda e, w=w, c=c: e.copy(out=Wo[:, :, c * 256:(c + 1) * 256], in_=w[:]), reads=[wk], writes=['o_Wo%d' % c])
            else:
                P.op('dve', lambda e, w=w, c=c: e.tensor_copy(out=Wo[:, :, c * 256:(c + 1) * 256], in_=w[:]), reads=[wk], writes=['o_Wo%d' % c])
        Wkeys = ['o_Wo%d' % c for c in range(4)]

        def o_load(ti):
            b = ti % NBA
            tok0 = ti * 128
            P.dma('sp', lambda e: e.dma_start(out=yc[b][:], in_=dap(k.ycat, tok0 * 768, [[768, 128], [1, 768]])),
                  reads=['ycat_r%d' % ti, 'ycat_t%d' % ti], writes=['o_yc%d' % b])
            P.dma('sp', lambda e: e.dma_start(out=yT[b][:, 2:4, :], in_=dap(k.yssmT, tok0, [[T, 128], [128 * T, 2], [1, 128]])),
                  reads=['yssmT%d' % ti], writes=['o_yT%d_s' % b])
            P.dma('sp', lambda e: e.dma_start(out=ht[b][:], in_=dap(k.hA, tok0 * D, [[D, 128], [1, D]])),
                  reads=['hA%d' % ti], writes=['o_h%d' % b])
        def o_tile(ti):
            b = ti % NBA
            tok0 = ti * 128

            def tr(e, b=b):
                r = None
                for kk in range(6):
                    r = e.transpose(out=k.bankT[:, kk * 128:(kk + 1) * 128], in_=yc[b][:, kk * 128:(kk + 1) * 128], identity=k.identb[:])
                return r
            P.op('pe', tr, reads=['o_yc%d' % b, 'identb'], writes=['bankT'])
            P.op('dve', lambda e, b=b: e.tensor_copy(out=yT[b][:, 0:2, :], in_=k.bankT[:, 0:256].rearrange("p (a b) -> p a b", a=2)),
                 reads=['bankT'], writes=['o_yT%d_r' % b])
            P.op('dve', lambda e, b=b: e.tensor_copy(out=yT[b][:, 4:8, :], in_=k.bankT[:, 256:768].rearrange("p (a b) -> p a b", a=4)),
                 reads=['bankT'], writes=['o_yT%d_t' % b])
            yield
            for half in range(2):
                bk = (2 * ti + half) % 4

                def mm(e, b=b, half=half, bk=bk):
                    r = None
                    for kk in range(8):
                        r = e.matmul(k.bank[bk][:, :], lhsT=yT[b][:, kk, :], rhs=Wo[:, kk, half * 512:(half + 1) * 512],
                                     start=(kk == 0), stop=(kk == 7))
                    return r
                P.op('pe', mm, reads=['o_yT%d_r' % b, 'o_yT%d_s' % b, 'o_yT%d_t' % b] + Wkeys, writes=['bank%d' % bk])
                yield
                P.op('dve', lambda e, b=b, half=half, bk=bk: e.scalar_tensor_tensor(
                    out=rt[b][:, half * 512:(half + 1) * 512], in0=ht[b][:, half * 512:(half + 1) * 512], scalar=ALPHA,
                    in1=k.bank[bk][:, :], op0=ALU.mult, op1=ALU.add),
                    reads=['o_h%d' % b, 'bank%d' % bk], writes=['o_r%d_%d' % (b, half)])
                yield
            P.res['o_r%d' % b] = P.res['o_r%d_1' % b]
            yield from layer_norm_tile(k, rt[b][:], 'o_r%d' % b, h1t[b][:], 'o_h1%d' % b, g1[:], 'o_g1', b1[:], 'o_b1', 'o%d_' % b, scr[b])
            P._update(P.res['o_h1%d' % b]['w'], ['o_r%d_0' % b, 'o_r%d_1' % b], [])
            P.op('act', lambda e, b=b: e.copy(out=h1bt[b][:], in_=h1t[b][:]), reads=['o_h1%d' % b], writes=['o_h1b%d' % b])
            yield
            P.dma('sp', lambda e, b=b, tok0=tok0: e.dma_start(out=dap(k.h1, tok0 * D, [[D, 128], [1, D]]), in_=h1t[b][:]),
                  reads=['o_h1%d' % b], writes=['h1_%d' % ti])
            P.dma('sp', lambda e, b=b, tok0=tok0: e.dma_start(out=dap(k.h1b, tok0 * D, [[D, 128], [1, D]]), in_=h1bt[b][:]),
                  reads=['o_h1b%d' % b], writes=['h1b_%d' % ti])

        run_window(o_tile, o_load, NT, NBA - 1)


TL = 256
SW = 4
NSET = 4
NCH = T // TL
TWO_PI = 2.0 * math.pi
CW1 = 6.28125
CW2 = TWO_PI - CW1
PI_LO = 3.1415925


def sincos(k, arg, argkey, out_sin, out_cos, outkey, n, scr, tag):
    P = k.P
    x, kf, ki = scr['x'], scr['kf'], scr['ki']
    for out, shift, nm in ((out_sin, 0.0, 's'), (out_cos, 0.5 * math.pi, 'c')):
        t = tag + nm
        P.op('dve', lambda e, shift=shift: e.tensor_scalar(out=x[:, 0:n], in0=arg, scalar1=shift, scalar2=None, op0=ALU.add),
             reads=[argkey], writes=[tag + 'x'])
        P.op('dve', lambda e: e.tensor_scalar(out=kf[:, 0:n], in0=x[:, 0:n], scalar1=1.0 / TWO_PI, scalar2=None, op0=ALU.mult),
             reads=[tag + 'x'], writes=[tag + 'kf'])
        P.op('dve', lambda e: e.tensor_copy(out=ki[:, 0:n], in_=kf[:, 0:n]), reads=[tag + 'kf'], writes=[tag + 'ki'])
        P.op('dve', lambda e: e.tensor_copy(out=kf[:, 0:n], in_=ki[:, 0:n]), reads=[tag + 'ki'], writes=[tag + 'kf'])
        P.op('dve', lambda e: e.scalar_tensor_tensor(out=x[:, 0:n], in0=kf[:, 0:n], scalar=-CW1, in1=x[:, 0:n], op0=ALU.mult, op1=ALU.add),
             reads=[tag + 'kf', tag + 'x'], writes=[tag + 'x'])
        P.op('dve', lambda e: e.scalar_tensor_tensor(out=x[:, 0:n], in0=kf[:, 0:n], scalar=-CW2, in1=x[:, 0:n], op0=ALU.mult, op1=ALU.add),
             reads=[tag + 'kf', tag + 'x'], writes=[tag + 'x'])
        P.op('dve', lambda e: e.tensor_scalar(out=x[:, 0:n], in0=x[:, 0:n], scalar1=-PI_LO, scalar2=PI_LO, op0=ALU.max, op1=ALU.min),
             reads=[tag + 'x'], writes=[tag + 'x'])
        P.op('act', lambda e, out=out: e.activation(out=out, in_=x[:, 0:n], func=AF.Sin), reads=[tag + 'x'], writes=[outkey + nm])


def phase_s(k, l):
    nc, P = k.nc, k.P
    P.barrier()
    with contextlib.ExitStack() as ps:
        def t(name, shape, dt=F32):
            return sb(k, "s_" + name, shape, dt, ps)
        lre, lim, ls = t("lre", [128, 16]), t("lim", [128, 16]), t("ls", [128, 16])
        bre, bim = t("bre", [128, 8, 16]), t("bim", [128, 8, 16])
        cre, cim = t("cre", [128, 16, 16]), t("cim", [128, 16, 16])
        dcol, bglu = t("dcol", [128, 2]), t("bglu", [128, 2])
        wst = t("wst", [128, 2, 256]); wglu = t("wglu", [128, 2, 256], BF16)
        step, aa, th, rr = t("step", [128, 16]), t("aa", [128, 16]), t("th", [128, 16]), t("rr", [128, 16])
        sn, cs, thT, sT, cT = (t(n, [128, 16]) for n in ("sn", "cs", "thT", "sT", "cT"))
        nsT = t("nsT", [128, 16])
        lbr, lbi, nr, d2, inv, cr, ci, tq = (t(n, [128, 16]) for n in ("lbr", "lbi", "nr", "d2", "inv", "cr", "ci", "tq"))
        scr = dict(x=t("scx", [128, TL]), kf=t("sckf", [128, TL]), ki=t("scki", [128, TL], I32))
        irow_i = t("irow_i", [128, TL], I32); irowT = t("irowT", [128, TL])
        arg = t("arg", [128, TL])
        t1, t2, bbr, bbi = (t(n, [128, 16]) for n in ("t1", "t2", "bbr", "bbi"))
        Bpad = [t("Bpad%d" % i, [128, 128]) for i in range(2)]
        LB = [[t("LB%d_%d" % (i, ri), [128, 128], BF16) for ri in range(2)] for i in range(16)]
        LC = [[t("LC%d_%d" % (i, ri), [128, 128], BF16) for ri in range(3)] for i in range(16)]
        SIN = [t("SIN%d" % i, [128, TL]) for i in range(16)]
        COS = [t("COS%d" % i, [128, TL]) for i in range(16)]
        Rt = [t("Rt%d" % i, [128, TL]) for i in range(16)]
        init = [t("init%d" % i, [128, 2]) for i in range(8)]
        tc_ = [t("tc%d" % i, [128, 2]) for i in range(8)]
        uraw = [t("uraw%d" % i, [128, 2, TL], BF16) for i in range(2)]
        uc = [t("uc%d" % i, [128, 2, TL], BF16) for i in range(2)]
        mm_ = [[t("m%d_%d" % (j, i), [128, TL]) for j in range(4)] for i in range(NSET)]
        zin = [t("zin%d" % i, [128, 2, TL]) for i in range(NSET)]
        zz = [t("z%d" % i, [128, 2, TL]) for i in range(NSET)]
        qq = [[t("q%d_%d" % (j, i), [128, TL], BF16) for j in range(4)] for i in range(NSET)]
        yfs = [t("yfs%d" % i, [128, 2, TL]) for i in range(2)]
        ys = [t("ys%d" % i, [128, 2, TL]) for i in range(2)]
        x2 = [t("x2%d" % i, [128, 2, TL]) for i in range(2)]
        gg = [t("g%d" % i, [128, 2, TL], BF16) for i in range(2)]
        sig = [t("sig%d" % i, [128, 2, TL]) for i in range(2)]
        yo = [t("yo%d" % i, [128, 2, TL], BF16) for i in range(2)]

        for tl, src, n, key in ((lre, k.s_lre, 16, 'lre'), (lim, k.s_lim, 16, 'lim'), (ls, k.s_ls, 16, 'ls'),
                                (bre, k.s_bre, 128, 'bre'), (bim, k.s_bim, 128, 'bim'),
                                (cre, k.s_cre, 256, 'cre'), (cim, k.s_cim, 256, 'cim'),
                                (dcol, k.s_d, 2, 'dcol'), (bglu, k.s_bglu, 2, 'bglu')):
            P.dma('sp', lambda e, tl=tl, src=src, n=n: e.dma_start(
                out=tl[:].rearrange("p a b -> p (a b)") if len(tl.shape) == 3 else tl[:],
                in_=dap(src, l * 128 * n, [[n, 128], [1, n]])), writes=['s_' + key])
        P.dma('sp', lambda e: e.dma_start(out=wst[:], in_=dap(k.s_wglu, l * 65536, [[256, 128], [128 * 256, 2], [1, 256]])), writes=['s_wst'])
        P.op('act', lambda e: e.copy(out=wglu[:], in_=wst[:]), reads=['s_wst'], writes=['s_wglu'])
        P.op('pool', lambda e: e.iota(irow_i[:], [[1, TL]], base=0, channel_multiplier=0), writes=['s_irow_i'])
        P.op('dve', lambda e: e.tensor_copy(out=irowT[:], in_=irow_i[:]), reads=['s_irow_i'], writes=['s_irowT'])
        P.op('act', lambda e: e.activation(out=step[:], in_=ls[:], func=AF.Exp), reads=['s_ls'], writes=['s_step'])
        P.op('dve', lambda e: e.tensor_tensor(out=aa[:], in0=lre[:], in1=step[:], op=ALU.mult), reads=['s_lre', 's_step'], writes=['s_aa'])
        P.op('dve', lambda e: e.tensor_tensor(out=th[:], in0=lim[:], in1=step[:], op=ALU.mult), reads=['s_lim', 's_step'], writes=['s_th'])
        P.op('act', lambda e: e.activation(out=rr[:], in_=aa[:], func=AF.Exp), reads=['s_aa'], writes=['s_rr'])
        sincos(k, th[:], 's_th', sn[:], cs[:], 's_th_', 16, scr, 's_sc_')
        P.op('dve', lambda e: e.tensor_scalar(out=thT[:], in0=th[:], scalar1=float(TL), scalar2=None, op0=ALU.mult), reads=['s_th'], writes=['s_thT'])
        sincos(k, thT[:], 's_thT', sT[:], cT[:], 's_thT_', 16, scr, 's_sc_')
        P.op('dve', lambda e: e.tensor_scalar(out=nsT[:], in0=sT[:], scalar1=-1.0, scalar2=None, op0=ALU.mult), reads=['s_thT_s'], writes=['s_nsT'])
        P.op('dve', lambda e: e.tensor_tensor(out=lbr[:], in0=rr[:], in1=cs[:], op=ALU.mult), reads=['s_rr', 's_th_c'], writes=['s_lbr'])
        P.op('dve', lambda e: e.tensor_tensor(out=lbi[:], in0=rr[:], in1=sn[:], op=ALU.mult), reads=['s_rr', 's_th_s'], writes=['s_lbi'])
        P.op('dve', lambda e: e.tensor_scalar(out=nr[:], in0=lbr[:], scalar1=-1.0, scalar2=None, op0=ALU.add), reads=['s_lbr'], writes=['s_nr'])
        P.op('dve', lambda e: e.tensor_tensor(out=d2[:], in0=lre[:], in1=lre[:], op=ALU.mult), reads=['s_lre'], writes=['s_d2'])
        P.op('dve', lambda e: e.tensor_tensor(out=tq[:], in0=lim[:], in1=lim[:], op=ALU.mult), reads=['s_lim'], writes=['s_tq'])
        P.op('dve', lambda e: e.tensor_tensor(out=d2[:], in0=d2[:], in1=tq[:], op=ALU.add), reads=['s_d2', 's_tq'], writes=['s_d2'])
        P.op('dve', lambda e: e.reciprocal(out=inv[:], in_=d2[:]), reads=['s_d2'], writes=['s_inv'])
        P.op('dve', lambda e: e.tensor_tensor(out=cr[:], in0=nr[:], in1=lre[:], op=ALU.mult), reads=['s_nr', 's_lre'], writes=['s_cr'])
        P.op('dve', lambda e: e.tensor_tensor(out=tq[:], in0=lbi[:], in1=lim[:], op=ALU.mult), reads=['s_lbi', 's_lim', 's_d2'], writes=['s_tq'])
        P.op('dve', lambda e: e.tensor_tensor(out=cr[:], in0=cr[:], in1=tq[:], op=ALU.add), reads=['s_cr', 's_tq'], writes=['s_cr'])
        P.op('dve', lambda e: e.tensor_tensor(out=cr[:], in0=cr[:], in1=inv[:], op=ALU.mult), reads=['s_cr', 's_inv'], writes=['s_cr'])
        P.op('dve', lambda e: e.tensor_tensor(out=ci[:], in0=lbi[:], in1=lre[:], op=ALU.mult), reads=['s_lbi', 's_lre'], writes=['s_ci'])
        P.op('dve', lambda e: e.tensor_tensor(out=tq[:], in0=nr[:], in1=lim[:], op=ALU.mult), reads=['s_nr', 's_lim', 's_cr'], writes=['s_tq'])
        P.op('dve', lambda e: e.tensor_tensor(out=ci[:], in0=ci[:], in1=tq[:], op=ALU.subtract), reads=['s_ci', 's_tq'], writes=['s_ci'])
        P.op('dve', lambda e: e.tensor_tensor(out=ci[:], in0=ci[:], in1=inv[:], op=ALU.mult), reads=['s_ci', 's_inv'], writes=['s_ci'])
        for idx in range(16):
            pair = idx // 2
            c0 = 32 * (pair % 4)
            ic = lambda tl, idx=idx: tl[:, idx:idx + 1]
            P.op('dve', lambda e, pair=pair, idx=idx: e.tensor_scalar(out=t1[:], in0=bim[:, pair, :], scalar1=ci[:, idx:idx + 1], scalar2=None, op0=ALU.mult),
                 reads=['s_bim', 's_ci'], writes=['s_t1'])
            P.op('dve', lambda e, pair=pair, idx=idx: e.scalar_tensor_tensor(out=bbr[:], in0=bre[:, pair, :], scalar=cr[:, idx:idx + 1], in1=t1[:],
                                                                              op0=ALU.mult, op1=ALU.subtract),
                 reads=['s_bre', 's_cr', 's_t1'], writes=['s_bbr'])
            P.op('dve', lambda e, pair=pair, idx=idx: e.tensor_scalar(out=t2[:], in0=bre[:, pair, :], scalar1=ci[:, idx:idx + 1], scalar2=None, op0=ALU.mult),
                 reads=['s_bre', 's_ci'], writes=['s_t2'])
            P.op('dve', lambda e, pair=pair, idx=idx: e.scalar_tensor_tensor(out=bbi[:], in0=bim[:, pair, :], scalar=cr[:, idx:idx + 1], in1=t2[:],
                                                                              op0=ALU.mult, op1=ALU.add),
                 reads=['s_bim', 's_cr', 's_t2'], writes=['s_bbi'])
            for ri, (bb, bk_) in enumerate(((bbr, 's_bbr'), (bbi, 's_bbi'))):
                bp = Bpad[ri]
                bpk = 's_Bpad%d' % ri
                P.op('pool', lambda e, bp=bp: e.memset(bp[:], 0.0), writes=[bpk])
                P.op('dve', lambda e, bp=bp, bb=bb, c0=c0: e.tensor_copy(out=bp[0:64, c0:c0 + 16], in_=bb[0:64, :]), reads=[bk_, bpk], writes=[bpk + 'a'])
                P.op('dve', lambda e, bp=bp, bb=bb, c0=c0: e.tensor_copy(out=bp[64:128, c0 + 16:c0 + 32], in_=bb[64:128, :]), reads=[bk_, bpk], writes=[bpk + 'b'])
                bkk = ri
                P.op('pe', lambda e, bp=bp, bkk=bkk: e.transpose(out=k.bank[bkk][:, 0:128], in_=bp[:], identity=k.identf[:]),
                     reads=[bpk, bpk + 'a', bpk + 'b', 'identf'], writes=['bank%d' % bkk])
                P._update(P.res['bank%d' % bkk]['w'], [bpk], [])
                P.op('act', lambda e, idx=idx, ri=ri, bkk=bkk: e.copy(out=LB[idx][ri][:], in_=k.bank[bkk][:, 0:128]),
                     reads=['bank%d' % bkk], writes=['s_LB%d' % idx + '_%d' % ri])
            for ri, (cc_, ck, sgn) in enumerate(((cre, 's_cre', 1.0), (cim, 's_cim', -1.0), (cre, 's_cre', -1.0))):
                lc = LC[idx][ri]
                lck = 's_LC%d_%d' % (idx, ri)
                P.op('pool', lambda e, lc=lc: e.memset(lc[:], 0.0), writes=[lck])
                P.op('dve', lambda e, lc=lc, cc_=cc_, c0=c0, sgn=sgn, idx=idx: e.tensor_scalar(
                    out=lc[0:64, c0:c0 + 16], in0=cc_[0:64, idx, :], scalar1=sgn, scalar2=None, op0=ALU.mult), reads=[ck, lck], writes=[lck + 'a'])
                P.op('dve', lambda e, lc=lc, cc_=cc_, c0=c0, sgn=sgn, idx=idx: e.tensor_scalar(
                    out=lc[64:128, c0 + 16:c0 + 32], in0=cc_[64:128, idx, :], scalar1=sgn, scalar2=None, op0=ALU.mult), reads=[ck, lck], writes=[lck + 'b'])
            P.op('dve', lambda e, idx=idx: e.tensor_scalar(out=arg[:], in0=irowT[:], scalar1=th[:, idx:idx + 1], scalar2=None, op0=ALU.mult),
                 reads=['s_irowT', 's_th', 's_tab%d_s' % (idx - 1), 's_tab%d_c' % (idx - 1)], writes=['s_arg'])
            sincos(k, arg[:], 's_arg', SIN[idx][:], COS[idx][:], 's_tab%d_' % idx, TL, scr, 's_sc_')
            P.op('pool', lambda e, idx=idx: e.tensor_copy(out=Rt[idx][:], in_=bc(rr, idx, TL, 16)), reads=['s_rr'], writes=['s_Rt%d' % idx])
        nu = 0
        for d in range(2):
            for pr in range(8):
                P.op('dve', lambda e, pr=pr: e.memset(init[pr][:], 0.0), writes=['s_init%d' % pr])
            for n in range(NCH):
                cf = n if d == 0 else NCH - 1 - n
                ub = n % 2
                P.dma('sp', lambda e, ub=ub, cf=cf: e.dma_start(out=uraw[ub][:], in_=dap(k.fm, 512 * T + cf * TL, [[T, 128], [128 * T, 2], [1, TL]])),
                      reads=['fm4_%d' % (cf * TL // 512), 'fm5_%d' % (cf * TL // 512)], writes=['s_uraw%d' % ub])
                if d == 0:
                    ucur, uck = uraw[ub], 's_uraw%d' % ub
                else:
                    for ft in range(2):
                        P.op('pool', lambda e, ub=ub, ft=ft: e.tensor_copy(out=uc[ub][:, ft, :], in_=dap(uraw[ub], ft * TL + TL - 1, [[2 * TL, 128], [-1, TL]])),
                             reads=['s_uraw%d' % ub], writes=['s_uc%d_%d' % (ub, ft)])
                    P.res['s_uc%d' % ub] = P.res['s_uc%d_1' % ub]
                    ucur, uck = uc[ub], 's_uc%d' % ub
                    P.dma('sp', lambda e, ub=ub, cf=cf: e.dma_start(out=yfs[ub][:], in_=dap(k.yf, cf * TL, [[T, 128], [128 * T, 2], [1, TL]])),
                          reads=['yf_%d' % cf], writes=['s_yfs%d' % ub])
                def unit(pr, nu_, n=n, d=d, ub=ub, ucur=ucur, uck=uck):
                    idx = pr * 2 + d
                    ft = pr // 4
                    u2 = nu_ % NSET
                    u3 = nu_ % NSET
                    bb = nu_ % 4
                    ucks = [uck] if d == 0 else ['s_uc%d_0' % ub, 's_uc%d_1' % ub]

                    def mmb(e, idx=idx, ft=ft, bb=bb, ucur=ucur):
                        e.matmul(k.bank[bb][:, 0:TL], lhsT=LB[idx][0][:], rhs=ucur[:, ft, :], start=True, stop=True)
                        return e.matmul(k.bank[bb][:, TL:2 * TL], lhsT=LB[idx][1][:], rhs=ucur[:, ft, :], start=True, stop=True)
                    P.op('pe', mmb, reads=ucks + ['s_LB%d_0' % idx, 's_LB%d_1' % idx], writes=['bank%d' % bb])
                    yield
                    m = mm_[u2]
                    mk = lambda j: 's_m%d_%d' % (j, u2)
                    tabs = ['s_tab%d_s' % idx, 's_tab%d_c' % idx]
                    br_, bi_ = k.bank[bb][:, 0:TL], k.bank[bb][:, TL:2 * TL]
                    P.op('dve', lambda e, m=m, br_=br_, idx=idx: e.tensor_tensor(out=m[0][:], in0=br_, in1=COS[idx][:], op=ALU.mult),
                         reads=['bank%d' % bb] + tabs, writes=[mk(0)])
                    yield
                    P.op('dve', lambda e, m=m, bi_=bi_, idx=idx: e.tensor_tensor(out=m[1][:], in0=bi_, in1=SIN[idx][:], op=ALU.mult),
                         reads=['bank%d' % bb] + tabs, writes=[mk(1)])
                    yield
                    P.op('dve', lambda e, m=m, bi_=bi_, idx=idx: e.tensor_tensor(out=m[2][:], in0=bi_, in1=COS[idx][:], op=ALU.mult),
                         reads=['bank%d' % bb] + tabs, writes=[mk(2)])
                    yield
                    P.op('dve', lambda e, m=m, br_=br_, idx=idx: e.tensor_tensor(out=m[3][:], in0=br_, in1=SIN[idx][:], op=ALU.mult),
                         reads=['bank%d' % bb] + tabs, writes=[mk(3)])
                    yield
                    zk = 's_zin%d' % u2
                    P.op('pool', lambda e, m=m, u2=u2: e.tensor_tensor(out=zin[u2][:, 0, :], in0=m[0][:], in1=m[1][:], op=ALU.add),
                         reads=[mk(0), mk(1)], writes=[zk + 'r'])
                    yield
                    P.op('pool', lambda e, m=m, u2=u2: e.tensor_tensor(out=zin[u2][:, 1, :], in0=m[2][:], in1=m[3][:], op=ALU.subtract),
                         reads=[mk(2), mk(3)], writes=[zk + 'i'])
                    yield
                    zt = zz[u3]
                    ztk = 's_z%d' % u3
                    ik = 's_init%d' % pr
                    P.op('dve', lambda e, zt=zt, u2=u2, idx=idx, pr=pr: e.tensor_tensor_scan(
                        out=zt[:, 0, :], data0=Rt[idx][:], data1=zin[u2][:, 0, :], initial=init[pr][:, 0:1], op0=ALU.mult, op1=ALU.add),
                        reads=[zk + 'r', 's_Rt%d' % idx, ik, ik + 'r', ik + 'i'], writes=[ztk + 'r'])
                    yield
                    P.op('dve', lambda e, zt=zt, u2=u2, idx=idx, pr=pr: e.tensor_tensor_scan(
                        out=zt[:, 1, :], data0=Rt[idx][:], data1=zin[u2][:, 1, :], initial=init[pr][:, 1:2], op0=ALU.mult, op1=ALU.add),
                        reads=[zk + 'i', 's_Rt%d' % idx, ik, ik + 'r', ik + 'i'], writes=[ztk + 'i'])
                    yield
                    if n < NCH - 1:
                        zl_r, zl_i = zt[:, 0, TL - 1:TL], zt[:, 1, TL - 1:TL]
                        tk = 's_tc%d' % pr
                        P.op('act', lambda e, pr=pr, idx=idx, zl_i=zl_i: e.activation(out=tc_[pr][:, 0:1], in_=zl_i, func=AF.Identity, scale=nsT[:, idx:idx + 1]),
                             reads=[ztk + 'i', 's_nsT'], writes=[tk + 'a'])
                        yield
                        P.op('act', lambda e, pr=pr, idx=idx, zl_r=zl_r: e.activation(out=tc_[pr][:, 1:2], in_=zl_r, func=AF.Identity, scale=sT[:, idx:idx + 1]),
                             reads=[ztk + 'r', 's_thT_s'], writes=[tk + 'b'])
                        yield
                        P.op('act', lambda e, pr=pr, idx=idx, zl_r=zl_r: e.activation(out=init[pr][:, 0:1], in_=zl_r, func=AF.Identity,
                                                                                     scale=cT[:, idx:idx + 1], bias=tc_[pr][:, 0:1]),
                             reads=[ztk + 'r', 's_thT_c', tk + 'a', ik], writes=[ik + 'r'])
                        yield
                        P.op('act', lambda e, pr=pr, idx=idx, zl_i=zl_i: e.activation(out=init[pr][:, 1:2], in_=zl_i, func=AF.Identity,
                                                                                     scale=cT[:, idx:idx + 1], bias=tc_[pr][:, 1:2]),
                             reads=[ztk + 'i', 's_thT_c', tk + 'b', ik], writes=[ik + 'i'])
                        P.res[ik] = P.res[ik + 'i']
                        P._update(P.res[ik]['w'], [], [])
                        yield
                    q = qq[u3]
                    qk = lambda j: 's_q%d_%d' % (j, u3)
                    P.op('pool', lambda e, q=q, zt=zt, idx=idx: e.tensor_tensor(out=q[0][:], in0=zt[:, 0, :], in1=COS[idx][:], op=ALU.mult),
                         reads=[ztk + 'r'] + tabs, writes=[qk(0)])
                    yield
                    P.op('dve', lambda e, q=q, zt=zt, idx=idx: e.tensor_tensor(out=q[1][:], in0=zt[:, 1, :], in1=SIN[idx][:], op=ALU.mult),
                         reads=[ztk + 'i'] + tabs, writes=[qk(1)])
                    yield
                    P.op('dve', lambda e, q=q, zt=zt, idx=idx: e.tensor_tensor(out=q[2][:], in0=zt[:, 0, :], in1=SIN[idx][:], op=ALU.mult),
                         reads=[ztk + 'r'] + tabs, writes=[qk(2)])
                    yield
                    P.op('dve', lambda e, q=q, zt=zt, idx=idx: e.tensor_tensor(out=q[3][:], in0=zt[:, 1, :], in1=COS[idx][:], op=ALU.mult),
                         reads=[ztk + 'i'] + tabs, writes=[qk(3)])
                    yield
                    yb = 4 + ft
                    first = (pr % 4 == 0)
                    last = (pr % 4 == 3)

                    def mmc(e, idx=idx, q=q, yb=yb, first=first, last=last):
                        e.matmul(k.bank[yb][:, 0:TL], lhsT=LC[idx][0][:], rhs=q[0][:], start=first, stop=False)
                        e.matmul(k.bank[yb][:, 0:TL], lhsT=LC[idx][2][:], rhs=q[1][:], start=False, stop=False)
                        e.matmul(k.bank[yb][:, 0:TL], lhsT=LC[idx][1][:], rhs=q[2][:], start=False, stop=False)
                        return e.matmul(k.bank[yb][:, 0:TL], lhsT=LC[idx][1][:], rhs=q[3][:], start=False, stop=last)
                    lckeys = ['s_LC%d_%d%s' % (idx, ri, sfx) for ri in range(3) for sfx in ('a', 'b')]
                    qkeys = [qk(j) for j in range(4)]
                    if first:
                        P.op('pe', mmc, reads=qkeys + lckeys, writes=['bank%d' % yb])
                        yield
                    else:
                        P.op('pe', mmc, reads=qkeys + ['bank%d' % yb] + lckeys, writes=['bank%d_acc' % yb])
                        yield
                        P.res['bank%d' % yb]['w'] = P.res['bank%d_acc' % yb]['w']
                    if last:
                        if d == 0:
                            P.op('act', lambda e, ub=ub, ft=ft, yb=yb: e.copy(out=ys[ub][:, ft, :], in_=k.bank[yb][:, 0:TL]),
                                 reads=['bank%d' % yb], writes=['s_ys%d_%d' % (ub, ft)])
                            yield
                        else:
                            P.op('dve', lambda e, ub=ub, ft=ft, yb=yb: e.tensor_tensor(
                                out=ys[ub][:, ft, :], in0=yfs[ub][:, ft, :], in1=dap(k.bank[yb], TL - 1, [[512, 128], [-1, TL]]), op=ALU.add),
                                reads=['bank%d' % yb, 's_yfs%d' % ub], writes=['s_ys%d_%d' % (ub, ft)])
                            yield

                for p0 in range(0, 8, SW):
                    gens = [unit(p0 + i_, nu + i_) for i_ in range(SW)]
                    nu += SW
                    while gens:
                        for g_ in list(gens):
                            try:
                                next(g_)
                            except StopIteration:
                                gens.remove(g_)
                if d == 0:
                    P.dma('sp', lambda e, ub=ub, cf=cf: e.dma_start(out=dap(k.yf, cf * TL, [[T, 128], [128 * T, 2], [1, TL]]), in_=ys[ub][:]),
                          reads=['s_ys%d_0' % ub, 's_ys%d_1' % ub], writes=['yf_%d' % cf])
                    P._update(P.res['yf_%d' % cf]['w'], ['s_ys%d_0' % ub, 's_ys%d_1' % ub], [])
                    continue
                ysk = ['s_ys%d_0' % ub, 's_ys%d_1' % ub]
                for ft in range(2):
                    P.op('dve', lambda e, ub=ub, ft=ft: e.scalar_tensor_tensor(
                        out=ys[ub][:, ft, :], in0=uraw[ub][:, ft, :], scalar=dcol[:, ft:ft + 1], in1=ys[ub][:, ft, :], op0=ALU.mult, op1=ALU.add),
                        reads=['s_uraw%d' % ub, 's_dcol', ysk[ft]], writes=[ysk[ft]])
                P.op('pool', lambda e, ub=ub: e.tensor_tensor(out=x2[ub][:], in0=ys[ub][:], in1=ys[ub][:], op=ALU.mult), reads=ysk, writes=['s_x2%d' % ub])
                P.op('pool', lambda e, ub=ub: e.tensor_scalar(out=x2[ub][:], in0=x2[ub][:], scalar1=0.044715, scalar2=1.0, op0=ALU.mult, op1=ALU.add),
                     reads=['s_x2%d' % ub], writes=['s_x2%d' % ub])
                P.op('pool', lambda e, ub=ub: e.tensor_tensor(out=x2[ub][:], in0=x2[ub][:], in1=ys[ub][:], op=ALU.mult), reads=['s_x2%d' % ub] + ysk, writes=['s_x2%d' % ub])
                P.op('act', lambda e, ub=ub: e.activation(out=x2[ub][:], in_=x2[ub][:], func=AF.Tanh, scale=math.sqrt(2.0 / math.pi)),
                     reads=['s_x2%d' % ub], writes=['s_x2%d' % ub])
                P.op('pool', lambda e, ub=ub: e.tensor_scalar(out=x2[ub][:], in0=x2[ub][:], scalar1=0.5, scalar2=0.5, op0=ALU.mult, op1=ALU.add),
                     reads=['s_x2%d' % ub], writes=['s_x2%d' % ub])
                P.op('pool', lambda e, ub=ub: e.tensor_tensor(out=ys[ub][:], in0=x2[ub][:], in1=ys[ub][:], op=ALU.mult), reads=['s_x2%d' % ub] + ysk, writes=['s_yg%d' % ub])
                P._update(P.res['s_yg%d' % ub]['w'], [], ysk)
                P.op('act', lambda e, ub=ub: e.copy(out=gg[ub][:], in_=ys[ub][:]), reads=['s_yg%d' % ub] + ysk, writes=['s_gg%d' % ub])
                for fo in range(2):
                    gb = 6

                    def mmg(e, ub=ub, fo=fo, gb=gb):
                        e.matmul(k.bank[gb][:, 0:TL], lhsT=wglu[:, 0, fo * 128:(fo + 1) * 128], rhs=gg[ub][:, 0, :], start=True, stop=False)
                        return e.matmul(k.bank[gb][:, 0:TL], lhsT=wglu[:, 1, fo * 128:(fo + 1) * 128], rhs=gg[ub][:, 1, :], start=False, stop=True)
                    P.op('pe', mmg, reads=['s_gg%d' % ub, 's_wglu'], writes=['bank%d' % gb])
                    P.op('act', lambda e, ub=ub, fo=fo, gb=gb: e.activation(out=sig[ub][:, fo, :], in_=k.bank[gb][:, 0:TL], func=AF.Sigmoid,
                                                                           bias=bglu[:, fo:fo + 1], scale=1.0),
                         reads=['bank%d' % gb, 's_bglu'], writes=['s_sig%d_%d' % (ub, fo)])
                P.op('dve', lambda e, ub=ub: e.tensor_tensor(out=yo[ub][:], in0=ys[ub][:], in1=sig[ub][:], op=ALU.mult),
                     reads=['s_yg%d' % ub, 's_sig%d_0' % ub, 's_sig%d_1' % ub] + ysk, writes=['s_yo%d' % ub])
                P.dma('sp', lambda e, ub=ub, cf=cf: e.dma_start(out=dap(k.yssmT, cf * TL, [[T, 128], [128 * T, 2], [1, TL]]), in_=yo[ub][:]),
                      reads=['s_yo%d' % ub], writes=['yssmT_c%d' % cf])
                P._update(P.res['yssmT_c%d' % cf]['w'], ['s_yo%d' % ub], [])
        for ti in range(NT):
            P.res['yssmT%d' % ti] = P.res['yssmT_c%d' % (ti * 128 // TL)]


CAP = 512
NEXP = 16
NBIS = 30


def phase_f(k, l, last):
    nc, P = k.nc, k.P
    P.barrier()
    dst = k.out if last else k.hA
    with contextlib.ExitStack() as ps:
        def tp(name, shape, dt=F32):
            return sb(k, "f_" + name, shape, dt, ps)
        idx_t = [[tp("idx%d_%d" % (e_, cb), [128, 1], I32) for cb in range(4)] for e_ in range(16)]
        gate_all = tp("gate_all", [128, 16, 4])
        p2 = contextlib.ExitStack()

        def t(name, shape, dt=F32):
            return sb(k, "f_" + name, shape, dt, p2)
        rw = t("rw", [128, 8, 16])
        aff = t("aff", [128, NT, 16])
        ones = t("ones", [128, 128])
        ustr = t("ustr", [128, 128])
        lo, thr, pc, gew = t("lo", [128, 16]), t("thr", [128, 16]), t("pc", [128, 16]), t("gew", [128, 16])
        cmpb = t("cmpb", [128, NT, 16])
        met = t("met", [128, 16, NT])
        incl = t("incl", [128, 16, NT])
        rpat_i = t("rpat_i", [128, 16, NT], I32)
        rpat = t("rpat", [128, 16, NT])
        posm = t("posm", [128, 16, NT])
        io_i = t("io_i", [128, 512], I32)
        io512 = t("io512", [128, 512], mybir.dt.float16)
        tvi = t("tvi", [128, NT, 16], I32)
        r1 = t("r1", [128, NT, 16])
        tv = t("tv", [128, NT, 16, 5], BF16)
        idf = t("idf", [128, 4])
        slot = t("slot", [128, 4, 5])
        oh = [t("oh%d" % i, [128, 512], BF16) for i in range(3)]
        with contextlib.ExitStack() as p1:
            h1t = [sb(k, "f_h1r%d" % i, [128, D], F32, p1) for i in range(3)]
            h1T = [sb(k, "f_h1T%d" % i, [128, 8, 128], F32, p1) for i in range(2)]
            ex = [sb(k, "f_ex%d" % i, [128, 16], F32, p1) for i in range(2)]
            sm = [dict((n, sb(k, "f_%s%d" % (n, i), [128, 1], F32, p1)) for n in ('mx', 'sum', 'rs')) for i in range(2)]
            P.dma('sp', lambda e: e.dma_start(out=rw[:], in_=dap(k.router_w, l * D * 16, [[16, 128], [128 * 16, 8], [1, 16]])), writes=['f_rw'])
            def f1_load(ti):
                b = ti % 3
                P.dma('sp', lambda e: e.dma_start(out=h1t[b][:], in_=dap(k.h1, ti * 128 * D, [[D, 128], [1, D]])),
                      reads=['h1_%d' % ti], writes=['f_h1r%d' % b])
            def f1_tile(ti):
                b = ti % 2
                b3 = ti % 3
                for hh in range(2):
                    bk = (2 * ti + hh) % 4

                    def tr(e, b3=b3, hh=hh, bk=bk):
                        r = None
                        for j in range(4):
                            kk = hh * 4 + j
                            r = e.transpose(out=k.bank[bk][:, j * 128:(j + 1) * 128], in_=h1t[b3][:, kk * 128:(kk + 1) * 128], identity=k.identf[:])
                        return r
                    P.op('pe', tr, reads=['f_h1r%d' % b3, 'identf'], writes=['bank%d' % bk])
                    yield
                    P.op('dve' if hh == 0 else 'act', lambda e, b=b, hh=hh, bk=bk: (e.tensor_copy if hh == 0 else e.copy)(
                        out=h1T[b][:, hh * 4:(hh + 1) * 4, :], in_=k.bank[bk][:, :].rearrange("p (a b) -> p a b", a=4)),
                        reads=['bank%d' % bk], writes=['f_h1T%d_%d' % (b, hh)])
                    yield
                lb = 4 + (ti % 2)

                def mml(e, b=b, lb=lb):
                    r = None
                    for kk in range(8):
                        r = e.matmul(k.bank[lb][:, 0:16], lhsT=h1T[b][:, kk, :], rhs=rw[:, kk, :], start=(kk == 0), stop=(kk == 7))
                    return r
                P.op('pe', mml, reads=['f_h1T%d_0' % b, 'f_h1T%d_1' % b, 'f_rw'], writes=['bank%d' % lb])
                yield
                m = sm[b]
                P.op('dve', lambda e, m=m, lb=lb: e.tensor_reduce(out=m['mx'][:], in_=k.bank[lb][:, 0:16], axis=AX.X, op=ALU.max, negate=True),
                     reads=['bank%d' % lb], writes=['f_mx%d' % b])
                yield
                P.op('act', lambda e, m=m, lb=lb, b=b: e.activation(out=ex[b][:], in_=k.bank[lb][:, 0:16], func=AF.Exp, bias=m['mx'][:], scale=1.0,
                                                                   accum_out=m['sum'][:]),
                     reads=['bank%d' % lb, 'f_mx%d' % b], writes=['f_ex%d' % b, 'f_sum%d' % b])
                yield
                P.op('dve', lambda e, m=m: e.reciprocal(out=m['rs'][:], in_=m['sum'][:]), reads=['f_sum%d' % b], writes=['f_rs%d' % b])
                yield
                P.op('dve', lambda e, m=m, b=b, ti=ti: e.tensor_scalar(out=aff[:, ti, :], in0=ex[b][:], scalar1=m['rs'][:], scalar2=None, op0=ALU.mult),
                     reads=['f_ex%d' % b, 'f_rs%d' % b], writes=['f_aff%d' % ti])
                yield

            run_window(f1_tile, f1_load, NT, 2)
        affk = ['f_aff%d' % ti for ti in range(NT)]
        P.op('dve', lambda e: e.memset(ones[:], 1.0), writes=['f_ones'])
        P.op('dve', lambda e: e.tensor_scalar(out=ustr[:], in0=k.dist[:], scalar1=0.0, scalar2=None, op0=ALU.is_gt), reads=['dist'], writes=['f_ustr'])
        P.op('dve', lambda e: e.memset(lo[:], 0.0), writes=['f_lo'])
        for it in range(NBIS):
            w = 0.5 ** (it + 1)
            P.op('dve', lambda e, w=w: e.tensor_scalar(out=thr[:], in0=lo[:], scalar1=w, scalar2=None, op0=ALU.add), reads=['f_lo'], writes=['f_thr'])
            P.op('dve', lambda e: e.tensor_tensor(out=cmpb[:], in0=aff[:], in1=dap(thr, 0, [[16, 128], [0, NT], [1, 16]]), op=ALU.is_ge),
                 reads=affk + ['f_thr'], writes=['f_cmpb'])
            P.op('dve', lambda e: e.tensor_reduce(out=pc[:], in_=cmpb[:].rearrange("p t e -> p e t"), axis=AX.X, op=ALU.add),
                 reads=['f_cmpb'], writes=['f_pc'])
            P.op('pe', lambda e: e.matmul(k.bank[0][:, 0:16], lhsT=ones[:], rhs=pc[:], start=True, stop=True),
                 reads=['f_ones', 'f_pc'], writes=['bank0'])
            P.op('dve', lambda e, w=w: e.tensor_scalar(out=gew[:], in0=k.bank[0][:, 0:16], scalar1=CAP - 0.5, scalar2=w, op0=ALU.is_ge, op1=ALU.mult),
                 reads=['bank0'], writes=['f_gew'])
            P.op('dve', lambda e: e.tensor_tensor(out=lo[:], in0=lo[:], in1=gew[:], op=ALU.add), reads=['f_lo', 'f_gew'], writes=['f_lo'])
        P.op('dve', lambda e: e.tensor_tensor(out=met[:], in0=aff[:].rearrange("p t e -> p e t"), in1=dap(lo, 0, [[16, 128], [1, 16], [0, NT]]), op=ALU.is_ge),
             reads=affk + ['f_lo'], writes=['f_met'])
        P.op('pool', lambda e: e.iota(rpat_i[:].rearrange("p a b -> p (a b)"), [[0, 16], [1, NT]], base=0, channel_multiplier=0), writes=['f_rpat_i'])
        P.op('dve', lambda e: e.tensor_copy(out=rpat[:], in_=rpat_i[:]), reads=['f_rpat_i'], writes=['f_rpat'])
        P.op('dve', lambda e: e.tensor_scalar(out=rpat[:], in0=rpat[:], scalar1=0.0, scalar2=None, op0=ALU.is_gt), reads=['f_rpat'], writes=['f_rpat'])
        P.op('dve', lambda e: e.tensor_tensor_scan(out=incl[:].rearrange("p a b -> p (a b)"), data0=rpat[:].rearrange("p a b -> p (a b)"),
                                                    data1=met[:].rearrange("p a b -> p (a b)"), initial=0.0, op0=ALU.mult, op1=ALU.add),
             reads=['f_rpat', 'f_met'], writes=['f_incl'])
        P.op('dve', lambda e: e.tensor_copy(out=pc[:], in_=incl[:, :, NT - 1]), reads=['f_incl'], writes=['f_pc'])
        P.op('pe', lambda e: e.matmul(k.bank[0][:, 0:16], lhsT=ustr[:], rhs=pc[:], start=True, stop=True), reads=['f_ustr', 'f_pc'], writes=['bank0'])
        P.op('dve', lambda e: e.tensor_tensor(out=posm[:], in0=incl[:], in1=met[:], op=ALU.subtract), reads=['f_incl', 'f_met'], writes=['f_posm'])
        P.op('dve', lambda e: e.tensor_tensor(out=posm[:], in0=posm[:], in1=dap(k.bank[0], 0, [[512, 128], [1, 16], [0, NT]]), op=ALU.add),
             reads=['f_posm', 'bank0'], writes=['f_posm'])
        P.op('dve', lambda e: e.scalar_tensor_tensor(out=posm[:], in0=posm[:], scalar=1.0, in1=met[:], op0=ALU.add, op1=ALU.mult),
             reads=['f_posm', 'f_met'], writes=['f_posm'])
        P.op('dve', lambda e: e.tensor_scalar(out=posm[:], in0=posm[:], scalar1=-1.0, scalar2=None, op0=ALU.add), reads=['f_posm'], writes=['f_posm'])
        P.op('pool', lambda e: e.iota(io_i[:], [[1, 512]], base=0, channel_multiplier=0), writes=['f_io_i'])
        P.op('dve', lambda e: e.tensor_copy(out=io512[:], in_=io_i[:]), reads=['f_io_i'], writes=['f_io512'])
        P.op('pool', lambda e: e.iota(tvi[:].rearrange("p a b -> p (a b)"), [[1, NT], [0, 16]], base=0, channel_multiplier=0), writes=['f_tvi'])
        P.op('dve', lambda e: e.tensor_copy(out=tv[:, :, :, 0], in_=tvi[:]), reads=['f_tvi'], writes=['f_tv0'])
        P.op('pool', lambda e: e.iota(tvi[:].rearrange("p a b -> p (a b)"), [[0, NT], [0, 16]], base=0, channel_multiplier=1), reads=['f_tv0'], writes=['f_tvi'])
        P.op('dve', lambda e: e.tensor_copy(out=tv[:, :, :, 1], in_=tvi[:]), reads=['f_tvi'], writes=['f_tv1'])
        P.op('dve', lambda e: e.tensor_copy(out=tv[:, :, :, 2], in_=aff[:]), reads=affk, writes=['f_tv2'])
        P.op('dve', lambda e: e.tensor_tensor(out=r1[:], in0=aff[:], in1=tv[:, :, :, 2], op=ALU.subtract), reads=affk + ['f_tv2'], writes=['f_r1'])
        P.op('dve', lambda e: e.tensor_copy(out=tv[:, :, :, 3], in_=r1[:]), reads=['f_r1'], writes=['f_tv3'])
        P.op('dve', lambda e: e.tensor_tensor(out=r1[:], in0=r1[:], in1=tv[:, :, :, 3], op=ALU.subtract), reads=['f_r1', 'f_tv3'], writes=['f_r1'])
        P.op('dve', lambda e: e.tensor_copy(out=tv[:, :, :, 4], in_=r1[:]), reads=['f_r1'], writes=['f_tv4'])
        tvk = ['f_tv%d' % i for i in range(5)]
        noh = 0
        for ex_ in range(NEXP):
            for ti in range(NT):
                o = noh % 3
                noh += 1
                P.op('dve', lambda e, o=o, ex_=ex_, ti=ti: e.tensor_scalar(out=oh[o][:], in0=io512[:], scalar1=posm[:, ex_, ti:ti + 1], scalar2=None, op0=ALU.is_equal),
                     reads=['f_io512', 'f_posm'], writes=['f_oh%d' % o])

                def mms(e, o=o, ex_=ex_, ti=ti):
                    r = None
                    for cb in range(4):
                        r = e.matmul(k.bank[cb][:, 0:5], lhsT=oh[o][:, cb * 128:(cb + 1) * 128], rhs=tv[:, ti, ex_, :],
                                     start=(ti == 0), stop=(ti == NT - 1))
                    return r
                P.op('pe', mms, reads=['f_oh%d' % o] + tvk, writes=['bank0_3'] if ti else ['bank0', 'bank1', 'bank2', 'bank3', 'bank0_3'])
            for cb in range(4):
                P.op('act', lambda e, cb=cb: e.copy(out=slot[:, cb, :], in_=k.bank[cb][:, 0:5]), reads=['bank0_3'], writes=['f_slot%d' % cb])
                P._update(P.res['f_slot%d' % cb]['w'], ['bank%d' % cb], [])
            slk = ['f_slot%d' % cb for cb in range(4)]
            P.op('dve', lambda e: e.scalar_tensor_tensor(out=idf[:], in0=slot[:, :, 0], scalar=128.0, in1=slot[:, :, 1], op0=ALU.mult, op1=ALU.add),
                 reads=slk, writes=['f_idf'])
            P.op('dve', lambda e, ex_=ex_: e.tensor_reduce(out=gate_all[:, ex_, :], in_=slot[:, :, 2:5], axis=AX.X, op=ALU.add),
                 reads=slk, writes=['f_gate%d' % ex_])
            for cb in range(4):
                P.op('dve', lambda e, ex_=ex_, cb=cb: e.tensor_copy(out=idx_t[ex_][cb][:], in_=idf[:, cb:cb + 1]), reads=['f_idf'], writes=['f_idx%d_%d' % (ex_, cb)])
            P.res['f_idx%d' % ex_] = P.res['f_idx%d_3' % ex_]
            P._update(P.res['f_idx%d' % ex_]['w'], slk + ['f_idf'], [])
        if DBG_F == 1:
            P.dma('sp', lambda e: e.dma_start(out=dap(k.dbg_gate, 0, [[64, 128], [1, 64]]), in_=gate_all[:].rearrange("p a b -> p (a b)")),
                  reads=['f_gate%d' % i for i in range(16)], writes=['dbg_gate'])
            P.dma('sp', lambda e: e.dma_start(out=dap(k.dbg_aff, 0, [[512, 128], [1, 512]]), in_=aff[:].rearrange("p a b -> p (a b)")),
                  reads=affk, writes=['dbg_aff'])
            P.dma('sp', lambda e: e.dma_start(out=dap(k.dbg_posm, 0, [[512, 128], [1, 512]]), in_=posm[:].rearrange("p a b -> p (a b)")),
                  reads=['f_posm'], writes=['dbg_posm'])
            finish(k)
            p2.close()
            return
        P.barrier()
        p2.close()
        t = tp
        zt = t("zt", [128, D])
        P.op('pool', lambda e: e.memset(zt[:], 0.0), writes=['f_zt'])
        for ti in range(NT):
            P.dma('sp', lambda e, ti=ti: e.dma_start(out=dap(k.ffn, ti * 128 * D, [[D, 128], [1, D]]), in_=zt[:]), reads=['f_zt'], writes=['ffn_z%d' % ti])
        zero_toks = [P.res['ffn_z%d' % ti]['w'] for ti in range(NT)]
        with contextlib.ExitStack() as p5:
            def t5(name, shape, dt=F32):
                return sb(k, "f_" + name, shape, dt, p5)
            xs = t5("xs", [128, 4, D], BF16)
            xsT = [t5("xsT%d" % i, [128, 8, 512], BF16) for i in range(2)]
            stgA = [t5("stgA%d" % i, [128, 8, 256]) for i in range(3)]
            stgB = [t5("stgB%d" % i, [128, 2, D]) for i in range(2)]
            wgb = [t5("wgb%d" % i, [128, 8, 512], BF16) for i in range(2)]
            wub = [t5("wub%d" % i, [128, 8, 512], BF16) for i in range(2)]
            wdb = t5("wdb", [128, 16, D], BF16)
            hdn = t5("hdn", [128, 16, 512], BF16)
            sl = [t5("sl%d" % i, [128, 512]) for i in range(2)]
            ost = [t5("ost%d" % i, [128, D]) for i in range(2)]
            cnt = dict(stg=0, stgB=0, cast=0, psg=0, ps2=0, ost=0)

            def gather(ex_):
                xb = ex_ % 2
                for cb in range(4):
                    P.dma('pool', lambda e, cb=cb, ex_=ex_: e.indirect_dma_start(
                        out=xs[:, cb, :], out_offset=None, in_=k.h1b[:, :],
                        in_offset=bass.IndirectOffsetOnAxis(ap=idx_t[ex_][cb][:, :], axis=0)),
                        reads=['f_idx%d' % ex_] + ['h1b_%d' % ti for ti in range(NT)], writes=['f_xs%d' % cb])

                    def trx(e, cb=cb):
                        r = None
                        for kk in range(8):
                            r = e.transpose(out=k.bankT[:, kk * 128:(kk + 1) * 128], in_=xs[:, cb, kk * 128:(kk + 1) * 128], identity=k.identb[:])
                        return r
                    P.op('pe', trx, reads=['f_xs%d' % cb, 'identb'], writes=['bankT'])
                    P.op('dve', lambda e, cb=cb, xb=xb: e.tensor_copy(out=xsT[xb][:, :, cb * 128:(cb + 1) * 128],
                                                                    in_=k.bankT[:].rearrange("p (a b) -> p a b", a=8)),
                         reads=['bankT'], writes=['f_xsT%d_%d' % (xb, cb)])

            def cast_engine():
                ce = 'act' if (cnt['cast'] % 3 != 2) else 'dve'
                cnt['cast'] += 1
                return ce

            def load_gu_piece(g, pi):
                ex_, j = g // 4, g % 4
                wb = g % 2
                wsrc, wdst, wkey = ((k.w_gate, wgb[wb], 'f_wgb%d' % wb), (k.w_up, wub[wb], 'f_wub%d' % wb))[pi // 2]
                hh = pi % 2
                sg = cnt['stg'] % 3
                cnt['stg'] += 1
                off = ((l * NEXP + ex_) * D) * 2048 + j * 512 + hh * 256
                P.dma('sp', lambda e: e.dma_start(out=stgA[sg][:], in_=dap(wsrc, off, [[2048, 128], [128 * 2048, 8], [1, 256]])),
                      writes=['f_stg%d' % sg])
                ce = cast_engine()
                P.op(ce, lambda e: (e.copy if ce == 'act' else e.tensor_copy)(out=wdst[:, :, hh * 256:(hh + 1) * 256], in_=stgA[sg][:]),
                     reads=['f_stg%d' % sg], writes=[wkey + '_%d' % hh])

            def load_d(g):
                ex_, j = g // 4, g % 4
                for hh in range(2):
                    sg = cnt['stgB'] % 2
                    cnt['stgB'] += 1
                    off = ((l * NEXP + ex_) * 2048 + j * 512 + hh * 256) * D
                    P.dma('sp', lambda e, sg=sg, off=off: e.dma_start(
                        out=stgB[sg][:], in_=dap(k.w_down, off, [[D, 128], [128 * D, 2], [1, D]])),
                        writes=['f_stgB%d' % sg])
                    ce = cast_engine()
                    kt0 = j * 4 + hh * 2
                    P.op(ce, lambda e, sg=sg, kt0=kt0, ce=ce: (e.copy if ce == 'act' else e.tensor_copy)(
                        out=wdb[:, kt0:kt0 + 2, :], in_=stgB[sg][:]),
                        reads=['f_stgB%d' % sg], writes=['f_wdb%d' % (kt0 // 2)])

            def phase1(g, fis):
                ex_, j = g // 4, g % 4
                wb = g % 2
                xb = ex_ % 2
                xk = ['f_xsT%d_%d' % (xb, cb) for cb in range(4)]
                wgk = ['f_wgb%d_0' % wb, 'f_wgb%d_1' % wb]
                wuk = ['f_wub%d_0' % wb, 'f_wub%d_1' % wb]
                for fi in fis:
                    ftile = j * 4 + fi
                    g_b = 3 + (cnt['psg'] % 2)
                    u_b = 5 + (cnt['psg'] % 2)
                    sb_ = cnt['psg'] % 2
                    cnt['psg'] += 1

                    def mg(e, g_b=g_b, wb=wb, fi=fi, xb=xb):
                        r = None
                        for kk in range(8):
                            r = e.matmul(k.bank[g_b][:, :], lhsT=wgb[wb][:, kk, fi * 128:(fi + 1) * 128], rhs=xsT[xb][:, kk, :], start=(kk == 0), stop=(kk == 7))
                        return r

                    def mu(e, u_b=u_b, wb=wb, fi=fi, xb=xb):
                        r = None
                        for kk in range(8):
                            r = e.matmul(k.bank[u_b][:, :], lhsT=wub[wb][:, kk, fi * 128:(fi + 1) * 128], rhs=xsT[xb][:, kk, :], start=(kk == 0), stop=(kk == 7))
                        return r
                    P.op('pe', mg, reads=wgk + xk, writes=['bank%d' % g_b])
                    P.op('pe', mu, reads=wuk + xk, writes=['bank%d' % u_b])
                    P.op('act', lambda e, sb_=sb_, g_b=g_b: e.activation(out=sl[sb_][:], in_=k.bank[g_b][:, :], func=AF.Silu),
                         reads=['bank%d' % g_b], writes=['f_sl%d' % sb_])
                    P.op('dve', lambda e, sb_=sb_, u_b=u_b, ftile=ftile: e.tensor_tensor(out=hdn[:, ftile, :], in0=k.bank[u_b][:, :], in1=sl[sb_][:], op=ALU.mult),
                         reads=['bank%d' % u_b, 'f_sl%d' % sb_], writes=['f_hdn%d' % ftile])

            def phase2(ex_):
                hk = ['f_hdn%d' % i for i in range(16)]
                wdk = ['f_wdb%d' % i for i in range(8)]
                for cb in range(4):
                    ob = cnt['ost'] % 2
                    cnt['ost'] += 1
                    for half in range(2):
                        pb = cnt['ps2'] % 3
                        cnt['ps2'] += 1

                        def md(e, pb=pb, cb=cb, half=half):
                            r = None
                            for kk in range(16):
                                r = e.matmul(k.bank[pb][:, :], lhsT=hdn[:, kk, cb * 128:(cb + 1) * 128], rhs=wdb[:, kk, half * 512:(half + 1) * 512],
                                             start=(kk == 0), stop=(kk == 15))
                            return r
                        P.op('pe', md, reads=hk + wdk, writes=['bank%d' % pb])
                        P.op('dve', lambda e, pb=pb, ob=ob, half=half, cb=cb, ex_=ex_: e.tensor_scalar(
                            out=ost[ob][:, half * 512:(half + 1) * 512], in0=k.bank[pb][:, :], scalar1=gate_all[:, ex_, cb:cb + 1], scalar2=None, op0=ALU.mult),
                            reads=['bank%d' % pb, 'f_gate%d' % ex_], writes=['f_ost%d_%d' % (ob, half)])
                    P.dma('pool', lambda e, ob=ob, cb=cb, ex_=ex_: e.indirect_dma_start(
                        out=k.ffn[:, :], out_offset=bass.IndirectOffsetOnAxis(ap=idx_t[ex_][cb][:, :], axis=0), in_=ost[ob][:], in_offset=None,
                        compute_op=ALU.add),
                        reads=['f_ost%d_0' % ob, 'f_ost%d_1' % ob, 'f_idx%d' % ex_] + (['ffn_z%d' % ti for ti in range(NT)] if ex_ == 0 and cb == 0 else []),
                        writes=['ffn_acc'])
                    P._update(P.res['ffn_acc']['w'], ['f_ost%d_0' % ob, 'f_ost%d_1' % ob], [])

            NG = NEXP * 4
            gather(0)
            for pi in range(4):
                load_gu_piece(0, pi)
            load_d(0)
            for g in range(NG):
                ex_, j = g // 4, g % 4
                if j == 2 and ex_ + 1 < NEXP:
                    gather(ex_ + 1)
                for fi in range(4):
                    if g + 1 < NG:
                        load_gu_piece(g + 1, fi)
                    phase1(g, [fi])
                if j == 3:
                    phase2(ex_)
                if g + 1 < NG:
                    load_d(g + 1)
        P.barrier()
        g2 = t("g2", [128, D]); b2 = t("b2", [128, D])
        h1t = [t("h1%d" % i, [128, D]) for i in range(3)]
        ft_ = [t("ft%d" % i, [128, D]) for i in range(3)]
        rt = [t("r%d" % i, [128, D]) for i in range(3)]
        h2t = [t("h2%d" % i, [128, D]) for i in range(3)]
        scr = [dict(stats=t("stats%d" % i, [128, 2, 6]), mv=t("mv%d" % i, [128, 2]), sd=t("sd%d" % i, [128, 1]), rstd=t("rstd%d" % i, [128, 1]),
                    nb=t("nb%d" % i, [128, 1]), xn=t("xn%d" % i, [128, D])) for i in range(3)]
        P.dma('sp', lambda e: e.dma_start(out=g2[:], in_=dap(k.ln2_g, l * D, [[0, 128], [1, D]])), writes=['f_g2'])
        P.dma('sp', lambda e: e.dma_start(out=b2[:], in_=dap(k.ln2_b, l * D, [[0, 128], [1, D]])), writes=['f_b2'])

        def f6_load(ti):
            b = ti % 3
            P.dma('sp', lambda e: e.dma_start(out=h1t[b][:], in_=dap(k.h1, ti * 128 * D, [[D, 128], [1, D]])),
                  reads=['h1_%d' % ti], writes=['f_h1%d' % b])
            P.dma('sp', lambda e: e.dma_start(out=ft_[b][:], in_=dap(k.ffn, ti * 128 * D, [[D, 128], [1, D]])),
                  reads=['ffn_acc'], writes=['f_ft%d' % b])
        def f6_tile(ti):
            b = ti % 3
            tok0 = ti * 128
            P.op('dve', lambda e, b=b: e.scalar_tensor_tensor(out=rt[b][:], in0=h1t[b][:], scalar=ALPHA, in1=ft_[b][:], op0=ALU.mult, op1=ALU.add),
                 reads=['f_h1%d' % b, 'f_ft%d' % b], writes=['f_r%d' % b])
            yield
            yield from layer_norm_tile(k, rt[b][:], 'f_r%d' % b, h2t[b][:], 'f_h2%d' % b, g2[:], 'f_g2', b2[:], 'f_b2', 'f%d_' % b, scr[b])
            P.dma('sp', lambda e, b=b, tok0=tok0: e.dma_start(out=dap(dst, tok0 * D, [[D, 128], [1, D]]), in_=h2t[b][:]),
                  reads=['f_h2%d' % b], writes=['hA%d' % ti])

        run_window(f6_tile, f6_load, NT, 2)


def finish(k):
    P = k.P
    for i, c in enumerate(P.dcnt):
        if c > 0:
            P._wait('sp', (('d', i), c))


def prep_inputs(inputs):
    w_in = np.asarray(inputs["w_in"])
    sl = lambda a, b: list(range(a, b))
    tm_cols = sl(256, 512) + sl(512, 768) + sl(768, 1024) + sl(1920, 2048)
    fm_cols = sl(0, 256) + sl(256, 512) + sl(1024, 1280) + sl(1280, 1792) + sl(1792, 1920)
    w_perm = np.ascontiguousarray(w_in[:, :, tm_cols + fm_cols])
    shared = {"ln_in_g": np.ascontiguousarray(inputs["ln_in_g"]), "ln_in_b": np.ascontiguousarray(inputs["ln_in_b"]),
              "w_in": w_perm,
              "ret_theta": np.ascontiguousarray(np.asarray(inputs["ret_theta"]).reshape(DEPTH, 8)),
              "attn_sink": np.ascontiguousarray(inputs["attn_sink"]),
              "w_out": np.ascontiguousarray(inputs["w_out"]),
              "ln1_g": np.ascontiguousarray(inputs["ln1_g"]), "ln1_b": np.ascontiguousarray(inputs["ln1_b"]),
              "ln2_g": np.ascontiguousarray(inputs["ln2_g"]), "ln2_b": np.ascontiguousarray(inputs["ln2_b"])}
    L = DEPTH
    A = lambda n: np.asarray(inputs[n], dtype=np.float32)
    rl = lambda x: np.ascontiguousarray(x.reshape(L, 2, 8, 2, 64).transpose(0, 3, 4, 2, 1).reshape(L, 128, 16))
    shared["s_lre"] = rl(A("ssm_lambda_re")); shared["s_lim"] = rl(A("ssm_lambda_im"))
    lsx = A("ssm_log_step").reshape(L, 2, 8, 2).transpose(0, 3, 2, 1)
    shared["s_ls"] = np.ascontiguousarray(np.broadcast_to(lsx[:, :, None, :, :], (L, 2, 64, 8, 2)).reshape(L, 128, 16))
    rb = lambda x: np.ascontiguousarray(x.reshape(L, 8, 2, 64, 16).transpose(0, 2, 3, 1, 4).reshape(L, 128, 8, 16))
    shared["s_bre"] = rb(A("ssm_b_re")); shared["s_bim"] = rb(A("ssm_b_im"))
    rc = lambda x: np.ascontiguousarray(x.reshape(L, 2, 8, 2, 16, 64).transpose(0, 3, 5, 2, 1, 4).reshape(L, 128, 16, 16))
    shared["s_cre"] = rc(A("ssm_c_re")); shared["s_cim"] = rc(A("ssm_c_im"))
    rd = lambda x: np.ascontiguousarray(x.reshape(L, 2, 128).transpose(0, 2, 1))
    shared["s_d"] = rd(A("ssm_d")); shared["s_bglu"] = rd(A("ssm_b_glu"))
    shared["s_wglu"] = np.ascontiguousarray(A("ssm_w_glu"))
    shared["router_w"] = np.ascontiguousarray(A("router_w"))
    shared["exp_w_gate"] = np.ascontiguousarray(A("exp_w_gate"))
    shared["exp_w_up"] = np.ascontiguousarray(A("exp_w_up"))
    shared["exp_w_down"] = np.ascontiguousarray(A("exp_w_down"))
    return shared


def kernel(**inputs):
    shared = prep_inputs(inputs)
    nc = build()
    x = np.asarray(inputs["x"])
    in_maps = []
    for c in range(NCORES):
        m = dict(shared)
        m["x"] = np.ascontiguousarray(x[c])
        in_maps.append(m)
    res = run_bass_kernel_spmd(nc, in_maps, core_ids=list(range(NCORES)))
    return np.stack([res.results[c]["out"] for c in range(NCORES)], axis=0)
```
